# Optimizing a Trainium2 kernel written in Bass

```python
import jax, jax.numpy as jnp
from jax import lax
import numpy as np


D_MODEL = 1024
BATCH = 4
SEQ = 8192
DEPTH = 4

HEAD_DIM = 64
ROPE_THETA = 10000.0
EPS = 1e-6
NEG = -1e30
TINY = 1e-30
FORCE_SCORE = 1e9

DIL_PAIRS = ((128, 1), (512, 4), (2048, 16))
N_DIL = 3
A_HEADS_PER_GROUP = 8
A_HEADS = N_DIL * A_HEADS_PER_GROUP
A_BLOCK = 128
A_OUT = A_HEADS_PER_GROUP * HEAD_DIM

B_Q_HEADS = 8
B_KV_HEADS = 2
B_GQA = B_Q_HEADS // B_KV_HEADS
CMP_LEN = 32
CMP_STRIDE = 16
CMP_HIDDEN = 256
SEL_LEN = 64
N_SELECT = 16
WIN = 512
B_BLOCK = 128
B_OUT = B_Q_HEADS * HEAD_DIM

D_FF = 2816
CONV_W = 3

A_QKV = 3 * A_HEADS * HEAD_DIM
B_QD = B_Q_HEADS * HEAD_DIM
B_KV = 3 * 2 * B_KV_HEADS * HEAD_DIM
B_GATE = 3 * B_Q_HEADS
MERGE = 2 * D_MODEL
N_IN = A_QKV + B_QD + B_KV + B_GATE + MERGE
IN_SPLITS = (A_QKV, A_QKV + B_QD, A_QKV + B_QD + B_KV, A_QKV + B_QD + B_KV + B_GATE)

kernel_name = 'hybrid_dilated_nsa_convffn'


def rmsnorm(x, g):
    xf = x.astype(jnp.float32)
    y = xf * lax.rsqrt(jnp.mean(xf * xf, axis=-1, keepdims=True) + EPS)
    return (y * g.astype(jnp.float32)).astype(x.dtype)


def rope_tables(pos):
    half = HEAD_DIM // 2
    inv_freq = ROPE_THETA ** (-jnp.arange(half, dtype=jnp.float32) / half)
    ang = pos.astype(jnp.float32)[:, None] * inv_freq[None, :]
    return jnp.cos(ang), jnp.sin(ang)


def apply_rope(x, cos, sin):
    half = HEAD_DIM // 2
    shape = (cos.shape[0],) + (1,) * (x.ndim - 3) + (half,)
    c = cos.reshape(shape).astype(x.dtype)
    s = sin.reshape(shape).astype(x.dtype)
    x1, x2 = x[..., :half], x[..., half:]
    return jnp.concatenate([x1 * c - x2 * s, x2 * c + x1 * s], axis=-1)


def banded_attention(q, k, v, max_dist, block):
    n, L, hk, g, dh = q.shape
    nb = L // block
    n_prev = -(-max_dist // block)
    qb = q.reshape(n, nb, block, hk, g, dh)

    def band(t):
        tb = t.reshape(n, nb, block, hk, dh)
        tp = jnp.pad(tb, ((0, 0), (n_prev, 0), (0, 0), (0, 0), (0, 0)))
        return jnp.concatenate([tp[:, i:i + nb] for i in range(n_prev + 1)], axis=2)

    kb, vb = band(k), band(v)
    s = jnp.einsum('nbqhgd,nbkhd->nbhgqk', qb, kb, preferred_element_type=jnp.float32) * (dh ** -0.5)
    qpos = jnp.arange(nb)[:, None] * block + jnp.arange(block)[None, :]
    kpos = (jnp.arange(nb)[:, None] - n_prev) * block + jnp.arange((n_prev + 1) * block)[None, :]
    dist = qpos[:, :, None] - kpos[:, None, :]
    valid = (dist >= 0) & (dist <= max_dist) & (kpos[:, None, :] >= 0)
    s = jnp.where(valid[None, :, None, None], s, NEG)
    lse = jax.nn.logsumexp(s, axis=-1)
    p = jnp.exp(s - lse[..., None])
    o = jnp.einsum('nbhgqk,nbkhd->nbqhgd', p.astype(v.dtype), vb)
    return o.reshape(n, L, hk, g, dh), lse.transpose(0, 1, 4, 2, 3).reshape(n, L, hk, g)


def dilated_mixer(q, k, v):
    B, S, _, dh = q.shape
    hpg = A_HEADS_PER_GROUP
    outs, lses = [], []
    for gi, (window, dil) in enumerate(DIL_PAIRS):
        hs = slice(gi * hpg, (gi + 1) * hpg)
        L = S // dil
        Lp = -(-L // A_BLOCK) * A_BLOCK

        def to_sub(t):
            t = t.reshape(B, L, dil, hpg, dh).transpose(0, 2, 1, 3, 4).reshape(B * dil, L, hpg, dh)
            return jnp.pad(t, ((0, 0), (0, Lp - L), (0, 0), (0, 0)))

        o, lse = banded_attention(to_sub(q[:, :, hs])[:, :, :, None], to_sub(k[:, :, hs]),
                                  to_sub(v[:, :, hs]), window // dil, A_BLOCK)
        o = o[:, :L, :, 0].reshape(B, dil, L, hpg, dh).transpose(0, 2, 1, 3, 4).reshape(B, S, hpg, dh)
        lse = lse[:, :L, :, 0].reshape(B, dil, L, hpg).transpose(0, 2, 1, 3).reshape(B, S, hpg)
        outs.append(o)
        lses.append(lse)
    wts = jax.nn.softmax(jnp.stack(lses, axis=0), axis=0)
    o = jnp.sum(wts[..., None] * jnp.stack(outs, axis=0).astype(jnp.float32), axis=0)
    return o.astype(q.dtype).reshape(B, S, A_OUT)


def compress(t, pos_emb, w1, w2):
    B, S, H, dh = t.shape
    r = CMP_LEN // CMP_STRIDE
    nc = S // CMP_STRIDE
    n_cmp = nc - r + 1
    ch = t.reshape(B, nc, CMP_STRIDE, H, dh)
    blk = jnp.concatenate([ch[:, i:i + n_cmp] for i in range(r)], axis=2)
    blk = blk + pos_emb[None, None, :, None, :].astype(t.dtype)
    flat = blk.transpose(0, 1, 3, 2, 4).reshape(B, n_cmp, H, CMP_LEN * dh)
    return jax.nn.gelu(flat @ w1) @ w2


def nsa_mixer(q, k_cmp, v_cmp, k_sel, v_sel, k_win, v_win, gate_logits):
    B, S, H, G, dh = q.shape
    scale = dh ** -0.5
    n_cmp = k_cmp.shape[1]
    n_slc = S // SEL_LEN
    n_sel = min(N_SELECT, n_slc)
    r = CMP_LEN // CMP_STRIDE
    nq = S // B_BLOCK
    cmp_end = jnp.arange(n_cmp) * CMP_STRIDE + CMP_LEN - 1
    blk_ids = jnp.arange(n_slc)
    kb_sel = k_sel.reshape(B, n_slc, SEL_LEN, H, dh).transpose(0, 3, 1, 2, 4)
    vb_sel = v_sel.reshape(B, n_slc, SEL_LEN, H, dh).transpose(0, 3, 1, 2, 4)
    b_idx = jnp.arange(B)[:, None, None, None]
    h_idx = jnp.arange(H)[None, None, :, None]
    q_blocks = q.reshape(B, nq, B_BLOCK, H, G, dh).transpose(1, 0, 2, 3, 4, 5)

    def step(args):
        qb, bi = args
        t = bi * B_BLOCK + jnp.arange(B_BLOCK)
        s = jnp.einsum('bqhgd,bchd->bqhgc', qb, k_cmp, preferred_element_type=jnp.float32) * scale
        valid = (cmp_end[None, :] <= t[:, None])[None, :, None, None, :]
        s_m = jnp.where(valid, s, NEG)
        m = jnp.max(s_m, axis=-1, keepdims=True)
        e = jnp.where(valid, jnp.exp(s_m - m), 0.0)
        p = e / jnp.maximum(jnp.sum(e, axis=-1, keepdims=True), TINY)
        o_cmp = jnp.einsum('bqhgc,bchd->bqhgd', p.astype(v_cmp.dtype), v_cmp)
        imp = jnp.sum(p, axis=3)
        chunk = sum(jnp.pad(imp, ((0, 0), (0, 0), (0, 0), (i, r - 1 - i))) for i in range(r))
        p_slc = chunk.reshape(B, B_BLOCK, H, n_slc, SEL_LEN // CMP_STRIDE).sum(-1)
        cur = t // SEL_LEN
        forced = (blk_ids[None] == 0) | (blk_ids[None] == cur[:, None]) | (blk_ids[None] == cur[:, None] - 1)
        causal_blk = blk_ids[None] * SEL_LEN <= t[:, None]
        score = jnp.where(forced[None, :, None], FORCE_SCORE,
                          jnp.where(causal_blk[None, :, None], p_slc, -1.0))
        _, sel = lax.top_k(score, n_sel)
        kg = kb_sel[b_idx, h_idx, sel].reshape(B, B_BLOCK, H, n_sel * SEL_LEN, dh)
        vg = vb_sel[b_idx, h_idx, sel].reshape(B, B_BLOCK, H, n_sel * SEL_LEN, dh)
        kpos = (sel[..., None] * SEL_LEN + jnp.arange(SEL_LEN)).reshape(B, B_BLOCK, H, n_sel * SEL_LEN)
        s2 = jnp.einsum('bqhgd,bqhkd->bqhgk', qb, kg, preferred_element_type=jnp.float32) * scale
        s2 = jnp.where((kpos <= t[None, :, None, None])[:, :, :, None, :], s2, NEG)
        p2 = jax.nn.softmax(s2, axis=-1)
        o_sel = jnp.einsum('bqhgk,bqhkd->bqhgd', p2.astype(vg.dtype), vg)
        return o_cmp, o_sel

    o_cmp, o_sel = lax.map(step, (q_blocks, jnp.arange(nq)))
    o_cmp = o_cmp.transpose(1, 0, 2, 3, 4, 5).reshape(B, S, H, G, dh)
    o_sel = o_sel.transpose(1, 0, 2, 3, 4, 5).reshape(B, S, H, G, dh)
    o_win, _ = banded_attention(q, k_win, v_win, WIN - 1, B_BLOCK)
    g = jax.nn.sigmoid(gate_logits.reshape(B, S, 3, H, G, 1))
    o = g[:, :, 0] * o_cmp + g[:, :, 1] * o_sel + g[:, :, 2] * o_win
    return o.reshape(B, S, B_OUT)


def causal_dwconv(t, w, b):
    S = t.shape[1]
    tp = jnp.pad(t, ((0, 0), (CONV_W - 1, 0), (0, 0)))
    out = b
    for i in range(CONV_W):
        out = out + w[i] * tp[:, i:i + S]
    return out


def setup_inputs(seed: int = 0) -> dict:
    key = jax.random.key(seed)
    ks = jax.random.split(key, 18)

    def nrm(k, shape, fan):
        return jax.random.normal(k, shape, jnp.float32) * fan ** -0.5

    def gain(k, shape):
        return 1.0 + 0.01 * jax.random.normal(k, shape, jnp.float32)

    return {
        'x': jax.random.normal(ks[0], (BATCH, SEQ, D_MODEL), jnp.float32),
        'norm_mix_g': gain(ks[1], (DEPTH, D_MODEL)),
        'w_in': nrm(ks[2], (DEPTH, D_MODEL, N_IN), D_MODEL),
        'a_q_g': gain(ks[3], (DEPTH, HEAD_DIM)),
        'a_k_g': gain(ks[4], (DEPTH, HEAD_DIM)),
        'b_q_g': gain(ks[5], (DEPTH, HEAD_DIM)),
        'b_k_g': gain(ks[6], (DEPTH, 3, HEAD_DIM)),
        'cmp_pos': 0.02 * jax.random.normal(ks[7], (DEPTH, 2, CMP_LEN, HEAD_DIM), jnp.float32),
        'cmp_w1': nrm(ks[8], (DEPTH, 2, CMP_LEN * HEAD_DIM, CMP_HIDDEN), CMP_LEN * HEAD_DIM),
        'cmp_w2': nrm(ks[9], (DEPTH, 2, CMP_HIDDEN, HEAD_DIM), CMP_HIDDEN),
        'w_proj_a': nrm(ks[10], (DEPTH, A_OUT, D_MODEL), A_OUT),
        'w_proj_b': nrm(ks[11], (DEPTH, B_OUT, D_MODEL), B_OUT),
        'w_out': nrm(ks[12], (DEPTH, D_MODEL, D_MODEL), D_MODEL),
        'norm_ffn_g': gain(ks[13], (DEPTH, D_MODEL)),
        'w_up': nrm(ks[14], (DEPTH, D_MODEL, 2 * D_FF), D_MODEL),
        'conv_w': nrm(ks[15], (DEPTH, CONV_W, D_FF), CONV_W),
        'conv_b': 0.01 * jax.random.normal(ks[16], (DEPTH, D_FF), jnp.float32),
        'w_down': nrm(ks[17], (DEPTH, D_FF, D_MODEL), D_FF),
    }


def reference(x, norm_mix_g, w_in, a_q_g, a_k_g, b_q_g, b_k_g, cmp_pos, cmp_w1, cmp_w2,
              w_proj_a, w_proj_b, w_out, norm_ffn_g, w_up, conv_w, conv_b, w_down):
    B, S, _ = x.shape
    cos, sin = rope_tables(jnp.arange(S))
    n_cmp = S // CMP_STRIDE - CMP_LEN // CMP_STRIDE + 1
    cos_c, sin_c = rope_tables(jnp.arange(n_cmp) * CMP_STRIDE + CMP_LEN - 1)
    for l in range(DEPTH):
        h = rmsnorm(x, norm_mix_g[l])
        proj = h @ w_in[l]
        a_qkv, b_q, b_kv, b_gate, merge = jnp.split(proj, IN_SPLITS, axis=-1)
        a_qkv = a_qkv.reshape(B, S, 3, A_HEADS, HEAD_DIM)
        qa = apply_rope(rmsnorm(a_qkv[:, :, 0], a_q_g[l]), cos, sin)
        ka = apply_rope(rmsnorm(a_qkv[:, :, 1], a_k_g[l]), cos, sin)
        out_a = dilated_mixer(qa, ka, a_qkv[:, :, 2])
        qb = apply_rope(rmsnorm(b_q.reshape(B, S, B_KV_HEADS, B_GQA, HEAD_DIM), b_q_g[l]), cos, sin)
        b_kv = b_kv.reshape(B, S, 3, 2, B_KV_HEADS, HEAD_DIM)
        k_cmp = compress(b_kv[:, :, 0, 0], cmp_pos[l, 0], cmp_w1[l, 0], cmp_w2[l, 0])
        k_cmp = apply_rope(rmsnorm(k_cmp, b_k_g[l, 0]), cos_c, sin_c)
        v_cmp = compress(b_kv[:, :, 0, 1], cmp_pos[l, 1], cmp_w1[l, 1], cmp_w2[l, 1])
        k_sel = apply_rope(rmsnorm(b_kv[:, :, 1, 0], b_k_g[l, 1]), cos, sin)
        k_win = apply_rope(rmsnorm(b_kv[:, :, 2, 0], b_k_g[l, 2]), cos, sin)
        out_b = nsa_mixer(qb, k_cmp, v_cmp, k_sel, b_kv[:, :, 1, 1], k_win, b_kv[:, :, 2, 1], b_gate)
        gate = jax.nn.sigmoid(merge.reshape(B, S, 2, D_MODEL))
        mixed = gate[:, :, 0] * (out_a @ w_proj_a[l]) + gate[:, :, 1] * (out_b @ w_proj_b[l])
        x = x + mixed @ w_out[l]
        h = rmsnorm(x, norm_ffn_g[l])
        g_in, u = jnp.split(h @ w_up[l], 2, axis=-1)
        x = x + (jax.nn.silu(causal_dwconv(g_in, conv_w[l], conv_b[l])) * u) @ w_down[l]
    return x
```

```python
import numpy as np
from contextlib import ExitStack
import concourse.bass as bass
import concourse.mybir as mybir
from concourse.bass_utils import run_bass_kernel_spmd

F32 = mybir.dt.float32
BF16 = mybir.dt.bfloat16
AF = mybir.ActivationFunctionType
ALU = mybir.AluOpType
AX = mybir.AxisListType

D = 1024
NIN = 7960
DFF = 2816
EPS = 1e-6
TINY = 1e-30
NEGB = -30000.0
DIL_PAIRS = ((128, 1), (512, 4), (2048, 16))
C_AQ, C_AK, C_AV, C_BQ, C_BKV, C_GATE, C_MERGE = 0, 1536, 3072, 4608, 5120, 5888, 5912
FUSED = False


class Res:
    __slots__ = ("w", "r")

    def __init__(self):
        self.w = None
        self.r = {}


class Buf:
    def __init__(self, t, nres=1):
        self.t = t
        self.res = [Res() for _ in range(nres)]

    def __getitem__(self, k):
        return self.t[k]

    @property
    def r0(self):
        return self.res[0]


COMPUTE = ("pe", "act", "dve", "pool")


class Prog:
    ND = 16

    def __init__(self, nc, ctx):
        self.nc = nc
        self.ops = {e: [] for e in ("pe", "act", "dve", "pool", "sp")}
        self.cnt = {e: 0 for e in COMPUTE}
        self.sem = {}
        for e in COMPUTE:
            self.sem[e] = ctx.enter_context(nc.semaphore("s_" + e))
        self.dma_val = {}
        for q in ("sp", "pool"):
            for k in range(self.ND):
                key = (q, k)
                self.sem[key] = ctx.enter_context(nc.semaphore(f"d_{q}_{k}"))
                self.dma_val[key] = 0
        self.rr = {"sp": 0, "pool": 0}
        self.seen = {e: {} for e in self.ops}
        self.n_inst = 0

    def _emit(self, eng, fn, deps, semkey, inc):
        best = {}
        for (sk, v) in deps:
            if best.get(sk, 0) < v:
                best[sk] = v
        waits = []
        for sk, v in best.items():
            if sk == eng and eng == "pe":
                continue
            if self.seen[eng].get(sk, 0) >= v:
                continue
            self.seen[eng][sk] = v
            waits.append((sk, v))
        self.ops[eng].append((waits, fn, semkey, inc))
        self.n_inst += 1 + len(waits)

    def op(self, eng, fn, reads=(), writes=(), dma=False):
        deps = []
        for r in reads:
            if r.w is not None:
                deps.append(r.w)
        for w in writes:
            if w.w is not None:
                deps.append(w.w)
            deps.extend(w.r.items())
        if dma:
            k = self.rr[eng]
            self.rr[eng] = (k + 1) % self.ND
            semkey = (eng, k)
            prev = self.dma_val[semkey]
            if prev > 0:
                deps.append((semkey, prev))
            val = prev + 16
            self.dma_val[semkey] = val
            inc = 16
        else:
            semkey = eng
            self.cnt[eng] += 1
            val = self.cnt[eng]
            inc = 1
        self._emit(eng, fn, deps, semkey, inc)
        for r in reads:
            if r.r.get(semkey, 0) < val:
                r.r[semkey] = val
        for w in writes:
            w.w = (semkey, val)
            w.r = {}
        return (semkey, val)

    def dma(self, q, out, in_, reads=(), writes=(), slow=False):
        if slow:
            fn = lambda e, out=out, in_=in_: e.dma_start(out=out, in_=in_, allow_slow_non_contiguous=True)
        else:
            fn = lambda e, out=out, in_=in_: e.dma_start(out=out, in_=in_)
        return self.op(q, fn, reads, writes, dma=True)

    def all_events(self):
        ev = [(e, self.cnt[e]) for e in COMPUTE if self.cnt[e] > 0]
        ev += [(k, v) for k, v in self.dma_val.items() if v > 0]
        return ev

    def barrier(self):
        ev = self.all_events()
        for eng in self.ops:
            waits = []
            for sk, v in ev:
                if sk == eng:
                    continue
                if self.seen[eng].get(sk, 0) >= v:
                    continue
                self.seen[eng][sk] = v
                waits.append((sk, v))
            if waits:
                self.ops[eng].append((waits, None, None, 0))
                self.n_inst += len(waits)

    def replay(self, eng, e):
        for (waits, fn, semkey, inc) in self.ops[eng]:
            for sk, v in waits:
                e.wait_ge(self.sem[sk], v)
            if fn is not None:
                ins = fn(e)
                ins.then_inc(self.sem[semkey], inc)


def C(name, *a, **k):
    return lambda e: getattr(e, name)(*a, **k)


def mmgroup(items):
    def fn(e):
        ins = None
        for (out, lhsT, rhs, st, sp) in items:
            ins = e.matmul(out, lhsT, rhs, start=st, stop=sp)
        return ins
    return fn


def host_consts(S):
    NT = S // 128
    NJ = S // 64
    NCB = S // 16 - 1
    NCT = (NCB + 127) // 128
    half = 32
    inv_freq = (np.float32(10000.0) ** (-(np.arange(half, dtype=np.float32)) / np.float32(half))).astype(np.float32)
    fidx = (np.arange(128) % 64) % 32

    def tabs(pos):
        ang = pos.astype(np.float32)[None, :] * inv_freq[fidx][:, None]
        return np.cos(ang).astype(np.float32), np.sin(ang).astype(np.float32)

    cos, sin = tabs(np.arange(S))
    posc = np.zeros(NCT * 128, dtype=np.float32)
    posc[:NCB] = np.arange(NCB) * 16 + 31
    cosc, sinc = tabs(posc)
    rot = np.zeros((128, 128), np.float32)
    for m in range(128):
        hb, d = (m // 64) * 64, m % 64
        if d < 32:
            rot[hb + d + 32, m] = -1.0
        else:
            rot[hb + d - 32, m] = 1.0
    bones = np.zeros((128, 128), np.float32)
    bones[:64, :64] = 1.0 / 64
    bones[64:, 64:] = 1.0 / 64
    ident = np.eye(128, dtype=np.float32)
    kk = np.arange(128)[:, None]
    qq = np.arange(128)[None, :]
    masks = np.stack([(kk >= qq), (kk <= qq), (kk > qq)], axis=1).astype(np.float32)
    mats = np.stack([rot, bones, ident], axis=1)
    eall = (np.arange(128)[:, None] == (np.arange(S)[None, :] // 64)).astype(np.float32)
    mm = np.zeros((NCT * 128, 128), np.float32)
    for c in range(NCB):
        for n in (c, c + 1):
            if n // 4 < 128:
                mm[c, n // 4] += 1.0
    mmat = mm.reshape(NCT, 128, 128).transpose(1, 0, 2).copy()
    cm = np.zeros((128, 17, 128), np.float32)
    cl = np.arange(128)[:, None]
    for v in range(16):
        cp = cl - 8 * v
        cm[:, v, :] = (16 * cp + 31 <= qq)
    cp = cl - 128
    cm[:, 16, :] = (16 * cp + 31 <= qq)
    tk = np.zeros((128, 256), np.float32)
    ta = np.zeros((128, 256), np.float32)
    for p in range(128):
        hi = 1 if p >= 64 else 0
        for m in range(256):
            r = m - 127
            if r > hi:
                ta[p, m] = -float(r - hi)
            elif r == hi:
                ta[p, m] = 2e9
            elif r == hi - 1:
                ta[p, m] = 1e9
            else:
                tk[p, m] = 1.0
    return dict(c_cos=cos, c_sin=sin, c_cosc=np.ascontiguousarray(cosc[:64]), c_sinc=np.ascontiguousarray(sinc[:64]),
                c_mats=mats, c_masks=masks, c_eall=eall, c_mmat=mmat, c_cm=cm, c_tk=tk, c_ta=ta)


def build(S, DEPTH, dbg=(), stop_after=None):
    NT = S // 128
    NST = S // 512
    NJ = 128
    NCB = S // 16 - 1
    NCT = (NCB + 127) // 128
    nc = bass.Bass("TRN2", target_bir_lowering=False)

    def din(name, shape, dt=F32):
        return nc.dram_tensor(name, list(shape), dt, kind="ExternalInput").ap()

    x_in = din("x", [S, D])
    norm_mix_g = din("norm_mix_g", [DEPTH, D])
    w_in = din("w_in", [DEPTH, D, NIN])
    a_q_g = din("a_q_g", [DEPTH, 64])
    a_k_g = din("a_k_g", [DEPTH, 64])
    b_q_g = din("b_q_g", [DEPTH, 64])
    b_k_g = din("b_k_g", [DEPTH, 3, 64])
    cmp_pos = din("cmp_pos", [DEPTH, 2, 32, 64])
    cmp_w1 = din("cmp_w1", [DEPTH, 2, 2048, 256])
    cmp_w2 = din("cmp_w2", [DEPTH, 2, 256, 64])
    w_proj_a = din("w_proj_a", [DEPTH, 512, D])
    w_proj_b = din("w_proj_b", [DEPTH, 512, D])
    w_out = din("w_out", [DEPTH, D, D])
    norm_ffn_g = din("norm_ffn_g", [DEPTH, D])
    w_up = din("w_up", [DEPTH, D, 2 * DFF])
    conv_w = din("conv_w", [DEPTH, 3, DFF])
    conv_b = din("conv_b", [DEPTH, DFF])
    w_down = din("w_down", [DEPTH, DFF, D])
    c_cos = din("c_cos", [128, S])
    c_sin = din("c_sin", [128, S])
    c_cosc = din("c_cosc", [64, NCT * 128])
    c_sinc = din("c_sinc", [64, NCT * 128])
    c_mats = din("c_mats", [128, 3, 128])
    c_masks = din("c_masks", [128, 3, 128])
    c_eall = din("c_eall", [128, S])
    c_mmat = din("c_mmat", [128, NCT, 128])
    c_cm = din("c_cm", [128, 17, 128])
    c_tk = din("c_tk", [128, 256])
    c_ta = din("c_ta", [128, 256])
    y_out = nc.dram_tensor("y", [S, D], F32, kind="ExternalOutput").ap()

    def dscr(name, shape, dt):
        kind = "ExternalOutput" if name in dbg else "Internal"
        return nc.dram_tensor(name, list(shape), dt, kind=kind).ap()

    QAT = dscr("QAT", [1536, S], BF16)
    KAT = dscr("KAT", [1536, S], BF16)
    VA = dscr("VA", [S, 1536], BF16)
    QBT = dscr("QBT", [512, S], BF16)
    KSELT = dscr("KSELT", [128, S], BF16)
    KWINT = dscr("KWINT", [128, S], BF16)
    VSEL = dscr("VSEL", [S, 128], BF16)
    VWIN = dscr("VWIN", [S, 128], BF16)
    KCRT = dscr("KCRT", [128, S], BF16)
    VCRT = dscr("VCRT", [128, S], BF16)
    GT = dscr("GT", [24, S], F32)
    MGT = dscr("MGT", [2048, S], BF16)
    OAT = dscr("OAT", [512, S], BF16)
    XR1 = dscr("XR1", [S, D], F32)
    XR2 = dscr("XR2", [S, D], F32)
    dres = {n: [Res() for _ in range(NST)] for n in
            ("QAT", "KAT", "VA", "QBT", "KSELT", "KWINT", "VSEL", "VWIN", "KCRT", "VCRT", "GT", "MGT", "OAT", "XR1", "XR2", "Y")}

    def dall(n):
        return dres[n]

    with ExitStack() as top:
        P = Prog(nc, top)

        uid = [0]

        def sbuf(cx, name, shape, dt, nres=1):
            uid[0] += 1
            t = cx.enter_context(nc.sbuf_tensor(f"{name}_{uid[0]}", list(shape), dt))
            return Buf(t, nres)

        def psum(cx, name, shape, dt=F32):
            uid[0] += 1
            t = cx.enter_context(nc.psum_tensor(f"{name}_{uid[0]}", list(shape), dt))
            return Buf(t)

        MATS = sbuf(top, "MATS", [128, 3, 128], BF16)
        MASKS = sbuf(top, "MASKS", [128, 3, 128], BF16)
        ONESF = sbuf(top, "ONESF", [128, 64], F32)
        EPSC = sbuf(top, "EPSC", [128, 1], F32)
        with ExitStack() as cx:
            st1 = sbuf(cx, "cst1", [128, 3, 128], F32)
            st2 = sbuf(cx, "cst2", [128, 3, 128], F32)
            P.dma("sp", st1[:], c_mats, writes=[st1.r0])
            P.dma("sp", st2[:], c_masks, writes=[st2.r0])
            P.op("dve", C("tensor_copy", MATS[:], st1[:]), reads=[st1.r0], writes=[MATS.r0])
            P.op("dve", C("tensor_copy", MASKS[:], st2[:]), reads=[st2.r0], writes=[MASKS.r0])
            P.op("pool", C("memset", ONESF[:], 1.0), writes=[ONESF.r0])
            P.op("pool", C("memset", EPSC[:], EPS), writes=[EPSC.r0])
            P.barrier()
        ROT = MATS[:, 0, :]
        BON = MATS[:, 1, :]
        IDN = MATS[:, 2, :]

        def norm_to_hT(cx_bufs, l_gT, src, tok0, hT, hT_res, tt):
            xt, junk, ss, rs, xs, tp = cx_bufs
            P.dma("sp", xt[:], src[tok0:tok0 + 128, :], reads=[], writes=[xt.r0])
            P.op("pool", C("memset", ss[:], 0.0), writes=[ss.r0])
            P.op("act", C("activation", out=junk[:], in_=xt[:], func=AF.Square, accum_out=ss[:]),
                 reads=[xt.r0], writes=[junk.r0, ss.r0])
            P.op("act", C("activation", out=rs[:], in_=ss[:], func=AF.Sqrt, bias=EPSC[:, 0:1], scale=1.0 / D),
                 reads=[ss.r0, EPSC.r0], writes=[rs.r0])
            P.op("dve", C("reciprocal", rs[:], rs[:]), reads=[rs.r0], writes=[rs.r0])
            P.op("dve", C("tensor_scalar", xs[:], xt[:], rs[:, 0:1], None, op0=ALU.mult),
                 reads=[xt.r0, rs.r0], writes=[xs.r0])
            items = [(tp[:, kc, :], xs[:, kc * 128:(kc + 1) * 128], IDN, True, True) for kc in range(8)]
            P.op("pe", mmgroup(items), reads=[xs.r0, MATS.r0], writes=[tp.r0])
            P.op("dve", C("tensor_tensor", out=hT[:, :, tt * 128:(tt + 1) * 128], in0=tp[:],
                                                  in1=l_gT[:, :].unsqueeze(2).to_broadcast([128, 8, 128]), op=ALU.mult),
                 reads=[tp.r0, l_gT.r0], writes=[hT_res])
            return xt

        def load_cast(cx, dst_ap_fn, src_ap_fn, nchunks, shape, dst_res, name, eng="pool"):
            stg = [sbuf(cx, f"{name}_stg{i}", shape, F32) for i in range(2)]
            for i in range(nchunks):
                s = stg[i % 2]
                P.dma("sp", s[:], src_ap_fn(i), writes=[s.r0])
                P.op(eng, C("tensor_copy", dst_ap_fn(i), s[:]), reads=[s.r0], writes=[dst_res])

        def phase_P(l, src, src_name):
            with ExitStack() as cx:
                WIN = sbuf(cx, "WIN", [128, 8, NIN], BF16)
                with ExitStack() as cx2:
                    Q4 = NIN // 4
                    load_cast(cx2, lambda i: WIN[:, i // 4, (i % 4) * Q4:(i % 4 + 1) * Q4],
                              lambda i: w_in[l, (i // 4) * 128:(i // 4 + 1) * 128, (i % 4) * Q4:(i % 4 + 1) * Q4],
                              32, [128, Q4], WIN.r0, "win")
                    P.barrier()
                gT = sbuf(cx, "gT", [128, 8], F32)
                P.dma("sp", gT[:], norm_mix_g[l].rearrange("(kc p) -> p kc", p=128), writes=[gT.r0], slow=True)
                GC = sbuf(cx, "GC", [128, 5], F32)
                for ci, gsrc in enumerate((a_q_g[l], a_k_g[l], b_q_g[l], b_k_g[l, 1], b_k_g[l, 2])):
                    for hb in range(2):
                        P.dma("sp", GC[hb * 64:(hb + 1) * 64, ci:ci + 1], gsrc.rearrange("(d o) -> d o", o=1),
                              writes=[GC.r0], slow=True)
                hTs = [sbuf(cx, f"hT{i}", [128, 8, 512], BF16) for i in range(2)]
                nb = [(sbuf(cx, f"nx{i}", [128, D], F32), sbuf(cx, f"nj{i}", [128, D], BF16), sbuf(cx, f"nss{i}", [128, 1], F32),
                       sbuf(cx, f"nrs{i}", [128, 1], F32), sbuf(cx, f"nxs{i}", [128, D], BF16),
                       psum(cx, f"ntp{i}", [128, 8, 128])) for i in range(1)]
                cs = [(sbuf(cx, f"cos{i}", [128, 512], F32), sbuf(cx, f"sin{i}", [128, 512], F32)) for i in range(2)]
                pa = [psum(cx, f"pa{i}", [128, 512]) for i in range(2)]
                p2 = [psum(cx, f"p2{i}", [128, 512]) for i in range(2)]
                p3 = [psum(cx, f"p3{i}", [128, 512]) for i in range(2)]
                sq = [sbuf(cx, f"sq{i}", [128, 512], BF16) for i in range(2)]
                xg = [sbuf(cx, f"xg{i}", [128, 512], BF16) for i in range(2)]
                rstd = [sbuf(cx, f"rstd{i}", [128, 512], F32) for i in range(2)]
                t1 = [sbuf(cx, f"t1{i}", [128, 512], F32) for i in range(2)]
                t2 = [sbuf(cx, f"t2{i}", [128, 512], F32) for i in range(2)]
                ob = [sbuf(cx, f"ob{i}", [128, 512], BF16) for i in range(3)]
                of = [sbuf(cx, f"of{i}", [128, 512], F32) for i in range(2)]
                gi = 0
                oi = 0
                for st in range(NST):
                    hT = hTs[st % 2]
                    c_t, s_t = cs[st % 2]
                    tsl = slice(st * 512, (st + 1) * 512)
                    P.dma("sp", c_t[:], c_cos[:, tsl], writes=[c_t.r0])
                    P.dma("sp", s_t[:], c_sin[:, tsl], writes=[s_t.r0])
                    for tt in range(4):
                        norm_to_hT(nb[0], gT, src, st * 512 + tt * 128, hT, hT.r0, tt)
                    wr_src = [dres[src_name][st]] if src_name else []
                    nr_groups = []
                    for g in range(12):
                        nr_groups.append((C_AQ + g * 128, 0, QAT, "QAT", g * 128))
                    for g in range(12):
                        nr_groups.append((C_AK + g * 128, 1, KAT, "KAT", g * 128))
                    for g in range(4):
                        nr_groups.append((C_BQ + g * 128, 2, QBT, "QBT", g * 128))
                    nr_groups.append((C_BKV + 256, 3, KSELT, "KSELT", 0))
                    nr_groups.append((C_BKV + 512, 4, KWINT, "KWINT", 0))
                    for (c0, gci, dst, dname, r0) in nr_groups:
                        b = gi % 2
                        gi += 1
                        items = [(pa[b][:], WIN[:, kc, c0:c0 + 128], hT[:, kc, :], kc == 0, kc == 7) for kc in range(8)]
                        P.op("pe", mmgroup(items), reads=[WIN.r0, hT.r0], writes=[pa[b].r0])
                        P.op("act", C("activation", out=sq[b][:], in_=pa[b][:], func=AF.Square),
                             reads=[pa[b].r0], writes=[sq[b].r0])
                        P.op("act", C("activation", out=xg[b][:], in_=pa[b][:], func=AF.Identity,
                                                                         scale=GC[:, gci:gci + 1]),
                             reads=[pa[b].r0, GC.r0], writes=[xg[b].r0])
                        P.op("pe", mmgroup([(p2[b][:], BON, sq[b][:], True, True)]), reads=[sq[b].r0, MATS.r0], writes=[p2[b].r0])
                        P.op("pe", mmgroup([(p3[b][:], ROT, xg[b][:], True, True)]), reads=[xg[b].r0, MATS.r0], writes=[p3[b].r0])
                        P.op("act", C("activation", out=rstd[b][:], in_=p2[b][:], func=AF.Sqrt, bias=EPSC[:, 0:1], scale=1.0),
                             reads=[p2[b].r0, EPSC.r0], writes=[rstd[b].r0])
                        P.op("dve", C("reciprocal", rstd[b][:], rstd[b][:]), reads=[rstd[b].r0], writes=[rstd[b].r0])
                        P.op("pool", C("tensor_tensor", out=t1[b][:], in0=xg[b][:], in1=c_t[:], op=ALU.mult),
                             reads=[xg[b].r0, c_t.r0], writes=[t1[b].r0])
                        P.op("dve", C("tensor_tensor", out=t2[b][:], in0=p3[b][:], in1=s_t[:], op=ALU.mult),
                             reads=[p3[b].r0, s_t.r0], writes=[t2[b].r0])
                        P.op("pool", C("tensor_tensor", out=t1[b][:], in0=t1[b][:], in1=t2[b][:], op=ALU.add),
                             reads=[t1[b].r0, t2[b].r0], writes=[t1[b].r0])
                        o = ob[oi % 3]
                        oi += 1
                        P.op("dve", C("tensor_tensor", out=o[:], in0=t1[b][:], in1=rstd[b][:], op=ALU.mult),
                             reads=[t1[b].r0, rstd[b].r0], writes=[o.r0])
                        P.dma("pool", dst[r0:r0 + 128, tsl], o[:], reads=[o.r0], writes=[dres[dname][st]])
                    sg = [(C_BKV + 0, 128, AF.Identity, KCRT, "KCRT", 0, BF16), (C_BKV + 128, 128, AF.Identity, VCRT, "VCRT", 0, BF16)]
                    for g in range(16):
                        sg.append((C_MERGE + g * 128, 128, AF.Sigmoid, MGT, "MGT", g * 128, BF16))
                    sg.append((C_GATE, 24, AF.Sigmoid, GT, "GT", 0, F32))
                    for (c0, w, func, dst, dname, r0, dt) in sg:
                        b = gi % 2
                        gi += 1
                        items = [(pa[b][0:w, :], WIN[:, kc, c0:c0 + w], hT[:, kc, :], kc == 0, kc == 7) for kc in range(8)]
                        P.op("pe", mmgroup(items), reads=[WIN.r0, hT.r0], writes=[pa[b].r0])
                        if dt == BF16:
                            o = ob[oi % 3]
                            oi += 1
                        else:
                            o = of[0]
                        P.op("act", C("activation", out=o[0:w, :], in_=pa[b][0:w, :], func=func),
                             reads=[pa[b].r0], writes=[o.r0])
                        P.dma("pool", dst[r0:r0 + w, tsl], o[0:w, :], reads=[o.r0], writes=[dres[dname][st]])
                    vb = [(C_AV, 512, VA, "VA", 0), (C_AV + 512, 512, VA, "VA", 512), (C_AV + 1024, 512, VA, "VA", 1024),
                          (C_BKV + 384, 128, VSEL, "VSEL", 0), (C_BKV + 640, 128, VWIN, "VWIN", 0)]
                    for tt in range(4):
                        t0 = st * 512 + tt * 128
                        for (c0, w, dst, dname, cc0) in vb:
                            b = gi % 2
                            gi += 1
                            items = [(pa[b][:, 0:w], hT[:, kc, tt * 128:(tt + 1) * 128], WIN[:, kc, c0:c0 + w], kc == 0, kc == 7)
                                     for kc in range(8)]
                            P.op("pe", mmgroup(items), reads=[WIN.r0, hT.r0], writes=[pa[b].r0])
                            o = ob[oi % 3]
                            oi += 1
                            P.op("act", C("activation", out=o[:, 0:w], in_=pa[b][:, 0:w], func=AF.Identity),
                                 reads=[pa[b].r0], writes=[o.r0])
                            P.dma("pool", dst[t0:t0 + 128, cc0:cc0 + w], o[:, 0:w], reads=[o.r0], writes=[dres[dname][st]])
                P.barrier()

        def phase_C(l, KCMPT, VCMP):
            with ExitStack() as cx:
                raw = sbuf(cx, "craw", [64, S], BF16)
                w1b = sbuf(cx, "cw1b", [64, 32, 256], BF16)
                w2b = sbuf(cx, "cw2b", [128, 2, 64], BF16)
                posb = sbuf(cx, "cposb", [64, 32], BF16)
                posf = sbuf(cx, "cposf", [64, 32], F32)
                w2f = sbuf(cx, "cw2f", [128, 2, 64], F32)
                gk = sbuf(cx, "cgk", [64, 1], F32)
                cosc = sbuf(cx, "ccos", [64, NCT * 128], F32)
                sinc = sbuf(cx, "csin", [64, NCT * 128], F32)
                ph = psum(cx, "cph", [128, 512])
                pb = psum(cx, "cpb", [128, 8])
                pk = psum(cx, "cpk", [128, 512])
                pq2 = psum(cx, "cp2", [128, 512])
                pq3 = psum(cx, "cp3", [128, 512])
                pv = psum(cx, "cpv", [128, 64])
                bcol = sbuf(cx, "cbcol", [128, 1], F32)
                xh = sbuf(cx, "cxh", [128, 512], F32)
                x2 = sbuf(cx, "cx2", [128, 512], F32)
                sg_ = sbuf(cx, "csg", [128, 512], F32)
                h1g = sbuf(cx, "ch1g", [128, 2, 512], BF16)
                sqc = sbuf(cx, "csq", [64, 512], BF16)
                xgc = sbuf(cx, "cxg", [64, 512], BF16)
                rsc = sbuf(cx, "crs", [64, 512], F32)
                t1c = sbuf(cx, "ct1", [64, 512], F32)
                t2c = sbuf(cx, "ct2", [64, 512], F32)
                okc = sbuf(cx, "cok", [64, 512], BF16)
                P.dma("sp", cosc[:], c_cosc, writes=[cosc.r0])
                P.dma("sp", sinc[:], c_sinc, writes=[sinc.r0])
                P.dma("sp", gk[:], b_k_g[l, 0].rearrange("(d o) -> d o", o=1), writes=[gk.r0], slow=True)
                P.op("pool", C("memset", KCMPT[:], 0.0), writes=[KCMPT.r0])
                P.op("pool", C("memset", VCMP[:], 0.0), writes=[VCMP.r0])
                P.op("pool", C("memset", VCMP[:, :, :, 64:65], 1.0), writes=[VCMP.r0])
                P.op("pool", C("memset", h1g[:], 0.0), writes=[h1g.r0])
                for typ in range(2):
                    with ExitStack() as cx2:
                        load_cast(cx2, lambda i: w1b[:, i * 8:(i + 1) * 8, :],
                                  lambda i: cmp_w1[l, typ].rearrange("(j d) h -> d j h", d=64)[:, i * 8:(i + 1) * 8, :],
                                  4, [64, 8, 256], w1b.r0, f"cw1_{typ}")
                        P.dma("sp", posf[:], cmp_pos[l, typ].rearrange("j d -> d j"), writes=[posf.r0], slow=True)
                        P.op("pool", C("tensor_copy", posb[:], posf[:]), reads=[posf.r0], writes=[posb.r0])
                        P.dma("sp", w2f[:], cmp_w2[l, typ].rearrange("(c p) d -> p c d", p=128), writes=[w2f.r0])
                        P.op("pool", C("tensor_copy", w2b[:], w2f[:]), reads=[w2f.r0], writes=[w2b.r0])
                        src = KCRT if typ == 0 else VCRT
                        sname = "KCRT" if typ == 0 else "VCRT"
                        for kvh in range(2):
                            P.dma("sp", raw[:], src[kvh * 64:(kvh + 1) * 64, :], reads=dall(sname), writes=[raw.r0])
                            for hc in range(2):
                                items = [(ph[:, 0:NCB], w1b[:, j, hc * 128:(hc + 1) * 128],
                                          raw[:, j:j + 16 * (NCB - 1) + 1:16], j == 0, j == 31) for j in range(32)]
                                P.op("pe", mmgroup(items), reads=[w1b.r0, raw.r0], writes=[ph.r0])
                                items = [(pb[:, 0:1], w1b[:, j, hc * 128:(hc + 1) * 128], posb[:, j:j + 1], j == 0, j == 31)
                                         for j in range(32)]
                                P.op("pe", mmgroup(items), reads=[w1b.r0, posb.r0], writes=[pb.r0])
                                P.op("dve", C("tensor_copy", bcol[:], pb[:, 0:1]), reads=[pb.r0], writes=[bcol.r0])
                                P.op("act", C("activation", out=xh[:, 0:NCB], in_=ph[:, 0:NCB], func=AF.Identity, bias=bcol[:, 0:1]),
                                     reads=[ph.r0, bcol.r0], writes=[xh.r0])
                                P.op("dve", C("tensor_tensor", out=x2[:, 0:NCB], in0=xh[:, 0:NCB], in1=xh[:, 0:NCB], op=ALU.mult),
                                     reads=[xh.r0], writes=[x2.r0])
                                P.op("dve", C("tensor_scalar", x2[:, 0:NCB], x2[:, 0:NCB], 0.044715, 1.0, op0=ALU.mult, op1=ALU.add),
                                     reads=[x2.r0], writes=[x2.r0])
                                P.op("dve", C("tensor_tensor", out=x2[:, 0:NCB], in0=x2[:, 0:NCB], in1=xh[:, 0:NCB], op=ALU.mult),
                                     reads=[x2.r0, xh.r0], writes=[x2.r0])
                                P.op("act", C("activation", out=sg_[:, 0:NCB], in_=x2[:, 0:NCB], func=AF.Sigmoid, scale=1.5957691216057308),
                                     reads=[x2.r0], writes=[sg_.r0])
                                P.op("dve", C("tensor_tensor", out=h1g[:, hc, 0:NCB], in0=xh[:, 0:NCB], in1=sg_[:, 0:NCB], op=ALU.mult),
                                     reads=[xh.r0, sg_.r0], writes=[h1g.r0])
                            if typ == 0:
                                items = [(pk[0:64, 0:NCB], w2b[:, hc, :], h1g[:, hc, 0:NCB], hc == 0, hc == 1) for hc in range(2)]
                                P.op("pe", mmgroup(items), reads=[w2b.r0, h1g.r0], writes=[pk.r0])
                                P.op("act", C("activation", out=sqc[:, 0:NCB], in_=pk[0:64, 0:NCB], func=AF.Square),
                                     reads=[pk.r0], writes=[sqc.r0])
                                P.op("act", C("activation", out=xgc[:, 0:NCB], in_=pk[0:64, 0:NCB], func=AF.Identity, scale=gk[:, 0:1]),
                                     reads=[pk.r0, gk.r0], writes=[xgc.r0])
                                P.op("pe", mmgroup([(pq2[0:64, 0:NCB], MATS[0:64, 1, 0:64], sqc[:, 0:NCB], True, True)]),
                                     reads=[sqc.r0, MATS.r0], writes=[pq2.r0])
                                P.op("pe", mmgroup([(pq3[0:64, 0:NCB], MATS[0:64, 0, 0:64], xgc[:, 0:NCB], True, True)]),
                                     reads=[xgc.r0, MATS.r0], writes=[pq3.r0])
                                P.op("act", C("activation", out=rsc[:, 0:NCB], in_=pq2[0:64, 0:NCB], func=AF.Sqrt, bias=EPSC[0:64, 0:1], scale=1.0),
                                     reads=[pq2.r0, EPSC.r0], writes=[rsc.r0])
                                P.op("dve", C("reciprocal", rsc[:, 0:NCB], rsc[:, 0:NCB]), reads=[rsc.r0], writes=[rsc.r0])
                                P.op("dve", C("tensor_tensor", out=t1c[:, 0:NCB], in0=xgc[:, 0:NCB], in1=cosc[:, 0:NCB], op=ALU.mult),
                                     reads=[xgc.r0, cosc.r0], writes=[t1c.r0])
                                P.op("dve", C("tensor_tensor", out=t2c[:, 0:NCB], in0=pq3[0:64, 0:NCB], in1=sinc[:, 0:NCB], op=ALU.mult),
                                     reads=[pq3.r0, sinc.r0], writes=[t2c.r0])
                                P.op("dve", C("tensor_tensor", out=t1c[:, 0:NCB], in0=t1c[:, 0:NCB], in1=t2c[:, 0:NCB], op=ALU.add),
                                     reads=[t1c.r0, t2c.r0], writes=[t1c.r0])
                                P.op("dve", C("tensor_tensor", out=okc[:, 0:NCB], in0=t1c[:, 0:NCB], in1=rsc[:, 0:NCB], op=ALU.mult),
                                     reads=[t1c.r0, rsc.r0], writes=[okc.r0])
                                P.dma("sp", KCMPT[kvh * 64:(kvh + 1) * 64, 0:NCB], okc[:, 0:NCB], reads=[okc.r0], writes=[KCMPT.r0])
                            else:
                                for ct in range(NCT):
                                    m = min(128, NCB - ct * 128)
                                    items = [(pv[0:m, :], h1g[:, hc, ct * 128:ct * 128 + m], w2b[:, hc, :], hc == 0, hc == 1) for hc in range(2)]
                                    P.op("pe", mmgroup(items), reads=[w2b.r0, h1g.r0], writes=[pv.r0])
                                    P.op("dve", C("tensor_copy", VCMP[0:m, kvh, ct, 0:64], pv[0:m, :]),
                                         reads=[pv.r0], writes=[VCMP.r0])
                        P.barrier()
                P.barrier()

        def phase_A1(l):
            with ExitStack() as cx:
                acc = sbuf(cx, "a_acc", [65, S], F32)
                Qt = [sbuf(cx, f"a_q{i}", [64, S], BF16) for i in range(2)]
                Kt = [sbuf(cx, f"a_k{i}", [64, S], BF16) for i in range(2)]
                Vt = [sbuf(cx, f"a_v{i}", [128, NT, 65], BF16) for i in range(2)]
                ps = [psum(cx, f"a_ps{i}", [128, 2, 128]) for i in range(2)]
                po = [psum(cx, f"a_po{i}", [128, 128]) for i in range(2)]
                pbx = [psum(cx, f"a_pb{i}", [128, 512]) for i in range(2)]
                pt = [sbuf(cx, f"a_pt{i}", [128, 2, 128], BF16) for i in range(3)]
                rd = sbuf(cx, "a_rd", [65, S], F32)
                oa = sbuf(cx, "a_oa", [64, S], BF16)
                for v in Vt:
                    P.op("pool", C("memset", v[:, :, 64:65], 1.0), writes=[v.r0])
                it = 0
                ld = 0
                for j in range(8):
                    for gi, (window, dil) in enumerate(DIL_PAIRS):
                        head = gi * 8 + j
                        r0 = head * 64
                        L = S // dil
                        nb = L // 128
                        q_, k_, v_ = Qt[ld % 2], Kt[ld % 2], Vt[ld % 2]
                        ld += 1
                        P.dma("sp", q_[:], QAT[r0:r0 + 64, :], reads=dall("QAT"), writes=[q_.r0])
                        P.dma("sp", k_[:], KAT[r0:r0 + 64, :], reads=dall("KAT"), writes=[k_.r0])
                        for r in range(dil):
                            srcv = VA[r::dil, r0:r0 + 64].rearrange("(jb kk) d -> kk jb d", kk=128)
                            for j0 in range(0, nb, 8):
                                j1 = min(nb, j0 + 8)
                                P.dma("sp", v_[:, r * nb + j0:r * nb + j1, 0:64], srcv[:, j0:j1, :], reads=dall("VA"), writes=[v_.r0])
                        for r in range(dil):
                            for Jb in range(nb):
                                def cols(J):
                                    s0 = J * 128 * dil + r
                                    return slice(s0, s0 + 127 * dil + 1, dil)
                                b = it % 2
                                p_t = pt[it % 3]
                                it += 1
                                items = []
                                if Jb > 0:
                                    items.append((ps[b][:, 0, :], k_[:, cols(Jb - 1)], q_[:, cols(Jb)], True, True))
                                items.append((ps[b][:, 1, :], k_[:, cols(Jb)], q_[:, cols(Jb)], True, True))
                                P.op("pe", mmgroup(items), reads=[k_.r0, q_.r0], writes=[ps[b].r0])
                                e0 = 0 if Jb > 0 else 1
                                P.op("act", C("activation", out=p_t[:, e0:2, :], in_=ps[b][:, e0:2, :], func=AF.Exp, scale=0.125),
                                     reads=[ps[b].r0], writes=[p_t.r0])
                                P.op("pool", C("tensor_tensor", out=p_t[:, e0:2, :], in0=p_t[:, e0:2, :], in1=MASKS[:, e0:2, :], op=ALU.mult),
                                     reads=[p_t.r0, MASKS.r0], writes=[p_t.r0])
                                items = []
                                if Jb > 0:
                                    items.append((po[b][0:65, :], v_[:, r * nb + Jb - 1, :], p_t[:, 0, :], True, False))
                                items.append((po[b][0:65, :], v_[:, r * nb + Jb, :], p_t[:, 1, :], Jb == 0, True))
                                P.op("pe", mmgroup(items), reads=[v_.r0, p_t.r0], writes=[po[b].r0])
                                c = cols(Jb)
                                if gi == 0:
                                    P.op("dve", C("tensor_copy", acc[:, c], po[b][0:65, :]),
                                         reads=[po[b].r0], writes=[acc.r0])
                                else:
                                    P.op("dve", C("tensor_tensor", out=acc[:, c], in0=acc[:, c], in1=po[b][0:65, :], op=ALU.add),
                                         reads=[po[b].r0, acc.r0], writes=[acc.r0])
                    P.op("dve", C("reciprocal", rd[64:65, :], acc[64:65, :]), reads=[acc.r0], writes=[rd.r0])
                    for ch in range(NST):
                        b = ch % 2
                        csl = slice(ch * 512, (ch + 1) * 512)
                        P.op("pe", mmgroup([(pbx[b][0:64, :], ONESF[64:65, 0:64], rd[64:65, csl], True, True)]),
                             reads=[rd.r0, ONESF.r0], writes=[pbx[b].r0])
                        P.op("dve", C("tensor_tensor", out=oa[:, csl], in0=acc[0:64, csl], in1=pbx[b][0:64, :], op=ALU.mult),
                             reads=[pbx[b].r0, acc.r0], writes=[oa.r0])
                    P.dma("sp", OAT[j * 64:(j + 1) * 64, :], oa[:], reads=[oa.r0], writes=dall("OAT"))
                P.barrier()

        def phase_A2(l, src, src_name, KCMPT, VCMP):
            with ExitStack() as cx:
                KS = sbuf(cx, "b_ks", [128, S], BF16)
                KW = sbuf(cx, "b_kw", [128, S], BF16)
                VS = sbuf(cx, "b_vs", [128, NT, 2, 65], BF16)
                VW = sbuf(cx, "b_vw", [128, NT, 2, 65], BF16)
                EALL = sbuf(cx, "b_eall", [128, S], BF16)
                MM = sbuf(cx, "b_mm", [128, NCT, 128], BF16)
                CM = sbuf(cx, "b_cm", [128, 17, 128], BF16)
                TK = sbuf(cx, "b_tk", [128, 256], F32)
                TA = sbuf(cx, "b_ta", [128, 256], F32)
                WPA = sbuf(cx, "b_wpa", [128, 4, D], BF16)
                WPB = sbuf(cx, "b_wpb", [64, 8, D], BF16)
                WO = sbuf(cx, "b_wo", [128, 8, D], BF16)
                with ExitStack() as cx2:
                    load_cast(cx2, lambda i: EALL[:, i * 1024:(i + 1) * 1024], lambda i: c_eall[:, i * 1024:(i + 1) * 1024],
                              S // 1024, [128, 1024], EALL.r0, "ea")
                    load_cast(cx2, lambda i: WPA[:, i, :], lambda i: w_proj_a[l, i * 128:(i + 1) * 128, :], 4, [128, D], WPA.r0, "wpa")
                    load_cast(cx2, lambda i: WPB[:, i, :], lambda i: w_proj_b[l, i * 64:(i + 1) * 64, :], 8, [64, D], WPB.r0, "wpb")
                    load_cast(cx2, lambda i: WO[:, i, :], lambda i: w_out[l, i * 128:(i + 1) * 128, :], 8, [128, D], WO.r0, "wo")
                    load_cast(cx2, lambda i: MM[:], lambda i: c_mmat, 1, [128, NCT, 128], MM.r0, "mm")
                    load_cast(cx2, lambda i: CM[:], lambda i: c_cm, 1, [128, 17, 128], CM.r0, "cm")
                    P.barrier()
                P.dma("sp", TK[:], c_tk, writes=[TK.r0])
                P.dma("sp", TA[:], c_ta, writes=[TA.r0])
                P.dma("sp", KS[:], KSELT, reads=dall("KSELT"), writes=[KS.r0])
                P.dma("sp", KW[:], KWINT, reads=dall("KWINT"), writes=[KW.r0])
                P.op("pool", C("memset", VS[:, :, :, 64:65], 1.0), writes=[VS.r0])
                P.op("pool", C("memset", VW[:, :, :, 64:65], 1.0), writes=[VW.r0])
                for h in range(2):
                    for k0 in range(0, NT, 8):
                        P.dma("sp", VS[:, k0:k0 + 8, h, 0:64], VSEL[k0 * 128:(k0 + 8) * 128, h * 64:(h + 1) * 64].rearrange("(kt kk) d -> kk kt d", kk=128),
                              reads=dall("VSEL"), writes=[VS.r0])
                        P.dma("sp", VW[:, k0:k0 + 8, h, 0:64], VWIN[k0 * 128:(k0 + 8) * 128, h * 64:(h + 1) * 64].rearrange("(kt kk) d -> kk kt d", kk=128),
                              reads=dall("VWIN"), writes=[VW.r0])
                ST = [psum(cx, f"b_st{i}", [128, 2, 512]) for i in range(2)]
                OT = [psum(cx, f"b_ot{i}", [128, 512]) for i in range(2)]
                UB = psum(cx, "b_ub", [128, 4, 128])
                MISC = psum(cx, "b_misc", [128, 512])
                QB = [sbuf(cx, f"b_qb{i}", [128, 4, 128], BF16) for i in range(2)]
                GR = [sbuf(cx, f"b_gr{i}", [65, 24, 128], F32) for i in range(1)]
                MG = [sbuf(cx, f"b_mg{i}", [128, 16, 128], BF16) for i in range(2)]
                OA = [sbuf(cx, f"b_oa{i}", [128, 4, 128], BF16) for i in range(2)]
                XT = [sbuf(cx, f"b_xt{i}", [128, D], F32) for i in range(1)]
                PC = sbuf(cx, "b_pc", [128, 4, 512], BF16)
                PS_ = [sbuf(cx, f"b_ps{i}", [128, 2, 512], BF16) for i in range(2)]
                rs4 = sbuf(cx, "b_rs4", [128, 4], F32)
                psl = sbuf(cx, "b_psl", [128, 128], F32)
                sc2 = sbuf(cx, "b_sc2", [128, 128], F32)
                m8a = sbuf(cx, "b_m8a", [128, 8], F32)
                m8b = sbuf(cx, "b_m8b", [128, 8], F32)
                selb = sbuf(cx, "b_selb", [128, 128], BF16)
                selT = sbuf(cx, "b_selT", [128, 4, 128], BF16)
                UBR = [sbuf(cx, f"b_ubr{i}", [65, 512], F32) for i in range(3)]
                wrow = sbuf(cx, "b_wrow", [65, 512], F32)
                obf = sbuf(cx, "b_obf", [64, 512], F32)
                otmp = sbuf(cx, "b_otmp", [64, 512], F32)
                OBt = sbuf(cx, "b_obt", [64, 8, 128], BF16)
                m1 = sbuf(cx, "b_m1", [128, 8, 128], F32)
                m2 = sbuf(cx, "b_m2", [128, 8, 128], F32)
                mx = sbuf(cx, "b_mx", [128, 8, 128], BF16)
                x1 = [sbuf(cx, f"b_x1{i}", [128, D], F32) for i in range(1)]
                sti = 0
                psi = 0
                oti = 0
                ubi = 0
                import os as _os
                for i in range(int(_os.environ.get('A2_I0', 0)), min(NT, int(_os.environ.get('A2_I1', NT)))):
                    t0 = i * 128
                    tsl = slice(t0, t0 + 128)
                    stq = i // 4
                    qb, gr, mg, oa_, xt = QB[i % 2], GR[0], MG[i % 2], OA[i % 2], XT[0]
                    for h in range(2):
                        P.dma("sp", qb[h * 64:(h + 1) * 64, :, :], QBT[h * 256:(h + 1) * 256, tsl].rearrange("(g d) t -> d g t", d=64),
                              reads=[dres["QBT"][stq]], writes=[qb.r0])
                    P.dma("sp", gr[64:65, :, :], GT[:, tsl].rearrange("(o r) t -> o r t", o=1), reads=[dres["GT"][stq]], writes=[gr.r0])
                    P.dma("sp", mg[:], MGT[:, tsl].rearrange("(c p) t -> p c t", p=128), reads=[dres["MGT"][stq]], writes=[mg.r0])
                    P.dma("sp", oa_[:], OAT[:, tsl].rearrange("(c p) t -> p c t", p=128), reads=[dres["OAT"][stq]], writes=[oa_.r0])
                    P.dma("sp", xt[:], src[tsl, :], reads=([dres[src_name][stq]] if src_name else []), writes=[xt.r0])
                    for h in range(2):
                        hs = slice(h * 64, (h + 1) * 64)
                        qh = qb[hs, :, :].rearrange("p g q -> p (g q)")

                        def score_tiles(kts, ksrc, extra_bias, dst, dsti):
                            nonlocal sti
                            b = sti % 2
                            sti += 1
                            items = []
                            for e_, kt in enumerate(kts):
                                items.append((ST[b][:, e_, :], ksrc[hs, kt * 128:(kt + 1) * 128], qh, True, not extra_bias))
                                if extra_bias:
                                    items.append((ST[b][:, e_, :], EALL[0:NJ, kt * 128:(kt + 1) * 128],
                                                  selT[0:NJ, :, :].rearrange("p g q -> p (g q)"), False, True))
                            rd_ = [qb.r0, ksrc_res[id(ksrc)]] + ([EALL.r0, selT.r0] if extra_bias else [])
                            P.op("pe", mmgroup(items), reads=rd_, writes=[ST[b].r0])
                            n = len(kts)
                            P.op("act", C("activation", out=dst[:, dsti:dsti + n, :], in_=ST[b][:, 0:n, :], func=AF.Exp, scale=0.125),
                                 reads=[ST[b].r0], writes=[dst.r0])

                        ksrc_res = {id(KCMPT): KCMPT.r0, id(KS): KS.r0, id(KW): KW.r0}

                        def maskmul(dst, di, mask_ap, mres, eng="pool"):
                            P.op(eng, C("tensor_tensor",
                                out=dst[:, di, :].rearrange("p (g q) -> p g q", g=4),
                                in0=dst[:, di, :].rearrange("p (g q) -> p g q", g=4),
                                in1=mask_ap.unsqueeze(1).to_broadcast([128, 4, 128]), op=ALU.mult),
                                 reads=[dst.r0, mres], writes=[dst.r0])

                        nct = i // 16 + 1
                        for c0 in range(0, nct, 2):
                            kts = list(range(c0, min(c0 + 2, nct)))
                            score_tiles(kts, KCMPT, False, PC, c0)
                        maskmul(PC, nct - 1, CM[:, i % 16, :], CM.r0)
                        if i % 16 == 0 and i > 0:
                            maskmul(PC, nct - 2, CM[:, 16, :], CM.r0)
                        otc = OT[oti % 2]
                        oti += 1
                        items = [(otc[0:65, :], VCMP[:, h, ct, :], PC[:, ct, :], ct == 0, ct == nct - 1) for ct in range(nct)]
                        P.op("pe", mmgroup(items), reads=[VCMP.r0, PC.r0], writes=[otc.r0])
                        items = []
                        for g in range(4):
                            for ct in range(nct):
                                items.append((UB[:, g, 0:NJ], PC[:, ct, g * 128:(g + 1) * 128], MM[:, ct, 0:NJ], ct == 0, ct == nct - 1))
                        P.op("pe", mmgroup(items), reads=[PC.r0, MM.r0], writes=[UB.r0])
                        P.op("dve", C("tensor_reduce", out=rs4[:], in_=UB[:, :, 0:NJ], axis=AX.X, op=ALU.add), reads=[UB.r0], writes=[rs4.r0])
                        P.op("dve", C("tensor_scalar", rs4[:], rs4[:], 0.5, None, op0=ALU.mult), reads=[rs4.r0], writes=[rs4.r0])
                        P.op("dve", C("tensor_scalar", rs4[:], rs4[:], TINY, None, op0=ALU.max), reads=[rs4.r0], writes=[rs4.r0])
                        P.op("dve", C("reciprocal", rs4[:], rs4[:]), reads=[rs4.r0], writes=[rs4.r0])
                        P.op("dve", C("tensor_scalar", psl[:, 0:NJ], UB[:, 0, 0:NJ], rs4[:, 0:1], None, op0=ALU.mult), reads=[UB.r0, rs4.r0], writes=[psl.r0])
                        for g in range(1, 4):
                            P.op("dve", C("scalar_tensor_tensor", out=psl[:, 0:NJ], in0=UB[:, g, 0:NJ], scalar=rs4[:, g:g + 1], in1=psl[:, 0:NJ],
                                                                          op0=ALU.mult, op1=ALU.add), reads=[UB.r0, rs4.r0, psl.r0], writes=[psl.r0])
                        o_tab = 127 - 2 * i
                        P.op("dve", C("tensor_tensor", out=psl[:, 0:NJ], in0=psl[:, 0:NJ], in1=TK[:, o_tab:o_tab + NJ], op=ALU.mult),
                             reads=[psl.r0, TK.r0], writes=[psl.r0])
                        P.op("dve", C("tensor_tensor", out=psl[:, 0:NJ], in0=psl[:, 0:NJ], in1=TA[:, o_tab:o_tab + NJ], op=ALU.add),
                             reads=[psl.r0, TA.r0], writes=[psl.r0])
                        P.op("dve", C("memset", psl[:, 0:1], 3e9), writes=[psl.r0])
                        P.op("dve", C("max", out=m8a[:], in_=psl[:, 0:NJ]), reads=[psl.r0], writes=[m8a.r0])
                        P.op("dve", C("match_replace", out=sc2[:, 0:NJ], in_to_replace=m8a[:], in_values=psl[:, 0:NJ], imm_value=-1e9),
                             reads=[psl.r0, m8a.r0], writes=[sc2.r0])
                        P.op("dve", C("max", out=m8b[:], in_=sc2[:, 0:NJ]), reads=[sc2.r0], writes=[m8b.r0])
                        P.op("dve", C("tensor_reduce", out=m8a[:, 0:1], in_=m8b[:], axis=AX.X, op=ALU.min), reads=[m8b.r0], writes=[m8a.r0])
                        P.op("dve", C("tensor_scalar", sc2[:, 0:NJ], psl[:, 0:NJ], m8a[:, 0:1], None, op0=ALU.is_lt),
                             reads=[psl.r0, m8a.r0], writes=[sc2.r0])
                        P.op("dve", C("tensor_scalar", selb[:, 0:NJ], sc2[:, 0:NJ], NEGB, None, op0=ALU.mult),
                             reads=[sc2.r0], writes=[selb.r0])

                        def dense_branch(kts_all, ksrc, vsrc, bias, masks):
                            nonlocal psi, oti
                            ot = OT[oti % 2]
                            oti += 1
                            nk = len(kts_all)
                            for c0 in range(0, nk, 2):
                                kts = kts_all[c0:c0 + 2]
                                pbuf = PS_[psi % 2]
                                psi += 1
                                score_tiles(kts, ksrc, bias, pbuf, 0)
                                for e_, kt in enumerate(kts):
                                    if kt in masks:
                                        maskmul(pbuf, e_, MASKS[:, masks[kt], :], MASKS.r0)
                                items = [(ot[0:65, :], vsrc[:, kt, h, :], pbuf[:, e_, :], kt == kts_all[0], kt == kts_all[-1])
                                         for e_, kt in enumerate(kts)]
                                P.op("pe", mmgroup(items), reads=[vsrc.r0, pbuf.r0], writes=[ot.r0])
                            return ot

                        def combine(ot, br, first):
                            nonlocal ubi
                            ub = UBR[ubi % 3]
                            ubi += 1
                            P.op("dve", C("tensor_copy", ub[:], ot[0:65, :]), reads=[ot.r0], writes=[ub.r0])
                            P.op("dve", C("tensor_scalar", wrow[64:65, :], ub[64:65, :], TINY, None, op0=ALU.max), reads=[ub.r0], writes=[wrow.r0])
                            P.op("dve", C("reciprocal", wrow[64:65, :], wrow[64:65, :]), reads=[wrow.r0], writes=[wrow.r0])
                            P.op("dve", C("tensor_tensor", out=wrow[64:65, :].rearrange("p (g q) -> p g q", g=4),
                                                                       in0=wrow[64:65, :].rearrange("p (g q) -> p g q", g=4),
                                                                       in1=gr[64:65, br * 8 + h * 4: br * 8 + h * 4 + 4, :], op=ALU.mult),
                                 reads=[wrow.r0, gr.r0], writes=[wrow.r0])
                            P.op("pe", mmgroup([(MISC[0:64, :], ONESF[64:65, 0:64], wrow[64:65, :], True, True)]), reads=[wrow.r0, ONESF.r0], writes=[MISC.r0])
                            if first:
                                P.op("dve", C("tensor_tensor", out=obf[:], in0=ub[0:64, :], in1=MISC[0:64, :], op=ALU.mult),
                                     reads=[ub.r0, MISC.r0], writes=[obf.r0])
                            else:
                                P.op("dve", C("tensor_tensor", out=otmp[:], in0=ub[0:64, :], in1=MISC[0:64, :], op=ALU.mult),
                                     reads=[ub.r0, MISC.r0], writes=[otmp.r0])
                                P.op("pool", C("tensor_tensor", out=obf[:], in0=obf[:], in1=otmp[:], op=ALU.add),
                                     reads=[obf.r0, otmp.r0], writes=[obf.r0])

                        wm = {i: 1}
                        if i - 4 >= 0:
                            wm[i - 4] = 2
                        otw = dense_branch(list(range(max(0, i - 4), i + 1)), KW, VW, False, wm)
                        P.op("pe", mmgroup([(MISC[0:NJ, 0:128], selb[:, 0:NJ], IDN, True, True)]), reads=[selb.r0, MATS.r0], writes=[MISC.r0])
                        P.op("dve", C("tensor_copy", selT[0:NJ, :, :], MISC[0:NJ, 0:128].unsqueeze(1).to_broadcast([NJ, 4, 128])),
                             reads=[MISC.r0], writes=[selT.r0])
                        combine(otc, 0, True)
                        ots = dense_branch(list(range(0, i + 1)), KS, VS, True, {i: 1})
                        combine(otw, 2, False)
                        combine(ots, 1, False)
                        P.op("pool", C("tensor_copy", OBt[:, h * 4:(h + 1) * 4, :], obf[:].rearrange("p (g q) -> p g q", g=4)),
                             reads=[obf.r0], writes=[OBt.r0])
                    ya = ST[sti % 2]
                    sti += 1
                    yb = ST[sti % 2]
                    sti += 1
                    yav = ya[:].rearrange("p a (b q) -> p (a b) q", q=128)
                    ybv = yb[:].rearrange("p a (b q) -> p (a b) q", q=128)
                    items = []
                    for cc in range(8):
                        for kc in range(4):
                            items.append((yav[:, cc, :], WPA[:, kc, cc * 128:(cc + 1) * 128], oa_[:, kc, :], kc == 0, kc == 3))
                    P.op("pe", mmgroup(items), reads=[WPA.r0, oa_.r0], writes=[ya.r0])
                    items = []
                    for cc in range(8):
                        for hd in range(8):
                            items.append((ybv[:, cc, :], WPB[:, hd, cc * 128:(cc + 1) * 128], OBt[:, hd, :], hd == 0, hd == 7))
                    P.op("pe", mmgroup(items), reads=[WPB.r0, OBt.r0], writes=[yb.r0])
                    P.op("dve", C("tensor_tensor", out=m1[:], in0=yav, in1=mg[:, 0:8, :], op=ALU.mult), reads=[ya.r0, mg.r0], writes=[m1.r0])
                    P.op("dve", C("tensor_tensor", out=m2[:], in0=ybv, in1=mg[:, 8:16, :], op=ALU.mult), reads=[yb.r0, mg.r0], writes=[m2.r0])
                    P.op("pool", C("tensor_tensor", out=mx[:], in0=m1[:], in1=m2[:], op=ALU.add), reads=[m1.r0, m2.r0], writes=[mx.r0])
                    z = ST[sti % 2]
                    sti += 1
                    items = []
                    for hf in range(2):
                        for kc in range(8):
                            items.append((z[:, hf, :], mx[:, kc, :], WO[:, kc, hf * 512:(hf + 1) * 512], kc == 0, kc == 7))
                    P.op("pe", mmgroup(items), reads=[mx.r0, WO.r0], writes=[z.r0])
                    xo = x1[0]
                    P.op("dve", C("tensor_tensor", out=xo[:], in0=xt[:], in1=z[:].rearrange("p a b -> p (a b)"), op=ALU.add),
                         reads=[xt.r0, z.r0], writes=[xo.r0])
                    P.dma("pool", XR1[tsl, :], xo[:], reads=[xo.r0], writes=[dres["XR1"][stq]])
                P.barrier()

        def phase_F(l, dst, dst_name):
            NF = DFF // 128
            with ExitStack() as cx:
                WU = sbuf(cx, "f_wu", [128, 8, 2 * DFF], BF16)
                WD = sbuf(cx, "f_wd", [128, NF, D], BF16)
                with ExitStack() as cx2:
                    H4 = 2 * DFF // 4
                    load_cast(cx2, lambda i: WU[:, i // 4, (i % 4) * H4:(i % 4 + 1) * H4],
                              lambda i: w_up[l, (i // 4) * 128:(i // 4 + 1) * 128, (i % 4) * H4:(i % 4 + 1) * H4], 32, [128, H4], WU.r0, "wu")
                    load_cast(cx2, lambda i: WD[:, i, :], lambda i: w_down[l, i * 128:(i + 1) * 128, :], NF, [128, D], WD.r0, "wd")
                    P.barrier()
                gT = sbuf(cx, "f_gT", [128, 8], F32)
                P.dma("sp", gT[:], norm_ffn_g[l].rearrange("(kc p) -> p kc", p=128), writes=[gT.r0], slow=True)
                CW = sbuf(cx, "f_cw", [128, 3, NF], F32)
                CB = sbuf(cx, "f_cb", [128, NF], F32)
                for k in range(3):
                    P.dma("sp", CW[:, k, :], conv_w[l, k].rearrange("(fc p) -> p fc", p=128), writes=[CW.r0], slow=True)
                P.dma("sp", CB[:], conv_b[l].rearrange("(fc p) -> p fc", p=128), writes=[CB.r0], slow=True)
                HAL = sbuf(cx, "f_hal", [128, NF, 2], F32)
                P.op("pool", C("memset", HAL[:], 0.0), writes=[HAL.r0])
                hT = sbuf(cx, "f_hT", [128, 8, 512], BF16)
                xts = [sbuf(cx, f"f_x{i}", [128, D], F32) for i in range(4)]
                junk = sbuf(cx, "f_nj", [128, D], BF16)
                ss = sbuf(cx, "f_ss", [128, 1], F32)
                rs = sbuf(cx, "f_rs", [128, 1], F32)
                xs = sbuf(cx, "f_xs", [128, D], BF16)
                tp = psum(cx, "f_tp", [128, 8, 128])
                pg = [psum(cx, f"f_pg{i}", [128, 512]) for i in range(2)]
                pu = [psum(cx, f"f_pu{i}", [128, 512]) for i in range(2)]
                pz = psum(cx, "f_pz", [128, 2, 512])
                Gt = [sbuf(cx, f"f_gt{i}", [128, 514], F32) for i in range(2)]
                cv = [sbuf(cx, f"f_cv{i}", [128, 512], F32) for i in range(2)]
                sl = [sbuf(cx, f"f_sl{i}", [128, 512], F32) for i in range(2)]
                actT = sbuf(cx, "f_act", [128, NF, 512], BF16, nres=NF)
                xo = [sbuf(cx, f"f_xo{i}", [128, D], F32) for i in range(1)]
                for st in range(NST):
                    for tt in range(4):
                        norm_to_hT((xts[tt], junk, ss, rs, xs, tp), gT, XR1, st * 512 + tt * 128, hT, hT.r0, tt)
                    for fc in range(NF):
                        b = fc % 2
                        items = [(pg[b][:], WU[:, kc, fc * 128:(fc + 1) * 128], hT[:, kc, :], kc == 0, kc == 7) for kc in range(8)]
                        P.op("pe", mmgroup(items), reads=[WU.r0, hT.r0], writes=[pg[b].r0])
                        items = [(pu[b][:], WU[:, kc, DFF + fc * 128:DFF + (fc + 1) * 128], hT[:, kc, :], kc == 0, kc == 7) for kc in range(8)]
                        P.op("pe", mmgroup(items), reads=[WU.r0, hT.r0], writes=[pu[b].r0])
                        g_ = Gt[b]
                        P.op("pool", C("tensor_copy", g_[:, 0:2], HAL[:, fc, :]), reads=[HAL.r0], writes=[g_.r0])
                        P.op("act", C("activation", out=g_[:, 2:514], in_=pg[b][:], func=AF.Identity), reads=[pg[b].r0], writes=[g_.r0])
                        P.op("pool", C("tensor_copy", HAL[:, fc, :], g_[:, 512:514]), reads=[g_.r0], writes=[HAL.r0])
                        c_ = cv[b]
                        P.op("dve", C("tensor_scalar", c_[:], g_[:, 2:514], CW[:, 2, fc:fc + 1], CB[:, fc:fc + 1], op0=ALU.mult, op1=ALU.add),
                             reads=[g_.r0, CW.r0, CB.r0], writes=[c_.r0])
                        P.op("dve", C("scalar_tensor_tensor", out=c_[:], in0=g_[:, 1:513], scalar=CW[:, 1, fc:fc + 1], in1=c_[:], op0=ALU.mult, op1=ALU.add),
                             reads=[g_.r0, CW.r0, c_.r0], writes=[c_.r0])
                        P.op("dve", C("scalar_tensor_tensor", out=c_[:], in0=g_[:, 0:512], scalar=CW[:, 0, fc:fc + 1], in1=c_[:], op0=ALU.mult, op1=ALU.add),
                             reads=[g_.r0, CW.r0, c_.r0], writes=[c_.r0])
                        s_ = sl[b]
                        P.op("act", C("activation", out=s_[:], in_=c_[:], func=AF.Silu), reads=[c_.r0], writes=[s_.r0])
                        P.op("dve", C("tensor_tensor", out=actT[:, fc, :], in0=s_[:], in1=pu[b][:], op=ALU.mult),
                             reads=[s_.r0, pu[b].r0], writes=[actT.res[fc]])
                    for tt in range(4):
                        t0 = st * 512 + tt * 128
                        items = []
                        for hf in range(2):
                            for fc in range(NF):
                                items.append((pz[:, hf, :], actT[:, fc, tt * 128:(tt + 1) * 128], WD[:, fc, hf * 512:(hf + 1) * 512], fc == 0, fc == NF - 1))
                        P.op("pe", mmgroup(items), reads=[WD.r0] + actT.res, writes=[pz.r0])
                        o = xo[0]
                        P.op("dve", C("tensor_tensor", out=o[:], in0=xts[tt][:], in1=pz[:].rearrange("p a b -> p (a b)"), op=ALU.add),
                             reads=[xts[tt].r0, pz.r0], writes=[o.r0])
                        P.dma("pool", dst[t0:t0 + 128, :], o[:], reads=[o.r0], writes=[dres[dst_name][st]])
                P.barrier()

        src, src_name = x_in, None
        for l in range(DEPTH):
            phase_P(l, src, src_name)
            if stop_after == "P":
                break
            with ExitStack() as lc:
                KCMPT = sbuf(lc, "KCMPT", [128, NCT * 128], BF16)
                VCMP = sbuf(lc, "VCMP", [128, 2, NCT, 65], BF16)
                phase_C(l, KCMPT, VCMP)
                if stop_after != "C":
                    phase_A1(l)
                    if stop_after != "A1":
                        phase_A2(l, src, src_name, KCMPT, VCMP)
            if stop_after in ("C", "A1", "A2"):
                break
            last = (l == DEPTH - 1)
            phase_F(l, y_out if last else XR2, "Y" if last else "XR2")
            src, src_name = XR2, "XR2"

        final_ev = P.all_events()
        with nc.Block() as block:
            @block.tensor
            def _(e):
                P.replay("pe", e)

            @block.scalar
            def _(e):
                P.replay("act", e)

            @block.vector
            def _(e):
                P.replay("dve", e)

            @block.gpsimd
            def _(e):
                P.replay("pool", e)

            @block.sync
            def _(e):
                P.replay("sp", e)
                for sk, v in final_ev:
                    e.wait_ge(P.sem[sk], v)
        nc._n_rec = P.n_inst
    return nc


_CACHE = {}


def _get_prog(S, depth):
    key = (S, depth)
    if key not in _CACHE:
        _CACHE[key] = build(S, depth)
    return _CACHE[key]


WNAMES = ("norm_mix_g", "w_in", "a_q_g", "a_k_g", "b_q_g", "b_k_g", "cmp_pos", "cmp_w1", "cmp_w2",
          "w_proj_a", "w_proj_b", "w_out", "norm_ffn_g", "w_up", "conv_w", "conv_b", "w_down")


def kernel(**inputs):
    x = np.ascontiguousarray(np.asarray(inputs["x"], dtype=np.float32))
    B, S, _ = x.shape
    depth = inputs["w_in"].shape[0]
    consts = host_consts(S)
    ws = {k: np.ascontiguousarray(np.asarray(inputs[k], dtype=np.float32)) for k in WNAMES}
    n = 8
    if FUSED:
        nc = _get_prog(S, depth)
        in_maps = []
        for c in range(n):
            m = {"x": x[c % B]}
            m.update(ws)
            m.update(consts)
            in_maps.append(m)
        res = run_bass_kernel_spmd(nc, in_maps, core_ids=list(range(n)))
        return np.stack([res.results[b]["y"] for b in range(B)], axis=0).astype(np.float32)
    nc = _get_prog(S, 1)
    cur = [x[c % B] for c in range(n)]
    for l in range(depth):
        in_maps = []
        for c in range(n):
            m = {"x": np.ascontiguousarray(cur[c])}
            m.update({k: np.ascontiguousarray(v[l:l + 1]) for k, v in ws.items()})
            m.update(consts)
            in_maps.append(m)
        res = run_bass_kernel_spmd(nc, in_maps, core_ids=list(range(n)))
        cur = [res.results[c]["y"] for c in range(n)]
    return np.stack([cur[b] for b in range(B)], axis=0).astype(np.float32)
```

```python
import numpy as np
from contextlib import ExitStack
import concourse.bass as bass
import concourse.mybir as mybir
from concourse.bass_utils import run_bass_kernel_spmd

F32 = mybir.dt.float32
BF16 = mybir.dt.bfloat16
AF = mybir.ActivationFunctionType
ALU = mybir.AluOpType
AX = mybir.AxisListType

D = 1024
NIN = 7960
DFF = 2816
EPS = 1e-6
TINY = 1e-30
NEGB = -30000.0
DIL_PAIRS = ((128, 1), (512, 4), (2048, 16))
C_AQ, C_AK, C_AV, C_BQ, C_BKV, C_GATE, C_MERGE = 0, 1536, 3072, 4608, 5120, 5888, 5912
FUSED = True


class Res:
    __slots__ = ("w", "r")

    def __init__(self):
        self.w = None
        self.r = {}


class Buf:
    def __init__(self, t, nres=1):
        self.t = t
        self.res = [Res() for _ in range(nres)]

    def __getitem__(self, k):
        return self.t[k]

    @property
    def r0(self):
        return self.res[0]


COMPUTE = ("pe", "act", "dve", "pool")


class Prog:
    ND = 16

    def __init__(self, nc, ctx):
        self.nc = nc
        self.ops = {e: [] for e in ("pe", "act", "dve", "pool", "sp")}
        self.cnt = {e: 0 for e in COMPUTE}
        self.sem = {}
        for e in COMPUTE:
            self.sem[e] = ctx.enter_context(nc.semaphore("s_" + e))
        self.dma_val = {}
        for q in ("sp", "pool"):
            for k in range(self.ND):
                key = (q, k)
                self.sem[key] = ctx.enter_context(nc.semaphore(f"d_{q}_{k}"))
                self.dma_val[key] = 0
        self.rr = {"sp": 0, "pool": 0}
        self.seen = {e: {} for e in self.ops}
        self.n_inst = 0

    def _emit(self, eng, fn, deps, semkey, inc):
        best = {}
        for (sk, v) in deps:
            if best.get(sk, 0) < v:
                best[sk] = v
        waits = []
        for sk, v in best.items():
            if sk == eng and eng == "pe":
                continue
            if self.seen[eng].get(sk, 0) >= v:
                continue
            self.seen[eng][sk] = v
            waits.append((sk, v))
        self.ops[eng].append((waits, fn, semkey, inc))
        self.n_inst += 1 + len(waits)

    def op(self, eng, fn, reads=(), writes=(), dma=False):
        deps = []
        for r in reads:
            if r.w is not None:
                deps.append(r.w)
        for w in writes:
            if w.w is not None:
                deps.append(w.w)
            deps.extend(w.r.items())
        if dma:
            k = self.rr[eng]
            self.rr[eng] = (k + 1) % self.ND
            semkey = (eng, k)
            prev = self.dma_val[semkey]
            if prev > 0:
                deps.append((semkey, prev))
            val = prev + 16
            self.dma_val[semkey] = val
            inc = 16
        else:
            semkey = eng
            self.cnt[eng] += 1
            val = self.cnt[eng]
            inc = 1
        self._emit(eng, fn, deps, semkey, inc)
        for r in reads:
            if r.r.get(semkey, 0) < val:
                r.r[semkey] = val
        for w in writes:
            w.w = (semkey, val)
            w.r = {}
        return (semkey, val)

    def dma(self, q, out, in_, reads=(), writes=(), slow=False):
        if slow:
            fn = lambda e, out=out, in_=in_: e.dma_start(out=out, in_=in_, allow_slow_non_contiguous=True)
        else:
            fn = lambda e, out=out, in_=in_: e.dma_start(out=out, in_=in_)
        return self.op(q, fn, reads, writes, dma=True)

    def all_events(self):
        ev = [(e, self.cnt[e]) for e in COMPUTE if self.cnt[e] > 0]
        ev += [(k, v) for k, v in self.dma_val.items() if v > 0]
        return ev

    def barrier(self):
        ev = self.all_events()
        for eng in self.ops:
            waits = []
            for sk, v in ev:
                if sk == eng:
                    continue
                if self.seen[eng].get(sk, 0) >= v:
                    continue
                self.seen[eng][sk] = v
                waits.append((sk, v))
            if waits:
                self.ops[eng].append((waits, None, None, 0))
                self.n_inst += len(waits)

    def replay(self, eng, e):
        for (waits, fn, semkey, inc) in self.ops[eng]:
            for sk, v in waits:
                e.wait_ge(self.sem[sk], v)
            if fn is not None:
                ins = fn(e)
                ins.then_inc(self.sem[semkey], inc)


def C(name, *a, **k):
    return lambda e: getattr(e, name)(*a, **k)


def mmgroup(items):
    def fn(e):
        ins = None
        for (out, lhsT, rhs, st, sp) in items:
            ins = e.matmul(out, lhsT, rhs, start=st, stop=sp)
        return ins
    return fn


def host_consts(S):
    NT = S // 128
    NJ = S // 64
    NCB = S // 16 - 1
    NCT = (NCB + 127) // 128
    half = 32
    inv_freq = (np.float32(10000.0) ** (-(np.arange(half, dtype=np.float32)) / np.float32(half))).astype(np.float32)
    fidx = (np.arange(128) % 64) % 32

    def tabs(pos):
        ang = pos.astype(np.float32)[None, :] * inv_freq[fidx][:, None]
        return np.cos(ang).astype(np.float32), np.sin(ang).astype(np.float32)

    cos, sin = tabs(np.arange(S))
    posc = np.zeros(NCT * 128, dtype=np.float32)
    posc[:NCB] = np.arange(NCB) * 16 + 31
    cosc, sinc = tabs(posc)
    rot = np.zeros((128, 128), np.float32)
    for m in range(128):
        hb, d = (m // 64) * 64, m % 64
        if d < 32:
            rot[hb + d + 32, m] = -1.0
        else:
            rot[hb + d - 32, m] = 1.0
    bones = np.zeros((128, 128), np.float32)
    bones[:64, :64] = 1.0 / 64
    bones[64:, 64:] = 1.0 / 64
    ident = np.eye(128, dtype=np.float32)
    kk = np.arange(128)[:, None]
    qq = np.arange(128)[None, :]
    masks = np.stack([(kk >= qq), (kk <= qq), (kk > qq)], axis=1).astype(np.float32)
    mats = np.stack([rot, bones, ident], axis=1)
    eall = (np.arange(128)[:, None] == (np.arange(S)[None, :] // 64)).astype(np.float32)
    mm = np.zeros((NCT * 128, 128), np.float32)
    for c in range(NCB):
        for n in (c, c + 1):
            if n // 4 < 128:
                mm[c, n // 4] += 1.0
    mmat = mm.reshape(NCT, 128, 128).transpose(1, 0, 2).copy()
    cm = np.zeros((128, 17, 128), np.float32)
    cl = np.arange(128)[:, None]
    for v in range(16):
        cp = cl - 8 * v
        cm[:, v, :] = (16 * cp + 31 <= qq)
    cp = cl - 128
    cm[:, 16, :] = (16 * cp + 31 <= qq)
    tk = np.zeros((128, 256), np.float32)
    ta = np.zeros((128, 256), np.float32)
    for p in range(128):
        hi = 1 if p >= 64 else 0
        for m in range(256):
            r = m - 127
            if r > hi:
                ta[p, m] = -float(r - hi)
            elif r == hi:
                ta[p, m] = 2e9
            elif r == hi - 1:
                ta[p, m] = 1e9
            else:
                tk[p, m] = 1.0
    return dict(c_cos=cos, c_sin=sin, c_cosc=np.ascontiguousarray(cosc[:64]), c_sinc=np.ascontiguousarray(sinc[:64]),
                c_mats=mats, c_masks=masks, c_eall=eall, c_mmat=mmat, c_cm=cm, c_tk=tk, c_ta=ta)


def build(S, DEPTH, dbg=(), stop_after=None):
    NT = S // 128
    NST = S // 512
    NJ = 128
    NCB = S // 16 - 1
    NCT = (NCB + 127) // 128
    nc = bass.Bass("TRN2", target_bir_lowering=False)

    def din(name, shape, dt=F32):
        return nc.dram_tensor(name, list(shape), dt, kind="ExternalInput").ap()

    x_in = din("x", [S, D])
    norm_mix_g = din("norm_mix_g", [DEPTH, D])
    w_in = din("w_in", [DEPTH, D, NIN])
    a_q_g = din("a_q_g", [DEPTH, 64])
    a_k_g = din("a_k_g", [DEPTH, 64])
    b_q_g = din("b_q_g", [DEPTH, 64])
    b_k_g = din("b_k_g", [DEPTH, 3, 64])
    cmp_pos = din("cmp_pos", [DEPTH, 2, 32, 64])
    cmp_w1 = din("cmp_w1", [DEPTH, 2, 2048, 256])
    cmp_w2 = din("cmp_w2", [DEPTH, 2, 256, 64])
    w_proj_a = din("w_proj_a", [DEPTH, 512, D])
    w_proj_b = din("w_proj_b", [DEPTH, 512, D])
    w_out = din("w_out", [DEPTH, D, D])
    norm_ffn_g = din("norm_ffn_g", [DEPTH, D])
    w_up = din("w_up", [DEPTH, D, 2 * DFF])
    conv_w = din("conv_w", [DEPTH, 3, DFF])
    conv_b = din("conv_b", [DEPTH, DFF])
    w_down = din("w_down", [DEPTH, DFF, D])
    c_cos = din("c_cos", [128, S])
    c_sin = din("c_sin", [128, S])
    c_cosc = din("c_cosc", [64, NCT * 128])
    c_sinc = din("c_sinc", [64, NCT * 128])
    c_mats = din("c_mats", [128, 3, 128])
    c_masks = din("c_masks", [128, 3, 128])
    c_eall = din("c_eall", [128, S])
    c_mmat = din("c_mmat", [128, NCT, 128])
    c_cm = din("c_cm", [128, 17, 128])
    c_tk = din("c_tk", [128, 256])
    c_ta = din("c_ta", [128, 256])
    y_out = nc.dram_tensor("y", [S, D], F32, kind="ExternalOutput").ap()

    def dscr(name, shape, dt):
        kind = "ExternalOutput" if name in dbg else "Internal"
        return nc.dram_tensor(name, list(shape), dt, kind=kind).ap()

    QAT = dscr("QAT", [1536, S], BF16)
    KAT = dscr("KAT", [1536, S], BF16)
    VA = dscr("VA", [S, 1536], BF16)
    QBT = dscr("QBT", [512, S], BF16)
    KSELT = dscr("KSELT", [128, S], BF16)
    KWINT = dscr("KWINT", [128, S], BF16)
    VSEL = dscr("VSEL", [S, 128], BF16)
    VWIN = dscr("VWIN", [S, 128], BF16)
    KCRT = dscr("KCRT", [128, S], BF16)
    VCRT = dscr("VCRT", [128, S], BF16)
    GT = dscr("GT", [24, S], F32)
    MGT = dscr("MGT", [2048, S], BF16)
    OAT = dscr("OAT", [512, S], BF16)
    XR1 = dscr("XR1", [S, D], F32)
    XR2 = dscr("XR2", [S, D], F32)
    dres = {n: [Res() for _ in range(NST)] for n in
            ("QAT", "KAT", "VA", "QBT", "KSELT", "KWINT", "VSEL", "VWIN", "KCRT", "VCRT", "GT", "MGT", "OAT", "XR1", "XR2", "Y")}

    def dall(n):
        return dres[n]

    with ExitStack() as top:
        P = Prog(nc, top)

        uid = [0]

        def sbuf(cx, name, shape, dt, nres=1):
            uid[0] += 1
            t = cx.enter_context(nc.sbuf_tensor(f"{name}_{uid[0]}", list(shape), dt))
            return Buf(t, nres)

        def psum(cx, name, shape, dt=F32):
            uid[0] += 1
            t = cx.enter_context(nc.psum_tensor(f"{name}_{uid[0]}", list(shape), dt))
            return Buf(t)

        MATS = sbuf(top, "MATS", [128, 3, 128], BF16)
        MASKS = sbuf(top, "MASKS", [128, 3, 128], BF16)
        ONESF = sbuf(top, "ONESF", [128, 64], F32)
        EPSC = sbuf(top, "EPSC", [128, 1], F32)
        with ExitStack() as cx:
            st1 = sbuf(cx, "cst1", [128, 3, 128], F32)
            st2 = sbuf(cx, "cst2", [128, 3, 128], F32)
            P.dma("sp", st1[:], c_mats, writes=[st1.r0])
            P.dma("sp", st2[:], c_masks, writes=[st2.r0])
            P.op("dve", C("tensor_copy", MATS[:], st1[:]), reads=[st1.r0], writes=[MATS.r0])
            P.op("dve", C("tensor_copy", MASKS[:], st2[:]), reads=[st2.r0], writes=[MASKS.r0])
            P.op("pool", C("memset", ONESF[:], 1.0), writes=[ONESF.r0])
            P.op("pool", C("memset", EPSC[:], EPS), writes=[EPSC.r0])
            P.barrier()
        ROT = MATS[:, 0, :]
        BON = MATS[:, 1, :]
        IDN = MATS[:, 2, :]

        def norm_to_hT(cx_bufs, l_gT, src, tok0, hT, hT_res, tt):
            xt, junk, ss, rs, xs, tp = cx_bufs
            P.dma("sp", xt[:], src[tok0:tok0 + 128, :], reads=[], writes=[xt.r0])
            P.op("pool", C("memset", ss[:], 0.0), writes=[ss.r0])
            P.op("act", C("activation", out=junk[:], in_=xt[:], func=AF.Square, accum_out=ss[:]),
                 reads=[xt.r0], writes=[junk.r0, ss.r0])
            P.op("act", C("activation", out=rs[:], in_=ss[:], func=AF.Sqrt, bias=EPSC[:, 0:1], scale=1.0 / D),
                 reads=[ss.r0, EPSC.r0], writes=[rs.r0])
            P.op("dve", C("reciprocal", rs[:], rs[:]), reads=[rs.r0], writes=[rs.r0])
            P.op("dve", C("tensor_scalar", xs[:], xt[:], rs[:, 0:1], None, op0=ALU.mult),
                 reads=[xt.r0, rs.r0], writes=[xs.r0])
            items = [(tp[:, kc, :], xs[:, kc * 128:(kc + 1) * 128], IDN, True, True) for kc in range(8)]
            P.op("pe", mmgroup(items), reads=[xs.r0, MATS.r0], writes=[tp.r0])
            P.op("dve", C("tensor_tensor", out=hT[:, :, tt * 128:(tt + 1) * 128], in0=tp[:],
                                                  in1=l_gT[:, :].unsqueeze(2).to_broadcast([128, 8, 128]), op=ALU.mult),
                 reads=[tp.r0, l_gT.r0], writes=[hT_res])
            return xt

        def load_cast(cx, dst_ap_fn, src_ap_fn, nchunks, shape, dst_res, name, eng="pool"):
            stg = [sbuf(cx, f"{name}_stg{i}", shape, F32) for i in range(2)]
            for i in range(nchunks):
                s = stg[i % 2]
                P.dma("sp", s[:], src_ap_fn(i), writes=[s.r0])
                P.op(eng, C("tensor_copy", dst_ap_fn(i), s[:]), reads=[s.r0], writes=[dst_res])

        def phase_P(l, src, src_name):
            with ExitStack() as cx:
                WIN = sbuf(cx, "WIN", [128, 8, NIN], BF16)
                with ExitStack() as cx2:
                    Q4 = NIN // 4
                    load_cast(cx2, lambda i: WIN[:, i // 4, (i % 4) * Q4:(i % 4 + 1) * Q4],
                              lambda i: w_in[l, (i // 4) * 128:(i // 4 + 1) * 128, (i % 4) * Q4:(i % 4 + 1) * Q4],
                              32, [128, Q4], WIN.r0, "win")
                    P.barrier()
                gT = sbuf(cx, "gT", [128, 8], F32)
                P.dma("sp", gT[:], norm_mix_g[l].rearrange("(kc p) -> p kc", p=128), writes=[gT.r0], slow=True)
                GC = sbuf(cx, "GC", [128, 5], F32)
                for ci, gsrc in enumerate((a_q_g[l], a_k_g[l], b_q_g[l], b_k_g[l, 1], b_k_g[l, 2])):
                    for hb in range(2):
                        P.dma("sp", GC[hb * 64:(hb + 1) * 64, ci:ci + 1], gsrc.rearrange("(d o) -> d o", o=1),
                              writes=[GC.r0], slow=True)
                hTs = [sbuf(cx, f"hT{i}", [128, 8, 512], BF16) for i in range(2)]
                nb = [(sbuf(cx, f"nx{i}", [128, D], F32), sbuf(cx, f"nj{i}", [128, D], BF16), sbuf(cx, f"nss{i}", [128, 1], F32),
                       sbuf(cx, f"nrs{i}", [128, 1], F32), sbuf(cx, f"nxs{i}", [128, D], BF16),
                       psum(cx, f"ntp{i}", [128, 8, 128])) for i in range(1)]
                cs = [(sbuf(cx, f"cos{i}", [128, 512], F32), sbuf(cx, f"sin{i}", [128, 512], F32)) for i in range(2)]
                pa = [psum(cx, f"pa{i}", [128, 512]) for i in range(2)]
                p2 = [psum(cx, f"p2{i}", [128, 512]) for i in range(2)]
                p3 = [psum(cx, f"p3{i}", [128, 512]) for i in range(2)]
                sq = [sbuf(cx, f"sq{i}", [128, 512], BF16) for i in range(2)]
                xg = [sbuf(cx, f"xg{i}", [128, 512], BF16) for i in range(2)]
                rstd = [sbuf(cx, f"rstd{i}", [128, 512], F32) for i in range(2)]
                t1 = [sbuf(cx, f"t1{i}", [128, 512], F32) for i in range(2)]
                t2 = [sbuf(cx, f"t2{i}", [128, 512], F32) for i in range(2)]
                ob = [sbuf(cx, f"ob{i}", [128, 512], BF16) for i in range(3)]
                of = [sbuf(cx, f"of{i}", [128, 512], F32) for i in range(2)]
                gi = 0
                oi = 0
                for st in range(NST):
                    hT = hTs[st % 2]
                    c_t, s_t = cs[st % 2]
                    tsl = slice(st * 512, (st + 1) * 512)
                    P.dma("sp", c_t[:], c_cos[:, tsl], writes=[c_t.r0])
                    P.dma("sp", s_t[:], c_sin[:, tsl], writes=[s_t.r0])
                    for tt in range(4):
                        norm_to_hT(nb[0], gT, src, st * 512 + tt * 128, hT, hT.r0, tt)
                    wr_src = [dres[src_name][st]] if src_name else []
                    nr_groups = []
                    for g in range(12):
                        nr_groups.append((C_AQ + g * 128, 0, QAT, "QAT", g * 128))
                    for g in range(12):
                        nr_groups.append((C_AK + g * 128, 1, KAT, "KAT", g * 128))
                    for g in range(4):
                        nr_groups.append((C_BQ + g * 128, 2, QBT, "QBT", g * 128))
                    nr_groups.append((C_BKV + 256, 3, KSELT, "KSELT", 0))
                    nr_groups.append((C_BKV + 512, 4, KWINT, "KWINT", 0))
                    for (c0, gci, dst, dname, r0) in nr_groups:
                        b = gi % 2
                        gi += 1
                        items = [(pa[b][:], WIN[:, kc, c0:c0 + 128], hT[:, kc, :], kc == 0, kc == 7) for kc in range(8)]
                        P.op("pe", mmgroup(items), reads=[WIN.r0, hT.r0], writes=[pa[b].r0])
                        P.op("act", C("activation", out=sq[b][:], in_=pa[b][:], func=AF.Square),
                             reads=[pa[b].r0], writes=[sq[b].r0])
                        P.op("act", C("activation", out=xg[b][:], in_=pa[b][:], func=AF.Identity,
                                                                         scale=GC[:, gci:gci + 1]),
                             reads=[pa[b].r0, GC.r0], writes=[xg[b].r0])
                        P.op("pe", mmgroup([(p2[b][:], BON, sq[b][:], True, True)]), reads=[sq[b].r0, MATS.r0], writes=[p2[b].r0])
                        P.op("pe", mmgroup([(p3[b][:], ROT, xg[b][:], True, True)]), reads=[xg[b].r0, MATS.r0], writes=[p3[b].r0])
                        P.op("act", C("activation", out=rstd[b][:], in_=p2[b][:], func=AF.Sqrt, bias=EPSC[:, 0:1], scale=1.0),
                             reads=[p2[b].r0, EPSC.r0], writes=[rstd[b].r0])
                        P.op("dve", C("reciprocal", rstd[b][:], rstd[b][:]), reads=[rstd[b].r0], writes=[rstd[b].r0])
                        P.op("pool", C("tensor_tensor", out=t1[b][:], in0=xg[b][:], in1=c_t[:], op=ALU.mult),
                             reads=[xg[b].r0, c_t.r0], writes=[t1[b].r0])
                        P.op("dve", C("tensor_tensor", out=t2[b][:], in0=p3[b][:], in1=s_t[:], op=ALU.mult),
                             reads=[p3[b].r0, s_t.r0], writes=[t2[b].r0])
                        P.op("pool", C("tensor_tensor", out=t1[b][:], in0=t1[b][:], in1=t2[b][:], op=ALU.add),
                             reads=[t1[b].r0, t2[b].r0], writes=[t1[b].r0])
                        o = ob[oi % 3]
                        oi += 1
                        P.op("dve", C("tensor_tensor", out=o[:], in0=t1[b][:], in1=rstd[b][:], op=ALU.mult),
                             reads=[t1[b].r0, rstd[b].r0], writes=[o.r0])
                        P.dma("pool", dst[r0:r0 + 128, tsl], o[:], reads=[o.r0], writes=[dres[dname][st]])
                    sg = [(C_BKV + 0, 128, AF.Identity, KCRT, "KCRT", 0, BF16), (C_BKV + 128, 128, AF.Identity, VCRT, "VCRT", 0, BF16)]
                    for g in range(16):
                        sg.append((C_MERGE + g * 128, 128, AF.Sigmoid, MGT, "MGT", g * 128, BF16))
                    sg.append((C_GATE, 24, AF.Sigmoid, GT, "GT", 0, F32))
                    for (c0, w, func, dst, dname, r0, dt) in sg:
                        b = gi % 2
                        gi += 1
                        items = [(pa[b][0:w, :], WIN[:, kc, c0:c0 + w], hT[:, kc, :], kc == 0, kc == 7) for kc in range(8)]
                        P.op("pe", mmgroup(items), reads=[WIN.r0, hT.r0], writes=[pa[b].r0])
                        if dt == BF16:
                            o = ob[oi % 3]
                            oi += 1
                        else:
                            o = of[0]
                        P.op("act", C("activation", out=o[0:w, :], in_=pa[b][0:w, :], func=func),
                             reads=[pa[b].r0], writes=[o.r0])
                        P.dma("pool", dst[r0:r0 + w, tsl], o[0:w, :], reads=[o.r0], writes=[dres[dname][st]])
                    vb = [(C_AV, 512, VA, "VA", 0), (C_AV + 512, 512, VA, "VA", 512), (C_AV + 1024, 512, VA, "VA", 1024),
                          (C_BKV + 384, 128, VSEL, "VSEL", 0), (C_BKV + 640, 128, VWIN, "VWIN", 0)]
                    for tt in range(4):
                        t0 = st * 512 + tt * 128
                        for (c0, w, dst, dname, cc0) in vb:
                            b = gi % 2
                            gi += 1
                            items = [(pa[b][:, 0:w], hT[:, kc, tt * 128:(tt + 1) * 128], WIN[:, kc, c0:c0 + w], kc == 0, kc == 7)
                                     for kc in range(8)]
                            P.op("pe", mmgroup(items), reads=[WIN.r0, hT.r0], writes=[pa[b].r0])
                            o = ob[oi % 3]
                            oi += 1
                            P.op("act", C("activation", out=o[:, 0:w], in_=pa[b][:, 0:w], func=AF.Identity),
                                 reads=[pa[b].r0], writes=[o.r0])
                            P.dma("pool", dst[t0:t0 + 128, cc0:cc0 + w], o[:, 0:w], reads=[o.r0], writes=[dres[dname][st]])
                P.barrier()

        def phase_C(l, KCMPT, VCMP):
            with ExitStack() as cx:
                raw = sbuf(cx, "craw", [64, S], BF16)
                w1b = sbuf(cx, "cw1b", [64, 32, 256], BF16)
                w2b = sbuf(cx, "cw2b", [128, 2, 64], BF16)
                posb = sbuf(cx, "cposb", [64, 32], BF16)
                posf = sbuf(cx, "cposf", [64, 32], F32)
                w2f = sbuf(cx, "cw2f", [128, 2, 64], F32)
                gk = sbuf(cx, "cgk", [64, 1], F32)
                cosc = sbuf(cx, "ccos", [64, NCT * 128], F32)
                sinc = sbuf(cx, "csin", [64, NCT * 128], F32)
                ph = psum(cx, "cph", [128, 512])
                pb = psum(cx, "cpb", [128, 8])
                pk = psum(cx, "cpk", [128, 512])
                pq2 = psum(cx, "cp2", [128, 512])
                pq3 = psum(cx, "cp3", [128, 512])
                pv = psum(cx, "cpv", [128, 64])
                bcol = sbuf(cx, "cbcol", [128, 1], F32)
                xh = sbuf(cx, "cxh", [128, 512], F32)
                x2 = sbuf(cx, "cx2", [128, 512], F32)
                sg_ = sbuf(cx, "csg", [128, 512], F32)
                h1g = sbuf(cx, "ch1g", [128, 2, 512], BF16)
                sqc = sbuf(cx, "csq", [64, 512], BF16)
                xgc = sbuf(cx, "cxg", [64, 512], BF16)
                rsc = sbuf(cx, "crs", [64, 512], F32)
                t1c = sbuf(cx, "ct1", [64, 512], F32)
                t2c = sbuf(cx, "ct2", [64, 512], F32)
                okc = sbuf(cx, "cok", [64, 512], BF16)
                P.dma("sp", cosc[:], c_cosc, writes=[cosc.r0])
                P.dma("sp", sinc[:], c_sinc, writes=[sinc.r0])
                P.dma("sp", gk[:], b_k_g[l, 0].rearrange("(d o) -> d o", o=1), writes=[gk.r0], slow=True)
                P.op("pool", C("memset", KCMPT[:], 0.0), writes=[KCMPT.r0])
                P.op("pool", C("memset", VCMP[:], 0.0), writes=[VCMP.r0])
                P.op("pool", C("memset", VCMP[:, :, :, 64:65], 1.0), writes=[VCMP.r0])
                P.op("pool", C("memset", h1g[:], 0.0), writes=[h1g.r0])
                for typ in range(2):
                    with ExitStack() as cx2:
                        load_cast(cx2, lambda i: w1b[:, i * 8:(i + 1) * 8, :],
                                  lambda i: cmp_w1[l, typ].rearrange("(j d) h -> d j h", d=64)[:, i * 8:(i + 1) * 8, :],
                                  4, [64, 8, 256], w1b.r0, f"cw1_{typ}")
                        P.dma("sp", posf[:], cmp_pos[l, typ].rearrange("j d -> d j"), writes=[posf.r0], slow=True)
                        P.op("pool", C("tensor_copy", posb[:], posf[:]), reads=[posf.r0], writes=[posb.r0])
                        P.dma("sp", w2f[:], cmp_w2[l, typ].rearrange("(c p) d -> p c d", p=128), writes=[w2f.r0])
                        P.op("pool", C("tensor_copy", w2b[:], w2f[:]), reads=[w2f.r0], writes=[w2b.r0])
                        src = KCRT if typ == 0 else VCRT
                        sname = "KCRT" if typ == 0 else "VCRT"
                        for kvh in range(2):
                            P.dma("sp", raw[:], src[kvh * 64:(kvh + 1) * 64, :], reads=dall(sname), writes=[raw.r0])
                            for hc in range(2):
                                items = [(ph[:, 0:NCB], w1b[:, j, hc * 128:(hc + 1) * 128],
                                          raw[:, j:j + 16 * (NCB - 1) + 1:16], j == 0, j == 31) for j in range(32)]
                                P.op("pe", mmgroup(items), reads=[w1b.r0, raw.r0], writes=[ph.r0])
                                items = [(pb[:, 0:1], w1b[:, j, hc * 128:(hc + 1) * 128], posb[:, j:j + 1], j == 0, j == 31)
                                         for j in range(32)]
                                P.op("pe", mmgroup(items), reads=[w1b.r0, posb.r0], writes=[pb.r0])
                                P.op("dve", C("tensor_copy", bcol[:], pb[:, 0:1]), reads=[pb.r0], writes=[bcol.r0])
                                P.op("act", C("activation", out=xh[:, 0:NCB], in_=ph[:, 0:NCB], func=AF.Identity, bias=bcol[:, 0:1]),
                                     reads=[ph.r0, bcol.r0], writes=[xh.r0])
                                P.op("dve", C("tensor_tensor", out=x2[:, 0:NCB], in0=xh[:, 0:NCB], in1=xh[:, 0:NCB], op=ALU.mult),
                                     reads=[xh.r0], writes=[x2.r0])
                                P.op("dve", C("tensor_scalar", x2[:, 0:NCB], x2[:, 0:NCB], 0.044715, 1.0, op0=ALU.mult, op1=ALU.add),
                                     reads=[x2.r0], writes=[x2.r0])
                                P.op("dve", C("tensor_tensor", out=x2[:, 0:NCB], in0=x2[:, 0:NCB], in1=xh[:, 0:NCB], op=ALU.mult),
                                     reads=[x2.r0, xh.r0], writes=[x2.r0])
                                P.op("act", C("activation", out=sg_[:, 0:NCB], in_=x2[:, 0:NCB], func=AF.Sigmoid, scale=1.5957691216057308),
                                     reads=[x2.r0], writes=[sg_.r0])
                                P.op("dve", C("tensor_tensor", out=h1g[:, hc, 0:NCB], in0=xh[:, 0:NCB], in1=sg_[:, 0:NCB], op=ALU.mult),
                                     reads=[xh.r0, sg_.r0], writes=[h1g.r0])
                            if typ == 0:
                                items = [(pk[0:64, 0:NCB], w2b[:, hc, :], h1g[:, hc, 0:NCB], hc == 0, hc == 1) for hc in range(2)]
                                P.op("pe", mmgroup(items), reads=[w2b.r0, h1g.r0], writes=[pk.r0])
                                P.op("act", C("activation", out=sqc[:, 0:NCB], in_=pk[0:64, 0:NCB], func=AF.Square),
                                     reads=[pk.r0], writes=[sqc.r0])
                                P.op("act", C("activation", out=xgc[:, 0:NCB], in_=pk[0:64, 0:NCB], func=AF.Identity, scale=gk[:, 0:1]),
                                     reads=[pk.r0, gk.r0], writes=[xgc.r0])
                                P.op("pe", mmgroup([(pq2[0:64, 0:NCB], MATS[0:64, 1, 0:64], sqc[:, 0:NCB], True, True)]),
                                     reads=[sqc.r0, MATS.r0], writes=[pq2.r0])
                                P.op("pe", mmgroup([(pq3[0:64, 0:NCB], MATS[0:64, 0, 0:64], xgc[:, 0:NCB], True, True)]),
                                     reads=[xgc.r0, MATS.r0], writes=[pq3.r0])
                                P.op("act", C("activation", out=rsc[:, 0:NCB], in_=pq2[0:64, 0:NCB], func=AF.Sqrt, bias=EPSC[0:64, 0:1], scale=1.0),
                                     reads=[pq2.r0, EPSC.r0], writes=[rsc.r0])
                                P.op("dve", C("reciprocal", rsc[:, 0:NCB], rsc[:, 0:NCB]), reads=[rsc.r0], writes=[rsc.r0])
                                P.op("dve", C("tensor_tensor", out=t1c[:, 0:NCB], in0=xgc[:, 0:NCB], in1=cosc[:, 0:NCB], op=ALU.mult),
                                     reads=[xgc.r0, cosc.r0], writes=[t1c.r0])
                                P.op("dve", C("tensor_tensor", out=t2c[:, 0:NCB], in0=pq3[0:64, 0:NCB], in1=sinc[:, 0:NCB], op=ALU.mult),
                                     reads=[pq3.r0, sinc.r0], writes=[t2c.r0])
                                P.op("dve", C("tensor_tensor", out=t1c[:, 0:NCB], in0=t1c[:, 0:NCB], in1=t2c[:, 0:NCB], op=ALU.add),
                                     reads=[t1c.r0, t2c.r0], writes=[t1c.r0])
                                P.op("dve", C("tensor_tensor", out=okc[:, 0:NCB], in0=t1c[:, 0:NCB], in1=rsc[:, 0:NCB], op=ALU.mult),
                                     reads=[t1c.r0, rsc.r0], writes=[okc.r0])
                                P.dma("sp", KCMPT[kvh * 64:(kvh + 1) * 64, 0:NCB], okc[:, 0:NCB], reads=[okc.r0], writes=[KCMPT.r0])
                            else:
                                for ct in range(NCT):
                                    m = min(128, NCB - ct * 128)
                                    items = [(pv[0:m, :], h1g[:, hc, ct * 128:ct * 128 + m], w2b[:, hc, :], hc == 0, hc == 1) for hc in range(2)]
                                    P.op("pe", mmgroup(items), reads=[w2b.r0, h1g.r0], writes=[pv.r0])
                                    P.op("dve", C("tensor_copy", VCMP[0:m, kvh, ct, 0:64], pv[0:m, :]),
                                         reads=[pv.r0], writes=[VCMP.r0])
                        P.barrier()
                P.barrier()

        def phase_A1(l):
            with ExitStack() as cx:
                acc = sbuf(cx, "a_acc", [65, S], F32)
                Qt = [sbuf(cx, f"a_q{i}", [64, S], BF16) for i in range(2)]
                Kt = [sbuf(cx, f"a_k{i}", [64, S], BF16) for i in range(2)]
                Vt = [sbuf(cx, f"a_v{i}", [128, NT, 65], BF16) for i in range(2)]
                ps = [psum(cx, f"a_ps{i}", [128, 2, 128]) for i in range(2)]
                po = [psum(cx, f"a_po{i}", [128, 128]) for i in range(2)]
                pbx = [psum(cx, f"a_pb{i}", [128, 512]) for i in range(2)]
                pt = [sbuf(cx, f"a_pt{i}", [128, 2, 128], BF16) for i in range(3)]
                rd = sbuf(cx, "a_rd", [65, S], F32)
                oa = sbuf(cx, "a_oa", [64, S], BF16)
                for v in Vt:
                    P.op("pool", C("memset", v[:, :, 64:65], 1.0), writes=[v.r0])
                it = 0
                ld = 0
                for j in range(8):
                    for gi, (window, dil) in enumerate(DIL_PAIRS):
                        head = gi * 8 + j
                        r0 = head * 64
                        L = S // dil
                        nb = L // 128
                        q_, k_, v_ = Qt[ld % 2], Kt[ld % 2], Vt[ld % 2]
                        ld += 1
                        P.dma("sp", q_[:], QAT[r0:r0 + 64, :], reads=dall("QAT"), writes=[q_.r0])
                        P.dma("sp", k_[:], KAT[r0:r0 + 64, :], reads=dall("KAT"), writes=[k_.r0])
                        for r in range(dil):
                            srcv = VA[r::dil, r0:r0 + 64].rearrange("(jb kk) d -> kk jb d", kk=128)
                            for j0 in range(0, nb, 8):
                                j1 = min(nb, j0 + 8)
                                P.dma("sp", v_[:, r * nb + j0:r * nb + j1, 0:64], srcv[:, j0:j1, :], reads=dall("VA"), writes=[v_.r0])
                        for r in range(dil):
                            for Jb in range(nb):
                                def cols(J):
                                    s0 = J * 128 * dil + r
                                    return slice(s0, s0 + 127 * dil + 1, dil)
                                b = it % 2
                                p_t = pt[it % 3]
                                it += 1
                                items = []
                                if Jb > 0:
                                    items.append((ps[b][:, 0, :], k_[:, cols(Jb - 1)], q_[:, cols(Jb)], True, True))
                                items.append((ps[b][:, 1, :], k_[:, cols(Jb)], q_[:, cols(Jb)], True, True))
                                P.op("pe", mmgroup(items), reads=[k_.r0, q_.r0], writes=[ps[b].r0])
                                e0 = 0 if Jb > 0 else 1
                                P.op("act", C("activation", out=p_t[:, e0:2, :], in_=ps[b][:, e0:2, :], func=AF.Exp, scale=0.125),
                                     reads=[ps[b].r0], writes=[p_t.r0])
                                P.op("pool", C("tensor_tensor", out=p_t[:, e0:2, :], in0=p_t[:, e0:2, :], in1=MASKS[:, e0:2, :], op=ALU.mult),
                                     reads=[p_t.r0, MASKS.r0], writes=[p_t.r0])
                                items = []
                                if Jb > 0:
                                    items.append((po[b][0:65, :], v_[:, r * nb + Jb - 1, :], p_t[:, 0, :], True, False))
                                items.append((po[b][0:65, :], v_[:, r * nb + Jb, :], p_t[:, 1, :], Jb == 0, True))
                                P.op("pe", mmgroup(items), reads=[v_.r0, p_t.r0], writes=[po[b].r0])
                                c = cols(Jb)
                                if gi == 0:
                                    P.op("dve", C("tensor_copy", acc[:, c], po[b][0:65, :]),
                                         reads=[po[b].r0], writes=[acc.r0])
                                else:
                                    P.op("dve", C("tensor_tensor", out=acc[:, c], in0=acc[:, c], in1=po[b][0:65, :], op=ALU.add),
                                         reads=[po[b].r0, acc.r0], writes=[acc.r0])
                    P.op("dve", C("reciprocal", rd[64:65, :], acc[64:65, :]), reads=[acc.r0], writes=[rd.r0])
                    for ch in range(NST):
                        b = ch % 2
                        csl = slice(ch * 512, (ch + 1) * 512)
                        P.op("pe", mmgroup([(pbx[b][0:64, :], ONESF[64:65, 0:64], rd[64:65, csl], True, True)]),
                             reads=[rd.r0, ONESF.r0], writes=[pbx[b].r0])
                        P.op("dve", C("tensor_tensor", out=oa[:, csl], in0=acc[0:64, csl], in1=pbx[b][0:64, :], op=ALU.mult),
                             reads=[pbx[b].r0, acc.r0], writes=[oa.r0])
                    P.dma("sp", OAT[j * 64:(j + 1) * 64, :], oa[:], reads=[oa.r0], writes=dall("OAT"))
                P.barrier()

        def phase_A2(l, src, src_name, KCMPT, VCMP):
            with ExitStack() as cx:
                KS = sbuf(cx, "b_ks", [128, S], BF16)
                KW = sbuf(cx, "b_kw", [128, S], BF16)
                VS = sbuf(cx, "b_vs", [128, NT, 2, 65], BF16)
                VW = sbuf(cx, "b_vw", [128, NT, 2, 65], BF16)
                EALL = sbuf(cx, "b_eall", [128, S], BF16)
                MM = sbuf(cx, "b_mm", [128, NCT, 128], BF16)
                CM = sbuf(cx, "b_cm", [128, 17, 128], BF16)
                TK = sbuf(cx, "b_tk", [128, 256], F32)
                TA = sbuf(cx, "b_ta", [128, 256], F32)
                WPA = sbuf(cx, "b_wpa", [128, 4, D], BF16)
                WPB = sbuf(cx, "b_wpb", [64, 8, D], BF16)
                WO = sbuf(cx, "b_wo", [128, 8, D], BF16)
                with ExitStack() as cx2:
                    load_cast(cx2, lambda i: EALL[:, i * 1024:(i + 1) * 1024], lambda i: c_eall[:, i * 1024:(i + 1) * 1024],
                              S // 1024, [128, 1024], EALL.r0, "ea")
                    load_cast(cx2, lambda i: WPA[:, i, :], lambda i: w_proj_a[l, i * 128:(i + 1) * 128, :], 4, [128, D], WPA.r0, "wpa")
                    load_cast(cx2, lambda i: WPB[:, i, :], lambda i: w_proj_b[l, i * 64:(i + 1) * 64, :], 8, [64, D], WPB.r0, "wpb")
                    load_cast(cx2, lambda i: WO[:, i, :], lambda i: w_out[l, i * 128:(i + 1) * 128, :], 8, [128, D], WO.r0, "wo")
                    load_cast(cx2, lambda i: MM[:], lambda i: c_mmat, 1, [128, NCT, 128], MM.r0, "mm")
                    load_cast(cx2, lambda i: CM[:], lambda i: c_cm, 1, [128, 17, 128], CM.r0, "cm")
                    P.barrier()
                P.dma("sp", TK[:], c_tk, writes=[TK.r0])
                P.dma("sp", TA[:], c_ta, writes=[TA.r0])
                P.dma("sp", KS[:], KSELT, reads=dall("KSELT"), writes=[KS.r0])
                P.dma("sp", KW[:], KWINT, reads=dall("KWINT"), writes=[KW.r0])
                P.op("pool", C("memset", VS[:, :, :, 64:65], 1.0), writes=[VS.r0])
                P.op("pool", C("memset", VW[:, :, :, 64:65], 1.0), writes=[VW.r0])
                for h in range(2):
                    for k0 in range(0, NT, 8):
                        P.dma("sp", VS[:, k0:k0 + 8, h, 0:64], VSEL[k0 * 128:(k0 + 8) * 128, h * 64:(h + 1) * 64].rearrange("(kt kk) d -> kk kt d", kk=128),
                              reads=dall("VSEL"), writes=[VS.r0])
                        P.dma("sp", VW[:, k0:k0 + 8, h, 0:64], VWIN[k0 * 128:(k0 + 8) * 128, h * 64:(h + 1) * 64].rearrange("(kt kk) d -> kk kt d", kk=128),
                              reads=dall("VWIN"), writes=[VW.r0])
                ST = [psum(cx, f"b_st{i}", [128, 2, 512]) for i in range(2)]
                OT = [psum(cx, f"b_ot{i}", [128, 512]) for i in range(2)]
                UB = psum(cx, "b_ub", [128, 4, 128])
                MISC = psum(cx, "b_misc", [128, 512])
                QB = [sbuf(cx, f"b_qb{i}", [128, 4, 128], BF16) for i in range(2)]
                GR = [sbuf(cx, f"b_gr{i}", [65, 24, 128], F32) for i in range(1)]
                MG = [sbuf(cx, f"b_mg{i}", [128, 16, 128], BF16) for i in range(2)]
                OA = [sbuf(cx, f"b_oa{i}", [128, 4, 128], BF16) for i in range(2)]
                XT = [sbuf(cx, f"b_xt{i}", [128, D], F32) for i in range(1)]
                PC = sbuf(cx, "b_pc", [128, 4, 512], BF16)
                PS_ = [sbuf(cx, f"b_ps{i}", [128, 2, 512], BF16) for i in range(2)]
                rs4 = sbuf(cx, "b_rs4", [128, 4], F32)
                psl = sbuf(cx, "b_psl", [128, 128], F32)
                sc2 = sbuf(cx, "b_sc2", [128, 128], F32)
                m8a = sbuf(cx, "b_m8a", [128, 8], F32)
                m8b = sbuf(cx, "b_m8b", [128, 8], F32)
                selb = sbuf(cx, "b_selb", [128, 128], BF16)
                selT = sbuf(cx, "b_selT", [128, 4, 128], BF16)
                UBR = [sbuf(cx, f"b_ubr{i}", [65, 512], F32) for i in range(3)]
                wrow = sbuf(cx, "b_wrow", [65, 512], F32)
                obf = sbuf(cx, "b_obf", [64, 512], F32)
                otmp = sbuf(cx, "b_otmp", [64, 512], F32)
                OBt = sbuf(cx, "b_obt", [64, 8, 128], BF16)
                m1 = sbuf(cx, "b_m1", [128, 8, 128], F32)
                m2 = sbuf(cx, "b_m2", [128, 8, 128], F32)
                mx = sbuf(cx, "b_mx", [128, 8, 128], BF16)
                x1 = [sbuf(cx, f"b_x1{i}", [128, D], F32) for i in range(1)]
                sti = 0
                psi = 0
                oti = 0
                ubi = 0
                import os as _os
                for i in range(int(_os.environ.get('A2_I0', 0)), min(NT, int(_os.environ.get('A2_I1', NT)))):
                    t0 = i * 128
                    tsl = slice(t0, t0 + 128)
                    stq = i // 4
                    qb, gr, mg, oa_, xt = QB[i % 2], GR[0], MG[i % 2], OA[i % 2], XT[0]
                    for h in range(2):
                        P.dma("sp", qb[h * 64:(h + 1) * 64, :, :], QBT[h * 256:(h + 1) * 256, tsl].rearrange("(g d) t -> d g t", d=64),
                              reads=[dres["QBT"][stq]], writes=[qb.r0])
                    P.dma("sp", gr[64:65, :, :], GT[:, tsl].rearrange("(o r) t -> o r t", o=1), reads=[dres["GT"][stq]], writes=[gr.r0])
                    P.dma("sp", mg[:], MGT[:, tsl].rearrange("(c p) t -> p c t", p=128), reads=[dres["MGT"][stq]], writes=[mg.r0])
                    P.dma("sp", oa_[:], OAT[:, tsl].rearrange("(c p) t -> p c t", p=128), reads=[dres["OAT"][stq]], writes=[oa_.r0])
                    P.dma("sp", xt[:], src[tsl, :], reads=([dres[src_name][stq]] if src_name else []), writes=[xt.r0])
                    for h in range(2):
                        hs = slice(h * 64, (h + 1) * 64)
                        qh = qb[hs, :, :].rearrange("p g q -> p (g q)")

                        def score_tiles(kts, ksrc, extra_bias, dst, dsti):
                            nonlocal sti
                            b = sti % 2
                            sti += 1
                            items = []
                            for e_, kt in enumerate(kts):
                                items.append((ST[b][:, e_, :], ksrc[hs, kt * 128:(kt + 1) * 128], qh, True, not extra_bias))
                                if extra_bias:
                                    items.append((ST[b][:, e_, :], EALL[0:NJ, kt * 128:(kt + 1) * 128],
                                                  selT[0:NJ, :, :].rearrange("p g q -> p (g q)"), False, True))
                            rd_ = [qb.r0, ksrc_res[id(ksrc)]] + ([EALL.r0, selT.r0] if extra_bias else [])
                            P.op("pe", mmgroup(items), reads=rd_, writes=[ST[b].r0])
                            n = len(kts)
                            P.op("act", C("activation", out=dst[:, dsti:dsti + n, :], in_=ST[b][:, 0:n, :], func=AF.Exp, scale=0.125),
                                 reads=[ST[b].r0], writes=[dst.r0])

                        ksrc_res = {id(KCMPT): KCMPT.r0, id(KS): KS.r0, id(KW): KW.r0}

                        def maskmul(dst, di, mask_ap, mres, eng="pool"):
                            P.op(eng, C("tensor_tensor",
                                out=dst[:, di, :].rearrange("p (g q) -> p g q", g=4),
                                in0=dst[:, di, :].rearrange("p (g q) -> p g q", g=4),
                                in1=mask_ap.unsqueeze(1).to_broadcast([128, 4, 128]), op=ALU.mult),
                                 reads=[dst.r0, mres], writes=[dst.r0])

                        nct = i // 16 + 1
                        for c0 in range(0, nct, 2):
                            kts = list(range(c0, min(c0 + 2, nct)))
                            score_tiles(kts, KCMPT, False, PC, c0)
                        maskmul(PC, nct - 1, CM[:, i % 16, :], CM.r0)
                        if i % 16 == 0 and i > 0:
                            maskmul(PC, nct - 2, CM[:, 16, :], CM.r0)
                        otc = OT[oti % 2]
                        oti += 1
                        items = [(otc[0:65, :], VCMP[:, h, ct, :], PC[:, ct, :], ct == 0, ct == nct - 1) for ct in range(nct)]
                        P.op("pe", mmgroup(items), reads=[VCMP.r0, PC.r0], writes=[otc.r0])
                        items = []
                        for g in range(4):
                            for ct in range(nct):
                                items.append((UB[:, g, 0:NJ], PC[:, ct, g * 128:(g + 1) * 128], MM[:, ct, 0:NJ], ct == 0, ct == nct - 1))
                        P.op("pe", mmgroup(items), reads=[PC.r0, MM.r0], writes=[UB.r0])
                        P.op("dve", C("tensor_reduce", out=rs4[:], in_=UB[:, :, 0:NJ], axis=AX.X, op=ALU.add), reads=[UB.r0], writes=[rs4.r0])
                        P.op("dve", C("tensor_scalar", rs4[:], rs4[:], 0.5, None, op0=ALU.mult), reads=[rs4.r0], writes=[rs4.r0])
                        P.op("dve", C("tensor_scalar", rs4[:], rs4[:], TINY, None, op0=ALU.max), reads=[rs4.r0], writes=[rs4.r0])
                        P.op("dve", C("reciprocal", rs4[:], rs4[:]), reads=[rs4.r0], writes=[rs4.r0])
                        P.op("dve", C("tensor_scalar", psl[:, 0:NJ], UB[:, 0, 0:NJ], rs4[:, 0:1], None, op0=ALU.mult), reads=[UB.r0, rs4.r0], writes=[psl.r0])
                        for g in range(1, 4):
                            P.op("dve", C("scalar_tensor_tensor", out=psl[:, 0:NJ], in0=UB[:, g, 0:NJ], scalar=rs4[:, g:g + 1], in1=psl[:, 0:NJ],
                                                                          op0=ALU.mult, op1=ALU.add), reads=[UB.r0, rs4.r0, psl.r0], writes=[psl.r0])
                        o_tab = 127 - 2 * i
                        P.op("dve", C("tensor_tensor", out=psl[:, 0:NJ], in0=psl[:, 0:NJ], in1=TK[:, o_tab:o_tab + NJ], op=ALU.mult),
                             reads=[psl.r0, TK.r0], writes=[psl.r0])
                        P.op("dve", C("tensor_tensor", out=psl[:, 0:NJ], in0=psl[:, 0:NJ], in1=TA[:, o_tab:o_tab + NJ], op=ALU.add),
                             reads=[psl.r0, TA.r0], writes=[psl.r0])
                        P.op("dve", C("memset", psl[:, 0:1], 3e9), writes=[psl.r0])
                        P.op("dve", C("max", out=m8a[:], in_=psl[:, 0:NJ]), reads=[psl.r0], writes=[m8a.r0])
                        P.op("dve", C("match_replace", out=sc2[:, 0:NJ], in_to_replace=m8a[:], in_values=psl[:, 0:NJ], imm_value=-1e9),
                             reads=[psl.r0, m8a.r0], writes=[sc2.r0])
                        P.op("dve", C("max", out=m8b[:], in_=sc2[:, 0:NJ]), reads=[sc2.r0], writes=[m8b.r0])
                        P.op("dve", C("tensor_reduce", out=m8a[:, 0:1], in_=m8b[:], axis=AX.X, op=ALU.min), reads=[m8b.r0], writes=[m8a.r0])
                        P.op("dve", C("tensor_scalar", sc2[:, 0:NJ], psl[:, 0:NJ], m8a[:, 0:1], None, op0=ALU.is_lt),
                             reads=[psl.r0, m8a.r0], writes=[sc2.r0])
                        P.op("dve", C("tensor_scalar", selb[:, 0:NJ], sc2[:, 0:NJ], NEGB, None, op0=ALU.mult),
                             reads=[sc2.r0], writes=[selb.r0])

                        def dense_branch(kts_all, ksrc, vsrc, bias, masks):
                            nonlocal psi, oti
                            ot = OT[oti % 2]
                            oti += 1
                            nk = len(kts_all)
                            for c0 in range(0, nk, 2):
                                kts = kts_all[c0:c0 + 2]
                                pbuf = PS_[psi % 2]
                                psi += 1
                                score_tiles(kts, ksrc, bias, pbuf, 0)
                                for e_, kt in enumerate(kts):
                                    if kt in masks:
                                        maskmul(pbuf, e_, MASKS[:, masks[kt], :], MASKS.r0)
                                items = [(ot[0:65, :], vsrc[:, kt, h, :], pbuf[:, e_, :], kt == kts_all[0], kt == kts_all[-1])
                                         for e_, kt in enumerate(kts)]
                                P.op("pe", mmgroup(items), reads=[vsrc.r0, pbuf.r0], writes=[ot.r0])
                            return ot

                        def combine(ot, br, first):
                            nonlocal ubi
                            ub = UBR[ubi % 3]
                            ubi += 1
                            P.op("dve", C("tensor_copy", ub[:], ot[0:65, :]), reads=[ot.r0], writes=[ub.r0])
                            P.op("dve", C("tensor_scalar", wrow[64:65, :], ub[64:65, :], TINY, None, op0=ALU.max), reads=[ub.r0], writes=[wrow.r0])
                            P.op("dve", C("reciprocal", wrow[64:65, :], wrow[64:65, :]), reads=[wrow.r0], writes=[wrow.r0])
                            P.op("dve", C("tensor_tensor", out=wrow[64:65, :].rearrange("p (g q) -> p g q", g=4),
                                                                       in0=wrow[64:65, :].rearrange("p (g q) -> p g q", g=4),
                                                                       in1=gr[64:65, br * 8 + h * 4: br * 8 + h * 4 + 4, :], op=ALU.mult),
                                 reads=[wrow.r0, gr.r0], writes=[wrow.r0])
                            P.op("pe", mmgroup([(MISC[0:64, :], ONESF[64:65, 0:64], wrow[64:65, :], True, True)]), reads=[wrow.r0, ONESF.r0], writes=[MISC.r0])
                            if first:
                                P.op("dve", C("tensor_tensor", out=obf[:], in0=ub[0:64, :], in1=MISC[0:64, :], op=ALU.mult),
                                     reads=[ub.r0, MISC.r0], writes=[obf.r0])
                            else:
                                P.op("dve", C("tensor_tensor", out=otmp[:], in0=ub[0:64, :], in1=MISC[0:64, :], op=ALU.mult),
                                     reads=[ub.r0, MISC.r0], writes=[otmp.r0])
                                P.op("pool", C("tensor_tensor", out=obf[:], in0=obf[:], in1=otmp[:], op=ALU.add),
                                     reads=[obf.r0, otmp.r0], writes=[obf.r0])

                        wm = {i: 1}
                        if i - 4 >= 0:
                            wm[i - 4] = 2
                        otw = dense_branch(list(range(max(0, i - 4), i + 1)), KW, VW, False, wm)
                        P.op("pe", mmgroup([(MISC[0:NJ, 0:128], selb[:, 0:NJ], IDN, True, True)]), reads=[selb.r0, MATS.r0], writes=[MISC.r0])
                        P.op("dve", C("tensor_copy", selT[0:NJ, :, :], MISC[0:NJ, 0:128].unsqueeze(1).to_broadcast([NJ, 4, 128])),
                             reads=[MISC.r0], writes=[selT.r0])
                        combine(otc, 0, True)
                        ots = dense_branch(list(range(0, i + 1)), KS, VS, True, {i: 1})
                        combine(otw, 2, False)
                        combine(ots, 1, False)
                        P.op("pool", C("tensor_copy", OBt[:, h * 4:(h + 1) * 4, :], obf[:].rearrange("p (g q) -> p g q", g=4)),
                             reads=[obf.r0], writes=[OBt.r0])
                    ya = ST[sti % 2]
                    sti += 1
                    yb = ST[sti % 2]
                    sti += 1
                    yav = ya[:].rearrange("p a (b q) -> p (a b) q", q=128)
                    ybv = yb[:].rearrange("p a (b q) -> p (a b) q", q=128)
                    items = []
                    for cc in range(8):
                        for kc in range(4):
                            items.append((yav[:, cc, :], WPA[:, kc, cc * 128:(cc + 1) * 128], oa_[:, kc, :], kc == 0, kc == 3))
                    P.op("pe", mmgroup(items), reads=[WPA.r0, oa_.r0], writes=[ya.r0])
                    items = []
                    for cc in range(8):
                        for hd in range(8):
                            items.append((ybv[:, cc, :], WPB[:, hd, cc * 128:(cc + 1) * 128], OBt[:, hd, :], hd == 0, hd == 7))
                    P.op("pe", mmgroup(items), reads=[WPB.r0, OBt.r0], writes=[yb.r0])
                    P.op("dve", C("tensor_tensor", out=m1[:], in0=yav, in1=mg[:, 0:8, :], op=ALU.mult), reads=[ya.r0, mg.r0], writes=[m1.r0])
                    P.op("dve", C("tensor_tensor", out=m2[:], in0=ybv, in1=mg[:, 8:16, :], op=ALU.mult), reads=[yb.r0, mg.r0], writes=[m2.r0])
                    P.op("pool", C("tensor_tensor", out=mx[:], in0=m1[:], in1=m2[:], op=ALU.add), reads=[m1.r0, m2.r0], writes=[mx.r0])
                    z = ST[sti % 2]
                    sti += 1
                    items = []
                    for hf in range(2):
                        for kc in range(8):
                            items.append((z[:, hf, :], mx[:, kc, :], WO[:, kc, hf * 512:(hf + 1) * 512], kc == 0, kc == 7))
                    P.op("pe", mmgroup(items), reads=[mx.r0, WO.r0], writes=[z.r0])
                    xo = x1[0]
                    P.op("dve", C("tensor_tensor", out=xo[:], in0=xt[:], in1=z[:].rearrange("p a b -> p (a b)"), op=ALU.add),
                         reads=[xt.r0, z.r0], writes=[xo.r0])
                    P.dma("pool", XR1[tsl, :], xo[:], reads=[xo.r0], writes=[dres["XR1"][stq]])
                P.barrier()

        def phase_F(l, dst, dst_name):
            NF = DFF // 128
            with ExitStack() as cx:
                WU = sbuf(cx, "f_wu", [128, 8, 2 * DFF], BF16)
                WD = sbuf(cx, "f_wd", [128, NF, D], BF16)
                with ExitStack() as cx2:
                    H4 = 2 * DFF // 4
                    load_cast(cx2, lambda i: WU[:, i // 4, (i % 4) * H4:(i % 4 + 1) * H4],
                              lambda i: w_up[l, (i // 4) * 128:(i // 4 + 1) * 128, (i % 4) * H4:(i % 4 + 1) * H4], 32, [128, H4], WU.r0, "wu")
                    load_cast(cx2, lambda i: WD[:, i, :], lambda i: w_down[l, i * 128:(i + 1) * 128, :], NF, [128, D], WD.r0, "wd")
                    P.barrier()
                gT = sbuf(cx, "f_gT", [128, 8], F32)
                P.dma("sp", gT[:], norm_ffn_g[l].rearrange("(kc p) -> p kc", p=128), writes=[gT.r0], slow=True)
                CW = sbuf(cx, "f_cw", [128, 3, NF], F32)
                CB = sbuf(cx, "f_cb", [128, NF], F32)
                for k in range(3):
                    P.dma("sp", CW[:, k, :], conv_w[l, k].rearrange("(fc p) -> p fc", p=128), writes=[CW.r0], slow=True)
                P.dma("sp", CB[:], conv_b[l].rearrange("(fc p) -> p fc", p=128), writes=[CB.r0], slow=True)
                HAL = sbuf(cx, "f_hal", [128, NF, 2], F32)
                P.op("pool", C("memset", HAL[:], 0.0), writes=[HAL.r0])
                hT = sbuf(cx, "f_hT", [128, 8, 512], BF16)
                xts = [sbuf(cx, f"f_x{i}", [128, D], F32) for i in range(4)]
                junk = sbuf(cx, "f_nj", [128, D], BF16)
                ss = sbuf(cx, "f_ss", [128, 1], F32)
                rs = sbuf(cx, "f_rs", [128, 1], F32)
                xs = sbuf(cx, "f_xs", [128, D], BF16)
                tp = psum(cx, "f_tp", [128, 8, 128])
                pg = [psum(cx, f"f_pg{i}", [128, 512]) for i in range(2)]
                pu = [psum(cx, f"f_pu{i}", [128, 512]) for i in range(2)]
                pz = psum(cx, "f_pz", [128, 2, 512])
                Gt = [sbuf(cx, f"f_gt{i}", [128, 514], F32) for i in range(2)]
                cv = [sbuf(cx, f"f_cv{i}", [128, 512], F32) for i in range(2)]
                sl = [sbuf(cx, f"f_sl{i}", [128, 512], F32) for i in range(2)]
                actT = sbuf(cx, "f_act", [128, NF, 512], BF16, nres=NF)
                xo = [sbuf(cx, f"f_xo{i}", [128, D], F32) for i in range(1)]
                for st in range(NST):
                    for tt in range(4):
                        norm_to_hT((xts[tt], junk, ss, rs, xs, tp), gT, XR1, st * 512 + tt * 128, hT, hT.r0, tt)
                    for fc in range(NF):
                        b = fc % 2
                        items = [(pg[b][:], WU[:, kc, fc * 128:(fc + 1) * 128], hT[:, kc, :], kc == 0, kc == 7) for kc in range(8)]
                        P.op("pe", mmgroup(items), reads=[WU.r0, hT.r0], writes=[pg[b].r0])
                        items = [(pu[b][:], WU[:, kc, DFF + fc * 128:DFF + (fc + 1) * 128], hT[:, kc, :], kc == 0, kc == 7) for kc in range(8)]
                        P.op("pe", mmgroup(items), reads=[WU.r0, hT.r0], writes=[pu[b].r0])
                        g_ = Gt[b]
                        P.op("pool", C("tensor_copy", g_[:, 0:2], HAL[:, fc, :]), reads=[HAL.r0], writes=[g_.r0])
                        P.op("act", C("activation", out=g_[:, 2:514], in_=pg[b][:], func=AF.Identity), reads=[pg[b].r0], writes=[g_.r0])
                        P.op("pool", C("tensor_copy", HAL[:, fc, :], g_[:, 512:514]), reads=[g_.r0], writes=[HAL.r0])
                        c_ = cv[b]
                        P.op("dve", C("tensor_scalar", c_[:], g_[:, 2:514], CW[:, 2, fc:fc + 1], CB[:, fc:fc + 1], op0=ALU.mult, op1=ALU.add),
                             reads=[g_.r0, CW.r0, CB.r0], writes=[c_.r0])
                        P.op("dve", C("scalar_tensor_tensor", out=c_[:], in0=g_[:, 1:513], scalar=CW[:, 1, fc:fc + 1], in1=c_[:], op0=ALU.mult, op1=ALU.add),
                             reads=[g_.r0, CW.r0, c_.r0], writes=[c_.r0])
                        P.op("dve", C("scalar_tensor_tensor", out=c_[:], in0=g_[:, 0:512], scalar=CW[:, 0, fc:fc + 1], in1=c_[:], op0=ALU.mult, op1=ALU.add),
                             reads=[g_.r0, CW.r0, c_.r0], writes=[c_.r0])
                        s_ = sl[b]
                        P.op("act", C("activation", out=s_[:], in_=c_[:], func=AF.Silu), reads=[c_.r0], writes=[s_.r0])
                        P.op("dve", C("tensor_tensor", out=actT[:, fc, :], in0=s_[:], in1=pu[b][:], op=ALU.mult),
                             reads=[s_.r0, pu[b].r0], writes=[actT.res[fc]])
                    for tt in range(4):
                        t0 = st * 512 + tt * 128
                        items = []
                        for hf in range(2):
                            for fc in range(NF):
                                items.append((pz[:, hf, :], actT[:, fc, tt * 128:(tt + 1) * 128], WD[:, fc, hf * 512:(hf + 1) * 512], fc == 0, fc == NF - 1))
                        P.op("pe", mmgroup(items), reads=[WD.r0] + actT.res, writes=[pz.r0])
                        o = xo[0]
                        P.op("dve", C("tensor_tensor", out=o[:], in0=xts[tt][:], in1=pz[:].rearrange("p a b -> p (a b)"), op=ALU.add),
                             reads=[xts[tt].r0, pz.r0], writes=[o.r0])
                        P.dma("pool", dst[t0:t0 + 128, :], o[:], reads=[o.r0], writes=[dres[dst_name][st]])
                P.barrier()

        src, src_name = x_in, None
        for l in range(DEPTH):
            phase_P(l, src, src_name)
            if stop_after == "P":
                break
            with ExitStack() as lc:
                KCMPT = sbuf(lc, "KCMPT", [128, NCT * 128], BF16)
                VCMP = sbuf(lc, "VCMP", [128, 2, NCT, 65], BF16)
                phase_C(l, KCMPT, VCMP)
                if stop_after != "C":
                    phase_A1(l)
                    if stop_after != "A1":
                        phase_A2(l, src, src_name, KCMPT, VCMP)
            if stop_after in ("C", "A1", "A2"):
                break
            last = (l == DEPTH - 1)
            phase_F(l, y_out if last else XR2, "Y" if last else "XR2")
            src, src_name = XR2, "XR2"

        final_ev = P.all_events()
        with nc.Block() as block:
            @block.tensor
            def _(e):
                P.replay("pe", e)

            @block.scalar
            def _(e):
                P.replay("act", e)

            @block.vector
            def _(e):
                P.replay("dve", e)

            @block.gpsimd
            def _(e):
                P.replay("pool", e)

            @block.sync
            def _(e):
                P.replay("sp", e)
                for sk, v in final_ev:
                    e.wait_ge(P.sem[sk], v)
        nc._n_rec = P.n_inst
    return nc


_CACHE = {}


def _get_prog(S, depth):
    key = (S, depth)
    if key not in _CACHE:
        _CACHE[key] = build(S, depth)
    return _CACHE[key]


WNAMES = ("norm_mix_g", "w_in", "a_q_g", "a_k_g", "b_q_g", "b_k_g", "cmp_pos", "cmp_w1", "cmp_w2",
          "w_proj_a", "w_proj_b", "w_out", "norm_ffn_g", "w_up", "conv_w", "conv_b", "w_down")


def kernel(**inputs):
    x = np.ascontiguousarray(np.asarray(inputs["x"], dtype=np.float32))
    B, S, _ = x.shape
    depth = inputs["w_in"].shape[0]
    consts = host_consts(S)
    ws = {k: np.ascontiguousarray(np.asarray(inputs[k], dtype=np.float32)) for k in WNAMES}
    n = 8
    if FUSED:
        nc = _get_prog(S, depth)
        in_maps = []
        for c in range(n):
            m = {"x": x[c % B]}
            m.update(ws)
            m.update(consts)
            in_maps.append(m)
        res = run_bass_kernel_spmd(nc, in_maps, core_ids=list(range(n)))
        return np.stack([res.results[b]["y"] for b in range(B)], axis=0).astype(np.float32)
    nc = _get_prog(S, 1)
    cur = [x[c % B] for c in range(n)]
    for l in range(depth):
        in_maps = []
        for c in range(n):
            m = {"x": np.ascontiguousarray(cur[c])}
            m.update({k: np.ascontiguousarray(v[l:l + 1]) for k, v in ws.items()})
            m.update(consts)
            in_maps.append(m)
        res = run_bass_kernel_spmd(nc, in_maps, core_ids=list(range(n)))
        cur = [res.results[c]["y"] for c in range(n)]
    return np.stack([cur[b] for b in range(B)], axis=0).astype(np.float32)
```

```python
import numpy as np
from contextlib import ExitStack
import concourse.bass as bass
import concourse.mybir as mybir
from concourse.bass_utils import run_bass_kernel_spmd

F32 = mybir.dt.float32
BF16 = mybir.dt.bfloat16
AF = mybir.ActivationFunctionType
ALU = mybir.AluOpType
AX = mybir.AxisListType

D = 1024
NIN = 7960
DFF = 2816
EPS = 1e-6
TINY = 1e-30
NEGB = -30000.0
DIL_PAIRS = ((128, 1), (512, 4), (2048, 16))
C_AQ, C_AK, C_AV, C_BQ, C_BKV, C_GATE, C_MERGE = 0, 1536, 3072, 4608, 5120, 5888, 5912
FUSED = True


class Res:
    __slots__ = ("w", "r")

    def __init__(self):
        self.w = None
        self.r = {}


class Buf:
    def __init__(self, t, nres=1):
        self.t = t
        self.res = [Res() for _ in range(nres)]

    def __getitem__(self, k):
        return self.t[k]

    @property
    def r0(self):
        return self.res[0]


COMPUTE = ("pe", "act", "dve", "pool")


class Prog:
    ND = 16

    def __init__(self, nc, ctx):
        self.nc = nc
        self.ops = {e: [] for e in ("pe", "act", "dve", "pool", "sp")}
        self.cnt = {e: 0 for e in COMPUTE}
        self.sem = {}
        for e in COMPUTE:
            self.sem[e] = ctx.enter_context(nc.semaphore("s_" + e))
        self.dma_val = {}
        for q in ("sp", "pool"):
            for k in range(self.ND):
                key = (q, k)
                self.sem[key] = ctx.enter_context(nc.semaphore(f"d_{q}_{k}"))
                self.dma_val[key] = 0
        self.rr = {"sp": 0, "pool": 0}
        self.seen = {e: {} for e in self.ops}
        self.n_inst = 0

    def _emit(self, eng, fn, deps, semkey, inc):
        best = {}
        for (sk, v) in deps:
            if best.get(sk, 0) < v:
                best[sk] = v
        waits = []
        for sk, v in best.items():
            if sk == eng and eng == "pe":
                continue
            if self.seen[eng].get(sk, 0) >= v:
                continue
            self.seen[eng][sk] = v
            waits.append((sk, v))
        self.ops[eng].append((waits, fn, semkey, inc))
        self.n_inst += 1 + len(waits)

    def op(self, eng, fn, reads=(), writes=(), dma=False):
        deps = []
        for r in reads:
            if r.w is not None:
                deps.append(r.w)
        for w in writes:
            if w.w is not None:
                deps.append(w.w)
            deps.extend(w.r.items())
        if dma:
            k = self.rr[eng]
            self.rr[eng] = (k + 1) % self.ND
            semkey = (eng, k)
            prev = self.dma_val[semkey]
            if prev > 0:
                deps.append((semkey, prev))
            val = prev + 16
            self.dma_val[semkey] = val
            inc = 16
        else:
            semkey = eng
            self.cnt[eng] += 1
            val = self.cnt[eng]
            inc = 1
        self._emit(eng, fn, deps, semkey, inc)
        for r in reads:
            if r.r.get(semkey, 0) < val:
                r.r[semkey] = val
        for w in writes:
            w.w = (semkey, val)
            w.r = {}
        return (semkey, val)

    def dma(self, q, out, in_, reads=(), writes=(), slow=False):
        if slow:
            fn = lambda e, out=out, in_=in_: e.dma_start(out=out, in_=in_, allow_slow_non_contiguous=True)
        else:
            fn = lambda e, out=out, in_=in_: e.dma_start(out=out, in_=in_)
        return self.op(q, fn, reads, writes, dma=True)

    def all_events(self):
        ev = [(e, self.cnt[e]) for e in COMPUTE if self.cnt[e] > 0]
        ev += [(k, v) for k, v in self.dma_val.items() if v > 0]
        return ev

    def barrier(self):
        ev = self.all_events()
        for eng in self.ops:
            waits = []
            for sk, v in ev:
                if sk == eng:
                    continue
                if self.seen[eng].get(sk, 0) >= v:
                    continue
                self.seen[eng][sk] = v
                waits.append((sk, v))
            if waits:
                self.ops[eng].append((waits, None, None, 0))
                self.n_inst += len(waits)

    def replay(self, eng, e):
        for (waits, fn, semkey, inc) in self.ops[eng]:
            for sk, v in waits:
                e.wait_ge(self.sem[sk], v)
            if fn is not None:
                ins = fn(e)
                ins.then_inc(self.sem[semkey], inc)


def C(name, *a, **k):
    return lambda e: getattr(e, name)(*a, **k)


def mmgroup(items):
    def fn(e):
        ins = None
        for (out, lhsT, rhs, st, sp) in items:
            ins = e.matmul(out, lhsT, rhs, start=st, stop=sp)
        return ins
    return fn


def host_consts(S):
    NT = S // 128
    NJ = S // 64
    NCB = S // 16 - 1
    NCT = (NCB + 127) // 128
    half = 32
    inv_freq = (np.float32(10000.0) ** (-(np.arange(half, dtype=np.float32)) / np.float32(half))).astype(np.float32)
    fidx = (np.arange(128) % 64) % 32

    def tabs(pos):
        ang = pos.astype(np.float32)[None, :] * inv_freq[fidx][:, None]
        return np.cos(ang).astype(np.float32), np.sin(ang).astype(np.float32)

    cos, sin = tabs(np.arange(S))
    posc = np.zeros(NCT * 128, dtype=np.float32)
    posc[:NCB] = np.arange(NCB) * 16 + 31
    cosc, sinc = tabs(posc)
    rot = np.zeros((128, 128), np.float32)
    for m in range(128):
        hb, d = (m // 64) * 64, m % 64
        if d < 32:
            rot[hb + d + 32, m] = -1.0
        else:
            rot[hb + d - 32, m] = 1.0
    bones = np.zeros((128, 128), np.float32)
    bones[:64, :64] = 1.0 / 64
    bones[64:, 64:] = 1.0 / 64
    ident = np.eye(128, dtype=np.float32)
    kk = np.arange(128)[:, None]
    qq = np.arange(128)[None, :]
    masks = np.stack([(kk >= qq), (kk <= qq), (kk > qq)], axis=1).astype(np.float32)
    mats = np.stack([rot, bones, ident], axis=1)
    eall = (np.arange(128)[:, None] == (np.arange(S)[None, :] // 64)).astype(np.float32)
    mm = np.zeros((NCT * 128, 128), np.float32)
    for c in range(NCB):
        for n in (c, c + 1):
            if n // 4 < 128:
                mm[c, n // 4] += 1.0
    mmat = mm.reshape(NCT, 128, 128).transpose(1, 0, 2).copy()
    cm = np.zeros((128, 17, 128), np.float32)
    cl = np.arange(128)[:, None]
    for v in range(16):
        cp = cl - 8 * v
        cm[:, v, :] = (16 * cp + 31 <= qq)
    cp = cl - 128
    cm[:, 16, :] = (16 * cp + 31 <= qq)
    tk = np.zeros((128, 256), np.float32)
    ta = np.zeros((128, 256), np.float32)
    for p in range(128):
        hi = 1 if p >= 64 else 0
        for m in range(256):
            r = m - 127
            if r > hi:
                ta[p, m] = -float(r - hi)
            elif r == hi:
                ta[p, m] = 2e9
            elif r == hi - 1:
                ta[p, m] = 1e9
            else:
                tk[p, m] = 1.0
    return dict(c_cos=cos, c_sin=sin, c_cosc=np.ascontiguousarray(cosc[:64]), c_sinc=np.ascontiguousarray(sinc[:64]),
                c_mats=mats, c_masks=masks, c_eall=eall, c_mmat=mmat, c_cm=cm, c_tk=tk, c_ta=ta)


def build(S, DEPTH, dbg=(), stop_after=None):
    NT = S // 128
    NST = S // 512
    NJ = 128
    NCB = S // 16 - 1
    NCT = (NCB + 127) // 128
    nc = bass.Bass("TRN2", target_bir_lowering=False)

    def din(name, shape, dt=F32):
        return nc.dram_tensor(name, list(shape), dt, kind="ExternalInput").ap()

    x_in = din("x", [S, D])
    norm_mix_g = din("norm_mix_g", [DEPTH, D])
    w_in = din("w_in", [DEPTH, D, NIN])
    a_q_g = din("a_q_g", [DEPTH, 64])
    a_k_g = din("a_k_g", [DEPTH, 64])
    b_q_g = din("b_q_g", [DEPTH, 64])
    b_k_g = din("b_k_g", [DEPTH, 3, 64])
    cmp_pos = din("cmp_pos", [DEPTH, 2, 32, 64])
    cmp_w1 = din("cmp_w1", [DEPTH, 2, 2048, 256])
    cmp_w2 = din("cmp_w2", [DEPTH, 2, 256, 64])
    w_proj_a = din("w_proj_a", [DEPTH, 512, D])
    w_proj_b = din("w_proj_b", [DEPTH, 512, D])
    w_out = din("w_out", [DEPTH, D, D])
    norm_ffn_g = din("norm_ffn_g", [DEPTH, D])
    w_up = din("w_up", [DEPTH, D, 2 * DFF])
    conv_w = din("conv_w", [DEPTH, 3, DFF])
    conv_b = din("conv_b", [DEPTH, DFF])
    w_down = din("w_down", [DEPTH, DFF, D])
    c_cos = din("c_cos", [128, S])
    c_sin = din("c_sin", [128, S])
    c_cosc = din("c_cosc", [64, NCT * 128])
    c_sinc = din("c_sinc", [64, NCT * 128])
    c_mats = din("c_mats", [128, 3, 128])
    c_masks = din("c_masks", [128, 3, 128])
    c_eall = din("c_eall", [128, S])
    c_mmat = din("c_mmat", [128, NCT, 128])
    c_cm = din("c_cm", [128, 17, 128])
    c_tk = din("c_tk", [128, 256])
    c_ta = din("c_ta", [128, 256])
    y_out = nc.dram_tensor("y", [S, D], F32, kind="ExternalOutput").ap()

    def dscr(name, shape, dt):
        kind = "ExternalOutput" if name in dbg else "Internal"
        return nc.dram_tensor(name, list(shape), dt, kind=kind).ap()

    QAT = dscr("QAT", [1536, S], BF16)
    KAT = dscr("KAT", [1536, S], BF16)
    VA = dscr("VA", [S, 1536], BF16)
    QBT = dscr("QBT", [512, S], BF16)
    KSELT = dscr("KSELT", [128, S], BF16)
    KWINT = dscr("KWINT", [128, S], BF16)
    VSEL = dscr("VSEL", [S, 128], BF16)
    VWIN = dscr("VWIN", [S, 128], BF16)
    KCRT = dscr("KCRT", [128, S], BF16)
    VCRT = dscr("VCRT", [128, S], BF16)
    GT = dscr("GT", [24, S], F32)
    MGT = dscr("MGT", [2048, S], BF16)
    OAT = dscr("OAT", [512, S], BF16)
    XR1 = dscr("XR1", [S, D], F32)
    XR2 = dscr("XR2", [S, D], F32)
    dres = {n: [Res() for _ in range(NST)] for n in
            ("QAT", "KAT", "VA", "QBT", "KSELT", "KWINT", "VSEL", "VWIN", "KCRT", "VCRT", "GT", "MGT", "OAT", "XR1", "XR2", "Y")}

    def dall(n):
        return dres[n]

    with ExitStack() as top:
        P = Prog(nc, top)

        uid = [0]

        def sbuf(cx, name, shape, dt, nres=1):
            uid[0] += 1
            t = cx.enter_context(nc.sbuf_tensor(f"{name}_{uid[0]}", list(shape), dt))
            return Buf(t, nres)

        def psum(cx, name, shape, dt=F32):
            uid[0] += 1
            t = cx.enter_context(nc.psum_tensor(f"{name}_{uid[0]}", list(shape), dt))
            return Buf(t)

        MATS = sbuf(top, "MATS", [128, 3, 128], BF16)
        MASKS = sbuf(top, "MASKS", [128, 3, 128], BF16)
        ONESF = sbuf(top, "ONESF", [128, 64], F32)
        EPSC = sbuf(top, "EPSC", [128, 1], F32)
        with ExitStack() as cx:
            st1 = sbuf(cx, "cst1", [128, 3, 128], F32)
            st2 = sbuf(cx, "cst2", [128, 3, 128], F32)
            P.dma("sp", st1[:], c_mats, writes=[st1.r0])
            P.dma("sp", st2[:], c_masks, writes=[st2.r0])
            P.op("dve", C("tensor_copy", MATS[:], st1[:]), reads=[st1.r0], writes=[MATS.r0])
            P.op("dve", C("tensor_copy", MASKS[:], st2[:]), reads=[st2.r0], writes=[MASKS.r0])
            P.op("pool", C("memset", ONESF[:], 1.0), writes=[ONESF.r0])
            P.op("pool", C("memset", EPSC[:], EPS), writes=[EPSC.r0])
            P.barrier()
        ROT = MATS[:, 0, :]
        BON = MATS[:, 1, :]
        IDN = MATS[:, 2, :]

        def norm_to_hT(cx_bufs, l_gT, src, tok0, hT, hT_res, tt):
            xt, junk, ss, rs, xs, tp = cx_bufs
            P.dma("sp", xt[:], src[tok0:tok0 + 128, :], reads=[], writes=[xt.r0])
            P.op("pool", C("memset", ss[:], 0.0), writes=[ss.r0])
            P.op("act", C("activation", out=junk[:], in_=xt[:], func=AF.Square, accum_out=ss[:]),
                 reads=[xt.r0], writes=[junk.r0, ss.r0])
            P.op("act", C("activation", out=rs[:], in_=ss[:], func=AF.Sqrt, bias=EPSC[:, 0:1], scale=1.0 / D),
                 reads=[ss.r0, EPSC.r0], writes=[rs.r0])
            P.op("dve", C("reciprocal", rs[:], rs[:]), reads=[rs.r0], writes=[rs.r0])
            P.op("dve", C("tensor_scalar", xs[:], xt[:], rs[:, 0:1], None, op0=ALU.mult),
                 reads=[xt.r0, rs.r0], writes=[xs.r0])
            items = [(tp[:, kc, :], xs[:, kc * 128:(kc + 1) * 128], IDN, True, True) for kc in range(8)]
            P.op("pe", mmgroup(items), reads=[xs.r0, MATS.r0], writes=[tp.r0])
            P.op("dve", C("tensor_tensor", out=hT[:, :, tt * 128:(tt + 1) * 128], in0=tp[:],
                                                  in1=l_gT[:, :].unsqueeze(2).to_broadcast([128, 8, 128]), op=ALU.mult),
                 reads=[tp.r0, l_gT.r0], writes=[hT_res])
            return xt

        def load_cast(cx, dst_ap_fn, src_ap_fn, nchunks, shape, dst_res, name, eng="pool"):
            stg = [sbuf(cx, f"{name}_stg{i}", shape, F32) for i in range(2)]
            for i in range(nchunks):
                s = stg[i % 2]
                P.dma("sp", s[:], src_ap_fn(i), writes=[s.r0])
                P.op(eng, C("tensor_copy", dst_ap_fn(i), s[:]), reads=[s.r0], writes=[dst_res])

        def phase_P(l, src, src_name):
            with ExitStack() as cx:
                WIN = sbuf(cx, "WIN", [128, 8, NIN], BF16)
                with ExitStack() as cx2:
                    Q4 = NIN // 4
                    load_cast(cx2, lambda i: WIN[:, i // 4, (i % 4) * Q4:(i % 4 + 1) * Q4],
                              lambda i: w_in[l, (i // 4) * 128:(i // 4 + 1) * 128, (i % 4) * Q4:(i % 4 + 1) * Q4],
                              32, [128, Q4], WIN.r0, "win")
                    P.barrier()
                gT = sbuf(cx, "gT", [128, 8], F32)
                P.dma("sp", gT[:], norm_mix_g[l].rearrange("(kc p) -> p kc", p=128), writes=[gT.r0], slow=True)
                GC = sbuf(cx, "GC", [128, 5], F32)
                for ci, gsrc in enumerate((a_q_g[l], a_k_g[l], b_q_g[l], b_k_g[l, 1], b_k_g[l, 2])):
                    for hb in range(2):
                        P.dma("sp", GC[hb * 64:(hb + 1) * 64, ci:ci + 1], gsrc.rearrange("(d o) -> d o", o=1),
                              writes=[GC.r0], slow=True)
                hTs = [sbuf(cx, f"hT{i}", [128, 8, 512], BF16) for i in range(2)]
                nb = [(sbuf(cx, f"nx{i}", [128, D], F32), sbuf(cx, f"nj{i}", [128, D], BF16), sbuf(cx, f"nss{i}", [128, 1], F32),
                       sbuf(cx, f"nrs{i}", [128, 1], F32), sbuf(cx, f"nxs{i}", [128, D], BF16),
                       psum(cx, f"ntp{i}", [128, 8, 128])) for i in range(1)]
                cs = [(sbuf(cx, f"cos{i}", [128, 512], F32), sbuf(cx, f"sin{i}", [128, 512], F32)) for i in range(2)]
                pa = [psum(cx, f"pa{i}", [128, 512]) for i in range(2)]
                p2 = [psum(cx, f"p2{i}", [128, 512]) for i in range(2)]
                p3 = [psum(cx, f"p3{i}", [128, 512]) for i in range(2)]
                sq = [sbuf(cx, f"sq{i}", [128, 512], BF16) for i in range(2)]
                xg = [sbuf(cx, f"xg{i}", [128, 512], BF16) for i in range(2)]
                rstd = [sbuf(cx, f"rstd{i}", [128, 512], F32) for i in range(2)]
                t1 = [sbuf(cx, f"t1{i}", [128, 512], F32) for i in range(2)]
                t2 = [sbuf(cx, f"t2{i}", [128, 512], F32) for i in range(2)]
                ob = [sbuf(cx, f"ob{i}", [128, 512], BF16) for i in range(3)]
                of = [sbuf(cx, f"of{i}", [128, 512], F32) for i in range(2)]
                gi = 0
                oi = 0
                for st in range(NST):
                    hT = hTs[st % 2]
                    c_t, s_t = cs[st % 2]
                    tsl = slice(st * 512, (st + 1) * 512)
                    P.dma("sp", c_t[:], c_cos[:, tsl], writes=[c_t.r0])
                    P.dma("sp", s_t[:], c_sin[:, tsl], writes=[s_t.r0])
                    for tt in range(4):
                        norm_to_hT(nb[0], gT, src, st * 512 + tt * 128, hT, hT.r0, tt)
                    wr_src = [dres[src_name][st]] if src_name else []
                    nr_groups = []
                    for g in range(12):
                        nr_groups.append((C_AQ + g * 128, 0, QAT, "QAT", g * 128))
                    for g in range(12):
                        nr_groups.append((C_AK + g * 128, 1, KAT, "KAT", g * 128))
                    for g in range(4):
                        nr_groups.append((C_BQ + g * 128, 2, QBT, "QBT", g * 128))
                    nr_groups.append((C_BKV + 256, 3, KSELT, "KSELT", 0))
                    nr_groups.append((C_BKV + 512, 4, KWINT, "KWINT", 0))
                    pend = None

                    def nr_epilogue(b, dst, dname, r0):
                        nonlocal oi
                        P.op("pe", mmgroup([(p2[b][:], BON, sq[b][:], True, True)]), reads=[sq[b].r0, MATS.r0], writes=[p2[b].r0])
                        P.op("pe", mmgroup([(p3[b][:], ROT, xg[b][:], True, True)]), reads=[xg[b].r0, MATS.r0], writes=[p3[b].r0])
                        P.op("act", C("activation", out=rstd[b][:], in_=p2[b][:], func=AF.Ln, bias=EPSC[:, 0:1], scale=1.0),
                             reads=[p2[b].r0, EPSC.r0], writes=[rstd[b].r0])
                        P.op("act", C("activation", out=rstd[b][:], in_=rstd[b][:], func=AF.Exp, scale=-0.5),
                             reads=[rstd[b].r0], writes=[rstd[b].r0])
                        P.op("pool", C("tensor_tensor", out=t1[b][:], in0=xg[b][:], in1=c_t[:], op=ALU.mult),
                             reads=[xg[b].r0, c_t.r0], writes=[t1[b].r0])
                        P.op("dve", C("tensor_tensor", out=t2[b][:], in0=p3[b][:], in1=s_t[:], op=ALU.mult),
                             reads=[p3[b].r0, s_t.r0], writes=[t2[b].r0])
                        P.op("dve", C("tensor_tensor", out=t2[b][:], in0=t1[b][:], in1=t2[b][:], op=ALU.add),
                             reads=[t1[b].r0, t2[b].r0], writes=[t2[b].r0])
                        o = ob[oi % 3]
                        oi += 1
                        P.op("dve", C("tensor_tensor", out=o[:], in0=t2[b][:], in1=rstd[b][:], op=ALU.mult),
                             reads=[t2[b].r0, rstd[b].r0], writes=[o.r0])
                        P.dma("pool", dst[r0:r0 + 128, tsl], o[:], reads=[o.r0], writes=[dres[dname][st]])
                    for (c0, gci, dst, dname, r0) in nr_groups:
                        b = gi % 2
                        gi += 1
                        items = [(pa[b][:], WIN[:, kc, c0:c0 + 128], hT[:, kc, :], kc == 0, kc == 7) for kc in range(8)]
                        P.op("pe", mmgroup(items), reads=[WIN.r0, hT.r0], writes=[pa[b].r0])
                        P.op("act", C("activation", out=sq[b][:], in_=pa[b][:], func=AF.Square),
                             reads=[pa[b].r0], writes=[sq[b].r0])
                        P.op("act", C("activation", out=xg[b][:], in_=pa[b][:], func=AF.Identity,
                                                                         scale=GC[:, gci:gci + 1]),
                             reads=[pa[b].r0, GC.r0], writes=[xg[b].r0])
                        if pend is not None:
                            nr_epilogue(*pend)
                        pend = (b, dst, dname, r0)
                    nr_epilogue(*pend)
                    sg = [(C_BKV + 0, 128, AF.Identity, KCRT, "KCRT", 0, BF16), (C_BKV + 128, 128, AF.Identity, VCRT, "VCRT", 0, BF16)]
                    for g in range(16):
                        sg.append((C_MERGE + g * 128, 128, AF.Sigmoid, MGT, "MGT", g * 128, BF16))
                    sg.append((C_GATE, 24, AF.Sigmoid, GT, "GT", 0, F32))
                    for (c0, w, func, dst, dname, r0, dt) in sg:
                        b = gi % 2
                        gi += 1
                        items = [(pa[b][0:w, :], WIN[:, kc, c0:c0 + w], hT[:, kc, :], kc == 0, kc == 7) for kc in range(8)]
                        P.op("pe", mmgroup(items), reads=[WIN.r0, hT.r0], writes=[pa[b].r0])
                        if dt == BF16:
                            o = ob[oi % 3]
                            oi += 1
                        else:
                            o = of[0]
                        P.op("act", C("activation", out=o[0:w, :], in_=pa[b][0:w, :], func=func),
                             reads=[pa[b].r0], writes=[o.r0])
                        P.dma("pool", dst[r0:r0 + w, tsl], o[0:w, :], reads=[o.r0], writes=[dres[dname][st]])
                    vb = [(C_AV, 512, VA, "VA", 0), (C_AV + 512, 512, VA, "VA", 512), (C_AV + 1024, 512, VA, "VA", 1024),
                          (C_BKV + 384, 128, VSEL, "VSEL", 0), (C_BKV + 640, 128, VWIN, "VWIN", 0)]
                    for tt in range(4):
                        t0 = st * 512 + tt * 128
                        for (c0, w, dst, dname, cc0) in vb:
                            b = gi % 2
                            gi += 1
                            items = [(pa[b][:, 0:w], hT[:, kc, tt * 128:(tt + 1) * 128], WIN[:, kc, c0:c0 + w], kc == 0, kc == 7)
                                     for kc in range(8)]
                            P.op("pe", mmgroup(items), reads=[WIN.r0, hT.r0], writes=[pa[b].r0])
                            o = ob[oi % 3]
                            oi += 1
                            P.op("act", C("activation", out=o[:, 0:w], in_=pa[b][:, 0:w], func=AF.Identity),
                                 reads=[pa[b].r0], writes=[o.r0])
                            P.dma("pool", dst[t0:t0 + 128, cc0:cc0 + w], o[:, 0:w], reads=[o.r0], writes=[dres[dname][st]])
                P.barrier()

        def phase_C(l, KCMPT, VCMP):
            with ExitStack() as cx:
                raw = sbuf(cx, "craw", [64, S], BF16)
                w1b = sbuf(cx, "cw1b", [64, 32, 256], BF16)
                w2b = sbuf(cx, "cw2b", [128, 2, 64], BF16)
                posb = sbuf(cx, "cposb", [64, 32], BF16)
                posf = sbuf(cx, "cposf", [64, 32], F32)
                w2f = sbuf(cx, "cw2f", [128, 2, 64], F32)
                gk = sbuf(cx, "cgk", [64, 1], F32)
                cosc = sbuf(cx, "ccos", [64, NCT * 128], F32)
                sinc = sbuf(cx, "csin", [64, NCT * 128], F32)
                ph = psum(cx, "cph", [128, 512])
                pb = psum(cx, "cpb", [128, 8])
                pk = psum(cx, "cpk", [128, 512])
                pq2 = psum(cx, "cp2", [128, 512])
                pq3 = psum(cx, "cp3", [128, 512])
                pv = psum(cx, "cpv", [128, 64])
                bcol = sbuf(cx, "cbcol", [128, 1], F32)
                xh = sbuf(cx, "cxh", [128, 512], F32)
                x2 = sbuf(cx, "cx2", [128, 512], F32)
                sg_ = sbuf(cx, "csg", [128, 512], F32)
                h1g = sbuf(cx, "ch1g", [128, 2, 512], BF16)
                sqc = sbuf(cx, "csq", [64, 512], BF16)
                xgc = sbuf(cx, "cxg", [64, 512], BF16)
                rsc = sbuf(cx, "crs", [64, 512], F32)
                t1c = sbuf(cx, "ct1", [64, 512], F32)
                t2c = sbuf(cx, "ct2", [64, 512], F32)
                okc = sbuf(cx, "cok", [64, 512], BF16)
                P.dma("sp", cosc[:], c_cosc, writes=[cosc.r0])
                P.dma("sp", sinc[:], c_sinc, writes=[sinc.r0])
                P.dma("sp", gk[:], b_k_g[l, 0].rearrange("(d o) -> d o", o=1), writes=[gk.r0], slow=True)
                P.op("pool", C("memset", KCMPT[:], 0.0), writes=[KCMPT.r0])
                P.op("pool", C("memset", VCMP[:], 0.0), writes=[VCMP.r0])
                P.op("pool", C("memset", VCMP[:, :, :, 64:65], 1.0), writes=[VCMP.r0])
                P.op("pool", C("memset", h1g[:], 0.0), writes=[h1g.r0])
                for typ in range(2):
                    with ExitStack() as cx2:
                        load_cast(cx2, lambda i: w1b[:, i * 8:(i + 1) * 8, :],
                                  lambda i: cmp_w1[l, typ].rearrange("(j d) h -> d j h", d=64)[:, i * 8:(i + 1) * 8, :],
                                  4, [64, 8, 256], w1b.r0, f"cw1_{typ}")
                        P.dma("sp", posf[:], cmp_pos[l, typ].rearrange("j d -> d j"), writes=[posf.r0], slow=True)
                        P.op("pool", C("tensor_copy", posb[:], posf[:]), reads=[posf.r0], writes=[posb.r0])
                        P.dma("sp", w2f[:], cmp_w2[l, typ].rearrange("(c p) d -> p c d", p=128), writes=[w2f.r0])
                        P.op("pool", C("tensor_copy", w2b[:], w2f[:]), reads=[w2f.r0], writes=[w2b.r0])
                        src = KCRT if typ == 0 else VCRT
                        sname = "KCRT" if typ == 0 else "VCRT"
                        for kvh in range(2):
                            P.dma("sp", raw[:], src[kvh * 64:(kvh + 1) * 64, :], reads=dall(sname), writes=[raw.r0])
                            for hc in range(2):
                                items = [(ph[:, 0:NCB], w1b[:, j, hc * 128:(hc + 1) * 128],
                                          raw[:, j:j + 16 * (NCB - 1) + 1:16], j == 0, j == 31) for j in range(32)]
                                P.op("pe", mmgroup(items), reads=[w1b.r0, raw.r0], writes=[ph.r0])
                                items = [(pb[:, 0:1], w1b[:, j, hc * 128:(hc + 1) * 128], posb[:, j:j + 1], j == 0, j == 31)
                                         for j in range(32)]
                                P.op("pe", mmgroup(items), reads=[w1b.r0, posb.r0], writes=[pb.r0])
                                P.op("dve", C("tensor_copy", bcol[:], pb[:, 0:1]), reads=[pb.r0], writes=[bcol.r0])
                                P.op("act", C("activation", out=xh[:, 0:NCB], in_=ph[:, 0:NCB], func=AF.Identity, bias=bcol[:, 0:1]),
                                     reads=[ph.r0, bcol.r0], writes=[xh.r0])
                                P.op("dve", C("tensor_tensor", out=x2[:, 0:NCB], in0=xh[:, 0:NCB], in1=xh[:, 0:NCB], op=ALU.mult),
                                     reads=[xh.r0], writes=[x2.r0])
                                P.op("dve", C("tensor_scalar", x2[:, 0:NCB], x2[:, 0:NCB], 0.044715, 1.0, op0=ALU.mult, op1=ALU.add),
                                     reads=[x2.r0], writes=[x2.r0])
                                P.op("dve", C("tensor_tensor", out=x2[:, 0:NCB], in0=x2[:, 0:NCB], in1=xh[:, 0:NCB], op=ALU.mult),
                                     reads=[x2.r0, xh.r0], writes=[x2.r0])
                                P.op("act", C("activation", out=sg_[:, 0:NCB], in_=x2[:, 0:NCB], func=AF.Sigmoid, scale=1.5957691216057308),
                                     reads=[x2.r0], writes=[sg_.r0])
                                P.op("dve", C("tensor_tensor", out=h1g[:, hc, 0:NCB], in0=xh[:, 0:NCB], in1=sg_[:, 0:NCB], op=ALU.mult),
                                     reads=[xh.r0, sg_.r0], writes=[h1g.r0])
                            if typ == 0:
                                items = [(pk[0:64, 0:NCB], w2b[:, hc, :], h1g[:, hc, 0:NCB], hc == 0, hc == 1) for hc in range(2)]
                                P.op("pe", mmgroup(items), reads=[w2b.r0, h1g.r0], writes=[pk.r0])
                                P.op("act", C("activation", out=sqc[:, 0:NCB], in_=pk[0:64, 0:NCB], func=AF.Square),
                                     reads=[pk.r0], writes=[sqc.r0])
                                P.op("act", C("activation", out=xgc[:, 0:NCB], in_=pk[0:64, 0:NCB], func=AF.Identity, scale=gk[:, 0:1]),
                                     reads=[pk.r0, gk.r0], writes=[xgc.r0])
                                P.op("pe", mmgroup([(pq2[0:64, 0:NCB], MATS[0:64, 1, 0:64], sqc[:, 0:NCB], True, True)]),
                                     reads=[sqc.r0, MATS.r0], writes=[pq2.r0])
                                P.op("pe", mmgroup([(pq3[0:64, 0:NCB], MATS[0:64, 0, 0:64], xgc[:, 0:NCB], True, True)]),
                                     reads=[xgc.r0, MATS.r0], writes=[pq3.r0])
                                P.op("act", C("activation", out=rsc[:, 0:NCB], in_=pq2[0:64, 0:NCB], func=AF.Sqrt, bias=EPSC[0:64, 0:1], scale=1.0),
                                     reads=[pq2.r0, EPSC.r0], writes=[rsc.r0])
                                P.op("dve", C("reciprocal", rsc[:, 0:NCB], rsc[:, 0:NCB]), reads=[rsc.r0], writes=[rsc.r0])
                                P.op("dve", C("tensor_tensor", out=t1c[:, 0:NCB], in0=xgc[:, 0:NCB], in1=cosc[:, 0:NCB], op=ALU.mult),
                                     reads=[xgc.r0, cosc.r0], writes=[t1c.r0])
                                P.op("dve", C("tensor_tensor", out=t2c[:, 0:NCB], in0=pq3[0:64, 0:NCB], in1=sinc[:, 0:NCB], op=ALU.mult),
                                     reads=[pq3.r0, sinc.r0], writes=[t2c.r0])
                                P.op("dve", C("tensor_tensor", out=t1c[:, 0:NCB], in0=t1c[:, 0:NCB], in1=t2c[:, 0:NCB], op=ALU.add),
                                     reads=[t1c.r0, t2c.r0], writes=[t1c.r0])
                                P.op("dve", C("tensor_tensor", out=okc[:, 0:NCB], in0=t1c[:, 0:NCB], in1=rsc[:, 0:NCB], op=ALU.mult),
                                     reads=[t1c.r0, rsc.r0], writes=[okc.r0])
                                P.dma("sp", KCMPT[kvh * 64:(kvh + 1) * 64, 0:NCB], okc[:, 0:NCB], reads=[okc.r0], writes=[KCMPT.r0])
                            else:
                                for ct in range(NCT):
                                    m = min(128, NCB - ct * 128)
                                    items = [(pv[0:m, :], h1g[:, hc, ct * 128:ct * 128 + m], w2b[:, hc, :], hc == 0, hc == 1) for hc in range(2)]
                                    P.op("pe", mmgroup(items), reads=[w2b.r0, h1g.r0], writes=[pv.r0])
                                    P.op("dve", C("tensor_copy", VCMP[0:m, kvh, ct, 0:64], pv[0:m, :]),
                                         reads=[pv.r0], writes=[VCMP.r0])
                        P.barrier()
                P.barrier()

        def phase_A1(l):
            with ExitStack() as cx:
                acc = sbuf(cx, "a_acc", [65, S], F32)
                Qt = [sbuf(cx, f"a_q{i}", [128, S], BF16) for i in range(2)]
                Kt = [sbuf(cx, f"a_k{i}", [128, S], BF16) for i in range(2)]
                for t_ in Qt + Kt:
                    P.op("pool", C("memset", t_[64:128, :], 0.0), writes=[t_.r0])
                Vt = [sbuf(cx, f"a_v{i}", [128, NT, 65], BF16) for i in range(2)]
                ps = [psum(cx, f"a_ps{i}", [128, 2, 128]) for i in range(2)]
                po = [psum(cx, f"a_po{i}", [128, 128]) for i in range(2)]
                pbx = [psum(cx, f"a_pb{i}", [128, 512]) for i in range(2)]
                pt = [sbuf(cx, f"a_pt{i}", [128, 2, 128], BF16) for i in range(4)]
                oa = sbuf(cx, "a_oa", [64, S], BF16)
                rd = sbuf(cx, "a_rd", [128, S], F32)
                P.op("pool", C("memset", rd[:], 0.0), writes=[rd.r0])
                for v in Vt:
                    P.op("pool", C("memset", v[:, :, 64:65], 1.0), writes=[v.r0])
                it = 0
                ld = 0
                for j in range(8):
                    for gi, (window, dil) in enumerate(DIL_PAIRS):
                        head = gi * 8 + j
                        r0 = head * 64
                        L = S // dil
                        nb = L // 128
                        q_, k_, v_ = Qt[ld % 2], Kt[ld % 2], Vt[ld % 2]
                        ld += 1
                        P.dma("sp", q_[0:64, :], QAT[r0:r0 + 64, :], reads=dall("QAT"), writes=[q_.r0])
                        P.dma("sp", k_[0:64, :], KAT[r0:r0 + 64, :], reads=dall("KAT"), writes=[k_.r0])
                        for r in range(dil):
                            srcv = VA[r::dil, r0:r0 + 64].rearrange("(jb kk) d -> kk jb d", kk=128)
                            for j0 in range(0, nb, 8):
                                j1 = min(nb, j0 + 8)
                                P.dma("sp", v_[:, r * nb + j0:r * nb + j1, 0:64], srcv[:, j0:j1, :], reads=dall("VA"), writes=[v_.r0])
                        pend = None

                        def pv_acc(b, p_t, r, Jb, c):
                            items = []
                            if Jb > 0:
                                items.append((po[b][0:65, :], v_[:, r * nb + Jb - 1, :], p_t[:, 0, :], True, False))
                            items.append((po[b][0:65, :], v_[:, r * nb + Jb, :], p_t[:, 1, :], Jb == 0, True))
                            P.op("pe", mmgroup(items), reads=[v_.r0, p_t.r0], writes=[po[b].r0])
                            if gi == 0:
                                P.op("dve", C("tensor_copy", acc[:, c], po[b][0:65, :]),
                                     reads=[po[b].r0], writes=[acc.r0])
                            else:
                                P.op("dve", C("tensor_tensor", out=acc[:, c], in0=acc[:, c], in1=po[b][0:65, :], op=ALU.add),
                                     reads=[po[b].r0, acc.r0], writes=[acc.r0])
                        for r in range(dil):
                            for Jb in range(nb):
                                def cols(J):
                                    s0 = J * 128 * dil + r
                                    return slice(s0, s0 + 127 * dil + 1, dil)
                                b = it % 2
                                p_t = pt[it % 4]
                                it += 1
                                items = []
                                if Jb > 0:
                                    items.append((ps[b][:, 0, :], k_[:, cols(Jb - 1)], q_[:, cols(Jb)], True, True))
                                items.append((ps[b][:, 1, :], k_[:, cols(Jb)], q_[:, cols(Jb)], True, True))
                                P.op("pe", mmgroup(items), reads=[k_.r0, q_.r0], writes=[ps[b].r0])
                                e0 = 0 if Jb > 0 else 1
                                P.op("act", C("activation", out=p_t[:, e0:2, :], in_=ps[b][:, e0:2, :], func=AF.Exp, scale=0.125),
                                     reads=[ps[b].r0], writes=[p_t.r0])
                                P.op("pool", C("tensor_tensor", out=p_t[:, e0:2, :], in0=p_t[:, e0:2, :], in1=MASKS[:, e0:2, :], op=ALU.mult),
                                     reads=[p_t.r0, MASKS.r0], writes=[p_t.r0])
                                if pend is not None:
                                    pv_acc(*pend)
                                pend = (b, p_t, r, Jb, cols(Jb))
                        pv_acc(*pend)
                    P.op("act", C("activation", out=rd[64:65, :], in_=acc[64:65, :], func=AF.Ln), reads=[acc.r0], writes=[rd.r0])
                    P.op("act", C("activation", out=rd[64:65, :], in_=rd[64:65, :], func=AF.Exp, scale=-1.0), reads=[rd.r0], writes=[rd.r0])
                    for ch in range(NST):
                        b = ch % 2
                        csl = slice(ch * 512, (ch + 1) * 512)
                        P.op("pe", mmgroup([(pbx[b][0:64, :], ONESF[:, 0:64], rd[:, csl], True, True)]),
                             reads=[rd.r0, ONESF.r0], writes=[pbx[b].r0])
                        P.op("dve", C("tensor_tensor", out=oa[:, csl], in0=acc[0:64, csl], in1=pbx[b][0:64, :], op=ALU.mult),
                             reads=[pbx[b].r0, acc.r0], writes=[oa.r0])
                    P.dma("sp", OAT[j * 64:(j + 1) * 64, :], oa[:], reads=[oa.r0], writes=dall("OAT"))
                P.barrier()

        def phase_A2(l, src, src_name, KCMPT, VCMP):
            with ExitStack() as cx:
                KS = sbuf(cx, "b_ks", [128, S], BF16)
                KW = sbuf(cx, "b_kw", [128, S], BF16)
                VS = sbuf(cx, "b_vs", [128, NT, 2, 65], BF16)
                VW = sbuf(cx, "b_vw", [128, NT, 2, 65], BF16)
                EALL = sbuf(cx, "b_eall", [128, S], BF16)
                MM = sbuf(cx, "b_mm", [128, NCT, 128], BF16)
                CM = sbuf(cx, "b_cm", [128, 17, 128], BF16)
                TK = sbuf(cx, "b_tk", [128, 256], F32)
                TA = sbuf(cx, "b_ta", [128, 256], F32)
                WPA = sbuf(cx, "b_wpa", [128, 4, D], BF16)
                WPB = sbuf(cx, "b_wpb", [64, 8, D], BF16)
                WO = sbuf(cx, "b_wo", [128, 8, D], BF16)
                with ExitStack() as cx2:
                    load_cast(cx2, lambda i: EALL[:, i * 1024:(i + 1) * 1024], lambda i: c_eall[:, i * 1024:(i + 1) * 1024],
                              S // 1024, [128, 1024], EALL.r0, "ea")
                    load_cast(cx2, lambda i: WPA[:, i, :], lambda i: w_proj_a[l, i * 128:(i + 1) * 128, :], 4, [128, D], WPA.r0, "wpa")
                    load_cast(cx2, lambda i: WPB[:, i, :], lambda i: w_proj_b[l, i * 64:(i + 1) * 64, :], 8, [64, D], WPB.r0, "wpb")
                    load_cast(cx2, lambda i: WO[:, i, :], lambda i: w_out[l, i * 128:(i + 1) * 128, :], 8, [128, D], WO.r0, "wo")
                    load_cast(cx2, lambda i: MM[:], lambda i: c_mmat, 1, [128, NCT, 128], MM.r0, "mm")
                    load_cast(cx2, lambda i: CM[:], lambda i: c_cm, 1, [128, 17, 128], CM.r0, "cm")
                    P.barrier()
                P.dma("sp", TK[:], c_tk, writes=[TK.r0])
                P.dma("sp", TA[:], c_ta, writes=[TA.r0])
                P.dma("sp", KS[:], KSELT, reads=dall("KSELT"), writes=[KS.r0])
                P.dma("sp", KW[:], KWINT, reads=dall("KWINT"), writes=[KW.r0])
                P.op("pool", C("memset", VS[:, :, :, 64:65], 1.0), writes=[VS.r0])
                P.op("pool", C("memset", VW[:, :, :, 64:65], 1.0), writes=[VW.r0])
                for h in range(2):
                    for k0 in range(0, NT, 8):
                        P.dma("sp", VS[:, k0:k0 + 8, h, 0:64], VSEL[k0 * 128:(k0 + 8) * 128, h * 64:(h + 1) * 64].rearrange("(kt kk) d -> kk kt d", kk=128),
                              reads=dall("VSEL"), writes=[VS.r0])
                        P.dma("sp", VW[:, k0:k0 + 8, h, 0:64], VWIN[k0 * 128:(k0 + 8) * 128, h * 64:(h + 1) * 64].rearrange("(kt kk) d -> kk kt d", kk=128),
                              reads=dall("VWIN"), writes=[VW.r0])
                ST = [psum(cx, f"b_st{i}", [128, 2, 512]) for i in range(2)]
                OT = [psum(cx, f"b_ot{i}", [128, 512]) for i in range(2)]
                UB = psum(cx, "b_ub", [128, 4, 128])
                MISC = psum(cx, "b_misc", [128, 512])
                QB = [sbuf(cx, f"b_qb{i}", [128, 2, 512], BF16) for i in range(2)]
                for q_ in QB:
                    P.op("pool", C("memset", q_[:], 0.0), writes=[q_.r0])
                GR = [sbuf(cx, f"b_gr{i}", [65, 24, 128], F32) for i in range(1)]
                MG = [sbuf(cx, f"b_mg{i}", [128, 16, 128], BF16) for i in range(2)]
                OA = [sbuf(cx, f"b_oa{i}", [128, 4, 128], BF16) for i in range(2)]
                XT = [sbuf(cx, f"b_xt{i}", [128, D], F32) for i in range(1)]
                PC = sbuf(cx, "b_pc", [128, 4, 512], BF16)
                PS_ = [sbuf(cx, f"b_ps{i}", [128, 2, 512], BF16) for i in range(3)]
                rs4 = sbuf(cx, "b_rs4", [128, 4], F32)
                psl = sbuf(cx, "b_psl", [128, 128], F32)
                sc2 = sbuf(cx, "b_sc2", [128, 128], F32)
                m8a = sbuf(cx, "b_m8a", [128, 8], F32)
                m8b = sbuf(cx, "b_m8b", [128, 8], F32)
                selb = sbuf(cx, "b_selb", [128, 128], BF16)
                selT = sbuf(cx, "b_selT", [128, 4, 128], BF16)
                UBR = [sbuf(cx, f"b_ubr{i}", [65, 512], F32) for i in range(3)]
                wrow = sbuf(cx, "b_wrow", [128, 512], F32)
                P.op("pool", C("memset", wrow[:], 0.0), writes=[wrow.r0])
                obf = sbuf(cx, "b_obf", [64, 512], F32)
                otmp = sbuf(cx, "b_otmp", [64, 512], F32)
                OBt = sbuf(cx, "b_obt", [64, 8, 128], BF16)
                m1 = sbuf(cx, "b_m1", [128, 8, 128], F32)
                m2 = sbuf(cx, "b_m2", [128, 8, 128], F32)
                mx = sbuf(cx, "b_mx", [128, 8, 128], BF16)
                x1 = [sbuf(cx, f"b_x1{i}", [128, D], F32) for i in range(1)]
                sti = 0
                psi = 0
                oti = 0
                ubi = 0
                import os as _os
                for i in range(int(_os.environ.get('A2_I0', 0)), min(NT, int(_os.environ.get('A2_I1', NT)))):
                    t0 = i * 128
                    tsl = slice(t0, t0 + 128)
                    stq = i // 4
                    qb, gr, mg, oa_, xt = QB[i % 2], GR[0], MG[i % 2], OA[i % 2], XT[0]
                    for h in range(2):
                        P.dma("sp", qb[h * 64:(h + 1) * 64, h, :].rearrange("p (g q) -> p g q", g=4), QBT[h * 256:(h + 1) * 256, tsl].rearrange("(g d) t -> d g t", d=64),
                              reads=[dres["QBT"][stq]], writes=[qb.r0])
                    P.dma("sp", gr[64:65, :, :], GT[:, tsl].rearrange("(o r) t -> o r t", o=1), reads=[dres["GT"][stq]], writes=[gr.r0])
                    P.dma("sp", mg[:], MGT[:, tsl].rearrange("(c p) t -> p c t", p=128), reads=[dres["MGT"][stq]], writes=[mg.r0])
                    P.dma("sp", oa_[:], OAT[:, tsl].rearrange("(c p) t -> p c t", p=128), reads=[dres["OAT"][stq]], writes=[oa_.r0])
                    P.dma("sp", xt[:], src[tsl, :], reads=([dres[src_name][stq]] if src_name else []), writes=[xt.r0])
                    for h in range(2):
                        hs = slice(h * 64, (h + 1) * 64)
                        qh = qb[:, h, :]

                        def score_tiles(kts, ksrc, extra_bias, dst, dsti):
                            nonlocal sti
                            b = sti % 2
                            sti += 1
                            items = []
                            for e_, kt in enumerate(kts):
                                items.append((ST[b][:, e_, :], ksrc[:, kt * 128:(kt + 1) * 128], qh, True, not extra_bias))
                                if extra_bias:
                                    items.append((ST[b][:, e_, :], EALL[0:NJ, kt * 128:(kt + 1) * 128],
                                                  selT[0:NJ, :, :].rearrange("p g q -> p (g q)"), False, True))
                            rd_ = [qb.r0, ksrc_res[id(ksrc)]] + ([EALL.r0, selT.r0] if extra_bias else [])
                            P.op("pe", mmgroup(items), reads=rd_, writes=[ST[b].r0])
                            n = len(kts)
                            P.op("act", C("activation", out=dst[:, dsti:dsti + n, :], in_=ST[b][:, 0:n, :], func=AF.Exp, scale=0.125),
                                 reads=[ST[b].r0], writes=[dst.r0])

                        ksrc_res = {id(KCMPT): KCMPT.r0, id(KS): KS.r0, id(KW): KW.r0}

                        def maskmul(dst, di, mask_ap, mres, eng="pool"):
                            P.op(eng, C("tensor_tensor",
                                out=dst[:, di, :].rearrange("p (g q) -> p g q", g=4),
                                in0=dst[:, di, :].rearrange("p (g q) -> p g q", g=4),
                                in1=mask_ap.unsqueeze(1).to_broadcast([128, 4, 128]), op=ALU.mult),
                                 reads=[dst.r0, mres], writes=[dst.r0])

                        nct = i // 16 + 1
                        for c0 in range(0, nct, 2):
                            kts = list(range(c0, min(c0 + 2, nct)))
                            score_tiles(kts, KCMPT, False, PC, c0)
                        maskmul(PC, nct - 1, CM[:, i % 16, :], CM.r0)
                        if i % 16 == 0 and i > 0:
                            maskmul(PC, nct - 2, CM[:, 16, :], CM.r0)
                        otc = OT[oti % 2]
                        oti += 1
                        items = [(otc[0:65, :], VCMP[:, h, ct, :], PC[:, ct, :], ct == 0, ct == nct - 1) for ct in range(nct)]
                        P.op("pe", mmgroup(items), reads=[VCMP.r0, PC.r0], writes=[otc.r0])
                        items = []
                        for g in range(4):
                            for ct in range(nct):
                                items.append((UB[:, g, 0:NJ], PC[:, ct, g * 128:(g + 1) * 128], MM[:, ct, 0:NJ], ct == 0, ct == nct - 1))
                        P.op("pe", mmgroup(items), reads=[PC.r0, MM.r0], writes=[UB.r0])
                        P.op("dve", C("tensor_reduce", out=rs4[:], in_=UB[:, :, 0:NJ], axis=AX.X, op=ALU.add), reads=[UB.r0], writes=[rs4.r0])
                        P.op("dve", C("tensor_scalar", rs4[:], rs4[:], 0.5, None, op0=ALU.mult), reads=[rs4.r0], writes=[rs4.r0])
                        P.op("dve", C("tensor_scalar", rs4[:], rs4[:], TINY, None, op0=ALU.max), reads=[rs4.r0], writes=[rs4.r0])
                        P.op("dve", C("reciprocal", rs4[:], rs4[:]), reads=[rs4.r0], writes=[rs4.r0])
                        P.op("dve", C("tensor_scalar", psl[:, 0:NJ], UB[:, 0, 0:NJ], rs4[:, 0:1], None, op0=ALU.mult), reads=[UB.r0, rs4.r0], writes=[psl.r0])
                        for g in range(1, 4):
                            P.op("dve", C("scalar_tensor_tensor", out=psl[:, 0:NJ], in0=UB[:, g, 0:NJ], scalar=rs4[:, g:g + 1], in1=psl[:, 0:NJ],
                                                                          op0=ALU.mult, op1=ALU.add), reads=[UB.r0, rs4.r0, psl.r0], writes=[psl.r0])
                        o_tab = 127 - 2 * i
                        P.op("dve", C("tensor_tensor", out=psl[:, 0:NJ], in0=psl[:, 0:NJ], in1=TK[:, o_tab:o_tab + NJ], op=ALU.mult),
                             reads=[psl.r0, TK.r0], writes=[psl.r0])
                        P.op("dve", C("tensor_tensor", out=psl[:, 0:NJ], in0=psl[:, 0:NJ], in1=TA[:, o_tab:o_tab + NJ], op=ALU.add),
                             reads=[psl.r0, TA.r0], writes=[psl.r0])
                        P.op("dve", C("memset", psl[:, 0:1], 3e9), writes=[psl.r0])
                        P.op("dve", C("max", out=m8a[:], in_=psl[:, 0:NJ]), reads=[psl.r0], writes=[m8a.r0])
                        P.op("dve", C("match_replace", out=sc2[:, 0:NJ], in_to_replace=m8a[:], in_values=psl[:, 0:NJ], imm_value=-1e9),
                             reads=[psl.r0, m8a.r0], writes=[sc2.r0])
                        P.op("dve", C("max", out=m8b[:], in_=sc2[:, 0:NJ]), reads=[sc2.r0], writes=[m8b.r0])
                        P.op("dve", C("tensor_reduce", out=m8a[:, 0:1], in_=m8b[:], axis=AX.X, op=ALU.min), reads=[m8b.r0], writes=[m8a.r0])
                        P.op("dve", C("tensor_scalar", sc2[:, 0:NJ], psl[:, 0:NJ], m8a[:, 0:1], None, op0=ALU.is_lt),
                             reads=[psl.r0, m8a.r0], writes=[sc2.r0])
                        P.op("dve", C("tensor_scalar", selb[:, 0:NJ], sc2[:, 0:NJ], NEGB, None, op0=ALU.mult),
                             reads=[sc2.r0], writes=[selb.r0])

                        def dense_branch(kts_all, ksrc, vsrc, bias, masks):
                            nonlocal psi, oti
                            ot = OT[oti % 2]
                            oti += 1
                            nk = len(kts_all)
                            pend = None

                            def pv(kts, pbuf):
                                items = [(ot[0:65, :], vsrc[:, kt, h, :], pbuf[:, e_, :], kt == kts_all[0], kt == kts_all[-1])
                                         for e_, kt in enumerate(kts)]
                                P.op("pe", mmgroup(items), reads=[vsrc.r0, pbuf.r0], writes=[ot.r0])
                            for c0 in range(0, nk, 2):
                                kts = kts_all[c0:c0 + 2]
                                pbuf = PS_[psi % 3]
                                psi += 1
                                score_tiles(kts, ksrc, bias, pbuf, 0)
                                for e_, kt in enumerate(kts):
                                    if kt in masks:
                                        maskmul(pbuf, e_, MASKS[:, masks[kt], :], MASKS.r0, eng="dve")
                                if pend is not None:
                                    pv(*pend)
                                pend = (kts, pbuf)
                            pv(*pend)
                            return ot

                        def combine(ot, br, first):
                            nonlocal ubi
                            ub = UBR[ubi % 3]
                            ubi += 1
                            P.op("dve", C("tensor_copy", ub[:], ot[0:65, :]), reads=[ot.r0], writes=[ub.r0])
                            P.op("dve", C("tensor_scalar", wrow[64:65, :], ub[64:65, :], TINY, None, op0=ALU.max), reads=[ub.r0], writes=[wrow.r0])
                            P.op("act", C("activation", out=wrow[64:65, :], in_=wrow[64:65, :], func=AF.Ln), reads=[wrow.r0], writes=[wrow.r0])
                            P.op("act", C("activation", out=wrow[64:65, :], in_=wrow[64:65, :], func=AF.Exp, scale=-1.0), reads=[wrow.r0], writes=[wrow.r0])
                            P.op("dve", C("tensor_tensor", out=wrow[64:65, :].rearrange("p (g q) -> p g q", g=4),
                                                                       in0=wrow[64:65, :].rearrange("p (g q) -> p g q", g=4),
                                                                       in1=gr[64:65, br * 8 + h * 4: br * 8 + h * 4 + 4, :], op=ALU.mult),
                                 reads=[wrow.r0, gr.r0], writes=[wrow.r0])
                            P.op("pe", mmgroup([(MISC[0:64, :], ONESF[:, 0:64], wrow[:, :], True, True)]), reads=[wrow.r0, ONESF.r0], writes=[MISC.r0])
                            if first:
                                P.op("dve", C("tensor_tensor", out=obf[:], in0=ub[0:64, :], in1=MISC[0:64, :], op=ALU.mult),
                                     reads=[ub.r0, MISC.r0], writes=[obf.r0])
                            else:
                                P.op("dve", C("tensor_tensor", out=otmp[:], in0=ub[0:64, :], in1=MISC[0:64, :], op=ALU.mult),
                                     reads=[ub.r0, MISC.r0], writes=[otmp.r0])
                                P.op("pool", C("tensor_tensor", out=obf[:], in0=obf[:], in1=otmp[:], op=ALU.add),
                                     reads=[obf.r0, otmp.r0], writes=[obf.r0])

                        wm = {i: 1}
                        if i - 4 >= 0:
                            wm[i - 4] = 2
                        otw = dense_branch(list(range(max(0, i - 4), i + 1)), KW, VW, False, wm)
                        P.op("pe", mmgroup([(MISC[0:NJ, 0:128], selb[:, 0:NJ], IDN, True, True)]), reads=[selb.r0, MATS.r0], writes=[MISC.r0])
                        P.op("dve", C("tensor_copy", selT[0:NJ, :, :], MISC[0:NJ, 0:128].unsqueeze(1).to_broadcast([NJ, 4, 128])),
                             reads=[MISC.r0], writes=[selT.r0])
                        combine(otc, 0, True)
                        ots = dense_branch(list(range(0, i + 1)), KS, VS, True, {i: 1})
                        combine(otw, 2, False)
                        combine(ots, 1, False)
                        P.op("pool", C("tensor_copy", OBt[:, h * 4:(h + 1) * 4, :], obf[:].rearrange("p (g q) -> p g q", g=4)),
                             reads=[obf.r0], writes=[OBt.r0])
                    ya = ST[sti % 2]
                    sti += 1
                    yb = ST[sti % 2]
                    sti += 1
                    yav = ya[:].rearrange("p a (b q) -> p (a b) q", q=128)
                    ybv = yb[:].rearrange("p a (b q) -> p (a b) q", q=128)
                    items = []
                    for cc in range(8):
                        for kc in range(4):
                            items.append((yav[:, cc, :], WPA[:, kc, cc * 128:(cc + 1) * 128], oa_[:, kc, :], kc == 0, kc == 3))
                    P.op("pe", mmgroup(items), reads=[WPA.r0, oa_.r0], writes=[ya.r0])
                    items = []
                    for cc in range(8):
                        for hd in range(8):
                            items.append((ybv[:, cc, :], WPB[:, hd, cc * 128:(cc + 1) * 128], OBt[:, hd, :], hd == 0, hd == 7))
                    P.op("pe", mmgroup(items), reads=[WPB.r0, OBt.r0], writes=[yb.r0])
                    P.op("dve", C("tensor_tensor", out=m1[:], in0=yav, in1=mg[:, 0:8, :], op=ALU.mult), reads=[ya.r0, mg.r0], writes=[m1.r0])
                    P.op("dve", C("tensor_tensor", out=m2[:], in0=ybv, in1=mg[:, 8:16, :], op=ALU.mult), reads=[yb.r0, mg.r0], writes=[m2.r0])
                    P.op("pool", C("tensor_tensor", out=mx[:], in0=m1[:], in1=m2[:], op=ALU.add), reads=[m1.r0, m2.r0], writes=[mx.r0])
                    z = ST[sti % 2]
                    sti += 1
                    items = []
                    for hf in range(2):
                        for kc in range(8):
                            items.append((z[:, hf, :], mx[:, kc, :], WO[:, kc, hf * 512:(hf + 1) * 512], kc == 0, kc == 7))
                    P.op("pe", mmgroup(items), reads=[mx.r0, WO.r0], writes=[z.r0])
                    xo = x1[0]
                    P.op("dve", C("tensor_tensor", out=xo[:], in0=xt[:], in1=z[:].rearrange("p a b -> p (a b)"), op=ALU.add),
                         reads=[xt.r0, z.r0], writes=[xo.r0])
                    P.dma("pool", XR1[tsl, :], xo[:], reads=[xo.r0], writes=[dres["XR1"][stq]])
                P.barrier()

        def phase_F(l, dst, dst_name):
            NF = DFF // 128
            with ExitStack() as cx:
                WU = sbuf(cx, "f_wu", [128, 8, 2 * DFF], BF16)
                WD = sbuf(cx, "f_wd", [128, NF, D], BF16)
                with ExitStack() as cx2:
                    H4 = 2 * DFF // 4
                    load_cast(cx2, lambda i: WU[:, i // 4, (i % 4) * H4:(i % 4 + 1) * H4],
                              lambda i: w_up[l, (i // 4) * 128:(i // 4 + 1) * 128, (i % 4) * H4:(i % 4 + 1) * H4], 32, [128, H4], WU.r0, "wu")
                    load_cast(cx2, lambda i: WD[:, i, :], lambda i: w_down[l, i * 128:(i + 1) * 128, :], NF, [128, D], WD.r0, "wd")
                    P.barrier()
                gT = sbuf(cx, "f_gT", [128, 8], F32)
                P.dma("sp", gT[:], norm_ffn_g[l].rearrange("(kc p) -> p kc", p=128), writes=[gT.r0], slow=True)
                CW = sbuf(cx, "f_cw", [128, 3, NF], F32)
                CB = sbuf(cx, "f_cb", [128, NF], F32)
                for k in range(3):
                    P.dma("sp", CW[:, k, :], conv_w[l, k].rearrange("(fc p) -> p fc", p=128), writes=[CW.r0], slow=True)
                P.dma("sp", CB[:], conv_b[l].rearrange("(fc p) -> p fc", p=128), writes=[CB.r0], slow=True)
                HAL = sbuf(cx, "f_hal", [128, NF, 2], F32)
                P.op("pool", C("memset", HAL[:], 0.0), writes=[HAL.r0])
                hT = sbuf(cx, "f_hT", [128, 8, 512], BF16)
                xts = [sbuf(cx, f"f_x{i}", [128, D], F32) for i in range(4)]
                junk = sbuf(cx, "f_nj", [128, D], BF16)
                ss = sbuf(cx, "f_ss", [128, 1], F32)
                rs = sbuf(cx, "f_rs", [128, 1], F32)
                xs = sbuf(cx, "f_xs", [128, D], BF16)
                tp = psum(cx, "f_tp", [128, 8, 128])
                pg = [psum(cx, f"f_pg{i}", [128, 512]) for i in range(2)]
                pu = [psum(cx, f"f_pu{i}", [128, 512]) for i in range(2)]
                pz = psum(cx, "f_pz", [128, 2, 512])
                Gt = [sbuf(cx, f"f_gt{i}", [128, 514], F32) for i in range(2)]
                cv = [sbuf(cx, f"f_cv{i}", [128, 512], F32) for i in range(2)]
                sl = [sbuf(cx, f"f_sl{i}", [128, 512], F32) for i in range(2)]
                actT = sbuf(cx, "f_act", [128, NF, 512], BF16, nres=NF)
                xo = [sbuf(cx, f"f_xo{i}", [128, D], F32) for i in range(1)]
                for st in range(NST):
                    for tt in range(4):
                        norm_to_hT((xts[tt], junk, ss, rs, xs, tp), gT, XR1, st * 512 + tt * 128, hT, hT.r0, tt)
                    for fc in range(NF):
                        b = fc % 2
                        items = [(pg[b][:], WU[:, kc, fc * 128:(fc + 1) * 128], hT[:, kc, :], kc == 0, kc == 7) for kc in range(8)]
                        P.op("pe", mmgroup(items), reads=[WU.r0, hT.r0], writes=[pg[b].r0])
                        items = [(pu[b][:], WU[:, kc, DFF + fc * 128:DFF + (fc + 1) * 128], hT[:, kc, :], kc == 0, kc == 7) for kc in range(8)]
                        P.op("pe", mmgroup(items), reads=[WU.r0, hT.r0], writes=[pu[b].r0])
                        g_ = Gt[b]
                        P.op("pool", C("tensor_copy", g_[:, 0:2], HAL[:, fc, :]), reads=[HAL.r0], writes=[g_.r0])
                        P.op("act", C("activation", out=g_[:, 2:514], in_=pg[b][:], func=AF.Identity), reads=[pg[b].r0], writes=[g_.r0])
                        P.op("pool", C("tensor_copy", HAL[:, fc, :], g_[:, 512:514]), reads=[g_.r0], writes=[HAL.r0])
                        c_ = cv[b]
                        P.op("dve", C("tensor_scalar", c_[:], g_[:, 2:514], CW[:, 2, fc:fc + 1], CB[:, fc:fc + 1], op0=ALU.mult, op1=ALU.add),
                             reads=[g_.r0, CW.r0, CB.r0], writes=[c_.r0])
                        P.op("dve", C("scalar_tensor_tensor", out=c_[:], in0=g_[:, 1:513], scalar=CW[:, 1, fc:fc + 1], in1=c_[:], op0=ALU.mult, op1=ALU.add),
                             reads=[g_.r0, CW.r0, c_.r0], writes=[c_.r0])
                        P.op("dve", C("scalar_tensor_tensor", out=c_[:], in0=g_[:, 0:512], scalar=CW[:, 0, fc:fc + 1], in1=c_[:], op0=ALU.mult, op1=ALU.add),
                             reads=[g_.r0, CW.r0, c_.r0], writes=[c_.r0])
                        s_ = sl[b]
                        P.op("act", C("activation", out=s_[:], in_=c_[:], func=AF.Silu), reads=[c_.r0], writes=[s_.r0])
                        P.op("dve", C("tensor_tensor", out=actT[:, fc, :], in0=s_[:], in1=pu[b][:], op=ALU.mult),
                             reads=[s_.r0, pu[b].r0], writes=[actT.res[fc]])
                    for tt in range(4):
                        t0 = st * 512 + tt * 128
                        items = []
                        for hf in range(2):
                            for fc in range(NF):
                                items.append((pz[:, hf, :], actT[:, fc, tt * 128:(tt + 1) * 128], WD[:, fc, hf * 512:(hf + 1) * 512], fc == 0, fc == NF - 1))
                        P.op("pe", mmgroup(items), reads=[WD.r0] + actT.res, writes=[pz.r0])
                        o = xo[0]
                        P.op("dve", C("tensor_tensor", out=o[:], in0=xts[tt][:], in1=pz[:].rearrange("p a b -> p (a b)"), op=ALU.add),
                             reads=[xts[tt].r0, pz.r0], writes=[o.r0])
                        P.dma("pool", dst[t0:t0 + 128, :], o[:], reads=[o.r0], writes=[dres[dst_name][st]])
                P.barrier()

        src, src_name = x_in, None
        for l in range(DEPTH):
            phase_P(l, src, src_name)
            if stop_after == "P":
                break
            with ExitStack() as lc:
                KCMPT = sbuf(lc, "KCMPT", [128, NCT * 128], BF16)
                VCMP = sbuf(lc, "VCMP", [128, 2, NCT, 65], BF16)
                phase_C(l, KCMPT, VCMP)
                if stop_after != "C":
                    phase_A1(l)
                    if stop_after != "A1":
                        phase_A2(l, src, src_name, KCMPT, VCMP)
            if stop_after in ("C", "A1", "A2"):
                break
            last = (l == DEPTH - 1)
            phase_F(l, y_out if last else XR2, "Y" if last else "XR2")
            src, src_name = XR2, "XR2"

        final_ev = P.all_events()
        with nc.Block() as block:
            @block.tensor
            def _(e):
                P.replay("pe", e)

            @block.scalar
            def _(e):
                P.replay("act", e)

            @block.vector
            def _(e):
                P.replay("dve", e)

            @block.gpsimd
            def _(e):
                P.replay("pool", e)

            @block.sync
            def _(e):
                P.replay("sp", e)
                for sk, v in final_ev:
                    e.wait_ge(P.sem[sk], v)
        nc._n_rec = P.n_inst
    return nc


_CACHE = {}


def _get_prog(S, depth):
    key = (S, depth)
    if key not in _CACHE:
        _CACHE[key] = build(S, depth)
    return _CACHE[key]


WNAMES = ("norm_mix_g", "w_in", "a_q_g", "a_k_g", "b_q_g", "b_k_g", "cmp_pos", "cmp_w1", "cmp_w2",
          "w_proj_a", "w_proj_b", "w_out", "norm_ffn_g", "w_up", "conv_w", "conv_b", "w_down")


def kernel(**inputs):
    x = np.ascontiguousarray(np.asarray(inputs["x"], dtype=np.float32))
    B, S, _ = x.shape
    depth = inputs["w_in"].shape[0]
    consts = host_consts(S)
    ws = {k: np.ascontiguousarray(np.asarray(inputs[k], dtype=np.float32)) for k in WNAMES}
    n = 8
    if FUSED:
        nc = _get_prog(S, depth)
        in_maps = []
        for c in range(n):
            m = {"x": x[c % B]}
            m.update(ws)
            m.update(consts)
            in_maps.append(m)
        res = run_bass_kernel_spmd(nc, in_maps, core_ids=list(range(n)))
        return np.stack([res.results[b]["y"] for b in range(B)], axis=0).astype(np.float32)
    nc = _get_prog(S, 1)
    cur = [x[c % B] for c in range(n)]
    for l in range(depth):
        in_maps = []
        for c in range(n):
            m = {"x": np.ascontiguousarray(cur[c])}
            m.update({k: np.ascontiguousarray(v[l:l + 1]) for k, v in ws.items()})
            m.update(consts)
            in_maps.append(m)
        res = run_bass_kernel_spmd(nc, in_maps, core_ids=list(range(n)))
        cur = [res.results[c]["y"] for c in range(n)]
    return np.stack([cur[b] for b in range(B)], axis=0).astype(np.float32)
```

```python
import numpy as np
from contextlib import ExitStack
import concourse.bass as bass
import concourse.mybir as mybir
from concourse.bass_utils import run_bass_kernel_spmd

F32 = mybir.dt.float32
BF16 = mybir.dt.bfloat16
AF = mybir.ActivationFunctionType
ALU = mybir.AluOpType
AX = mybir.AxisListType

D = 1024
NIN = 7960
DFF = 2816
EPS = 1e-6
TINY = 1e-30
NEGB = -30000.0
DIL_PAIRS = ((128, 1), (512, 4), (2048, 16))
C_AQ, C_AK, C_AV, C_BQ, C_BKV, C_GATE, C_MERGE = 0, 1536, 3072, 4608, 5120, 5888, 5912
FUSED = True


class Res:
    __slots__ = ("w", "r")

    def __init__(self):
        self.w = None
        self.r = {}


class Buf:
    def __init__(self, t, nres=1):
        self.t = t
        self.res = [Res() for _ in range(nres)]

    def __getitem__(self, k):
        return self.t[k]

    @property
    def r0(self):
        return self.res[0]


COMPUTE = ("pe", "act", "dve", "pool")


class Prog:
    ND = 16

    def __init__(self, nc, ctx):
        self.nc = nc
        self.ops = {e: [] for e in ("pe", "act", "dve", "pool", "sp")}
        self.cnt = {e: 0 for e in COMPUTE}
        self.sem = {}
        for e in COMPUTE:
            self.sem[e] = ctx.enter_context(nc.semaphore("s_" + e))
        self.dma_val = {}
        for q in ("sp", "pool"):
            for k in range(self.ND):
                key = (q, k)
                self.sem[key] = ctx.enter_context(nc.semaphore(f"d_{q}_{k}"))
                self.dma_val[key] = 0
        self.rr = {"sp": 0, "pool": 0}
        self.seen = {e: {} for e in self.ops}
        self.n_inst = 0

    def _emit(self, eng, fn, deps, semkey, inc):
        best = {}
        for (sk, v) in deps:
            if best.get(sk, 0) < v:
                best[sk] = v
        waits = []
        for sk, v in best.items():
            if sk == eng and eng == "pe":
                continue
            if self.seen[eng].get(sk, 0) >= v:
                continue
            self.seen[eng][sk] = v
            waits.append((sk, v))
        self.ops[eng].append((waits, fn, semkey, inc))
        self.n_inst += 1 + len(waits)

    def op(self, eng, fn, reads=(), writes=(), dma=False):
        deps = []
        for r in reads:
            if r.w is not None:
                deps.append(r.w)
        for w in writes:
            if w.w is not None:
                deps.append(w.w)
            deps.extend(w.r.items())
        if dma:
            k = self.rr[eng]
            self.rr[eng] = (k + 1) % self.ND
            semkey = (eng, k)
            prev = self.dma_val[semkey]
            if prev > 0:
                deps.append((semkey, prev))
            val = prev + 16
            self.dma_val[semkey] = val
            inc = 16
        else:
            semkey = eng
            self.cnt[eng] += 1
            val = self.cnt[eng]
            inc = 1
        self._emit(eng, fn, deps, semkey, inc)
        for r in reads:
            if r.r.get(semkey, 0) < val:
                r.r[semkey] = val
        for w in writes:
            w.w = (semkey, val)
            w.r = {}
        return (semkey, val)

    def dma(self, q, out, in_, reads=(), writes=(), slow=False):
        if slow:
            fn = lambda e, out=out, in_=in_: e.dma_start(out=out, in_=in_, allow_slow_non_contiguous=True)
        else:
            fn = lambda e, out=out, in_=in_: e.dma_start(out=out, in_=in_)
        return self.op(q, fn, reads, writes, dma=True)

    def all_events(self):
        ev = [(e, self.cnt[e]) for e in COMPUTE if self.cnt[e] > 0]
        ev += [(k, v) for k, v in self.dma_val.items() if v > 0]
        return ev

    def barrier(self):
        ev = self.all_events()
        for eng in self.ops:
            waits = []
            for sk, v in ev:
                if sk == eng:
                    continue
                if self.seen[eng].get(sk, 0) >= v:
                    continue
                self.seen[eng][sk] = v
                waits.append((sk, v))
            if waits:
                self.ops[eng].append((waits, None, None, 0))
                self.n_inst += len(waits)

    def replay(self, eng, e):
        for (waits, fn, semkey, inc) in self.ops[eng]:
            for sk, v in waits:
                e.wait_ge(self.sem[sk], v)
            if fn is not None:
                ins = fn(e)
                ins.then_inc(self.sem[semkey], inc)


def C(name, *a, **k):
    return lambda e: getattr(e, name)(*a, **k)


def mmgroup(items):
    def fn(e):
        ins = None
        for (out, lhsT, rhs, st, sp) in items:
            ins = e.matmul(out, lhsT, rhs, start=st, stop=sp)
        return ins
    return fn


def host_consts(S):
    NT = S // 128
    NJ = S // 64
    NCB = S // 16 - 1
    NCT = (NCB + 127) // 128
    half = 32
    inv_freq = (np.float32(10000.0) ** (-(np.arange(half, dtype=np.float32)) / np.float32(half))).astype(np.float32)
    fidx = (np.arange(128) % 64) % 32

    def tabs(pos):
        ang = pos.astype(np.float32)[None, :] * inv_freq[fidx][:, None]
        return np.cos(ang).astype(np.float32), np.sin(ang).astype(np.float32)

    cos, sin = tabs(np.arange(S))
    posc = np.zeros(NCT * 128, dtype=np.float32)
    posc[:NCB] = np.arange(NCB) * 16 + 31
    cosc, sinc = tabs(posc)
    rot = np.zeros((128, 128), np.float32)
    for m in range(128):
        hb, d = (m // 64) * 64, m % 64
        if d < 32:
            rot[hb + d + 32, m] = -1.0
        else:
            rot[hb + d - 32, m] = 1.0
    bones = np.zeros((128, 128), np.float32)
    bones[:64, :64] = 1.0 / 64
    bones[64:, 64:] = 1.0 / 64
    ident = np.eye(128, dtype=np.float32)
    kk = np.arange(128)[:, None]
    qq = np.arange(128)[None, :]
    masks = np.stack([(kk >= qq), (kk <= qq), (kk > qq)], axis=1).astype(np.float32)
    mats = np.stack([rot, bones, ident], axis=1)
    eall = (np.arange(128)[:, None] == (np.arange(S)[None, :] // 64)).astype(np.float32)
    mm = np.zeros((NCT * 128, 128), np.float32)
    for c in range(NCB):
        for n in (c, c + 1):
            if n // 4 < 128:
                mm[c, n // 4] += 1.0
    mmat = mm.reshape(NCT, 128, 128).transpose(1, 0, 2).copy()
    cm = np.zeros((128, 17, 128), np.float32)
    cl = np.arange(128)[:, None]
    for v in range(16):
        cp = cl - 8 * v
        cm[:, v, :] = (16 * cp + 31 <= qq)
    cp = cl - 128
    cm[:, 16, :] = (16 * cp + 31 <= qq)
    tk = np.zeros((128, 256), np.float32)
    ta = np.zeros((128, 256), np.float32)
    for p in range(128):
        hi = 1 if p >= 64 else 0
        for m in range(256):
            r = m - 127
            if r > hi:
                ta[p, m] = -float(r - hi)
            elif r == hi:
                ta[p, m] = 2e9
            elif r == hi - 1:
                ta[p, m] = 1e9
            else:
                tk[p, m] = 1.0
    return dict(c_cos=cos, c_sin=sin, c_cosc=np.ascontiguousarray(cosc[:64]), c_sinc=np.ascontiguousarray(sinc[:64]),
                c_mats=mats, c_masks=masks, c_eall=eall, c_mmat=mmat, c_cm=cm, c_tk=tk, c_ta=ta)


def build(S, DEPTH, dbg=(), stop_after=None):
    NT = S // 128
    NST = S // 512
    NJ = 128
    NCB = S // 16 - 1
    NCT = (NCB + 127) // 128
    nc = bass.Bass("TRN2", target_bir_lowering=False)

    def din(name, shape, dt=F32):
        return nc.dram_tensor(name, list(shape), dt, kind="ExternalInput").ap()

    x_in = din("x", [S, D])
    norm_mix_g = din("norm_mix_g", [DEPTH, D])
    w_in = din("w_in", [DEPTH, D, NIN])
    a_q_g = din("a_q_g", [DEPTH, 64])
    a_k_g = din("a_k_g", [DEPTH, 64])
    b_q_g = din("b_q_g", [DEPTH, 64])
    b_k_g = din("b_k_g", [DEPTH, 3, 64])
    cmp_pos = din("cmp_pos", [DEPTH, 2, 32, 64])
    cmp_w1 = din("cmp_w1", [DEPTH, 2, 2048, 256])
    cmp_w2 = din("cmp_w2", [DEPTH, 2, 256, 64])
    w_proj_a = din("w_proj_a", [DEPTH, 512, D])
    w_proj_b = din("w_proj_b", [DEPTH, 512, D])
    w_out = din("w_out", [DEPTH, D, D])
    norm_ffn_g = din("norm_ffn_g", [DEPTH, D])
    w_up = din("w_up", [DEPTH, D, 2 * DFF])
    conv_w = din("conv_w", [DEPTH, 3, DFF])
    conv_b = din("conv_b", [DEPTH, DFF])
    w_down = din("w_down", [DEPTH, DFF, D])
    c_cos = din("c_cos", [128, S])
    c_sin = din("c_sin", [128, S])
    c_cosc = din("c_cosc", [64, NCT * 128])
    c_sinc = din("c_sinc", [64, NCT * 128])
    c_mats = din("c_mats", [128, 3, 128])
    c_masks = din("c_masks", [128, 3, 128])
    c_eall = din("c_eall", [128, S])
    c_mmat = din("c_mmat", [128, NCT, 128])
    c_cm = din("c_cm", [128, 17, 128])
    c_tk = din("c_tk", [128, 256])
    c_ta = din("c_ta", [128, 256])
    y_out = nc.dram_tensor("y", [S, D], F32, kind="ExternalOutput").ap()

    def dscr(name, shape, dt):
        kind = "ExternalOutput" if name in dbg else "Internal"
        return nc.dram_tensor(name, list(shape), dt, kind=kind).ap()

    QAT = dscr("QAT", [1536, S], BF16)
    KAT = dscr("KAT", [1536, S], BF16)
    VA = dscr("VA", [S, 1536], BF16)
    QBT = dscr("QBT", [512, S], BF16)
    KSELT = dscr("KSELT", [128, S], BF16)
    KWINT = dscr("KWINT", [128, S], BF16)
    VSEL = dscr("VSEL", [S, 128], BF16)
    VWIN = dscr("VWIN", [S, 128], BF16)
    KCRT = dscr("KCRT", [128, S], BF16)
    VCRT = dscr("VCRT", [128, S], BF16)
    GT = dscr("GT", [24, S], F32)
    MGT = dscr("MGT", [2048, S], BF16)
    OAT = dscr("OAT", [512, S], BF16)
    XR1 = dscr("XR1", [S, D], F32)
    XR2 = dscr("XR2", [S, D], F32)
    WSCR = dscr("WSCR", [6, 512], F32)
    wscr_res = [Res() for _ in range(6)]
    dres = {n: [Res() for _ in range(NST)] for n in
            ("QAT", "KAT", "VA", "QBT", "KSELT", "KWINT", "VSEL", "VWIN", "KCRT", "VCRT", "GT", "MGT", "OAT", "XR1", "XR2", "Y")}

    def dall(n):
        return dres[n]

    with ExitStack() as top:
        P = Prog(nc, top)

        uid = [0]

        def sbuf(cx, name, shape, dt, nres=1):
            uid[0] += 1
            t = cx.enter_context(nc.sbuf_tensor(f"{name}_{uid[0]}", list(shape), dt))
            return Buf(t, nres)

        def psum(cx, name, shape, dt=F32):
            uid[0] += 1
            t = cx.enter_context(nc.psum_tensor(f"{name}_{uid[0]}", list(shape), dt))
            return Buf(t)

        MATS = sbuf(top, "MATS", [128, 3, 128], BF16)
        MASKS = sbuf(top, "MASKS", [128, 3, 128], BF16)
        ONESF = sbuf(top, "ONESF", [128, 64], F32)
        EPSC = sbuf(top, "EPSC", [128, 1], F32)
        with ExitStack() as cx:
            st1 = sbuf(cx, "cst1", [128, 3, 128], F32)
            st2 = sbuf(cx, "cst2", [128, 3, 128], F32)
            P.dma("sp", st1[:], c_mats, writes=[st1.r0])
            P.dma("sp", st2[:], c_masks, writes=[st2.r0])
            P.op("dve", C("tensor_copy", MATS[:], st1[:]), reads=[st1.r0], writes=[MATS.r0])
            P.op("dve", C("tensor_copy", MASKS[:], st2[:]), reads=[st2.r0], writes=[MASKS.r0])
            P.op("pool", C("memset", ONESF[:], 1.0), writes=[ONESF.r0])
            P.op("pool", C("memset", EPSC[:], EPS), writes=[EPSC.r0])
            P.barrier()
        ROT = MATS[:, 0, :]
        BON = MATS[:, 1, :]
        IDN = MATS[:, 2, :]

        def norm_to_hT(cx_bufs, l_gT, src, tok0, hT, hT_res, tt):
            xt, junk, ss, rs, xs, tp = cx_bufs
            P.dma("sp", xt[:], src[tok0:tok0 + 128, :], reads=[], writes=[xt.r0])
            P.op("pool", C("memset", ss[:], 0.0), writes=[ss.r0])
            P.op("act", C("activation", out=junk[:], in_=xt[:], func=AF.Square, accum_out=ss[:]),
                 reads=[xt.r0], writes=[junk.r0, ss.r0])
            P.op("act", C("activation", out=rs[:], in_=ss[:], func=AF.Sqrt, bias=EPSC[:, 0:1], scale=1.0 / D),
                 reads=[ss.r0, EPSC.r0], writes=[rs.r0])
            P.op("dve", C("reciprocal", rs[:], rs[:]), reads=[rs.r0], writes=[rs.r0])
            P.op("dve", C("tensor_scalar", xs[:], xt[:], rs[:, 0:1], None, op0=ALU.mult),
                 reads=[xt.r0, rs.r0], writes=[xs.r0])
            items = [(tp[:, kc, :], xs[:, kc * 128:(kc + 1) * 128], IDN, True, True) for kc in range(8)]
            P.op("pe", mmgroup(items), reads=[xs.r0, MATS.r0], writes=[tp.r0])
            P.op("dve", C("tensor_tensor", out=hT[:, :, tt * 128:(tt + 1) * 128], in0=tp[:],
                                                  in1=l_gT[:, :].unsqueeze(2).to_broadcast([128, 8, 128]), op=ALU.mult),
                 reads=[tp.r0, l_gT.r0], writes=[hT_res])
            return xt

        def load_cast(cx, dst_ap_fn, src_ap_fn, nchunks, shape, dst_res, name, eng="pool"):
            stg = [sbuf(cx, f"{name}_stg{i}", shape, F32) for i in range(2)]
            for i in range(nchunks):
                s = stg[i % 2]
                P.dma("sp", s[:], src_ap_fn(i), writes=[s.r0])
                P.op(eng, C("tensor_copy", dst_ap_fn(i), s[:]), reads=[s.r0], writes=[dst_res])

        def phase_P(l, src, src_name):
            with ExitStack() as cx:
                WIN = sbuf(cx, "WIN", [128, 8, NIN], BF16)
                with ExitStack() as cx2:
                    Q4 = NIN // 4
                    load_cast(cx2, lambda i: WIN[:, i // 4, (i % 4) * Q4:(i % 4 + 1) * Q4],
                              lambda i: w_in[l, (i // 4) * 128:(i // 4 + 1) * 128, (i % 4) * Q4:(i % 4 + 1) * Q4],
                              32, [128, Q4], WIN.r0, "win")
                    P.barrier()
                gT = sbuf(cx, "gT", [128, 8], F32)
                P.dma("sp", gT[:], norm_mix_g[l].rearrange("(kc p) -> p kc", p=128), writes=[gT.r0], slow=True)
                GC = sbuf(cx, "GC", [128, 5], F32)
                for ci, gsrc in enumerate((a_q_g[l], a_k_g[l], b_q_g[l], b_k_g[l, 1], b_k_g[l, 2])):
                    for hb in range(2):
                        P.dma("sp", GC[hb * 64:(hb + 1) * 64, ci:ci + 1], gsrc.rearrange("(d o) -> d o", o=1),
                              writes=[GC.r0], slow=True)
                hTs = [sbuf(cx, f"hT{i}", [128, 8, 512], BF16) for i in range(2)]
                nb = [(sbuf(cx, f"nx{i}", [128, D], F32), sbuf(cx, f"nj{i}", [128, D], BF16), sbuf(cx, f"nss{i}", [128, 1], F32),
                       sbuf(cx, f"nrs{i}", [128, 1], F32), sbuf(cx, f"nxs{i}", [128, D], BF16),
                       psum(cx, f"ntp{i}", [128, 8, 128])) for i in range(1)]
                cs = [(sbuf(cx, f"cos{i}", [128, 512], F32), sbuf(cx, f"sin{i}", [128, 512], F32)) for i in range(2)]
                pa = [psum(cx, f"pa{i}", [128, 512]) for i in range(2)]
                p2 = [psum(cx, f"p2{i}", [128, 512]) for i in range(2)]
                p3 = [psum(cx, f"p3{i}", [128, 512]) for i in range(2)]
                sq = [sbuf(cx, f"sq{i}", [128, 512], BF16) for i in range(2)]
                xg = [sbuf(cx, f"xg{i}", [128, 512], BF16) for i in range(2)]
                rstd = [sbuf(cx, f"rstd{i}", [128, 512], F32) for i in range(2)]
                t1 = [sbuf(cx, f"t1{i}", [128, 512], F32) for i in range(2)]
                t2 = [sbuf(cx, f"t2{i}", [128, 512], F32) for i in range(2)]
                ob = [sbuf(cx, f"ob{i}", [128, 512], BF16) for i in range(3)]
                of = [sbuf(cx, f"of{i}", [128, 512], F32) for i in range(2)]
                gi = 0
                oi = 0
                for st in range(NST):
                    hT = hTs[st % 2]
                    c_t, s_t = cs[st % 2]
                    tsl = slice(st * 512, (st + 1) * 512)
                    P.dma("sp", c_t[:], c_cos[:, tsl], writes=[c_t.r0])
                    P.dma("sp", s_t[:], c_sin[:, tsl], writes=[s_t.r0])
                    for tt in range(4):
                        norm_to_hT(nb[0], gT, src, st * 512 + tt * 128, hT, hT.r0, tt)
                    wr_src = [dres[src_name][st]] if src_name else []
                    nr_groups = []
                    for g in range(12):
                        nr_groups.append((C_AQ + g * 128, 0, QAT, "QAT", g * 128))
                    for g in range(12):
                        nr_groups.append((C_AK + g * 128, 1, KAT, "KAT", g * 128))
                    for g in range(4):
                        nr_groups.append((C_BQ + g * 128, 2, QBT, "QBT", g * 128))
                    nr_groups.append((C_BKV + 256, 3, KSELT, "KSELT", 0))
                    nr_groups.append((C_BKV + 512, 4, KWINT, "KWINT", 0))
                    pend = None

                    def nr_epilogue(b, dst, dname, r0):
                        nonlocal oi
                        P.op("pe", mmgroup([(p2[b][:], BON, sq[b][:], True, True)]), reads=[sq[b].r0, MATS.r0], writes=[p2[b].r0])
                        P.op("pe", mmgroup([(p3[b][:], ROT, xg[b][:], True, True)]), reads=[xg[b].r0, MATS.r0], writes=[p3[b].r0])
                        P.op("act", C("activation", out=rstd[b][:], in_=p2[b][:], func=AF.Ln, bias=EPSC[:, 0:1], scale=1.0),
                             reads=[p2[b].r0, EPSC.r0], writes=[rstd[b].r0])
                        P.op("act", C("activation", out=rstd[b][:], in_=rstd[b][:], func=AF.Exp, scale=-0.5),
                             reads=[rstd[b].r0], writes=[rstd[b].r0])
                        P.op("pool", C("tensor_tensor", out=t1[b][:], in0=xg[b][:], in1=c_t[:], op=ALU.mult),
                             reads=[xg[b].r0, c_t.r0], writes=[t1[b].r0])
                        P.op("dve", C("tensor_tensor", out=t2[b][:], in0=p3[b][:], in1=s_t[:], op=ALU.mult),
                             reads=[p3[b].r0, s_t.r0], writes=[t2[b].r0])
                        P.op("dve", C("tensor_tensor", out=t2[b][:], in0=t1[b][:], in1=t2[b][:], op=ALU.add),
                             reads=[t1[b].r0, t2[b].r0], writes=[t2[b].r0])
                        o = ob[oi % 3]
                        oi += 1
                        P.op("dve", C("tensor_tensor", out=o[:], in0=t2[b][:], in1=rstd[b][:], op=ALU.mult),
                             reads=[t2[b].r0, rstd[b].r0], writes=[o.r0])
                        P.dma("pool", dst[r0:r0 + 128, tsl], o[:], reads=[o.r0], writes=[dres[dname][st]])
                    for (c0, gci, dst, dname, r0) in nr_groups:
                        b = gi % 2
                        gi += 1
                        items = [(pa[b][:], WIN[:, kc, c0:c0 + 128], hT[:, kc, :], kc == 0, kc == 7) for kc in range(8)]
                        P.op("pe", mmgroup(items), reads=[WIN.r0, hT.r0], writes=[pa[b].r0])
                        P.op("act", C("activation", out=sq[b][:], in_=pa[b][:], func=AF.Square),
                             reads=[pa[b].r0], writes=[sq[b].r0])
                        P.op("act", C("activation", out=xg[b][:], in_=pa[b][:], func=AF.Identity,
                                                                         scale=GC[:, gci:gci + 1]),
                             reads=[pa[b].r0, GC.r0], writes=[xg[b].r0])
                        if pend is not None:
                            nr_epilogue(*pend)
                        pend = (b, dst, dname, r0)
                    nr_epilogue(*pend)
                    sg = [(C_BKV + 0, 128, AF.Identity, KCRT, "KCRT", 0, BF16), (C_BKV + 128, 128, AF.Identity, VCRT, "VCRT", 0, BF16)]
                    for g in range(16):
                        sg.append((C_MERGE + g * 128, 128, AF.Sigmoid, MGT, "MGT", g * 128, BF16))
                    sg.append((C_GATE, 24, AF.Sigmoid, GT, "GT", 0, F32))
                    for (c0, w, func, dst, dname, r0, dt) in sg:
                        b = gi % 2
                        gi += 1
                        items = [(pa[b][0:w, :], WIN[:, kc, c0:c0 + w], hT[:, kc, :], kc == 0, kc == 7) for kc in range(8)]
                        P.op("pe", mmgroup(items), reads=[WIN.r0, hT.r0], writes=[pa[b].r0])
                        if dt == BF16:
                            o = ob[oi % 3]
                            oi += 1
                        else:
                            o = of[0]
                        P.op("act", C("activation", out=o[0:w, :], in_=pa[b][0:w, :], func=func),
                             reads=[pa[b].r0], writes=[o.r0])
                        P.dma("pool", dst[r0:r0 + w, tsl], o[0:w, :], reads=[o.r0], writes=[dres[dname][st]])
                    vb = [(C_AV, 512, VA, "VA", 0), (C_AV + 512, 512, VA, "VA", 512), (C_AV + 1024, 512, VA, "VA", 1024),
                          (C_BKV + 384, 128, VSEL, "VSEL", 0), (C_BKV + 640, 128, VWIN, "VWIN", 0)]
                    for tt in range(4):
                        t0 = st * 512 + tt * 128
                        for (c0, w, dst, dname, cc0) in vb:
                            b = gi % 2
                            gi += 1
                            items = [(pa[b][:, 0:w], hT[:, kc, tt * 128:(tt + 1) * 128], WIN[:, kc, c0:c0 + w], kc == 0, kc == 7)
                                     for kc in range(8)]
                            P.op("pe", mmgroup(items), reads=[WIN.r0, hT.r0], writes=[pa[b].r0])
                            o = ob[oi % 3]
                            oi += 1
                            P.op("act", C("activation", out=o[:, 0:w], in_=pa[b][:, 0:w], func=AF.Identity),
                                 reads=[pa[b].r0], writes=[o.r0])
                            P.dma("pool", dst[t0:t0 + 128, cc0:cc0 + w], o[:, 0:w], reads=[o.r0], writes=[dres[dname][st]])
                P.barrier()

        def phase_C(l, KCMPT, VCMP):
            with ExitStack() as cx:
                raw = sbuf(cx, "craw", [64, S], BF16)
                w1b = sbuf(cx, "cw1b", [64, 32, 256], BF16)
                w2b = sbuf(cx, "cw2b", [128, 2, 64], BF16)
                posb = sbuf(cx, "cposb", [64, 32], BF16)
                posf = sbuf(cx, "cposf", [64, 32], F32)
                w2f = sbuf(cx, "cw2f", [128, 2, 64], F32)
                gk = sbuf(cx, "cgk", [64, 1], F32)
                cosc = sbuf(cx, "ccos", [64, NCT * 128], F32)
                sinc = sbuf(cx, "csin", [64, NCT * 128], F32)
                ph = psum(cx, "cph", [128, 512])
                pb = psum(cx, "cpb", [128, 8])
                pk = psum(cx, "cpk", [128, 512])
                pq2 = psum(cx, "cp2", [128, 512])
                pq3 = psum(cx, "cp3", [128, 512])
                pv = psum(cx, "cpv", [128, 64])
                bcol = sbuf(cx, "cbcol", [128, 1], F32)
                xh = sbuf(cx, "cxh", [128, 512], F32)
                x2 = sbuf(cx, "cx2", [128, 512], F32)
                sg_ = sbuf(cx, "csg", [128, 512], F32)
                h1g = sbuf(cx, "ch1g", [128, 2, 512], BF16)
                sqc = sbuf(cx, "csq", [64, 512], BF16)
                xgc = sbuf(cx, "cxg", [64, 512], BF16)
                rsc = sbuf(cx, "crs", [64, 512], F32)
                t1c = sbuf(cx, "ct1", [64, 512], F32)
                t2c = sbuf(cx, "ct2", [64, 512], F32)
                okc = sbuf(cx, "cok", [64, 512], BF16)
                P.dma("sp", cosc[:], c_cosc, writes=[cosc.r0])
                P.dma("sp", sinc[:], c_sinc, writes=[sinc.r0])
                P.dma("sp", gk[:], b_k_g[l, 0].rearrange("(d o) -> d o", o=1), writes=[gk.r0], slow=True)
                P.op("pool", C("memset", KCMPT[:], 0.0), writes=[KCMPT.r0])
                P.op("pool", C("memset", VCMP[:], 0.0), writes=[VCMP.r0])
                P.op("pool", C("memset", VCMP[:, :, :, 64:65], 1.0), writes=[VCMP.r0])
                P.op("pool", C("memset", h1g[:], 0.0), writes=[h1g.r0])
                for typ in range(2):
                    with ExitStack() as cx2:
                        load_cast(cx2, lambda i: w1b[:, i * 8:(i + 1) * 8, :],
                                  lambda i: cmp_w1[l, typ].rearrange("(j d) h -> d j h", d=64)[:, i * 8:(i + 1) * 8, :],
                                  4, [64, 8, 256], w1b.r0, f"cw1_{typ}")
                        P.dma("sp", posf[:], cmp_pos[l, typ].rearrange("j d -> d j"), writes=[posf.r0], slow=True)
                        P.op("pool", C("tensor_copy", posb[:], posf[:]), reads=[posf.r0], writes=[posb.r0])
                        P.dma("sp", w2f[:], cmp_w2[l, typ].rearrange("(c p) d -> p c d", p=128), writes=[w2f.r0])
                        P.op("pool", C("tensor_copy", w2b[:], w2f[:]), reads=[w2f.r0], writes=[w2b.r0])
                        src = KCRT if typ == 0 else VCRT
                        sname = "KCRT" if typ == 0 else "VCRT"
                        for kvh in range(2):
                            P.dma("sp", raw[:], src[kvh * 64:(kvh + 1) * 64, :], reads=dall(sname), writes=[raw.r0])
                            for hc in range(2):
                                items = [(ph[:, 0:NCB], w1b[:, j, hc * 128:(hc + 1) * 128],
                                          raw[:, j:j + 16 * (NCB - 1) + 1:16], j == 0, j == 31) for j in range(32)]
                                P.op("pe", mmgroup(items), reads=[w1b.r0, raw.r0], writes=[ph.r0])
                                items = [(pb[:, 0:1], w1b[:, j, hc * 128:(hc + 1) * 128], posb[:, j:j + 1], j == 0, j == 31)
                                         for j in range(32)]
                                P.op("pe", mmgroup(items), reads=[w1b.r0, posb.r0], writes=[pb.r0])
                                P.op("dve", C("tensor_copy", bcol[:], pb[:, 0:1]), reads=[pb.r0], writes=[bcol.r0])
                                P.op("act", C("activation", out=xh[:, 0:NCB], in_=ph[:, 0:NCB], func=AF.Identity, bias=bcol[:, 0:1]),
                                     reads=[ph.r0, bcol.r0], writes=[xh.r0])
                                P.op("dve", C("tensor_tensor", out=x2[:, 0:NCB], in0=xh[:, 0:NCB], in1=xh[:, 0:NCB], op=ALU.mult),
                                     reads=[xh.r0], writes=[x2.r0])
                                P.op("dve", C("tensor_scalar", x2[:, 0:NCB], x2[:, 0:NCB], 0.044715, 1.0, op0=ALU.mult, op1=ALU.add),
                                     reads=[x2.r0], writes=[x2.r0])
                                P.op("dve", C("tensor_tensor", out=x2[:, 0:NCB], in0=x2[:, 0:NCB], in1=xh[:, 0:NCB], op=ALU.mult),
                                     reads=[x2.r0, xh.r0], writes=[x2.r0])
                                P.op("act", C("activation", out=sg_[:, 0:NCB], in_=x2[:, 0:NCB], func=AF.Sigmoid, scale=1.5957691216057308),
                                     reads=[x2.r0], writes=[sg_.r0])
                                P.op("dve", C("tensor_tensor", out=h1g[:, hc, 0:NCB], in0=xh[:, 0:NCB], in1=sg_[:, 0:NCB], op=ALU.mult),
                                     reads=[xh.r0, sg_.r0], writes=[h1g.r0])
                            if typ == 0:
                                items = [(pk[0:64, 0:NCB], w2b[:, hc, :], h1g[:, hc, 0:NCB], hc == 0, hc == 1) for hc in range(2)]
                                P.op("pe", mmgroup(items), reads=[w2b.r0, h1g.r0], writes=[pk.r0])
                                P.op("act", C("activation", out=sqc[:, 0:NCB], in_=pk[0:64, 0:NCB], func=AF.Square),
                                     reads=[pk.r0], writes=[sqc.r0])
                                P.op("act", C("activation", out=xgc[:, 0:NCB], in_=pk[0:64, 0:NCB], func=AF.Identity, scale=gk[:, 0:1]),
                                     reads=[pk.r0, gk.r0], writes=[xgc.r0])
                                P.op("pe", mmgroup([(pq2[0:64, 0:NCB], MATS[0:64, 1, 0:64], sqc[:, 0:NCB], True, True)]),
                                     reads=[sqc.r0, MATS.r0], writes=[pq2.r0])
                                P.op("pe", mmgroup([(pq3[0:64, 0:NCB], MATS[0:64, 0, 0:64], xgc[:, 0:NCB], True, True)]),
                                     reads=[xgc.r0, MATS.r0], writes=[pq3.r0])
                                P.op("act", C("activation", out=rsc[:, 0:NCB], in_=pq2[0:64, 0:NCB], func=AF.Sqrt, bias=EPSC[0:64, 0:1], scale=1.0),
                                     reads=[pq2.r0, EPSC.r0], writes=[rsc.r0])
                                P.op("dve", C("reciprocal", rsc[:, 0:NCB], rsc[:, 0:NCB]), reads=[rsc.r0], writes=[rsc.r0])
                                P.op("dve", C("tensor_tensor", out=t1c[:, 0:NCB], in0=xgc[:, 0:NCB], in1=cosc[:, 0:NCB], op=ALU.mult),
                                     reads=[xgc.r0, cosc.r0], writes=[t1c.r0])
                                P.op("dve", C("tensor_tensor", out=t2c[:, 0:NCB], in0=pq3[0:64, 0:NCB], in1=sinc[:, 0:NCB], op=ALU.mult),
                                     reads=[pq3.r0, sinc.r0], writes=[t2c.r0])
                                P.op("dve", C("tensor_tensor", out=t1c[:, 0:NCB], in0=t1c[:, 0:NCB], in1=t2c[:, 0:NCB], op=ALU.add),
                                     reads=[t1c.r0, t2c.r0], writes=[t1c.r0])
                                P.op("dve", C("tensor_tensor", out=okc[:, 0:NCB], in0=t1c[:, 0:NCB], in1=rsc[:, 0:NCB], op=ALU.mult),
                                     reads=[t1c.r0, rsc.r0], writes=[okc.r0])
                                P.dma("sp", KCMPT[kvh * 64:(kvh + 1) * 64, 0:NCB], okc[:, 0:NCB], reads=[okc.r0], writes=[KCMPT.r0])
                            else:
                                for ct in range(NCT):
                                    m = min(128, NCB - ct * 128)
                                    items = [(pv[0:m, :], h1g[:, hc, ct * 128:ct * 128 + m], w2b[:, hc, :], hc == 0, hc == 1) for hc in range(2)]
                                    P.op("pe", mmgroup(items), reads=[w2b.r0, h1g.r0], writes=[pv.r0])
                                    P.op("dve", C("tensor_copy", VCMP[0:m, kvh, ct, 0:64], pv[0:m, :]),
                                         reads=[pv.r0], writes=[VCMP.r0])
                        P.barrier()
                P.barrier()

        def phase_A1(l):
            with ExitStack() as cx:
                acc = sbuf(cx, "a_acc", [65, S], F32)
                Qt = [sbuf(cx, f"a_q{i}", [128, S], BF16) for i in range(2)]
                Kt = [sbuf(cx, f"a_k{i}", [128, S], BF16) for i in range(2)]
                for t_ in Qt + Kt:
                    P.op("pool", C("memset", t_[64:128, :], 0.0), writes=[t_.r0])
                Vt = [sbuf(cx, f"a_v{i}", [128, NT, 65], BF16) for i in range(2)]
                ps = [psum(cx, f"a_ps{i}", [128, 2, 128]) for i in range(2)]
                po = [psum(cx, f"a_po{i}", [128, 128]) for i in range(2)]
                pbx = [psum(cx, f"a_pb{i}", [128, 512]) for i in range(2)]
                pt = [sbuf(cx, f"a_pt{i}", [128, 2, 128], BF16) for i in range(4)]
                oa = sbuf(cx, "a_oa", [64, S], BF16)
                rd = sbuf(cx, "a_rd", [128, S], F32)
                P.op("pool", C("memset", rd[:], 0.0), writes=[rd.r0])
                for v in Vt:
                    P.op("pool", C("memset", v[:, :, 64:65], 1.0), writes=[v.r0])
                it = 0
                ld = 0
                for j in range(8):
                    for gi, (window, dil) in enumerate(DIL_PAIRS):
                        head = gi * 8 + j
                        r0 = head * 64
                        L = S // dil
                        nb = L // 128
                        q_, k_, v_ = Qt[ld % 2], Kt[ld % 2], Vt[ld % 2]
                        ld += 1
                        P.dma("sp", q_[0:64, :], QAT[r0:r0 + 64, :], reads=dall("QAT"), writes=[q_.r0])
                        P.dma("sp", k_[0:64, :], KAT[r0:r0 + 64, :], reads=dall("KAT"), writes=[k_.r0])
                        for r in range(dil):
                            srcv = VA[r::dil, r0:r0 + 64].rearrange("(jb kk) d -> kk jb d", kk=128)
                            for j0 in range(0, nb, 8):
                                j1 = min(nb, j0 + 8)
                                P.dma("sp", v_[:, r * nb + j0:r * nb + j1, 0:64], srcv[:, j0:j1, :], reads=dall("VA"), writes=[v_.r0])
                        pend = None

                        def pv_acc(b, p_t, r, Jb, c):
                            items = []
                            if Jb > 0:
                                items.append((po[b][0:65, :], v_[:, r * nb + Jb - 1, :], p_t[:, 0, :], True, False))
                            items.append((po[b][0:65, :], v_[:, r * nb + Jb, :], p_t[:, 1, :], Jb == 0, True))
                            P.op("pe", mmgroup(items), reads=[v_.r0, p_t.r0], writes=[po[b].r0])
                            if gi == 0:
                                P.op("dve", C("tensor_copy", acc[:, c], po[b][0:65, :]),
                                     reads=[po[b].r0], writes=[acc.r0])
                            else:
                                P.op("dve", C("tensor_tensor", out=acc[:, c], in0=acc[:, c], in1=po[b][0:65, :], op=ALU.add),
                                     reads=[po[b].r0, acc.r0], writes=[acc.r0])
                        for r in range(dil):
                            for Jb in range(nb):
                                def cols(J):
                                    s0 = J * 128 * dil + r
                                    return slice(s0, s0 + 127 * dil + 1, dil)
                                b = it % 2
                                p_t = pt[it % 4]
                                it += 1
                                items = []
                                if Jb > 0:
                                    items.append((ps[b][:, 0, :], k_[:, cols(Jb - 1)], q_[:, cols(Jb)], True, True))
                                items.append((ps[b][:, 1, :], k_[:, cols(Jb)], q_[:, cols(Jb)], True, True))
                                P.op("pe", mmgroup(items), reads=[k_.r0, q_.r0], writes=[ps[b].r0])
                                e0 = 0 if Jb > 0 else 1
                                P.op("act", C("activation", out=p_t[:, e0:2, :], in_=ps[b][:, e0:2, :], func=AF.Exp, scale=0.125),
                                     reads=[ps[b].r0], writes=[p_t.r0])
                                P.op("dve", C("tensor_tensor", out=p_t[:, e0:2, :], in0=p_t[:, e0:2, :], in1=MASKS[:, e0:2, :], op=ALU.mult),
                                     reads=[p_t.r0, MASKS.r0], writes=[p_t.r0])
                                if pend is not None:
                                    pv_acc(*pend)
                                pend = (b, p_t, r, Jb, cols(Jb))
                        pv_acc(*pend)
                    P.op("act", C("activation", out=rd[64:65, :], in_=acc[64:65, :], func=AF.Ln), reads=[acc.r0], writes=[rd.r0])
                    P.op("act", C("activation", out=rd[64:65, :], in_=rd[64:65, :], func=AF.Exp, scale=-1.0), reads=[rd.r0], writes=[rd.r0])
                    for ch in range(NST):
                        b = ch % 2
                        csl = slice(ch * 512, (ch + 1) * 512)
                        P.op("pe", mmgroup([(pbx[b][0:64, :], ONESF[:, 0:64], rd[:, csl], True, True)]),
                             reads=[rd.r0, ONESF.r0], writes=[pbx[b].r0])
                        P.op("dve", C("tensor_tensor", out=oa[:, csl], in0=acc[0:64, csl], in1=pbx[b][0:64, :], op=ALU.mult),
                             reads=[pbx[b].r0, acc.r0], writes=[oa.r0])
                    P.dma("sp", OAT[j * 64:(j + 1) * 64, :], oa[:], reads=[oa.r0], writes=dall("OAT"))
                P.barrier()

        def phase_A2(l, src, src_name, KCMPT, VCMP):
            with ExitStack() as cx:
                KS = sbuf(cx, "b_ks", [128, S], BF16)
                KW = sbuf(cx, "b_kw", [128, S], BF16)
                VS = sbuf(cx, "b_vs", [128, NT, 2, 65], BF16)
                VW = sbuf(cx, "b_vw", [128, NT, 2, 65], BF16)
                EALL = sbuf(cx, "b_eall", [128, S], BF16)
                MM = sbuf(cx, "b_mm", [128, NCT, 128], BF16)
                CM = sbuf(cx, "b_cm", [128, 17, 128], BF16)
                TK = sbuf(cx, "b_tk", [128, 256], F32)
                TA = sbuf(cx, "b_ta", [128, 256], F32)
                WPA = sbuf(cx, "b_wpa", [128, 4, D], BF16)
                WPB = sbuf(cx, "b_wpb", [128, 8, D], BF16)
                P.op("pool", C("memset", WPB[64:128, :, :], 0.0), writes=[WPB.r0])
                WO = sbuf(cx, "b_wo", [128, 8, D], BF16)
                with ExitStack() as cx2:
                    load_cast(cx2, lambda i: EALL[:, i * 1024:(i + 1) * 1024], lambda i: c_eall[:, i * 1024:(i + 1) * 1024],
                              S // 1024, [128, 1024], EALL.r0, "ea")
                    load_cast(cx2, lambda i: WPA[:, i, :], lambda i: w_proj_a[l, i * 128:(i + 1) * 128, :], 4, [128, D], WPA.r0, "wpa")
                    load_cast(cx2, lambda i: WPB[0:64, i, :], lambda i: w_proj_b[l, i * 64:(i + 1) * 64, :], 8, [64, D], WPB.r0, "wpb")
                    load_cast(cx2, lambda i: WO[:, i, :], lambda i: w_out[l, i * 128:(i + 1) * 128, :], 8, [128, D], WO.r0, "wo")
                    load_cast(cx2, lambda i: MM[:], lambda i: c_mmat, 1, [128, NCT, 128], MM.r0, "mm")
                    load_cast(cx2, lambda i: CM[:], lambda i: c_cm, 1, [128, 17, 128], CM.r0, "cm")
                    P.barrier()
                P.dma("sp", TK[:], c_tk, writes=[TK.r0])
                P.dma("sp", TA[:], c_ta, writes=[TA.r0])
                P.dma("sp", KS[:], KSELT, reads=dall("KSELT"), writes=[KS.r0])
                P.dma("sp", KW[:], KWINT, reads=dall("KWINT"), writes=[KW.r0])
                P.op("pool", C("memset", VS[:, :, :, 64:65], 1.0), writes=[VS.r0])
                P.op("pool", C("memset", VW[:, :, :, 64:65], 1.0), writes=[VW.r0])
                for h in range(2):
                    for k0 in range(0, NT, 8):
                        P.dma("sp", VS[:, k0:k0 + 8, h, 0:64], VSEL[k0 * 128:(k0 + 8) * 128, h * 64:(h + 1) * 64].rearrange("(kt kk) d -> kk kt d", kk=128),
                              reads=dall("VSEL"), writes=[VS.r0])
                        P.dma("sp", VW[:, k0:k0 + 8, h, 0:64], VWIN[k0 * 128:(k0 + 8) * 128, h * 64:(h + 1) * 64].rearrange("(kt kk) d -> kk kt d", kk=128),
                              reads=dall("VWIN"), writes=[VW.r0])
                ST = [psum(cx, f"b_st{i}", [128, 2, 512]) for i in range(2)]
                OT = [psum(cx, f"b_ot{i}", [128, 512]) for i in range(2)]
                UB = psum(cx, "b_ub", [128, 4, 128])
                MISC = psum(cx, "b_misc", [128, 512])
                QB = [sbuf(cx, f"b_qb{i}", [128, 2, 512], BF16) for i in range(2)]
                for q_ in QB:
                    P.op("pool", C("memset", q_[:], 0.0), writes=[q_.r0])
                GR = [sbuf(cx, f"b_gr{i}", [65, 24, 128], F32) for i in range(1)]
                MG = [sbuf(cx, f"b_mg{i}", [128, 16, 128], BF16) for i in range(2)]
                OA = [sbuf(cx, f"b_oa{i}", [128, 4, 128], BF16) for i in range(2)]
                XT = [sbuf(cx, f"b_xt{i}", [128, D], F32) for i in range(2)]
                PC = sbuf(cx, "b_pc", [128, 4, 512], BF16)
                PS_ = [sbuf(cx, f"b_ps{i}", [128, 2, 512], BF16) for i in range(2)]
                rs4 = sbuf(cx, "b_rs4", [128, 4], F32)
                psl = sbuf(cx, "b_psl", [128, 128], F32)
                sc2 = sbuf(cx, "b_sc2", [128, 128], F32)
                m8a = sbuf(cx, "b_m8a", [128, 8], F32)
                m8b = sbuf(cx, "b_m8b", [128, 8], F32)
                selb = sbuf(cx, "b_selb", [128, 128], BF16)
                selT = sbuf(cx, "b_selT", [128, 4, 128], BF16)
                UBR = [sbuf(cx, f"b_ubr{i}", [65, 512], F32) for i in range(3)]
                wrow = sbuf(cx, "b_wrow", [128, 512], F32)
                P.op("pool", C("memset", wrow[:], 0.0), writes=[wrow.r0])
                obf = sbuf(cx, "b_obf", [64, 512], F32)
                BCB = [sbuf(cx, f"b_bc{i}", [64, 512], F32) for i in range(2)]
                otmp = sbuf(cx, "b_otmp", [64, 512], F32)
                OBT = [sbuf(cx, f"b_obt{i}", [128, 8, 128], BF16) for i in range(2)]
                for o_ in OBT:
                    P.op("pool", C("memset", o_[:], 0.0), writes=[o_.r0])
                m1 = sbuf(cx, "b_m1", [128, 8, 128], F32)
                m2 = sbuf(cx, "b_m2", [128, 8, 128], F32)
                mx = sbuf(cx, "b_mx", [128, 8, 128], BF16)
                sti = 0
                pend_proj = None

                def proj(oa_, mg, xt, OBt, tsl, stq):
                    nonlocal sti
                    ya = ST[sti % 2]
                    sti += 1
                    yb = ST[sti % 2]
                    sti += 1
                    yav = ya[:].rearrange("p a (b q) -> p (a b) q", q=128)
                    ybv = yb[:].rearrange("p a (b q) -> p (a b) q", q=128)
                    items = []
                    for cc in range(8):
                        for kc in range(4):
                            items.append((yav[:, cc, :], WPA[:, kc, cc * 128:(cc + 1) * 128], oa_[:, kc, :], kc == 0, kc == 3))
                    P.op("pe", mmgroup(items), reads=[WPA.r0, oa_.r0], writes=[ya.r0])
                    items = []
                    for cc in range(8):
                        for hd in range(8):
                            items.append((ybv[:, cc, :], WPB[:, hd, cc * 128:(cc + 1) * 128], OBt[:, hd, :], hd == 0, hd == 7))
                    P.op("pe", mmgroup(items), reads=[WPB.r0, OBt.r0], writes=[yb.r0])
                    P.op("dve", C("tensor_tensor", out=m1[:], in0=yav, in1=mg[:, 0:8, :], op=ALU.mult), reads=[ya.r0, mg.r0], writes=[m1.r0])
                    P.op("dve", C("tensor_tensor", out=m2[:], in0=ybv, in1=mg[:, 8:16, :], op=ALU.mult), reads=[yb.r0, mg.r0], writes=[m2.r0])
                    P.op("pool", C("tensor_tensor", out=mx[:], in0=m1[:], in1=m2[:], op=ALU.add), reads=[m1.r0, m2.r0], writes=[mx.r0])

                def proj2(oa_, mg, xt, OBt, tsl, stq):
                    nonlocal sti
                    z = ST[sti % 2]
                    sti += 1
                    items = []
                    for hf in range(2):
                        for kc in range(8):
                            items.append((z[:, hf, :], mx[:, kc, :], WO[:, kc, hf * 512:(hf + 1) * 512], kc == 0, kc == 7))
                    P.op("pe", mmgroup(items), reads=[mx.r0, WO.r0], writes=[z.r0])
                    P.op("dve", C("tensor_tensor", out=xt[:], in0=xt[:], in1=z[:].rearrange("p a b -> p (a b)"), op=ALU.add),
                         reads=[xt.r0, z.r0], writes=[xt.r0])
                    P.dma("pool", XR1[tsl, :], xt[:], reads=[xt.r0], writes=[dres["XR1"][stq]])
                psi = 0
                oti = 0
                ubi = 0
                import os as _os
                for i in range(int(_os.environ.get('A2_I0', 0)), min(NT, int(_os.environ.get('A2_I1', NT)))):
                    t0 = i * 128
                    tsl = slice(t0, t0 + 128)
                    stq = i // 4
                    qb, gr, mg, oa_, xt = QB[i % 2], GR[0], MG[i % 2], OA[i % 2], XT[i % 2]
                    OBt = OBT[i % 2]
                    for h in range(2):
                        P.dma("sp", qb[h * 64:(h + 1) * 64, h, :].rearrange("p (g q) -> p g q", g=4), QBT[h * 256:(h + 1) * 256, tsl].rearrange("(g d) t -> d g t", d=64),
                              reads=[dres["QBT"][stq]], writes=[qb.r0])
                    P.dma("sp", gr[64:65, :, :], GT[:, tsl].rearrange("(o r) t -> o r t", o=1), reads=[dres["GT"][stq]], writes=[gr.r0])
                    P.dma("sp", mg[:], MGT[:, tsl].rearrange("(c p) t -> p c t", p=128), reads=[dres["MGT"][stq]], writes=[mg.r0])
                    P.dma("sp", oa_[:], OAT[:, tsl].rearrange("(c p) t -> p c t", p=128), reads=[dres["OAT"][stq]], writes=[oa_.r0])
                    P.dma("sp", xt[:], src[tsl, :], reads=([dres[src_name][stq]] if src_name else []), writes=[xt.r0])
                    for h in range(2):
                        hs = slice(h * 64, (h + 1) * 64)
                        qh = qb[:, h, :]

                        def score_tiles(kts, ksrc, extra_bias, dst, dsti):
                            nonlocal sti
                            b = sti % 2
                            sti += 1
                            items = []
                            for e_, kt in enumerate(kts):
                                items.append((ST[b][:, e_, :], ksrc[:, kt * 128:(kt + 1) * 128], qh, True, not extra_bias))
                                if extra_bias:
                                    items.append((ST[b][:, e_, :], EALL[0:NJ, kt * 128:(kt + 1) * 128],
                                                  selT[0:NJ, :, :].rearrange("p g q -> p (g q)"), False, True))
                            rd_ = [qb.r0, ksrc_res[id(ksrc)]] + ([EALL.r0, selT.r0] if extra_bias else [])
                            P.op("pe", mmgroup(items), reads=rd_, writes=[ST[b].r0])
                            n = len(kts)
                            P.op("act", C("activation", out=dst[:, dsti:dsti + n, :], in_=ST[b][:, 0:n, :], func=AF.Exp, scale=0.125),
                                 reads=[ST[b].r0], writes=[dst.r0])

                        ksrc_res = {id(KCMPT): KCMPT.r0, id(KS): KS.r0, id(KW): KW.r0}

                        def maskmul(dst, di, mask_ap, mres, eng="pool"):
                            P.op(eng, C("tensor_tensor",
                                out=dst[:, di, :].rearrange("p (g q) -> p g q", g=4),
                                in0=dst[:, di, :].rearrange("p (g q) -> p g q", g=4),
                                in1=mask_ap.unsqueeze(1).to_broadcast([128, 4, 128]), op=ALU.mult),
                                 reads=[dst.r0, mres], writes=[dst.r0])

                        nct = i // 16 + 1
                        for c0 in range(0, nct, 2):
                            kts = list(range(c0, min(c0 + 2, nct)))
                            score_tiles(kts, KCMPT, False, PC, c0)
                        if h == 1 and pend_proj is not None:
                            proj(*pend_proj)
                        maskmul(PC, nct - 1, CM[:, i % 16, :], CM.r0)
                        if i % 16 == 0 and i > 0:
                            maskmul(PC, nct - 2, CM[:, 16, :], CM.r0)
                        otc = OT[oti % 2]
                        oti += 1
                        items = [(otc[0:65, :], VCMP[:, h, ct, :], PC[:, ct, :], ct == 0, ct == nct - 1) for ct in range(nct)]
                        P.op("pe", mmgroup(items), reads=[VCMP.r0, PC.r0], writes=[otc.r0])
                        items = []
                        for g in range(4):
                            for ct in range(nct):
                                items.append((UB[:, g, 0:NJ], PC[:, ct, g * 128:(g + 1) * 128], MM[:, ct, 0:NJ], ct == 0, ct == nct - 1))
                        P.op("pe", mmgroup(items), reads=[PC.r0, MM.r0], writes=[UB.r0])
                        P.op("dve", C("tensor_reduce", out=rs4[:], in_=UB[:, :, 0:NJ], axis=AX.X, op=ALU.add), reads=[UB.r0], writes=[rs4.r0])
                        P.op("dve", C("tensor_scalar", rs4[:], rs4[:], 0.5, None, op0=ALU.mult), reads=[rs4.r0], writes=[rs4.r0])
                        P.op("dve", C("tensor_scalar", rs4[:], rs4[:], TINY, None, op0=ALU.max), reads=[rs4.r0], writes=[rs4.r0])
                        P.op("dve", C("reciprocal", rs4[:], rs4[:]), reads=[rs4.r0], writes=[rs4.r0])
                        P.op("dve", C("tensor_scalar", psl[:, 0:NJ], UB[:, 0, 0:NJ], rs4[:, 0:1], None, op0=ALU.mult), reads=[UB.r0, rs4.r0], writes=[psl.r0])
                        for g in range(1, 4):
                            P.op("dve", C("scalar_tensor_tensor", out=psl[:, 0:NJ], in0=UB[:, g, 0:NJ], scalar=rs4[:, g:g + 1], in1=psl[:, 0:NJ],
                                                                          op0=ALU.mult, op1=ALU.add), reads=[UB.r0, rs4.r0, psl.r0], writes=[psl.r0])
                        o_tab = 127 - 2 * i
                        P.op("dve", C("tensor_tensor", out=psl[:, 0:NJ], in0=psl[:, 0:NJ], in1=TK[:, o_tab:o_tab + NJ], op=ALU.mult),
                             reads=[psl.r0, TK.r0], writes=[psl.r0])
                        P.op("dve", C("tensor_tensor", out=psl[:, 0:NJ], in0=psl[:, 0:NJ], in1=TA[:, o_tab:o_tab + NJ], op=ALU.add),
                             reads=[psl.r0, TA.r0], writes=[psl.r0])
                        P.op("dve", C("memset", psl[:, 0:1], 3e9), writes=[psl.r0])
                        P.op("dve", C("max", out=m8a[:], in_=psl[:, 0:NJ]), reads=[psl.r0], writes=[m8a.r0])
                        P.op("dve", C("match_replace", out=sc2[:, 0:NJ], in_to_replace=m8a[:], in_values=psl[:, 0:NJ], imm_value=-1e9),
                             reads=[psl.r0, m8a.r0], writes=[sc2.r0])
                        P.op("dve", C("max", out=m8b[:], in_=sc2[:, 0:NJ]), reads=[sc2.r0], writes=[m8b.r0])
                        P.op("dve", C("tensor_reduce", out=m8a[:, 0:1], in_=m8b[:], axis=AX.X, op=ALU.min), reads=[m8b.r0], writes=[m8a.r0])
                        P.op("dve", C("tensor_scalar", sc2[:, 0:NJ], psl[:, 0:NJ], m8a[:, 0:1], None, op0=ALU.is_lt),
                             reads=[psl.r0, m8a.r0], writes=[sc2.r0])
                        P.op("dve", C("tensor_scalar", selb[:, 0:NJ], sc2[:, 0:NJ], NEGB, None, op0=ALU.mult),
                             reads=[sc2.r0], writes=[selb.r0])

                        def dense_branch(kts_all, ksrc, vsrc, bias, masks, meng="dve"):
                            nonlocal psi, oti
                            ot = OT[oti % 2]
                            oti += 1
                            nk = len(kts_all)
                            pend = None

                            def pv(kts, pbuf):
                                items = [(ot[0:65, :], vsrc[:, kt, h, :], pbuf[:, e_, :], kt == kts_all[0], kt == kts_all[-1])
                                         for e_, kt in enumerate(kts)]
                                P.op("pe", mmgroup(items), reads=[vsrc.r0, pbuf.r0], writes=[ot.r0])
                            for c0 in range(0, nk, 2):
                                kts = kts_all[c0:c0 + 2]
                                pbuf = PS_[psi % 2]
                                psi += 1
                                score_tiles(kts, ksrc, bias, pbuf, 0)
                                for e_, kt in enumerate(kts):
                                    if kt in masks:
                                        maskmul(pbuf, e_, MASKS[:, masks[kt], :], MASKS.r0, eng=meng)
                                if pend is not None:
                                    pv(*pend)
                                pend = (kts, pbuf)
                            pv(*pend)
                            return ot

                        def combine(ot, br, first):
                            nonlocal ubi
                            ub = UBR[ubi % 3]
                            ubi += 1
                            P.op("dve", C("tensor_copy", ub[:], ot[0:65, :]), reads=[ot.r0], writes=[ub.r0])
                            P.op("dve", C("tensor_scalar", wrow[64:65, :], ub[64:65, :], TINY, None, op0=ALU.max), reads=[ub.r0], writes=[wrow.r0])
                            P.op("act", C("activation", out=wrow[64:65, :], in_=wrow[64:65, :], func=AF.Ln), reads=[wrow.r0], writes=[wrow.r0])
                            P.op("act", C("activation", out=wrow[64:65, :], in_=wrow[64:65, :], func=AF.Exp, scale=-1.0), reads=[wrow.r0], writes=[wrow.r0])
                            P.op("dve", C("tensor_tensor", out=wrow[64:65, :].rearrange("p (g q) -> p g q", g=4),
                                                                       in0=wrow[64:65, :].rearrange("p (g q) -> p g q", g=4),
                                                                       in1=gr[64:65, br * 8 + h * 4: br * 8 + h * 4 + 4, :], op=ALU.mult),
                                 reads=[wrow.r0, gr.r0], writes=[wrow.r0])
                            bc = BCB[ubi % 2]
                            ws = ubi % 6
                            P.dma("sp", WSCR[ws:ws + 1, :], wrow[64:65, :], reads=[wrow.r0], writes=[wscr_res[ws]])
                            P.dma("sp", bc[:], WSCR[ws:ws + 1, :].to_broadcast([64, 512]), reads=[wscr_res[ws]], writes=[bc.r0])
                            if first:
                                P.op("dve", C("tensor_tensor", out=obf[:], in0=ub[0:64, :], in1=bc[:], op=ALU.mult),
                                     reads=[ub.r0, bc.r0], writes=[obf.r0])
                            else:
                                P.op("dve", C("tensor_tensor", out=otmp[:], in0=ub[0:64, :], in1=bc[:], op=ALU.mult),
                                     reads=[ub.r0, bc.r0], writes=[otmp.r0])
                                P.op("pool", C("tensor_tensor", out=obf[:], in0=obf[:], in1=otmp[:], op=ALU.add),
                                     reads=[obf.r0, otmp.r0], writes=[obf.r0])

                        wm = {i: 1}
                        if i - 4 >= 0:
                            wm[i - 4] = 2
                        otw = dense_branch(list(range(max(0, i - 4), i + 1)), KW, VW, False, wm, meng="pool")
                        P.op("pe", mmgroup([(MISC[0:NJ, 0:128], selb[:, 0:NJ], IDN, True, True)]), reads=[selb.r0, MATS.r0], writes=[MISC.r0])
                        P.op("dve", C("tensor_copy", selT[0:NJ, :, :], MISC[0:NJ, 0:128].unsqueeze(1).to_broadcast([NJ, 4, 128])),
                             reads=[MISC.r0], writes=[selT.r0])
                        combine(otc, 0, True)
                        ots = dense_branch(list(range(0, i + 1)), KS, VS, True, {i: 1})
                        combine(otw, 2, False)
                        combine(ots, 1, False)
                        P.op("pool", C("tensor_copy", OBt[0:64, h * 4:(h + 1) * 4, :], obf[:].rearrange("p (g q) -> p g q", g=4)),
                             reads=[obf.r0], writes=[OBt.r0])
                        if pend_proj is not None and h == 1:
                            proj2(*pend_proj)
                            pend_proj = None
                    pend_proj = (oa_, mg, xt, OBt, tsl, stq)
                if pend_proj is not None:
                    proj(*pend_proj)
                    proj2(*pend_proj)
                P.barrier()

        def phase_F(l, dst, dst_name):
            NF = DFF // 128
            with ExitStack() as cx:
                WU = sbuf(cx, "f_wu", [128, 8, 2 * DFF], BF16)
                WD = sbuf(cx, "f_wd", [128, NF, D], BF16)
                with ExitStack() as cx2:
                    H4 = 2 * DFF // 4
                    load_cast(cx2, lambda i: WU[:, i // 4, (i % 4) * H4:(i % 4 + 1) * H4],
                              lambda i: w_up[l, (i // 4) * 128:(i // 4 + 1) * 128, (i % 4) * H4:(i % 4 + 1) * H4], 32, [128, H4], WU.r0, "wu")
                    load_cast(cx2, lambda i: WD[:, i, :], lambda i: w_down[l, i * 128:(i + 1) * 128, :], NF, [128, D], WD.r0, "wd")
                    P.barrier()
                gT = sbuf(cx, "f_gT", [128, 8], F32)
                P.dma("sp", gT[:], norm_ffn_g[l].rearrange("(kc p) -> p kc", p=128), writes=[gT.r0], slow=True)
                CW = sbuf(cx, "f_cw", [128, 3, NF], F32)
                CB = sbuf(cx, "f_cb", [128, NF], F32)
                for k in range(3):
                    P.dma("sp", CW[:, k, :], conv_w[l, k].rearrange("(fc p) -> p fc", p=128), writes=[CW.r0], slow=True)
                P.dma("sp", CB[:], conv_b[l].rearrange("(fc p) -> p fc", p=128), writes=[CB.r0], slow=True)
                HAL = sbuf(cx, "f_hal", [128, NF, 2], F32)
                P.op("pool", C("memset", HAL[:], 0.0), writes=[HAL.r0])
                hT = sbuf(cx, "f_hT", [128, 8, 512], BF16)
                xts = [sbuf(cx, f"f_x{i}", [128, D], F32) for i in range(4)]
                junk = sbuf(cx, "f_nj", [128, D], BF16)
                ss = sbuf(cx, "f_ss", [128, 1], F32)
                rs = sbuf(cx, "f_rs", [128, 1], F32)
                xs = sbuf(cx, "f_xs", [128, D], BF16)
                tp = psum(cx, "f_tp", [128, 8, 128])
                pg = [psum(cx, f"f_pg{i}", [128, 512]) for i in range(2)]
                pu = [psum(cx, f"f_pu{i}", [128, 512]) for i in range(2)]
                pz = psum(cx, "f_pz", [128, 2, 512])
                Gt = [sbuf(cx, f"f_gt{i}", [128, 514], F32) for i in range(2)]
                cv = [sbuf(cx, f"f_cv{i}", [128, 512], F32) for i in range(2)]
                sl = [sbuf(cx, f"f_sl{i}", [128, 512], F32) for i in range(2)]
                actT = sbuf(cx, "f_act", [128, NF, 512], BF16, nres=NF)
                xo = [sbuf(cx, f"f_xo{i}", [128, D], F32) for i in range(1)]
                for st in range(NST):
                    for tt in range(4):
                        norm_to_hT((xts[tt], junk, ss, rs, xs, tp), gT, XR1, st * 512 + tt * 128, hT, hT.r0, tt)
                    for fc in range(NF):
                        b = fc % 2
                        items = [(pg[b][:], WU[:, kc, fc * 128:(fc + 1) * 128], hT[:, kc, :], kc == 0, kc == 7) for kc in range(8)]
                        P.op("pe", mmgroup(items), reads=[WU.r0, hT.r0], writes=[pg[b].r0])
                        items = [(pu[b][:], WU[:, kc, DFF + fc * 128:DFF + (fc + 1) * 128], hT[:, kc, :], kc == 0, kc == 7) for kc in range(8)]
                        P.op("pe", mmgroup(items), reads=[WU.r0, hT.r0], writes=[pu[b].r0])
                        g_ = Gt[b]
                        P.op("pool", C("tensor_copy", g_[:, 0:2], HAL[:, fc, :]), reads=[HAL.r0], writes=[g_.r0])
                        P.op("act", C("activation", out=g_[:, 2:514], in_=pg[b][:], func=AF.Identity), reads=[pg[b].r0], writes=[g_.r0])
                        P.op("pool", C("tensor_copy", HAL[:, fc, :], g_[:, 512:514]), reads=[g_.r0], writes=[HAL.r0])
                        c_ = cv[b]
                        P.op("dve", C("tensor_scalar", c_[:], g_[:, 2:514], CW[:, 2, fc:fc + 1], CB[:, fc:fc + 1], op0=ALU.mult, op1=ALU.add),
                             reads=[g_.r0, CW.r0, CB.r0], writes=[c_.r0])
                        P.op("dve", C("scalar_tensor_tensor", out=c_[:], in0=g_[:, 1:513], scalar=CW[:, 1, fc:fc + 1], in1=c_[:], op0=ALU.mult, op1=ALU.add),
                             reads=[g_.r0, CW.r0, c_.r0], writes=[c_.r0])
                        P.op("dve", C("scalar_tensor_tensor", out=c_[:], in0=g_[:, 0:512], scalar=CW[:, 0, fc:fc + 1], in1=c_[:], op0=ALU.mult, op1=ALU.add),
                             reads=[g_.r0, CW.r0, c_.r0], writes=[c_.r0])
                        s_ = sl[b]
                        P.op("act", C("activation", out=s_[:], in_=c_[:], func=AF.Silu), reads=[c_.r0], writes=[s_.r0])
                        P.op("dve", C("tensor_tensor", out=actT[:, fc, :], in0=s_[:], in1=pu[b][:], op=ALU.mult),
                             reads=[s_.r0, pu[b].r0], writes=[actT.res[fc]])
                    for tt in range(4):
                        t0 = st * 512 + tt * 128
                        items = []
                        for hf in range(2):
                            for fc in range(NF):
                                items.append((pz[:, hf, :], actT[:, fc, tt * 128:(tt + 1) * 128], WD[:, fc, hf * 512:(hf + 1) * 512], fc == 0, fc == NF - 1))
                        P.op("pe", mmgroup(items), reads=[WD.r0] + actT.res, writes=[pz.r0])
                        o = xo[0]
                        P.op("dve", C("tensor_tensor", out=o[:], in0=xts[tt][:], in1=pz[:].rearrange("p a b -> p (a b)"), op=ALU.add),
                             reads=[xts[tt].r0, pz.r0], writes=[o.r0])
                        P.dma("pool", dst[t0:t0 + 128, :], o[:], reads=[o.r0], writes=[dres[dst_name][st]])
                P.barrier()

        src, src_name = x_in, None
        for l in range(DEPTH):
            phase_P(l, src, src_name)
            if stop_after == "P":
                break
            with ExitStack() as lc:
                KCMPT = sbuf(lc, "KCMPT", [128, NCT * 128], BF16)
                VCMP = sbuf(lc, "VCMP", [128, 2, NCT, 65], BF16)
                phase_C(l, KCMPT, VCMP)
                if stop_after != "C":
                    phase_A1(l)
                    if stop_after != "A1":
                        phase_A2(l, src, src_name, KCMPT, VCMP)
            if stop_after in ("C", "A1", "A2"):
                break
            last = (l == DEPTH - 1)
            phase_F(l, y_out if last else XR2, "Y" if last else "XR2")
            src, src_name = XR2, "XR2"

        final_ev = P.all_events()
        with nc.Block() as block:
            @block.tensor
            def _(e):
                P.replay("pe", e)

            @block.scalar
            def _(e):
                P.replay("act", e)

            @block.vector
            def _(e):
                P.replay("dve", e)

            @block.gpsimd
            def _(e):
                P.replay("pool", e)

            @block.sync
            def _(e):
                P.replay("sp", e)
                for sk, v in final_ev:
                    e.wait_ge(P.sem[sk], v)
        nc._n_rec = P.n_inst
    return nc


_CACHE = {}


def _get_prog(S, depth):
    key = (S, depth)
    if key not in _CACHE:
        _CACHE[key] = build(S, depth)
    return _CACHE[key]


WNAMES = ("norm_mix_g", "w_in", "a_q_g", "a_k_g", "b_q_g", "b_k_g", "cmp_pos", "cmp_w1", "cmp_w2",
          "w_proj_a", "w_proj_b", "w_out", "norm_ffn_g", "w_up", "conv_w", "conv_b", "w_down")


def kernel(**inputs):
    x = np.ascontiguousarray(np.asarray(inputs["x"], dtype=np.float32))
    B, S, _ = x.shape
    depth = inputs["w_in"].shape[0]
    consts = host_consts(S)
    ws = {k: np.ascontiguousarray(np.asarray(inputs[k], dtype=np.float32)) for k in WNAMES}
    n = 8
    if FUSED:
        nc = _get_prog(S, depth)
        in_maps = []
        for c in range(n):
            m = {"x": x[c % B]}
            m.update(ws)
            m.update(consts)
            in_maps.append(m)
        res = run_bass_kernel_spmd(nc, in_maps, core_ids=list(range(n)))
        return np.stack([res.results[b]["y"] for b in range(B)], axis=0).astype(np.float32)
    nc = _get_prog(S, 1)
    cur = [x[c % B] for c in range(n)]
    for l in range(depth):
        in_maps = []
        for c in range(n):
            m = {"x": np.ascontiguousarray(cur[c])}
            m.update({k: np.ascontiguousarray(v[l:l + 1]) for k, v in ws.items()})
            m.update(consts)
            in_maps.append(m)
        res = run_bass_kernel_spmd(nc, in_maps, core_ids=list(range(n)))
        cur = [res.results[c]["y"] for c in range(n)]
    return np.stack([cur[b] for b in range(B)], axis=0).astype(np.float32)
```

```python
import numpy as np
from contextlib import ExitStack
import concourse.bass as bass
import concourse.mybir as mybir
from concourse.bass_utils import run_bass_kernel_spmd

F32 = mybir.dt.float32
BF16 = mybir.dt.bfloat16
AF = mybir.ActivationFunctionType
ALU = mybir.AluOpType
AX = mybir.AxisListType

D = 1024
NIN = 7960
DFF = 2816
EPS = 1e-6
TINY = 1e-30
NEGB = -30000.0
DIL_PAIRS = ((128, 1), (512, 4), (2048, 16))
C_AQ, C_AK, C_AV, C_BQ, C_BKV, C_GATE, C_MERGE = 0, 1536, 3072, 4608, 5120, 5888, 5912
FUSED = True


class Res:
    __slots__ = ("w", "r")

    def __init__(self):
        self.w = None
        self.r = {}


class Buf:
    def __init__(self, t, nres=1):
        self.t = t
        self.res = [Res() for _ in range(nres)]

    def __getitem__(self, k):
        return self.t[k]

    @property
    def r0(self):
        return self.res[0]


COMPUTE = ("pe", "act", "dve", "pool")


class Prog:
    ND = 16

    def __init__(self, nc, ctx):
        self.nc = nc
        self.ops = {e: [] for e in ("pe", "act", "dve", "pool", "sp")}
        self.cnt = {e: 0 for e in COMPUTE}
        self.sem = {}
        for e in COMPUTE:
            self.sem[e] = ctx.enter_context(nc.semaphore("s_" + e))
        self.dma_val = {}
        for q in ("sp", "pool"):
            for k in range(self.ND):
                key = (q, k)
                self.sem[key] = ctx.enter_context(nc.semaphore(f"d_{q}_{k}"))
                self.dma_val[key] = 0
        self.rr = {"sp": 0, "pool": 0}
        self.seen = {e: {} for e in self.ops}
        self.n_inst = 0

    def _emit(self, eng, fn, deps, semkey, inc):
        best = {}
        for (sk, v) in deps:
            if best.get(sk, 0) < v:
                best[sk] = v
        waits = []
        for sk, v in best.items():
            if sk == eng and eng == "pe":
                continue
            if self.seen[eng].get(sk, 0) >= v:
                continue
            self.seen[eng][sk] = v
            waits.append((sk, v))
        self.ops[eng].append((waits, fn, semkey, inc))
        self.n_inst += 1 + len(waits)

    def op(self, eng, fn, reads=(), writes=(), dma=False):
        deps = []
        for r in reads:
            if r.w is not None:
                deps.append(r.w)
        for w in writes:
            if w.w is not None:
                deps.append(w.w)
            deps.extend(w.r.items())
        if dma:
            k = self.rr[eng]
            self.rr[eng] = (k + 1) % self.ND
            semkey = (eng, k)
            prev = self.dma_val[semkey]
            if prev > 0:
                deps.append((semkey, prev))
            val = prev + 16
            self.dma_val[semkey] = val
            inc = 16
        else:
            semkey = eng
            self.cnt[eng] += 1
            val = self.cnt[eng]
            inc = 1
        self._emit(eng, fn, deps, semkey, inc)
        for r in reads:
            if r.r.get(semkey, 0) < val:
                r.r[semkey] = val
        for w in writes:
            w.w = (semkey, val)
            w.r = {}
        return (semkey, val)

    def dma(self, q, out, in_, reads=(), writes=(), slow=False):
        if slow:
            fn = lambda e, out=out, in_=in_: e.dma_start(out=out, in_=in_, allow_slow_non_contiguous=True)
        else:
            fn = lambda e, out=out, in_=in_: e.dma_start(out=out, in_=in_)
        return self.op(q, fn, reads, writes, dma=True)

    def all_events(self):
        ev = [(e, self.cnt[e]) for e in COMPUTE if self.cnt[e] > 0]
        ev += [(k, v) for k, v in self.dma_val.items() if v > 0]
        return ev

    def barrier(self):
        ev = self.all_events()
        for eng in self.ops:
            waits = []
            for sk, v in ev:
                if sk == eng:
                    continue
                if self.seen[eng].get(sk, 0) >= v:
                    continue
                self.seen[eng][sk] = v
                waits.append((sk, v))
            if waits:
                self.ops[eng].append((waits, None, None, 0))
                self.n_inst += len(waits)

    def replay(self, eng, e):
        for (waits, fn, semkey, inc) in self.ops[eng]:
            for sk, v in waits:
                e.wait_ge(self.sem[sk], v)
            if fn is not None:
                ins = fn(e)
                ins.then_inc(self.sem[semkey], inc)


def C(name, *a, **k):
    return lambda e: getattr(e, name)(*a, **k)


def mmgroup(items):
    def fn(e):
        ins = None
        for (out, lhsT, rhs, st, sp) in items:
            ins = e.matmul(out, lhsT, rhs, start=st, stop=sp)
        return ins
    return fn


def host_consts(S):
    NT = S // 128
    NJ = S // 64
    NCB = S // 16 - 1
    NCT = (NCB + 127) // 128
    half = 32
    inv_freq = (np.float32(10000.0) ** (-(np.arange(half, dtype=np.float32)) / np.float32(half))).astype(np.float32)
    fidx = (np.arange(128) % 64) % 32

    def tabs(pos):
        ang = pos.astype(np.float32)[None, :] * inv_freq[fidx][:, None]
        return np.cos(ang).astype(np.float32), np.sin(ang).astype(np.float32)

    cos, sin = tabs(np.arange(S))
    posc = np.zeros(NCT * 128, dtype=np.float32)
    posc[:NCB] = np.arange(NCB) * 16 + 31
    cosc, sinc = tabs(posc)
    rot = np.zeros((128, 128), np.float32)
    for m in range(128):
        hb, d = (m // 64) * 64, m % 64
        if d < 32:
            rot[hb + d + 32, m] = -1.0
        else:
            rot[hb + d - 32, m] = 1.0
    bones = np.zeros((128, 128), np.float32)
    bones[:64, :64] = 1.0 / 64
    bones[64:, 64:] = 1.0 / 64
    ident = np.eye(128, dtype=np.float32)
    kk = np.arange(128)[:, None]
    qq = np.arange(128)[None, :]
    masks = np.stack([(kk >= qq), (kk <= qq), (kk > qq)], axis=1).astype(np.float32)
    mats = np.stack([rot, bones, ident], axis=1)
    eall = (np.arange(128)[:, None] == (np.arange(S)[None, :] // 64)).astype(np.float32)
    mm = np.zeros((NCT * 128, 128), np.float32)
    for c in range(NCB):
        for n in (c, c + 1):
            if n // 4 < 128:
                mm[c, n // 4] += 1.0
    mmat = mm.reshape(NCT, 128, 128).transpose(1, 0, 2).copy()
    cm = np.zeros((128, 17, 128), np.float32)
    cl = np.arange(128)[:, None]
    for v in range(16):
        cp = cl - 8 * v
        cm[:, v, :] = (16 * cp + 31 <= qq)
    cp = cl - 128
    cm[:, 16, :] = (16 * cp + 31 <= qq)
    tk = np.zeros((128, 256), np.float32)
    ta = np.zeros((128, 256), np.float32)
    for p in range(128):
        hi = 1 if p >= 64 else 0
        for m in range(256):
            r = m - 127
            if r > hi:
                ta[p, m] = -float(r - hi)
            elif r == hi:
                ta[p, m] = 2e9
            elif r == hi - 1:
                ta[p, m] = 1e9
            else:
                tk[p, m] = 1.0
    return dict(c_cos=cos, c_sin=sin, c_cosc=np.ascontiguousarray(cosc[:64]), c_sinc=np.ascontiguousarray(sinc[:64]),
                c_mats=mats, c_masks=masks, c_eall=eall, c_mmat=mmat, c_cm=cm, c_tk=tk, c_ta=ta)


def build(S, DEPTH, dbg=(), stop_after=None):
    NT = S // 128
    NST = S // 512
    NJ = 128
    NCB = S // 16 - 1
    NCT = (NCB + 127) // 128
    nc = bass.Bass("TRN2", target_bir_lowering=False)

    def din(name, shape, dt=F32):
        return nc.dram_tensor(name, list(shape), dt, kind="ExternalInput").ap()

    x_in = din("x", [S, D])
    norm_mix_g = din("norm_mix_g", [DEPTH, D])
    w_in = din("w_in", [DEPTH, D, NIN])
    a_q_g = din("a_q_g", [DEPTH, 64])
    a_k_g = din("a_k_g", [DEPTH, 64])
    b_q_g = din("b_q_g", [DEPTH, 64])
    b_k_g = din("b_k_g", [DEPTH, 3, 64])
    cmp_pos = din("cmp_pos", [DEPTH, 2, 32, 64])
    cmp_w1 = din("cmp_w1", [DEPTH, 2, 2048, 256])
    cmp_w2 = din("cmp_w2", [DEPTH, 2, 256, 64])
    w_proj_a = din("w_proj_a", [DEPTH, 512, D])
    w_proj_b = din("w_proj_b", [DEPTH, 512, D])
    w_out = din("w_out", [DEPTH, D, D])
    norm_ffn_g = din("norm_ffn_g", [DEPTH, D])
    w_up = din("w_up", [DEPTH, D, 2 * DFF])
    conv_w = din("conv_w", [DEPTH, 3, DFF])
    conv_b = din("conv_b", [DEPTH, DFF])
    w_down = din("w_down", [DEPTH, DFF, D])
    c_cos = din("c_cos", [128, S])
    c_sin = din("c_sin", [128, S])
    c_cosc = din("c_cosc", [64, NCT * 128])
    c_sinc = din("c_sinc", [64, NCT * 128])
    c_mats = din("c_mats", [128, 3, 128])
    c_masks = din("c_masks", [128, 3, 128])
    c_eall = din("c_eall", [128, S])
    c_mmat = din("c_mmat", [128, NCT, 128])
    c_cm = din("c_cm", [128, 17, 128])
    c_tk = din("c_tk", [128, 256])
    c_ta = din("c_ta", [128, 256])
    y_out = nc.dram_tensor("y", [S, D], F32, kind="ExternalOutput").ap()

    def dscr(name, shape, dt):
        kind = "ExternalOutput" if name in dbg else "Internal"
        return nc.dram_tensor(name, list(shape), dt, kind=kind).ap()

    QAT = dscr("QAT", [1536, S], BF16)
    KAT = dscr("KAT", [1536, S], BF16)
    VA = dscr("VA", [S, 1536], BF16)
    QBT = dscr("QBT", [512, S], BF16)
    KSELT = dscr("KSELT", [128, S], BF16)
    KWINT = dscr("KWINT", [128, S], BF16)
    VSEL = dscr("VSEL", [S, 128], BF16)
    VWIN = dscr("VWIN", [S, 128], BF16)
    KCRT = dscr("KCRT", [128, S], BF16)
    VCRT = dscr("VCRT", [128, S], BF16)
    GT = dscr("GT", [24, S], F32)
    MGT = dscr("MGT", [2048, S], BF16)
    OAT = dscr("OAT", [512, S], BF16)
    XR1 = dscr("XR1", [S, D], F32)
    XR2 = dscr("XR2", [S, D], F32)
    WSCR = dscr("WSCR", [6, 512], F32)
    wscr_res = [Res() for _ in range(6)]
    dres = {n: [Res() for _ in range(NST)] for n in
            ("QAT", "KAT", "VA", "QBT", "KSELT", "KWINT", "VSEL", "VWIN", "KCRT", "VCRT", "GT", "MGT", "OAT", "XR1", "XR2", "Y")}

    def dall(n):
        return dres[n]

    with ExitStack() as top:
        P = Prog(nc, top)

        uid = [0]

        def sbuf(cx, name, shape, dt, nres=1):
            uid[0] += 1
            t = cx.enter_context(nc.sbuf_tensor(f"{name}_{uid[0]}", list(shape), dt))
            return Buf(t, nres)

        def psum(cx, name, shape, dt=F32):
            uid[0] += 1
            t = cx.enter_context(nc.psum_tensor(f"{name}_{uid[0]}", list(shape), dt))
            return Buf(t)

        MATS = sbuf(top, "MATS", [128, 3, 128], BF16)
        MASKS = sbuf(top, "MASKS", [128, 3, 128], BF16)
        ONESF = sbuf(top, "ONESF", [128, 64], F32)
        EPSC = sbuf(top, "EPSC", [128, 1], F32)
        with ExitStack() as cx:
            st1 = sbuf(cx, "cst1", [128, 3, 128], F32)
            st2 = sbuf(cx, "cst2", [128, 3, 128], F32)
            P.dma("sp", st1[:], c_mats, writes=[st1.r0])
            P.dma("sp", st2[:], c_masks, writes=[st2.r0])
            P.op("dve", C("tensor_copy", MATS[:], st1[:]), reads=[st1.r0], writes=[MATS.r0])
            P.op("dve", C("tensor_copy", MASKS[:], st2[:]), reads=[st2.r0], writes=[MASKS.r0])
            P.op("pool", C("memset", ONESF[:], 1.0), writes=[ONESF.r0])
            P.op("pool", C("memset", EPSC[:], EPS), writes=[EPSC.r0])
            P.barrier()
        ROT = MATS[:, 0, :]
        BON = MATS[:, 1, :]
        IDN = MATS[:, 2, :]

        def norm_to_hT(cx_bufs, l_gT, src, tok0, hT, hT_res, tt):
            xt, junk, ss, rs, xs, tp = cx_bufs
            P.dma("sp", xt[:], src[tok0:tok0 + 128, :], reads=[], writes=[xt.r0])
            P.op("pool", C("memset", ss[:], 0.0), writes=[ss.r0])
            P.op("act", C("activation", out=junk[:], in_=xt[:], func=AF.Square, accum_out=ss[:]),
                 reads=[xt.r0], writes=[junk.r0, ss.r0])
            P.op("act", C("activation", out=rs[:], in_=ss[:], func=AF.Sqrt, bias=EPSC[:, 0:1], scale=1.0 / D),
                 reads=[ss.r0, EPSC.r0], writes=[rs.r0])
            P.op("dve", C("reciprocal", rs[:], rs[:]), reads=[rs.r0], writes=[rs.r0])
            P.op("dve", C("tensor_scalar", xs[:], xt[:], rs[:, 0:1], None, op0=ALU.mult),
                 reads=[xt.r0, rs.r0], writes=[xs.r0])
            items = [(tp[:, kc, :], xs[:, kc * 128:(kc + 1) * 128], IDN, True, True) for kc in range(8)]
            P.op("pe", mmgroup(items), reads=[xs.r0, MATS.r0], writes=[tp.r0])
            P.op("dve", C("tensor_tensor", out=hT[:, :, tt * 128:(tt + 1) * 128], in0=tp[:],
                                                  in1=l_gT[:, :].unsqueeze(2).to_broadcast([128, 8, 128]), op=ALU.mult),
                 reads=[tp.r0, l_gT.r0], writes=[hT_res])
            return xt

        def load_cast(cx, dst_ap_fn, src_ap_fn, nchunks, shape, dst_res, name, eng="pool"):
            nb_ = min(4, nchunks)
            stg = [sbuf(cx, f"{name}_stg{i}", shape, F32) for i in range(nb_)]
            for i in range(nchunks):
                s = stg[i % nb_]
                P.dma("sp", s[:], src_ap_fn(i), writes=[s.r0])
                ce = eng if i % 2 == 0 else "dve"
                P.op(ce, C("tensor_copy", dst_ap_fn(i), s[:]), reads=[s.r0], writes=[dst_res])

        def phase_P(l, src, src_name):
            with ExitStack() as cx:
                WIN = sbuf(cx, "WIN", [128, 8, NIN], BF16)
                with ExitStack() as cx2:
                    Q4 = NIN // 4
                    load_cast(cx2, lambda i: WIN[:, i // 4, (i % 4) * Q4:(i % 4 + 1) * Q4],
                              lambda i: w_in[l, (i // 4) * 128:(i // 4 + 1) * 128, (i % 4) * Q4:(i % 4 + 1) * Q4],
                              32, [128, Q4], WIN.r0, "win")
                    P.barrier()
                gT = sbuf(cx, "gT", [128, 8], F32)
                P.dma("sp", gT[:], norm_mix_g[l].rearrange("(kc p) -> p kc", p=128), writes=[gT.r0], slow=True)
                GC = sbuf(cx, "GC", [128, 5], F32)
                for ci, gsrc in enumerate((a_q_g[l], a_k_g[l], b_q_g[l], b_k_g[l, 1], b_k_g[l, 2])):
                    for hb in range(2):
                        P.dma("sp", GC[hb * 64:(hb + 1) * 64, ci:ci + 1], gsrc.rearrange("(d o) -> d o", o=1),
                              writes=[GC.r0], slow=True)
                hTs = [sbuf(cx, f"hT{i}", [128, 8, 512], BF16) for i in range(2)]
                nb = [(sbuf(cx, f"nx{i}", [128, D], F32), sbuf(cx, f"nj{i}", [128, D], BF16), sbuf(cx, f"nss{i}", [128, 1], F32),
                       sbuf(cx, f"nrs{i}", [128, 1], F32), sbuf(cx, f"nxs{i}", [128, D], BF16),
                       psum(cx, f"ntp{i}", [128, 8, 128])) for i in range(1)]
                cs = [(sbuf(cx, f"cos{i}", [128, 512], F32), sbuf(cx, f"sin{i}", [128, 512], F32)) for i in range(2)]
                pa = [psum(cx, f"pa{i}", [128, 512]) for i in range(2)]
                p2 = [psum(cx, f"p2{i}", [128, 512]) for i in range(2)]
                p3 = [psum(cx, f"p3{i}", [128, 512]) for i in range(2)]
                sq = [sbuf(cx, f"sq{i}", [128, 512], BF16) for i in range(2)]
                xg = [sbuf(cx, f"xg{i}", [128, 512], BF16) for i in range(2)]
                rstd = [sbuf(cx, f"rstd{i}", [128, 512], F32) for i in range(2)]
                t1 = [sbuf(cx, f"t1{i}", [128, 512], F32) for i in range(2)]
                t2 = [sbuf(cx, f"t2{i}", [128, 512], F32) for i in range(2)]
                ob = [sbuf(cx, f"ob{i}", [128, 512], BF16) for i in range(3)]
                of = [sbuf(cx, f"of{i}", [128, 512], F32) for i in range(2)]
                gi = 0
                oi = 0
                for st in range(NST):
                    hT = hTs[st % 2]
                    c_t, s_t = cs[st % 2]
                    tsl = slice(st * 512, (st + 1) * 512)
                    P.dma("sp", c_t[:], c_cos[:, tsl], writes=[c_t.r0])
                    P.dma("sp", s_t[:], c_sin[:, tsl], writes=[s_t.r0])
                    for tt in range(4):
                        norm_to_hT(nb[0], gT, src, st * 512 + tt * 128, hT, hT.r0, tt)
                    wr_src = [dres[src_name][st]] if src_name else []
                    nr_groups = []
                    for g in range(12):
                        nr_groups.append((C_AQ + g * 128, 0, QAT, "QAT", g * 128))
                    for g in range(12):
                        nr_groups.append((C_AK + g * 128, 1, KAT, "KAT", g * 128))
                    for g in range(4):
                        nr_groups.append((C_BQ + g * 128, 2, QBT, "QBT", g * 128))
                    nr_groups.append((C_BKV + 256, 3, KSELT, "KSELT", 0))
                    nr_groups.append((C_BKV + 512, 4, KWINT, "KWINT", 0))
                    pend = None

                    def nr_epilogue(b, dst, dname, r0):
                        nonlocal oi
                        P.op("pe", mmgroup([(p2[b][:], BON, sq[b][:], True, True)]), reads=[sq[b].r0, MATS.r0], writes=[p2[b].r0])
                        P.op("pe", mmgroup([(p3[b][:], ROT, xg[b][:], True, True)]), reads=[xg[b].r0, MATS.r0], writes=[p3[b].r0])
                        P.op("act", C("activation", out=rstd[b][:], in_=p2[b][:], func=AF.Ln, bias=EPSC[:, 0:1], scale=1.0),
                             reads=[p2[b].r0, EPSC.r0], writes=[rstd[b].r0])
                        P.op("act", C("activation", out=rstd[b][:], in_=rstd[b][:], func=AF.Exp, scale=-0.5),
                             reads=[rstd[b].r0], writes=[rstd[b].r0])
                        P.op("pool", C("tensor_tensor", out=t1[b][:], in0=xg[b][:], in1=c_t[:], op=ALU.mult),
                             reads=[xg[b].r0, c_t.r0], writes=[t1[b].r0])
                        P.op("dve", C("tensor_tensor", out=t2[b][:], in0=p3[b][:], in1=s_t[:], op=ALU.mult),
                             reads=[p3[b].r0, s_t.r0], writes=[t2[b].r0])
                        P.op("dve", C("tensor_tensor", out=t2[b][:], in0=t1[b][:], in1=t2[b][:], op=ALU.add),
                             reads=[t1[b].r0, t2[b].r0], writes=[t2[b].r0])
                        o = ob[oi % 3]
                        oi += 1
                        P.op("dve", C("tensor_tensor", out=o[:], in0=t2[b][:], in1=rstd[b][:], op=ALU.mult),
                             reads=[t2[b].r0, rstd[b].r0], writes=[o.r0])
                        P.dma("pool", dst[r0:r0 + 128, tsl], o[:], reads=[o.r0], writes=[dres[dname][st]])
                    for (c0, gci, dst, dname, r0) in nr_groups:
                        b = gi % 2
                        gi += 1
                        items = [(pa[b][:], WIN[:, kc, c0:c0 + 128], hT[:, kc, :], kc == 0, kc == 7) for kc in range(8)]
                        P.op("pe", mmgroup(items), reads=[WIN.r0, hT.r0], writes=[pa[b].r0])
                        P.op("act", C("activation", out=sq[b][:], in_=pa[b][:], func=AF.Square),
                             reads=[pa[b].r0], writes=[sq[b].r0])
                        P.op("act", C("activation", out=xg[b][:], in_=pa[b][:], func=AF.Identity,
                                                                         scale=GC[:, gci:gci + 1]),
                             reads=[pa[b].r0, GC.r0], writes=[xg[b].r0])
                        if pend is not None:
                            nr_epilogue(*pend)
                        pend = (b, dst, dname, r0)
                    nr_epilogue(*pend)
                    sg = [(C_BKV + 0, 128, AF.Identity, KCRT, "KCRT", 0, BF16), (C_BKV + 128, 128, AF.Identity, VCRT, "VCRT", 0, BF16)]
                    for g in range(16):
                        sg.append((C_MERGE + g * 128, 128, AF.Sigmoid, MGT, "MGT", g * 128, BF16))
                    sg.append((C_GATE, 24, AF.Sigmoid, GT, "GT", 0, F32))
                    for (c0, w, func, dst, dname, r0, dt) in sg:
                        b = gi % 2
                        gi += 1
                        items = [(pa[b][0:w, :], WIN[:, kc, c0:c0 + w], hT[:, kc, :], kc == 0, kc == 7) for kc in range(8)]
                        P.op("pe", mmgroup(items), reads=[WIN.r0, hT.r0], writes=[pa[b].r0])
                        if dt == BF16:
                            o = ob[oi % 3]
                            oi += 1
                        else:
                            o = of[0]
                        P.op("act", C("activation", out=o[0:w, :], in_=pa[b][0:w, :], func=func),
                             reads=[pa[b].r0], writes=[o.r0])
                        P.dma("pool", dst[r0:r0 + w, tsl], o[0:w, :], reads=[o.r0], writes=[dres[dname][st]])
                    vb = [(C_AV, 512, VA, "VA", 0), (C_AV + 512, 512, VA, "VA", 512), (C_AV + 1024, 512, VA, "VA", 1024),
                          (C_BKV + 384, 128, VSEL, "VSEL", 0), (C_BKV + 640, 128, VWIN, "VWIN", 0)]
                    for tt in range(4):
                        t0 = st * 512 + tt * 128
                        for (c0, w, dst, dname, cc0) in vb:
                            b = gi % 2
                            gi += 1
                            items = [(pa[b][:, 0:w], hT[:, kc, tt * 128:(tt + 1) * 128], WIN[:, kc, c0:c0 + w], kc == 0, kc == 7)
                                     for kc in range(8)]
                            P.op("pe", mmgroup(items), reads=[WIN.r0, hT.r0], writes=[pa[b].r0])
                            o = ob[oi % 3]
                            oi += 1
                            P.op("act", C("activation", out=o[:, 0:w], in_=pa[b][:, 0:w], func=AF.Identity),
                                 reads=[pa[b].r0], writes=[o.r0])
                            P.dma("pool", dst[t0:t0 + 128, cc0:cc0 + w], o[:, 0:w], reads=[o.r0], writes=[dres[dname][st]])
                P.barrier()

        def phase_C(l, KCMPT, VCMP):
            with ExitStack() as cx:
                raw = sbuf(cx, "craw", [64, S], BF16)
                w1b = sbuf(cx, "cw1b", [64, 32, 256], BF16)
                w2b = sbuf(cx, "cw2b", [128, 2, 64], BF16)
                posb = sbuf(cx, "cposb", [64, 32], BF16)
                posf = sbuf(cx, "cposf", [64, 32], F32)
                w2f = sbuf(cx, "cw2f", [128, 2, 64], F32)
                gk = sbuf(cx, "cgk", [64, 1], F32)
                cosc = sbuf(cx, "ccos", [64, NCT * 128], F32)
                sinc = sbuf(cx, "csin", [64, NCT * 128], F32)
                ph = psum(cx, "cph", [128, 512])
                pb = psum(cx, "cpb", [128, 8])
                pk = psum(cx, "cpk", [128, 512])
                pq2 = psum(cx, "cp2", [128, 512])
                pq3 = psum(cx, "cp3", [128, 512])
                pv = psum(cx, "cpv", [128, 64])
                bcol = sbuf(cx, "cbcol", [128, 1], F32)
                xh = sbuf(cx, "cxh", [128, 512], F32)
                x2 = sbuf(cx, "cx2", [128, 512], F32)
                sg_ = sbuf(cx, "csg", [128, 512], F32)
                h1g = sbuf(cx, "ch1g", [128, 2, 512], BF16)
                sqc = sbuf(cx, "csq", [64, 512], BF16)
                xgc = sbuf(cx, "cxg", [64, 512], BF16)
                rsc = sbuf(cx, "crs", [64, 512], F32)
                t1c = sbuf(cx, "ct1", [64, 512], F32)
                t2c = sbuf(cx, "ct2", [64, 512], F32)
                okc = sbuf(cx, "cok", [64, 512], BF16)
                P.dma("sp", cosc[:], c_cosc, writes=[cosc.r0])
                P.dma("sp", sinc[:], c_sinc, writes=[sinc.r0])
                P.dma("sp", gk[:], b_k_g[l, 0].rearrange("(d o) -> d o", o=1), writes=[gk.r0], slow=True)
                P.op("pool", C("memset", KCMPT[:], 0.0), writes=[KCMPT.r0])
                P.op("pool", C("memset", VCMP[:], 0.0), writes=[VCMP.r0])
                P.op("pool", C("memset", VCMP[:, :, :, 64:65], 1.0), writes=[VCMP.r0])
                P.op("pool", C("memset", h1g[:], 0.0), writes=[h1g.r0])
                for typ in range(2):
                    with ExitStack() as cx2:
                        load_cast(cx2, lambda i: w1b[:, i * 8:(i + 1) * 8, :],
                                  lambda i: cmp_w1[l, typ].rearrange("(j d) h -> d j h", d=64)[:, i * 8:(i + 1) * 8, :],
                                  4, [64, 8, 256], w1b.r0, f"cw1_{typ}")
                        P.dma("sp", posf[:], cmp_pos[l, typ].rearrange("j d -> d j"), writes=[posf.r0], slow=True)
                        P.op("pool", C("tensor_copy", posb[:], posf[:]), reads=[posf.r0], writes=[posb.r0])
                        P.dma("sp", w2f[:], cmp_w2[l, typ].rearrange("(c p) d -> p c d", p=128), writes=[w2f.r0])
                        P.op("pool", C("tensor_copy", w2b[:], w2f[:]), reads=[w2f.r0], writes=[w2b.r0])
                        src = KCRT if typ == 0 else VCRT
                        sname = "KCRT" if typ == 0 else "VCRT"
                        for kvh in range(2):
                            P.dma("sp", raw[:], src[kvh * 64:(kvh + 1) * 64, :], reads=dall(sname), writes=[raw.r0])
                            for hc in range(2):
                                items = [(ph[:, 0:NCB], w1b[:, j, hc * 128:(hc + 1) * 128],
                                          raw[:, j:j + 16 * (NCB - 1) + 1:16], j == 0, j == 31) for j in range(32)]
                                P.op("pe", mmgroup(items), reads=[w1b.r0, raw.r0], writes=[ph.r0])
                                items = [(pb[:, 0:1], w1b[:, j, hc * 128:(hc + 1) * 128], posb[:, j:j + 1], j == 0, j == 31)
                                         for j in range(32)]
                                P.op("pe", mmgroup(items), reads=[w1b.r0, posb.r0], writes=[pb.r0])
                                P.op("dve", C("tensor_copy", bcol[:], pb[:, 0:1]), reads=[pb.r0], writes=[bcol.r0])
                                P.op("act", C("activation", out=xh[:, 0:NCB], in_=ph[:, 0:NCB], func=AF.Identity, bias=bcol[:, 0:1]),
                                     reads=[ph.r0, bcol.r0], writes=[xh.r0])
                                P.op("dve", C("tensor_tensor", out=x2[:, 0:NCB], in0=xh[:, 0:NCB], in1=xh[:, 0:NCB], op=ALU.mult),
                                     reads=[xh.r0], writes=[x2.r0])
                                P.op("dve", C("tensor_scalar", x2[:, 0:NCB], x2[:, 0:NCB], 0.044715, 1.0, op0=ALU.mult, op1=ALU.add),
                                     reads=[x2.r0], writes=[x2.r0])
                                P.op("dve", C("tensor_tensor", out=x2[:, 0:NCB], in0=x2[:, 0:NCB], in1=xh[:, 0:NCB], op=ALU.mult),
                                     reads=[x2.r0, xh.r0], writes=[x2.r0])
                                P.op("act", C("activation", out=sg_[:, 0:NCB], in_=x2[:, 0:NCB], func=AF.Sigmoid, scale=1.5957691216057308),
                                     reads=[x2.r0], writes=[sg_.r0])
                                P.op("dve", C("tensor_tensor", out=h1g[:, hc, 0:NCB], in0=xh[:, 0:NCB], in1=sg_[:, 0:NCB], op=ALU.mult),
                                     reads=[xh.r0, sg_.r0], writes=[h1g.r0])
                            if typ == 0:
                                items = [(pk[0:64, 0:NCB], w2b[:, hc, :], h1g[:, hc, 0:NCB], hc == 0, hc == 1) for hc in range(2)]
                                P.op("pe", mmgroup(items), reads=[w2b.r0, h1g.r0], writes=[pk.r0])
                                P.op("act", C("activation", out=sqc[:, 0:NCB], in_=pk[0:64, 0:NCB], func=AF.Square),
                                     reads=[pk.r0], writes=[sqc.r0])
                                P.op("act", C("activation", out=xgc[:, 0:NCB], in_=pk[0:64, 0:NCB], func=AF.Identity, scale=gk[:, 0:1]),
                                     reads=[pk.r0, gk.r0], writes=[xgc.r0])
                                P.op("pe", mmgroup([(pq2[0:64, 0:NCB], MATS[0:64, 1, 0:64], sqc[:, 0:NCB], True, True)]),
                                     reads=[sqc.r0, MATS.r0], writes=[pq2.r0])
                                P.op("pe", mmgroup([(pq3[0:64, 0:NCB], MATS[0:64, 0, 0:64], xgc[:, 0:NCB], True, True)]),
                                     reads=[xgc.r0, MATS.r0], writes=[pq3.r0])
                                P.op("act", C("activation", out=rsc[:, 0:NCB], in_=pq2[0:64, 0:NCB], func=AF.Sqrt, bias=EPSC[0:64, 0:1], scale=1.0),
                                     reads=[pq2.r0, EPSC.r0], writes=[rsc.r0])
                                P.op("dve", C("reciprocal", rsc[:, 0:NCB], rsc[:, 0:NCB]), reads=[rsc.r0], writes=[rsc.r0])
                                P.op("dve", C("tensor_tensor", out=t1c[:, 0:NCB], in0=xgc[:, 0:NCB], in1=cosc[:, 0:NCB], op=ALU.mult),
                                     reads=[xgc.r0, cosc.r0], writes=[t1c.r0])
                                P.op("dve", C("tensor_tensor", out=t2c[:, 0:NCB], in0=pq3[0:64, 0:NCB], in1=sinc[:, 0:NCB], op=ALU.mult),
                                     reads=[pq3.r0, sinc.r0], writes=[t2c.r0])
                                P.op("dve", C("tensor_tensor", out=t1c[:, 0:NCB], in0=t1c[:, 0:NCB], in1=t2c[:, 0:NCB], op=ALU.add),
                                     reads=[t1c.r0, t2c.r0], writes=[t1c.r0])
                                P.op("dve", C("tensor_tensor", out=okc[:, 0:NCB], in0=t1c[:, 0:NCB], in1=rsc[:, 0:NCB], op=ALU.mult),
                                     reads=[t1c.r0, rsc.r0], writes=[okc.r0])
                                P.dma("sp", KCMPT[kvh * 64:(kvh + 1) * 64, 0:NCB], okc[:, 0:NCB], reads=[okc.r0], writes=[KCMPT.r0])
                            else:
                                for ct in range(NCT):
                                    m = min(128, NCB - ct * 128)
                                    items = [(pv[0:m, :], h1g[:, hc, ct * 128:ct * 128 + m], w2b[:, hc, :], hc == 0, hc == 1) for hc in range(2)]
                                    P.op("pe", mmgroup(items), reads=[w2b.r0, h1g.r0], writes=[pv.r0])
                                    P.op("dve", C("tensor_copy", VCMP[0:m, kvh, ct, 0:64], pv[0:m, :]),
                                         reads=[pv.r0], writes=[VCMP.r0])
                        P.barrier()
                P.barrier()

        def phase_A1(l):
            with ExitStack() as cx:
                acc = sbuf(cx, "a_acc", [65, S], F32)
                Qt = [sbuf(cx, f"a_q{i}", [128, S], BF16) for i in range(2)]
                Kt = [sbuf(cx, f"a_k{i}", [128, S], BF16) for i in range(2)]
                for t_ in Qt + Kt:
                    P.op("pool", C("memset", t_[64:128, :], 0.0), writes=[t_.r0])
                Vt = [sbuf(cx, f"a_v{i}", [128, NT, 65], BF16) for i in range(2)]
                ps = [psum(cx, f"a_ps{i}", [128, 2, 128]) for i in range(2)]
                po = [psum(cx, f"a_po{i}", [128, 128]) for i in range(2)]
                pbx = [psum(cx, f"a_pb{i}", [128, 512]) for i in range(2)]
                pt = [sbuf(cx, f"a_pt{i}", [128, 2, 128], BF16) for i in range(4)]
                oa = sbuf(cx, "a_oa", [64, S], BF16)
                rd = sbuf(cx, "a_rd", [128, S], F32)
                P.op("pool", C("memset", rd[:], 0.0), writes=[rd.r0])
                for v in Vt:
                    P.op("pool", C("memset", v[:, :, 64:65], 1.0), writes=[v.r0])
                it = 0
                ld = 0
                for j in range(8):
                    for gi, (window, dil) in enumerate(DIL_PAIRS):
                        head = gi * 8 + j
                        r0 = head * 64
                        L = S // dil
                        nb = L // 128
                        q_, k_, v_ = Qt[ld % 2], Kt[ld % 2], Vt[ld % 2]
                        ld += 1
                        P.dma("sp", q_[0:64, :], QAT[r0:r0 + 64, :], reads=dall("QAT"), writes=[q_.r0])
                        P.dma("sp", k_[0:64, :], KAT[r0:r0 + 64, :], reads=dall("KAT"), writes=[k_.r0])
                        for r in range(dil):
                            srcv = VA[r::dil, r0:r0 + 64].rearrange("(jb kk) d -> kk jb d", kk=128)
                            for j0 in range(0, nb, 8):
                                j1 = min(nb, j0 + 8)
                                P.dma("sp", v_[:, r * nb + j0:r * nb + j1, 0:64], srcv[:, j0:j1, :], reads=dall("VA"), writes=[v_.r0])
                        pend = None

                        def pv_acc(b, p_t, r, Jb, c):
                            items = []
                            if Jb > 0:
                                items.append((po[b][0:65, :], v_[:, r * nb + Jb - 1, :], p_t[:, 0, :], True, False))
                            items.append((po[b][0:65, :], v_[:, r * nb + Jb, :], p_t[:, 1, :], Jb == 0, True))
                            P.op("pe", mmgroup(items), reads=[v_.r0, p_t.r0], writes=[po[b].r0])
                            if gi == 0:
                                P.op("dve", C("tensor_copy", acc[:, c], po[b][0:65, :]),
                                     reads=[po[b].r0], writes=[acc.r0])
                            else:
                                P.op("dve", C("tensor_tensor", out=acc[:, c], in0=acc[:, c], in1=po[b][0:65, :], op=ALU.add),
                                     reads=[po[b].r0, acc.r0], writes=[acc.r0])
                        for r in range(dil):
                            for Jb in range(nb):
                                def cols(J):
                                    s0 = J * 128 * dil + r
                                    return slice(s0, s0 + 127 * dil + 1, dil)
                                b = it % 2
                                p_t = pt[it % 4]
                                it += 1
                                items = []
                                if Jb > 0:
                                    items.append((ps[b][:, 0, :], k_[:, cols(Jb - 1)], q_[:, cols(Jb)], True, True))
                                items.append((ps[b][:, 1, :], k_[:, cols(Jb)], q_[:, cols(Jb)], True, True))
                                P.op("pe", mmgroup(items), reads=[k_.r0, q_.r0], writes=[ps[b].r0])
                                e0 = 0 if Jb > 0 else 1
                                P.op("act", C("activation", out=p_t[:, e0:2, :], in_=ps[b][:, e0:2, :], func=AF.Exp, scale=0.125),
                                     reads=[ps[b].r0], writes=[p_t.r0])
                                P.op("dve", C("tensor_tensor", out=p_t[:, e0:2, :], in0=p_t[:, e0:2, :], in1=MASKS[:, e0:2, :], op=ALU.mult),
                                     reads=[p_t.r0, MASKS.r0], writes=[p_t.r0])
                                if pend is not None:
                                    pv_acc(*pend)
                                pend = (b, p_t, r, Jb, cols(Jb))
                        pv_acc(*pend)
                    P.op("act", C("activation", out=rd[64:65, :], in_=acc[64:65, :], func=AF.Ln), reads=[acc.r0], writes=[rd.r0])
                    P.op("act", C("activation", out=rd[64:65, :], in_=rd[64:65, :], func=AF.Exp, scale=-1.0), reads=[rd.r0], writes=[rd.r0])
                    for ch in range(NST):
                        b = ch % 2
                        csl = slice(ch * 512, (ch + 1) * 512)
                        P.op("pe", mmgroup([(pbx[b][0:64, :], ONESF[:, 0:64], rd[:, csl], True, True)]),
                             reads=[rd.r0, ONESF.r0], writes=[pbx[b].r0])
                        P.op("dve", C("tensor_tensor", out=oa[:, csl], in0=acc[0:64, csl], in1=pbx[b][0:64, :], op=ALU.mult),
                             reads=[pbx[b].r0, acc.r0], writes=[oa.r0])
                    P.dma("sp", OAT[j * 64:(j + 1) * 64, :], oa[:], reads=[oa.r0], writes=dall("OAT"))
                P.barrier()

        def phase_A2(l, src, src_name, KCMPT, VCMP):
            with ExitStack() as cx:
                KS = sbuf(cx, "b_ks", [128, S], BF16)
                KW = sbuf(cx, "b_kw", [128, S], BF16)
                VS = sbuf(cx, "b_vs", [128, NT, 2, 65], BF16)
                VW = sbuf(cx, "b_vw", [128, NT, 2, 65], BF16)
                EALL = sbuf(cx, "b_eall", [128, S], BF16)
                MM = sbuf(cx, "b_mm", [128, NCT, 128], BF16)
                CM = sbuf(cx, "b_cm", [128, 17, 128], BF16)
                TK = sbuf(cx, "b_tk", [128, 256], F32)
                TA = sbuf(cx, "b_ta", [128, 256], F32)
                WPA = sbuf(cx, "b_wpa", [128, 4, D], BF16)
                WPB = sbuf(cx, "b_wpb", [128, 8, D], BF16)
                P.op("pool", C("memset", WPB[64:128, :, :], 0.0), writes=[WPB.r0])
                WO = sbuf(cx, "b_wo", [128, 8, D], BF16)
                with ExitStack() as cx2:
                    load_cast(cx2, lambda i: EALL[:, i * 1024:(i + 1) * 1024], lambda i: c_eall[:, i * 1024:(i + 1) * 1024],
                              S // 1024, [128, 1024], EALL.r0, "ea")
                    load_cast(cx2, lambda i: WPA[:, i, :], lambda i: w_proj_a[l, i * 128:(i + 1) * 128, :], 4, [128, D], WPA.r0, "wpa")
                    load_cast(cx2, lambda i: WPB[0:64, i, :], lambda i: w_proj_b[l, i * 64:(i + 1) * 64, :], 8, [64, D], WPB.r0, "wpb")
                    load_cast(cx2, lambda i: WO[:, i, :], lambda i: w_out[l, i * 128:(i + 1) * 128, :], 8, [128, D], WO.r0, "wo")
                    load_cast(cx2, lambda i: MM[:], lambda i: c_mmat, 1, [128, NCT, 128], MM.r0, "mm")
                    load_cast(cx2, lambda i: CM[:], lambda i: c_cm, 1, [128, 17, 128], CM.r0, "cm")
                    P.barrier()
                P.dma("sp", TK[:], c_tk, writes=[TK.r0])
                P.dma("sp", TA[:], c_ta, writes=[TA.r0])
                P.dma("sp", KS[:], KSELT, reads=dall("KSELT"), writes=[KS.r0])
                P.dma("sp", KW[:], KWINT, reads=dall("KWINT"), writes=[KW.r0])
                P.op("pool", C("memset", VS[:, :, :, 64:65], 1.0), writes=[VS.r0])
                P.op("pool", C("memset", VW[:, :, :, 64:65], 1.0), writes=[VW.r0])
                for h in range(2):
                    for k0 in range(0, NT, 8):
                        P.dma("sp", VS[:, k0:k0 + 8, h, 0:64], VSEL[k0 * 128:(k0 + 8) * 128, h * 64:(h + 1) * 64].rearrange("(kt kk) d -> kk kt d", kk=128),
                              reads=dall("VSEL"), writes=[VS.r0])
                        P.dma("sp", VW[:, k0:k0 + 8, h, 0:64], VWIN[k0 * 128:(k0 + 8) * 128, h * 64:(h + 1) * 64].rearrange("(kt kk) d -> kk kt d", kk=128),
                              reads=dall("VWIN"), writes=[VW.r0])
                ST = [psum(cx, f"b_st{i}", [128, 2, 512]) for i in range(2)]
                OT = [psum(cx, f"b_ot{i}", [128, 512]) for i in range(2)]
                UB = psum(cx, "b_ub", [128, 4, 128])
                MISC = psum(cx, "b_misc", [128, 512])
                QB = [sbuf(cx, f"b_qb{i}", [128, 2, 512], BF16) for i in range(2)]
                for q_ in QB:
                    P.op("pool", C("memset", q_[:], 0.0), writes=[q_.r0])
                GR = [sbuf(cx, f"b_gr{i}", [65, 24, 128], F32) for i in range(1)]
                MG = [sbuf(cx, f"b_mg{i}", [128, 16, 128], BF16) for i in range(2)]
                OA = [sbuf(cx, f"b_oa{i}", [128, 4, 128], BF16) for i in range(2)]
                XT = [sbuf(cx, f"b_xt{i}", [128, D], F32) for i in range(2)]
                PC = sbuf(cx, "b_pc", [128, 4, 512], BF16)
                PS_ = [sbuf(cx, f"b_ps{i}", [128, 2, 512], BF16) for i in range(2)]
                rs4 = sbuf(cx, "b_rs4", [128, 4], F32)
                psl = sbuf(cx, "b_psl", [128, 128], F32)
                sc2 = sbuf(cx, "b_sc2", [128, 128], F32)
                m8a = sbuf(cx, "b_m8a", [128, 8], F32)
                m8b = sbuf(cx, "b_m8b", [128, 8], F32)
                selb = sbuf(cx, "b_selb", [128, 128], BF16)
                selT = sbuf(cx, "b_selT", [128, 4, 128], BF16)
                UBR = [sbuf(cx, f"b_ubr{i}", [65, 512], F32) for i in range(3)]
                wrow = sbuf(cx, "b_wrow", [128, 512], F32)
                P.op("pool", C("memset", wrow[:], 0.0), writes=[wrow.r0])
                obf = sbuf(cx, "b_obf", [64, 512], F32)
                BCB = [sbuf(cx, f"b_bc{i}", [64, 512], F32) for i in range(2)]
                otmp = sbuf(cx, "b_otmp", [64, 512], F32)
                OBT = [sbuf(cx, f"b_obt{i}", [128, 8, 128], BF16) for i in range(2)]
                for o_ in OBT:
                    P.op("pool", C("memset", o_[:], 0.0), writes=[o_.r0])
                m1 = sbuf(cx, "b_m1", [128, 8, 128], F32)
                m2 = sbuf(cx, "b_m2", [128, 8, 128], F32)
                mx = sbuf(cx, "b_mx", [128, 8, 128], BF16)
                sti = 0
                pend_proj = None

                def proj(oa_, mg, xt, OBt, tsl, stq):
                    nonlocal sti
                    ya = ST[sti % 2]
                    sti += 1
                    yb = ST[sti % 2]
                    sti += 1
                    yav = ya[:].rearrange("p a (b q) -> p (a b) q", q=128)
                    ybv = yb[:].rearrange("p a (b q) -> p (a b) q", q=128)
                    items = []
                    for cc in range(8):
                        for kc in range(4):
                            items.append((yav[:, cc, :], WPA[:, kc, cc * 128:(cc + 1) * 128], oa_[:, kc, :], kc == 0, kc == 3))
                    P.op("pe", mmgroup(items), reads=[WPA.r0, oa_.r0], writes=[ya.r0])
                    items = []
                    for cc in range(8):
                        for hd in range(8):
                            items.append((ybv[:, cc, :], WPB[:, hd, cc * 128:(cc + 1) * 128], OBt[:, hd, :], hd == 0, hd == 7))
                    P.op("pe", mmgroup(items), reads=[WPB.r0, OBt.r0], writes=[yb.r0])
                    P.op("dve", C("tensor_tensor", out=m1[:], in0=yav, in1=mg[:, 0:8, :], op=ALU.mult), reads=[ya.r0, mg.r0], writes=[m1.r0])
                    P.op("dve", C("tensor_tensor", out=m2[:], in0=ybv, in1=mg[:, 8:16, :], op=ALU.mult), reads=[yb.r0, mg.r0], writes=[m2.r0])
                    P.op("pool", C("tensor_tensor", out=mx[:], in0=m1[:], in1=m2[:], op=ALU.add), reads=[m1.r0, m2.r0], writes=[mx.r0])

                def proj2(oa_, mg, xt, OBt, tsl, stq):
                    nonlocal sti
                    z = ST[sti % 2]
                    sti += 1
                    items = []
                    for hf in range(2):
                        for kc in range(8):
                            items.append((z[:, hf, :], mx[:, kc, :], WO[:, kc, hf * 512:(hf + 1) * 512], kc == 0, kc == 7))
                    P.op("pe", mmgroup(items), reads=[mx.r0, WO.r0], writes=[z.r0])
                    P.op("dve", C("tensor_tensor", out=xt[:], in0=xt[:], in1=z[:].rearrange("p a b -> p (a b)"), op=ALU.add),
                         reads=[xt.r0, z.r0], writes=[xt.r0])
                    P.dma("pool", XR1[tsl, :], xt[:], reads=[xt.r0], writes=[dres["XR1"][stq]])
                psi = 0
                oti = 0
                ubi = 0
                import os as _os
                for i in range(int(_os.environ.get('A2_I0', 0)), min(NT, int(_os.environ.get('A2_I1', NT)))):
                    t0 = i * 128
                    tsl = slice(t0, t0 + 128)
                    stq = i // 4
                    qb, gr, mg, oa_, xt = QB[i % 2], GR[0], MG[i % 2], OA[i % 2], XT[i % 2]
                    OBt = OBT[i % 2]
                    for h in range(2):
                        P.dma("sp", qb[h * 64:(h + 1) * 64, h, :].rearrange("p (g q) -> p g q", g=4), QBT[h * 256:(h + 1) * 256, tsl].rearrange("(g d) t -> d g t", d=64),
                              reads=[dres["QBT"][stq]], writes=[qb.r0])
                    P.dma("sp", gr[64:65, :, :], GT[:, tsl].rearrange("(o r) t -> o r t", o=1), reads=[dres["GT"][stq]], writes=[gr.r0])
                    P.dma("sp", mg[:], MGT[:, tsl].rearrange("(c p) t -> p c t", p=128), reads=[dres["MGT"][stq]], writes=[mg.r0])
                    P.dma("sp", oa_[:], OAT[:, tsl].rearrange("(c p) t -> p c t", p=128), reads=[dres["OAT"][stq]], writes=[oa_.r0])
                    P.dma("sp", xt[:], src[tsl, :], reads=([dres[src_name][stq]] if src_name else []), writes=[xt.r0])
                    for h in range(2):
                        hs = slice(h * 64, (h + 1) * 64)
                        qh = qb[:, h, :]

                        def score_tiles(kts, ksrc, extra_bias, dst, dsti):
                            nonlocal sti
                            b = sti % 2
                            sti += 1
                            items = []
                            for e_, kt in enumerate(kts):
                                items.append((ST[b][:, e_, :], ksrc[:, kt * 128:(kt + 1) * 128], qh, True, not extra_bias))
                                if extra_bias:
                                    items.append((ST[b][:, e_, :], EALL[0:NJ, kt * 128:(kt + 1) * 128],
                                                  selT[0:NJ, :, :].rearrange("p g q -> p (g q)"), False, True))
                            rd_ = [qb.r0, ksrc_res[id(ksrc)]] + ([EALL.r0, selT.r0] if extra_bias else [])
                            P.op("pe", mmgroup(items), reads=rd_, writes=[ST[b].r0])
                            n = len(kts)
                            P.op("act", C("activation", out=dst[:, dsti:dsti + n, :], in_=ST[b][:, 0:n, :], func=AF.Exp, scale=0.125),
                                 reads=[ST[b].r0], writes=[dst.r0])

                        ksrc_res = {id(KCMPT): KCMPT.r0, id(KS): KS.r0, id(KW): KW.r0}

                        def maskmul(dst, di, mask_ap, mres, eng="pool"):
                            P.op(eng, C("tensor_tensor",
                                out=dst[:, di, :].rearrange("p (g q) -> p g q", g=4),
                                in0=dst[:, di, :].rearrange("p (g q) -> p g q", g=4),
                                in1=mask_ap.unsqueeze(1).to_broadcast([128, 4, 128]), op=ALU.mult),
                                 reads=[dst.r0, mres], writes=[dst.r0])

                        nct = i // 16 + 1
                        for c0 in range(0, nct, 2):
                            kts = list(range(c0, min(c0 + 2, nct)))
                            score_tiles(kts, KCMPT, False, PC, c0)
                        if h == 1 and pend_proj is not None:
                            proj(*pend_proj)
                        maskmul(PC, nct - 1, CM[:, i % 16, :], CM.r0)
                        if i % 16 == 0 and i > 0:
                            maskmul(PC, nct - 2, CM[:, 16, :], CM.r0)
                        otc = OT[oti % 2]
                        oti += 1
                        items = [(otc[0:65, :], VCMP[:, h, ct, :], PC[:, ct, :], ct == 0, ct == nct - 1) for ct in range(nct)]
                        P.op("pe", mmgroup(items), reads=[VCMP.r0, PC.r0], writes=[otc.r0])
                        items = []
                        for g in range(4):
                            for ct in range(nct):
                                items.append((UB[:, g, 0:NJ], PC[:, ct, g * 128:(g + 1) * 128], MM[:, ct, 0:NJ], ct == 0, ct == nct - 1))
                        P.op("pe", mmgroup(items), reads=[PC.r0, MM.r0], writes=[UB.r0])
                        P.op("dve", C("tensor_reduce", out=rs4[:], in_=UB[:, :, 0:NJ], axis=AX.X, op=ALU.add), reads=[UB.r0], writes=[rs4.r0])
                        P.op("dve", C("tensor_scalar", rs4[:], rs4[:], 0.5, None, op0=ALU.mult), reads=[rs4.r0], writes=[rs4.r0])
                        P.op("dve", C("tensor_scalar", rs4[:], rs4[:], TINY, None, op0=ALU.max), reads=[rs4.r0], writes=[rs4.r0])
                        P.op("dve", C("reciprocal", rs4[:], rs4[:]), reads=[rs4.r0], writes=[rs4.r0])
                        P.op("dve", C("tensor_scalar", psl[:, 0:NJ], UB[:, 0, 0:NJ], rs4[:, 0:1], None, op0=ALU.mult), reads=[UB.r0, rs4.r0], writes=[psl.r0])
                        for g in range(1, 4):
                            P.op("dve", C("scalar_tensor_tensor", out=psl[:, 0:NJ], in0=UB[:, g, 0:NJ], scalar=rs4[:, g:g + 1], in1=psl[:, 0:NJ],
                                                                          op0=ALU.mult, op1=ALU.add), reads=[UB.r0, rs4.r0, psl.r0], writes=[psl.r0])
                        o_tab = 127 - 2 * i
                        P.op("dve", C("tensor_tensor", out=psl[:, 0:NJ], in0=psl[:, 0:NJ], in1=TK[:, o_tab:o_tab + NJ], op=ALU.mult),
                             reads=[psl.r0, TK.r0], writes=[psl.r0])
                        P.op("dve", C("tensor_tensor", out=psl[:, 0:NJ], in0=psl[:, 0:NJ], in1=TA[:, o_tab:o_tab + NJ], op=ALU.add),
                             reads=[psl.r0, TA.r0], writes=[psl.r0])
                        P.op("dve", C("memset", psl[:, 0:1], 3e9), writes=[psl.r0])
                        P.op("dve", C("max", out=m8a[:], in_=psl[:, 0:NJ]), reads=[psl.r0], writes=[m8a.r0])
                        P.op("dve", C("match_replace", out=sc2[:, 0:NJ], in_to_replace=m8a[:], in_values=psl[:, 0:NJ], imm_value=-1e9),
                             reads=[psl.r0, m8a.r0], writes=[sc2.r0])
                        P.op("dve", C("max", out=m8b[:], in_=sc2[:, 0:NJ]), reads=[sc2.r0], writes=[m8b.r0])
                        P.op("dve", C("tensor_reduce", out=m8a[:, 0:1], in_=m8b[:], axis=AX.X, op=ALU.min), reads=[m8b.r0], writes=[m8a.r0])
                        P.op("dve", C("tensor_scalar", sc2[:, 0:NJ], psl[:, 0:NJ], m8a[:, 0:1], None, op0=ALU.is_lt),
                             reads=[psl.r0, m8a.r0], writes=[sc2.r0])
                        P.op("dve", C("tensor_scalar", selb[:, 0:NJ], sc2[:, 0:NJ], NEGB, None, op0=ALU.mult),
                             reads=[sc2.r0], writes=[selb.r0])

                        def dense_branch(kts_all, ksrc, vsrc, bias, masks, meng="dve"):
                            nonlocal psi, oti
                            ot = OT[oti % 2]
                            oti += 1
                            nk = len(kts_all)
                            pend = None

                            def pv(kts, pbuf):
                                items = [(ot[0:65, :], vsrc[:, kt, h, :], pbuf[:, e_, :], kt == kts_all[0], kt == kts_all[-1])
                                         for e_, kt in enumerate(kts)]
                                P.op("pe", mmgroup(items), reads=[vsrc.r0, pbuf.r0], writes=[ot.r0])
                            for c0 in range(0, nk, 2):
                                kts = kts_all[c0:c0 + 2]
                                pbuf = PS_[psi % 2]
                                psi += 1
                                score_tiles(kts, ksrc, bias, pbuf, 0)
                                for e_, kt in enumerate(kts):
                                    if kt in masks:
                                        maskmul(pbuf, e_, MASKS[:, masks[kt], :], MASKS.r0, eng=meng)
                                if pend is not None:
                                    pv(*pend)
                                pend = (kts, pbuf)
                            pv(*pend)
                            return ot

                        def combine(ot, br, first):
                            nonlocal ubi
                            ub = UBR[ubi % 3]
                            ubi += 1
                            P.op("dve", C("tensor_copy", ub[:], ot[0:65, :]), reads=[ot.r0], writes=[ub.r0])
                            P.op("dve", C("tensor_scalar", wrow[64:65, :], ub[64:65, :], TINY, None, op0=ALU.max), reads=[ub.r0], writes=[wrow.r0])
                            P.op("act", C("activation", out=wrow[64:65, :], in_=wrow[64:65, :], func=AF.Ln), reads=[wrow.r0], writes=[wrow.r0])
                            P.op("act", C("activation", out=wrow[64:65, :], in_=wrow[64:65, :], func=AF.Exp, scale=-1.0), reads=[wrow.r0], writes=[wrow.r0])
                            P.op("dve", C("tensor_tensor", out=wrow[64:65, :].rearrange("p (g q) -> p g q", g=4),
                                                                       in0=wrow[64:65, :].rearrange("p (g q) -> p g q", g=4),
                                                                       in1=gr[64:65, br * 8 + h * 4: br * 8 + h * 4 + 4, :], op=ALU.mult),
                                 reads=[wrow.r0, gr.r0], writes=[wrow.r0])
                            bc = BCB[ubi % 2]
                            ws = ubi % 6
                            P.dma("sp", WSCR[ws:ws + 1, :], wrow[64:65, :], reads=[wrow.r0], writes=[wscr_res[ws]])
                            P.dma("sp", bc[:], WSCR[ws:ws + 1, :].to_broadcast([64, 512]), reads=[wscr_res[ws]], writes=[bc.r0])
                            if first:
                                P.op("dve", C("tensor_tensor", out=obf[:], in0=ub[0:64, :], in1=bc[:], op=ALU.mult),
                                     reads=[ub.r0, bc.r0], writes=[obf.r0])
                            else:
                                P.op("dve", C("tensor_tensor", out=otmp[:], in0=ub[0:64, :], in1=bc[:], op=ALU.mult),
                                     reads=[ub.r0, bc.r0], writes=[otmp.r0])
                                P.op("pool", C("tensor_tensor", out=obf[:], in0=obf[:], in1=otmp[:], op=ALU.add),
                                     reads=[obf.r0, otmp.r0], writes=[obf.r0])

                        wm = {i: 1}
                        if i - 4 >= 0:
                            wm[i - 4] = 2
                        otw = dense_branch(list(range(max(0, i - 4), i + 1)), KW, VW, False, wm, meng="pool")
                        P.op("pe", mmgroup([(MISC[0:NJ, 0:128], selb[:, 0:NJ], IDN, True, True)]), reads=[selb.r0, MATS.r0], writes=[MISC.r0])
                        P.op("dve", C("tensor_copy", selT[0:NJ, :, :], MISC[0:NJ, 0:128].unsqueeze(1).to_broadcast([NJ, 4, 128])),
                             reads=[MISC.r0], writes=[selT.r0])
                        combine(otc, 0, True)
                        ots = dense_branch(list(range(0, i + 1)), KS, VS, True, {i: 1})
                        combine(otw, 2, False)
                        combine(ots, 1, False)
                        P.op("pool", C("tensor_copy", OBt[0:64, h * 4:(h + 1) * 4, :], obf[:].rearrange("p (g q) -> p g q", g=4)),
                             reads=[obf.r0], writes=[OBt.r0])
                        if pend_proj is not None and h == 1:
                            proj2(*pend_proj)
                            pend_proj = None
                    pend_proj = (oa_, mg, xt, OBt, tsl, stq)
                if pend_proj is not None:
                    proj(*pend_proj)
                    proj2(*pend_proj)
                P.barrier()

        def phase_F(l, dst, dst_name):
            NF = DFF // 128
            with ExitStack() as cx:
                WU = sbuf(cx, "f_wu", [128, 8, 2 * DFF], BF16)
                WD = sbuf(cx, "f_wd", [128, NF, D], BF16)
                with ExitStack() as cx2:
                    H4 = 2 * DFF // 4
                    load_cast(cx2, lambda i: WU[:, i // 4, (i % 4) * H4:(i % 4 + 1) * H4],
                              lambda i: w_up[l, (i // 4) * 128:(i // 4 + 1) * 128, (i % 4) * H4:(i % 4 + 1) * H4], 32, [128, H4], WU.r0, "wu")
                    load_cast(cx2, lambda i: WD[:, i, :], lambda i: w_down[l, i * 128:(i + 1) * 128, :], NF, [128, D], WD.r0, "wd")
                    P.barrier()
                gT = sbuf(cx, "f_gT", [128, 8], F32)
                P.dma("sp", gT[:], norm_ffn_g[l].rearrange("(kc p) -> p kc", p=128), writes=[gT.r0], slow=True)
                CW = sbuf(cx, "f_cw", [128, 3, NF], F32)
                CB = sbuf(cx, "f_cb", [128, NF], F32)
                for k in range(3):
                    P.dma("sp", CW[:, k, :], conv_w[l, k].rearrange("(fc p) -> p fc", p=128), writes=[CW.r0], slow=True)
                P.dma("sp", CB[:], conv_b[l].rearrange("(fc p) -> p fc", p=128), writes=[CB.r0], slow=True)
                HAL = sbuf(cx, "f_hal", [128, NF, 2], F32)
                P.op("pool", C("memset", HAL[:], 0.0), writes=[HAL.r0])
                hT = sbuf(cx, "f_hT", [128, 8, 512], BF16)
                xts = [sbuf(cx, f"f_x{i}", [128, D], F32) for i in range(4)]
                junk = sbuf(cx, "f_nj", [128, D], BF16)
                ss = sbuf(cx, "f_ss", [128, 1], F32)
                rs = sbuf(cx, "f_rs", [128, 1], F32)
                xs = sbuf(cx, "f_xs", [128, D], BF16)
                tp = psum(cx, "f_tp", [128, 8, 128])
                pg = [psum(cx, f"f_pg{i}", [128, 512]) for i in range(2)]
                pu = [psum(cx, f"f_pu{i}", [128, 512]) for i in range(2)]
                pz = psum(cx, "f_pz", [128, 2, 512])
                Gt = [sbuf(cx, f"f_gt{i}", [128, 514], F32) for i in range(2)]
                cv = [sbuf(cx, f"f_cv{i}", [128, 512], F32) for i in range(2)]
                sl = [sbuf(cx, f"f_sl{i}", [128, 512], F32) for i in range(2)]
                actT = sbuf(cx, "f_act", [128, NF, 512], BF16, nres=NF)
                xo = [sbuf(cx, f"f_xo{i}", [128, D], F32) for i in range(1)]
                for st in range(NST):
                    for tt in range(4):
                        norm_to_hT((xts[tt], junk, ss, rs, xs, tp), gT, XR1, st * 512 + tt * 128, hT, hT.r0, tt)
                    for fc in range(NF):
                        b = fc % 2
                        items = [(pg[b][:], WU[:, kc, fc * 128:(fc + 1) * 128], hT[:, kc, :], kc == 0, kc == 7) for kc in range(8)]
                        P.op("pe", mmgroup(items), reads=[WU.r0, hT.r0], writes=[pg[b].r0])
                        items = [(pu[b][:], WU[:, kc, DFF + fc * 128:DFF + (fc + 1) * 128], hT[:, kc, :], kc == 0, kc == 7) for kc in range(8)]
                        P.op("pe", mmgroup(items), reads=[WU.r0, hT.r0], writes=[pu[b].r0])
                        g_ = Gt[b]
                        P.op("pool", C("tensor_copy", g_[:, 0:2], HAL[:, fc, :]), reads=[HAL.r0], writes=[g_.r0])
                        P.op("act", C("activation", out=g_[:, 2:514], in_=pg[b][:], func=AF.Identity), reads=[pg[b].r0], writes=[g_.r0])
                        P.op("pool", C("tensor_copy", HAL[:, fc, :], g_[:, 512:514]), reads=[g_.r0], writes=[HAL.r0])
                        c_ = cv[b]
                        P.op("dve", C("tensor_scalar", c_[:], g_[:, 2:514], CW[:, 2, fc:fc + 1], CB[:, fc:fc + 1], op0=ALU.mult, op1=ALU.add),
                             reads=[g_.r0, CW.r0, CB.r0], writes=[c_.r0])
                        P.op("dve", C("scalar_tensor_tensor", out=c_[:], in0=g_[:, 1:513], scalar=CW[:, 1, fc:fc + 1], in1=c_[:], op0=ALU.mult, op1=ALU.add),
                             reads=[g_.r0, CW.r0, c_.r0], writes=[c_.r0])
                        P.op("dve", C("scalar_tensor_tensor", out=c_[:], in0=g_[:, 0:512], scalar=CW[:, 0, fc:fc + 1], in1=c_[:], op0=ALU.mult, op1=ALU.add),
                             reads=[g_.r0, CW.r0, c_.r0], writes=[c_.r0])
                        s_ = sl[b]
                        P.op("act", C("activation", out=s_[:], in_=c_[:], func=AF.Silu), reads=[c_.r0], writes=[s_.r0])
                        P.op("dve", C("tensor_tensor", out=actT[:, fc, :], in0=s_[:], in1=pu[b][:], op=ALU.mult),
                             reads=[s_.r0, pu[b].r0], writes=[actT.res[fc]])
                    for tt in range(4):
                        t0 = st * 512 + tt * 128
                        items = []
                        for hf in range(2):
                            for fc in range(NF):
                                items.append((pz[:, hf, :], actT[:, fc, tt * 128:(tt + 1) * 128], WD[:, fc, hf * 512:(hf + 1) * 512], fc == 0, fc == NF - 1))
                        P.op("pe", mmgroup(items), reads=[WD.r0] + actT.res, writes=[pz.r0])
                        o = xo[0]
                        P.op("dve", C("tensor_tensor", out=o[:], in0=xts[tt][:], in1=pz[:].rearrange("p a b -> p (a b)"), op=ALU.add),
                             reads=[xts[tt].r0, pz.r0], writes=[o.r0])
                        P.dma("pool", dst[t0:t0 + 128, :], o[:], reads=[o.r0], writes=[dres[dst_name][st]])
                P.barrier()

        src, src_name = x_in, None
        for l in range(DEPTH):
            phase_P(l, src, src_name)
            if stop_after == "P":
                break
            with ExitStack() as lc:
                KCMPT = sbuf(lc, "KCMPT", [128, NCT * 128], BF16)
                VCMP = sbuf(lc, "VCMP", [128, 2, NCT, 65], BF16)
                phase_C(l, KCMPT, VCMP)
                if stop_after != "C":
                    phase_A1(l)
                    if stop_after != "A1":
                        phase_A2(l, src, src_name, KCMPT, VCMP)
            if stop_after in ("C", "A1", "A2"):
                break
            last = (l == DEPTH - 1)
            phase_F(l, y_out if last else XR2, "Y" if last else "XR2")
            src, src_name = XR2, "XR2"

        final_ev = P.all_events()
        with nc.Block() as block:
            @block.tensor
            def _(e):
                P.replay("pe", e)

            @block.scalar
            def _(e):
                P.replay("act", e)

            @block.vector
            def _(e):
                P.replay("dve", e)

            @block.gpsimd
            def _(e):
                P.replay("pool", e)

            @block.sync
            def _(e):
                P.replay("sp", e)
                for sk, v in final_ev:
                    e.wait_ge(P.sem[sk], v)
        nc._n_rec = P.n_inst
    return nc


_CACHE = {}


def _get_prog(S, depth):
    key = (S, depth)
    if key not in _CACHE:
        _CACHE[key] = build(S, depth)
    return _CACHE[key]


WNAMES = ("norm_mix_g", "w_in", "a_q_g", "a_k_g", "b_q_g", "b_k_g", "cmp_pos", "cmp_w1", "cmp_w2",
          "w_proj_a", "w_proj_b", "w_out", "norm_ffn_g", "w_up", "conv_w", "conv_b", "w_down")


def kernel(**inputs):
    x = np.ascontiguousarray(np.asarray(inputs["x"], dtype=np.float32))
    B, S, _ = x.shape
    depth = inputs["w_in"].shape[0]
    consts = host_consts(S)
    ws = {k: np.ascontiguousarray(np.asarray(inputs[k], dtype=np.float32)) for k in WNAMES}
    n = 8
    if FUSED:
        nc = _get_prog(S, depth)
        in_maps = []
        for c in range(n):
            m = {"x": x[c % B]}
            m.update(ws)
            m.update(consts)
            in_maps.append(m)
        res = run_bass_kernel_spmd(nc, in_maps, core_ids=list(range(n)))
        return np.stack([res.results[b]["y"] for b in range(B)], axis=0).astype(np.float32)
    nc = _get_prog(S, 1)
    cur = [x[c % B] for c in range(n)]
    for l in range(depth):
        in_maps = []
        for c in range(n):
            m = {"x": np.ascontiguousarray(cur[c])}
            m.update({k: np.ascontiguousarray(v[l:l + 1]) for k, v in ws.items()})
            m.update(consts)
            in_maps.append(m)
        res = run_bass_kernel_spmd(nc, in_maps, core_ids=list(range(n)))
        cur = [res.results[c]["y"] for c in range(n)]
    return np.stack([cur[b] for b in range(B)], axis=0).astype(np.float32)
```

```python
import numpy as np
from contextlib import ExitStack
import concourse.bass as bass
import concourse.mybir as mybir
from concourse.bass_utils import run_bass_kernel_spmd

F32 = mybir.dt.float32
BF16 = mybir.dt.bfloat16
AF = mybir.ActivationFunctionType
ALU = mybir.AluOpType
AX = mybir.AxisListType

D = 1024
NIN = 7960
DFF = 2816
EPS = 1e-6
TINY = 1e-30
NEGB = -30000.0
DIL_PAIRS = ((128, 1), (512, 4), (2048, 16))
C_AQ, C_AK, C_AV, C_BQ, C_BKV, C_GATE, C_MERGE = 0, 1536, 3072, 4608, 5120, 5888, 5912
FUSED = True


class Res:
    __slots__ = ("w", "r")

    def __init__(self):
        self.w = None
        self.r = {}


class Buf:
    def __init__(self, t, nres=1):
        self.t = t
        self.res = [Res() for _ in range(nres)]

    def __getitem__(self, k):
        return self.t[k]

    @property
    def r0(self):
        return self.res[0]


COMPUTE = ("pe", "act", "dve", "pool")


class Prog:
    ND = 16

    def __init__(self, nc, ctx):
        self.nc = nc
        self.ops = {e: [] for e in ("pe", "act", "dve", "pool", "sp")}
        self.cnt = {e: 0 for e in COMPUTE}
        self.sem = {}
        for e in COMPUTE:
            self.sem[e] = ctx.enter_context(nc.semaphore("s_" + e))
        self.dma_val = {}
        for q in ("sp", "pool"):
            for k in range(self.ND):
                key = (q, k)
                self.sem[key] = ctx.enter_context(nc.semaphore(f"d_{q}_{k}"))
                self.dma_val[key] = 0
        self.rr = {"sp": 0, "pool": 0}
        self.seen = {e: {} for e in self.ops}
        self.n_inst = 0

    def _emit(self, eng, fn, deps, semkey, inc):
        best = {}
        for (sk, v) in deps:
            if best.get(sk, 0) < v:
                best[sk] = v
        waits = []
        for sk, v in best.items():
            if sk == eng and eng == "pe":
                continue
            if self.seen[eng].get(sk, 0) >= v:
                continue
            self.seen[eng][sk] = v
            waits.append((sk, v))
        self.ops[eng].append((waits, fn, semkey, inc))
        self.n_inst += 1 + len(waits)

    def op(self, eng, fn, reads=(), writes=(), dma=False):
        deps = []
        for r in reads:
            if r.w is not None:
                deps.append(r.w)
        for w in writes:
            if w.w is not None:
                deps.append(w.w)
            deps.extend(w.r.items())
        if dma:
            k = self.rr[eng]
            self.rr[eng] = (k + 1) % self.ND
            semkey = (eng, k)
            prev = self.dma_val[semkey]
            if prev > 0:
                deps.append((semkey, prev))
            val = prev + 16
            self.dma_val[semkey] = val
            inc = 16
        else:
            semkey = eng
            self.cnt[eng] += 1
            val = self.cnt[eng]
            inc = 1
        self._emit(eng, fn, deps, semkey, inc)
        for r in reads:
            if r.r.get(semkey, 0) < val:
                r.r[semkey] = val
        for w in writes:
            w.w = (semkey, val)
            w.r = {}
        return (semkey, val)

    def dma(self, q, out, in_, reads=(), writes=(), slow=False):
        if slow:
            fn = lambda e, out=out, in_=in_: e.dma_start(out=out, in_=in_, allow_slow_non_contiguous=True)
        else:
            fn = lambda e, out=out, in_=in_: e.dma_start(out=out, in_=in_)
        return self.op(q, fn, reads, writes, dma=True)

    def all_events(self):
        ev = [(e, self.cnt[e]) for e in COMPUTE if self.cnt[e] > 0]
        ev += [(k, v) for k, v in self.dma_val.items() if v > 0]
        return ev

    def barrier(self):
        ev = self.all_events()
        for eng in self.ops:
            waits = []
            for sk, v in ev:
                if sk == eng:
                    continue
                if self.seen[eng].get(sk, 0) >= v:
                    continue
                self.seen[eng][sk] = v
                waits.append((sk, v))
            if waits:
                self.ops[eng].append((waits, None, None, 0))
                self.n_inst += len(waits)

    def replay(self, eng, e):
        for (waits, fn, semkey, inc) in self.ops[eng]:
            for sk, v in waits:
                e.wait_ge(self.sem[sk], v)
            if fn is not None:
                ins = fn(e)
                ins.then_inc(self.sem[semkey], inc)


def C(name, *a, **k):
    return lambda e: getattr(e, name)(*a, **k)


def mmgroup(items):
    def fn(e):
        ins = None
        for (out, lhsT, rhs, st, sp) in items:
            ins = e.matmul(out, lhsT, rhs, start=st, stop=sp)
        return ins
    return fn


def host_consts(S):
    NT = S // 128
    NJ = S // 64
    NCB = S // 16 - 1
    NCT = (NCB + 127) // 128
    half = 32
    inv_freq = (np.float32(10000.0) ** (-(np.arange(half, dtype=np.float32)) / np.float32(half))).astype(np.float32)
    fidx = (np.arange(128) % 64) % 32

    def tabs(pos):
        ang = pos.astype(np.float32)[None, :] * inv_freq[fidx][:, None]
        return np.cos(ang).astype(np.float32), np.sin(ang).astype(np.float32)

    cos, sin = tabs(np.arange(S))
    posc = np.zeros(NCT * 128, dtype=np.float32)
    posc[:NCB] = np.arange(NCB) * 16 + 31
    cosc, sinc = tabs(posc)
    rot = np.zeros((128, 128), np.float32)
    for m in range(128):
        hb, d = (m // 64) * 64, m % 64
        if d < 32:
            rot[hb + d + 32, m] = -1.0
        else:
            rot[hb + d - 32, m] = 1.0
    bones = np.zeros((128, 128), np.float32)
    bones[:64, :64] = 1.0 / 64
    bones[64:, 64:] = 1.0 / 64
    ident = np.eye(128, dtype=np.float32)
    kk = np.arange(128)[:, None]
    qq = np.arange(128)[None, :]
    masks = np.stack([(kk >= qq), (kk <= qq), (kk > qq)], axis=1).astype(np.float32)
    mats = np.stack([rot, bones, ident], axis=1)
    eall = (np.arange(128)[:, None] == (np.arange(S)[None, :] // 64)).astype(np.float32)
    mm = np.zeros((NCT * 128, 128), np.float32)
    for c in range(NCB):
        for n in (c, c + 1):
            if n // 4 < 128:
                mm[c, n // 4] += 1.0
    mmat = mm.reshape(NCT, 128, 128).transpose(1, 0, 2).copy()
    cm = np.zeros((128, 17, 128), np.float32)
    cl = np.arange(128)[:, None]
    for v in range(16):
        cp = cl - 8 * v
        cm[:, v, :] = (16 * cp + 31 <= qq)
    cp = cl - 128
    cm[:, 16, :] = (16 * cp + 31 <= qq)
    tk = np.zeros((128, 256), np.float32)
    ta = np.zeros((128, 256), np.float32)
    for p in range(128):
        hi = 1 if p >= 64 else 0
        for m in range(256):
            r = m - 127
            if r > hi:
                ta[p, m] = -float(r - hi)
            elif r == hi:
                ta[p, m] = 2e9
            elif r == hi - 1:
                ta[p, m] = 1e9
            else:
                tk[p, m] = 1.0
    return dict(c_cos=cos, c_sin=sin, c_cosc=np.ascontiguousarray(cosc[:64]), c_sinc=np.ascontiguousarray(sinc[:64]),
                c_mats=mats, c_masks=masks, c_eall=eall, c_mmat=mmat, c_cm=cm, c_tk=tk, c_ta=ta)


def build(S, DEPTH, dbg=(), stop_after=None):
    NT = S // 128
    NST = S // 512
    NJ = 128
    NCB = S // 16 - 1
    NCT = (NCB + 127) // 128
    nc = bass.Bass("TRN2", target_bir_lowering=False)

    def din(name, shape, dt=F32):
        return nc.dram_tensor(name, list(shape), dt, kind="ExternalInput").ap()

    x_in = din("x", [S, D])
    norm_mix_g = din("norm_mix_g", [DEPTH, D])
    w_in = din("w_in", [DEPTH, D, NIN])
    a_q_g = din("a_q_g", [DEPTH, 64])
    a_k_g = din("a_k_g", [DEPTH, 64])
    b_q_g = din("b_q_g", [DEPTH, 64])
    b_k_g = din("b_k_g", [DEPTH, 3, 64])
    cmp_pos = din("cmp_pos", [DEPTH, 2, 32, 64])
    cmp_w1 = din("cmp_w1", [DEPTH, 2, 2048, 256])
    cmp_w2 = din("cmp_w2", [DEPTH, 2, 256, 64])
    w_proj_a = din("w_proj_a", [DEPTH, 512, D])
    w_proj_b = din("w_proj_b", [DEPTH, 512, D])
    w_out = din("w_out", [DEPTH, D, D])
    norm_ffn_g = din("norm_ffn_g", [DEPTH, D])
    w_up = din("w_up", [DEPTH, D, 2 * DFF])
    conv_w = din("conv_w", [DEPTH, 3, DFF])
    conv_b = din("conv_b", [DEPTH, DFF])
    w_down = din("w_down", [DEPTH, DFF, D])
    c_cos = din("c_cos", [128, S])
    c_sin = din("c_sin", [128, S])
    c_cosc = din("c_cosc", [64, NCT * 128])
    c_sinc = din("c_sinc", [64, NCT * 128])
    c_mats = din("c_mats", [128, 3, 128])
    c_masks = din("c_masks", [128, 3, 128])
    c_eall = din("c_eall", [128, S])
    c_mmat = din("c_mmat", [128, NCT, 128])
    c_cm = din("c_cm", [128, 17, 128])
    c_tk = din("c_tk", [128, 256])
    c_ta = din("c_ta", [128, 256])
    y_out = nc.dram_tensor("y", [S, D], F32, kind="ExternalOutput").ap()

    def dscr(name, shape, dt):
        kind = "ExternalOutput" if name in dbg else "Internal"
        return nc.dram_tensor(name, list(shape), dt, kind=kind).ap()

    QAT = dscr("QAT", [1536, S], BF16)
    KAT = dscr("KAT", [1536, S], BF16)
    VA = dscr("VA", [S, 1536], BF16)
    QBT = dscr("QBT", [512, S], BF16)
    KSELT = dscr("KSELT", [128, S], BF16)
    KWINT = dscr("KWINT", [128, S], BF16)
    VSEL = dscr("VSEL", [S, 128], BF16)
    VWIN = dscr("VWIN", [S, 128], BF16)
    KCRT = dscr("KCRT", [128, S], BF16)
    VCRT = dscr("VCRT", [128, S], BF16)
    GT = dscr("GT", [24, S], F32)
    MGT = dscr("MGT", [2048, S], BF16)
    OAT = dscr("OAT", [512, S], BF16)
    XR1 = dscr("XR1", [S, D], F32)
    XR2 = dscr("XR2", [S, D], F32)
    WSCR = dscr("WSCR", [6, 512], F32)
    wscr_res = [Res() for _ in range(6)]
    dres = {n: [Res() for _ in range(NST)] for n in
            ("QAT", "KAT", "VA", "QBT", "KSELT", "KWINT", "VSEL", "VWIN", "KCRT", "VCRT", "GT", "MGT", "OAT", "XR1", "XR2", "Y")}

    def dall(n):
        return dres[n]

    with ExitStack() as top:
        P = Prog(nc, top)

        uid = [0]

        def sbuf(cx, name, shape, dt, nres=1):
            uid[0] += 1
            t = cx.enter_context(nc.sbuf_tensor(f"{name}_{uid[0]}", list(shape), dt))
            return Buf(t, nres)

        def psum(cx, name, shape, dt=F32):
            uid[0] += 1
            t = cx.enter_context(nc.psum_tensor(f"{name}_{uid[0]}", list(shape), dt))
            return Buf(t)

        MATS = sbuf(top, "MATS", [128, 3, 128], BF16)
        MASKS = sbuf(top, "MASKS", [128, 3, 128], BF16)
        ONESF = sbuf(top, "ONESF", [128, 64], F32)
        EPSC = sbuf(top, "EPSC", [128, 1], F32)
        with ExitStack() as cx:
            st1 = sbuf(cx, "cst1", [128, 3, 128], F32)
            st2 = sbuf(cx, "cst2", [128, 3, 128], F32)
            P.dma("sp", st1[:], c_mats, writes=[st1.r0])
            P.dma("sp", st2[:], c_masks, writes=[st2.r0])
            P.op("dve", C("tensor_copy", MATS[:], st1[:]), reads=[st1.r0], writes=[MATS.r0])
            P.op("dve", C("tensor_copy", MASKS[:], st2[:]), reads=[st2.r0], writes=[MASKS.r0])
            P.op("pool", C("memset", ONESF[:], 1.0), writes=[ONESF.r0])
            P.op("pool", C("memset", EPSC[:], EPS), writes=[EPSC.r0])
            P.barrier()
        ROT = MATS[:, 0, :]
        BON = MATS[:, 1, :]
        IDN = MATS[:, 2, :]

        def norm_to_hT(cx_bufs, l_gT, src, tok0, hT, hT_res, tt):
            xt, junk, ss, rs, xs, tp = cx_bufs
            P.dma("sp", xt[:], src[tok0:tok0 + 128, :], reads=[], writes=[xt.r0])
            P.op("pool", C("memset", ss[:], 0.0), writes=[ss.r0])
            P.op("act", C("activation", out=junk[:], in_=xt[:], func=AF.Square, accum_out=ss[:]),
                 reads=[xt.r0], writes=[junk.r0, ss.r0])
            P.op("act", C("activation", out=rs[:], in_=ss[:], func=AF.Sqrt, bias=EPSC[:, 0:1], scale=1.0 / D),
                 reads=[ss.r0, EPSC.r0], writes=[rs.r0])
            P.op("dve", C("reciprocal", rs[:], rs[:]), reads=[rs.r0], writes=[rs.r0])
            P.op("dve", C("tensor_scalar", xs[:], xt[:], rs[:, 0:1], None, op0=ALU.mult),
                 reads=[xt.r0, rs.r0], writes=[xs.r0])
            items = [(tp[:, kc, :], xs[:, kc * 128:(kc + 1) * 128], IDN, True, True) for kc in range(8)]
            P.op("pe", mmgroup(items), reads=[xs.r0, MATS.r0], writes=[tp.r0])
            P.op("dve", C("tensor_tensor", out=hT[:, :, tt * 128:(tt + 1) * 128], in0=tp[:],
                                                  in1=l_gT[:, :].unsqueeze(2).to_broadcast([128, 8, 128]), op=ALU.mult),
                 reads=[tp.r0, l_gT.r0], writes=[hT_res])
            return xt

        def load_cast(cx, dst_ap_fn, src_ap_fn, nchunks, shape, dst_res, name, eng="pool"):
            nb_ = min(4, nchunks)
            stg = [sbuf(cx, f"{name}_stg{i}", shape, F32) for i in range(nb_)]
            for i in range(nchunks):
                s = stg[i % nb_]
                P.dma("sp", s[:], src_ap_fn(i), writes=[s.r0])
                ce = eng if i % 2 == 0 else "dve"
                P.op(ce, C("tensor_copy", dst_ap_fn(i), s[:]), reads=[s.r0], writes=[dst_res])

        def phase_P(l, src, src_name):
            with ExitStack() as cx:
                WIN = sbuf(cx, "WIN", [128, 8, NIN], BF16)
                with ExitStack() as cx2:
                    Q4 = NIN // 4
                    load_cast(cx2, lambda i: WIN[:, i // 4, (i % 4) * Q4:(i % 4 + 1) * Q4],
                              lambda i: w_in[l, (i // 4) * 128:(i // 4 + 1) * 128, (i % 4) * Q4:(i % 4 + 1) * Q4],
                              32, [128, Q4], WIN.r0, "win")
                    P.barrier()
                gT = sbuf(cx, "gT", [128, 8], F32)
                P.dma("sp", gT[:], norm_mix_g[l].rearrange("(kc p) -> p kc", p=128), writes=[gT.r0], slow=True)
                GC = sbuf(cx, "GC", [128, 5], F32)
                for ci, gsrc in enumerate((a_q_g[l], a_k_g[l], b_q_g[l], b_k_g[l, 1], b_k_g[l, 2])):
                    for hb in range(2):
                        P.dma("sp", GC[hb * 64:(hb + 1) * 64, ci:ci + 1], gsrc.rearrange("(d o) -> d o", o=1),
                              writes=[GC.r0], slow=True)
                hTs = [sbuf(cx, f"hT{i}", [128, 8, 512], BF16) for i in range(2)]
                nb = [(sbuf(cx, f"nx{i}", [128, D], F32), sbuf(cx, f"nj{i}", [128, D], BF16), sbuf(cx, f"nss{i}", [128, 1], F32),
                       sbuf(cx, f"nrs{i}", [128, 1], F32), sbuf(cx, f"nxs{i}", [128, D], BF16),
                       psum(cx, f"ntp{i}", [128, 8, 128])) for i in range(1)]
                cs = [(sbuf(cx, f"cos{i}", [128, 512], F32), sbuf(cx, f"sin{i}", [128, 512], F32)) for i in range(2)]
                pa = [psum(cx, f"pa{i}", [128, 512]) for i in range(2)]
                p2 = [psum(cx, f"p2{i}", [128, 512]) for i in range(2)]
                p3 = [psum(cx, f"p3{i}", [128, 512]) for i in range(2)]
                sq = [sbuf(cx, f"sq{i}", [128, 512], BF16) for i in range(2)]
                xg = [sbuf(cx, f"xg{i}", [128, 512], BF16) for i in range(2)]
                rstd = [sbuf(cx, f"rstd{i}", [128, 512], F32) for i in range(2)]
                t1 = [sbuf(cx, f"t1{i}", [128, 512], F32) for i in range(2)]
                t2 = [sbuf(cx, f"t2{i}", [128, 512], F32) for i in range(2)]
                ob = [sbuf(cx, f"ob{i}", [128, 512], BF16) for i in range(3)]
                of = [sbuf(cx, f"of{i}", [128, 512], F32) for i in range(2)]
                gi = 0
                oi = 0
                for st in range(NST):
                    hT = hTs[st % 2]
                    c_t, s_t = cs[st % 2]
                    tsl = slice(st * 512, (st + 1) * 512)
                    P.dma("sp", c_t[:], c_cos[:, tsl], writes=[c_t.r0])
                    P.dma("sp", s_t[:], c_sin[:, tsl], writes=[s_t.r0])
                    for tt in range(4):
                        norm_to_hT(nb[0], gT, src, st * 512 + tt * 128, hT, hT.r0, tt)
                    wr_src = [dres[src_name][st]] if src_name else []
                    nr_groups = []
                    for g in range(12):
                        nr_groups.append((C_AQ + g * 128, 0, QAT, "QAT", g * 128))
                    for g in range(12):
                        nr_groups.append((C_AK + g * 128, 1, KAT, "KAT", g * 128))
                    for g in range(4):
                        nr_groups.append((C_BQ + g * 128, 2, QBT, "QBT", g * 128))
                    nr_groups.append((C_BKV + 256, 3, KSELT, "KSELT", 0))
                    nr_groups.append((C_BKV + 512, 4, KWINT, "KWINT", 0))
                    pend = None

                    def nr_epilogue(b, dst, dname, r0):
                        nonlocal oi
                        P.op("pe", mmgroup([(p2[b][:], BON, sq[b][:], True, True)]), reads=[sq[b].r0, MATS.r0], writes=[p2[b].r0])
                        P.op("pe", mmgroup([(p3[b][:], ROT, xg[b][:], True, True)]), reads=[xg[b].r0, MATS.r0], writes=[p3[b].r0])
                        P.op("act", C("activation", out=rstd[b][:], in_=p2[b][:], func=AF.Ln, bias=EPSC[:, 0:1], scale=1.0),
                             reads=[p2[b].r0, EPSC.r0], writes=[rstd[b].r0])
                        P.op("act", C("activation", out=rstd[b][:], in_=rstd[b][:], func=AF.Exp, scale=-0.5),
                             reads=[rstd[b].r0], writes=[rstd[b].r0])
                        P.op("pool", C("tensor_tensor", out=t1[b][:], in0=xg[b][:], in1=c_t[:], op=ALU.mult),
                             reads=[xg[b].r0, c_t.r0], writes=[t1[b].r0])
                        P.op("dve", C("tensor_tensor", out=t2[b][:], in0=p3[b][:], in1=s_t[:], op=ALU.mult),
                             reads=[p3[b].r0, s_t.r0], writes=[t2[b].r0])
                        P.op("dve", C("tensor_tensor", out=t2[b][:], in0=t1[b][:], in1=t2[b][:], op=ALU.add),
                             reads=[t1[b].r0, t2[b].r0], writes=[t2[b].r0])
                        o = ob[oi % 3]
                        oi += 1
                        P.op("dve", C("tensor_tensor", out=o[:], in0=t2[b][:], in1=rstd[b][:], op=ALU.mult),
                             reads=[t2[b].r0, rstd[b].r0], writes=[o.r0])
                        P.dma("pool", dst[r0:r0 + 128, tsl], o[:], reads=[o.r0], writes=[dres[dname][st]])
                    for (c0, gci, dst, dname, r0) in nr_groups:
                        b = gi % 2
                        gi += 1
                        items = [(pa[b][:], WIN[:, kc, c0:c0 + 128], hT[:, kc, :], kc == 0, kc == 7) for kc in range(8)]
                        P.op("pe", mmgroup(items), reads=[WIN.r0, hT.r0], writes=[pa[b].r0])
                        P.op("act", C("activation", out=sq[b][:], in_=pa[b][:], func=AF.Square),
                             reads=[pa[b].r0], writes=[sq[b].r0])
                        P.op("act", C("activation", out=xg[b][:], in_=pa[b][:], func=AF.Identity,
                                                                         scale=GC[:, gci:gci + 1]),
                             reads=[pa[b].r0, GC.r0], writes=[xg[b].r0])
                        if pend is not None:
                            nr_epilogue(*pend)
                        pend = (b, dst, dname, r0)
                    nr_epilogue(*pend)
                    sg = [(C_BKV + 0, 128, AF.Identity, KCRT, "KCRT", 0, BF16), (C_BKV + 128, 128, AF.Identity, VCRT, "VCRT", 0, BF16)]
                    for g in range(16):
                        sg.append((C_MERGE + g * 128, 128, AF.Sigmoid, MGT, "MGT", g * 128, BF16))
                    sg.append((C_GATE, 24, AF.Sigmoid, GT, "GT", 0, F32))
                    for (c0, w, func, dst, dname, r0, dt) in sg:
                        b = gi % 2
                        gi += 1
                        items = [(pa[b][0:w, :], WIN[:, kc, c0:c0 + w], hT[:, kc, :], kc == 0, kc == 7) for kc in range(8)]
                        P.op("pe", mmgroup(items), reads=[WIN.r0, hT.r0], writes=[pa[b].r0])
                        if dt == BF16:
                            o = ob[oi % 3]
                            oi += 1
                        else:
                            o = of[0]
                        P.op("act", C("activation", out=o[0:w, :], in_=pa[b][0:w, :], func=func),
                             reads=[pa[b].r0], writes=[o.r0])
                        P.dma("pool", dst[r0:r0 + w, tsl], o[0:w, :], reads=[o.r0], writes=[dres[dname][st]])
                    vb = [(C_AV, 512, VA, "VA", 0), (C_AV + 512, 512, VA, "VA", 512), (C_AV + 1024, 512, VA, "VA", 1024),
                          (C_BKV + 384, 128, VSEL, "VSEL", 0), (C_BKV + 640, 128, VWIN, "VWIN", 0)]
                    for tt in range(4):
                        t0 = st * 512 + tt * 128
                        for (c0, w, dst, dname, cc0) in vb:
                            b = gi % 2
                            gi += 1
                            items = [(pa[b][:, 0:w], hT[:, kc, tt * 128:(tt + 1) * 128], WIN[:, kc, c0:c0 + w], kc == 0, kc == 7)
                                     for kc in range(8)]
                            P.op("pe", mmgroup(items), reads=[WIN.r0, hT.r0], writes=[pa[b].r0])
                            o = ob[oi % 3]
                            oi += 1
                            P.op("act", C("activation", out=o[:, 0:w], in_=pa[b][:, 0:w], func=AF.Identity),
                                 reads=[pa[b].r0], writes=[o.r0])
                            P.dma("pool", dst[t0:t0 + 128, cc0:cc0 + w], o[:, 0:w], reads=[o.r0], writes=[dres[dname][st]])
                P.barrier()

        def phase_C(l, KCMPT, VCMP):
            with ExitStack() as cx:
                raw = sbuf(cx, "craw", [64, S], BF16)
                w1b = sbuf(cx, "cw1b", [64, 32, 256], BF16)
                w2b = sbuf(cx, "cw2b", [128, 2, 64], BF16)
                posb = sbuf(cx, "cposb", [64, 32], BF16)
                posf = sbuf(cx, "cposf", [64, 32], F32)
                w2f = sbuf(cx, "cw2f", [128, 2, 64], F32)
                gk = sbuf(cx, "cgk", [64, 1], F32)
                cosc = sbuf(cx, "ccos", [64, NCT * 128], F32)
                sinc = sbuf(cx, "csin", [64, NCT * 128], F32)
                ph = psum(cx, "cph", [128, 512])
                pb = psum(cx, "cpb", [128, 8])
                pk = psum(cx, "cpk", [128, 512])
                pq2 = psum(cx, "cp2", [128, 512])
                pq3 = psum(cx, "cp3", [128, 512])
                pv = psum(cx, "cpv", [128, 64])
                bcol = sbuf(cx, "cbcol", [128, 1], F32)
                xh = sbuf(cx, "cxh", [128, 512], F32)
                x2 = sbuf(cx, "cx2", [128, 512], F32)
                sg_ = sbuf(cx, "csg", [128, 512], F32)
                h1g = sbuf(cx, "ch1g", [128, 2, 512], BF16)
                sqc = sbuf(cx, "csq", [64, 512], BF16)
                xgc = sbuf(cx, "cxg", [64, 512], BF16)
                rsc = sbuf(cx, "crs", [64, 512], F32)
                t1c = sbuf(cx, "ct1", [64, 512], F32)
                t2c = sbuf(cx, "ct2", [64, 512], F32)
                okc = sbuf(cx, "cok", [64, 512], BF16)
                P.dma("sp", cosc[:], c_cosc, writes=[cosc.r0])
                P.dma("sp", sinc[:], c_sinc, writes=[sinc.r0])
                P.dma("sp", gk[:], b_k_g[l, 0].rearrange("(d o) -> d o", o=1), writes=[gk.r0], slow=True)
                P.op("pool", C("memset", KCMPT[:], 0.0), writes=[KCMPT.r0])
                P.op("pool", C("memset", VCMP[:], 0.0), writes=[VCMP.r0])
                P.op("pool", C("memset", VCMP[:, :, :, 64:65], 1.0), writes=[VCMP.r0])
                P.op("pool", C("memset", h1g[:], 0.0), writes=[h1g.r0])
                for typ in range(2):
                    with ExitStack() as cx2:
                        load_cast(cx2, lambda i: w1b[:, i * 8:(i + 1) * 8, :],
                                  lambda i: cmp_w1[l, typ].rearrange("(j d) h -> d j h", d=64)[:, i * 8:(i + 1) * 8, :],
                                  4, [64, 8, 256], w1b.r0, f"cw1_{typ}")
                        P.dma("sp", posf[:], cmp_pos[l, typ].rearrange("j d -> d j"), writes=[posf.r0], slow=True)
                        P.op("pool", C("tensor_copy", posb[:], posf[:]), reads=[posf.r0], writes=[posb.r0])
                        P.dma("sp", w2f[:], cmp_w2[l, typ].rearrange("(c p) d -> p c d", p=128), writes=[w2f.r0])
                        P.op("pool", C("tensor_copy", w2b[:], w2f[:]), reads=[w2f.r0], writes=[w2b.r0])
                        src = KCRT if typ == 0 else VCRT
                        sname = "KCRT" if typ == 0 else "VCRT"
                        for kvh in range(2):
                            P.dma("sp", raw[:], src[kvh * 64:(kvh + 1) * 64, :], reads=dall(sname), writes=[raw.r0])
                            for hc in range(2):
                                items = [(ph[:, 0:NCB], w1b[:, j, hc * 128:(hc + 1) * 128],
                                          raw[:, j:j + 16 * (NCB - 1) + 1:16], j == 0, j == 31) for j in range(32)]
                                P.op("pe", mmgroup(items), reads=[w1b.r0, raw.r0], writes=[ph.r0])
                                items = [(pb[:, 0:1], w1b[:, j, hc * 128:(hc + 1) * 128], posb[:, j:j + 1], j == 0, j == 31)
                                         for j in range(32)]
                                P.op("pe", mmgroup(items), reads=[w1b.r0, posb.r0], writes=[pb.r0])
                                P.op("dve", C("tensor_copy", bcol[:], pb[:, 0:1]), reads=[pb.r0], writes=[bcol.r0])
                                P.op("act", C("activation", out=xh[:, 0:NCB], in_=ph[:, 0:NCB], func=AF.Identity, bias=bcol[:, 0:1]),
                                     reads=[ph.r0, bcol.r0], writes=[xh.r0])
                                P.op("dve", C("tensor_tensor", out=x2[:, 0:NCB], in0=xh[:, 0:NCB], in1=xh[:, 0:NCB], op=ALU.mult),
                                     reads=[xh.r0], writes=[x2.r0])
                                P.op("dve", C("tensor_scalar", x2[:, 0:NCB], x2[:, 0:NCB], 0.044715, 1.0, op0=ALU.mult, op1=ALU.add),
                                     reads=[x2.r0], writes=[x2.r0])
                                P.op("dve", C("tensor_tensor", out=x2[:, 0:NCB], in0=x2[:, 0:NCB], in1=xh[:, 0:NCB], op=ALU.mult),
                                     reads=[x2.r0, xh.r0], writes=[x2.r0])
                                P.op("act", C("activation", out=sg_[:, 0:NCB], in_=x2[:, 0:NCB], func=AF.Sigmoid, scale=1.5957691216057308),
                                     reads=[x2.r0], writes=[sg_.r0])
                                P.op("dve", C("tensor_tensor", out=h1g[:, hc, 0:NCB], in0=xh[:, 0:NCB], in1=sg_[:, 0:NCB], op=ALU.mult),
                                     reads=[xh.r0, sg_.r0], writes=[h1g.r0])
                            if typ == 0:
                                items = [(pk[0:64, 0:NCB], w2b[:, hc, :], h1g[:, hc, 0:NCB], hc == 0, hc == 1) for hc in range(2)]
                                P.op("pe", mmgroup(items), reads=[w2b.r0, h1g.r0], writes=[pk.r0])
                                P.op("act", C("activation", out=sqc[:, 0:NCB], in_=pk[0:64, 0:NCB], func=AF.Square),
                                     reads=[pk.r0], writes=[sqc.r0])
                                P.op("act", C("activation", out=xgc[:, 0:NCB], in_=pk[0:64, 0:NCB], func=AF.Identity, scale=gk[:, 0:1]),
                                     reads=[pk.r0, gk.r0], writes=[xgc.r0])
                                P.op("pe", mmgroup([(pq2[0:64, 0:NCB], MATS[0:64, 1, 0:64], sqc[:, 0:NCB], True, True)]),
                                     reads=[sqc.r0, MATS.r0], writes=[pq2.r0])
                                P.op("pe", mmgroup([(pq3[0:64, 0:NCB], MATS[0:64, 0, 0:64], xgc[:, 0:NCB], True, True)]),
                                     reads=[xgc.r0, MATS.r0], writes=[pq3.r0])
                                P.op("act", C("activation", out=rsc[:, 0:NCB], in_=pq2[0:64, 0:NCB], func=AF.Sqrt, bias=EPSC[0:64, 0:1], scale=1.0),
                                     reads=[pq2.r0, EPSC.r0], writes=[rsc.r0])
                                P.op("dve", C("reciprocal", rsc[:, 0:NCB], rsc[:, 0:NCB]), reads=[rsc.r0], writes=[rsc.r0])
                                P.op("dve", C("tensor_tensor", out=t1c[:, 0:NCB], in0=xgc[:, 0:NCB], in1=cosc[:, 0:NCB], op=ALU.mult),
                                     reads=[xgc.r0, cosc.r0], writes=[t1c.r0])
                                P.op("dve", C("tensor_tensor", out=t2c[:, 0:NCB], in0=pq3[0:64, 0:NCB], in1=sinc[:, 0:NCB], op=ALU.mult),
                                     reads=[pq3.r0, sinc.r0], writes=[t2c.r0])
                                P.op("dve", C("tensor_tensor", out=t1c[:, 0:NCB], in0=t1c[:, 0:NCB], in1=t2c[:, 0:NCB], op=ALU.add),
                                     reads=[t1c.r0, t2c.r0], writes=[t1c.r0])
                                P.op("dve", C("tensor_tensor", out=okc[:, 0:NCB], in0=t1c[:, 0:NCB], in1=rsc[:, 0:NCB], op=ALU.mult),
                                     reads=[t1c.r0, rsc.r0], writes=[okc.r0])
                                P.dma("sp", KCMPT[kvh * 64:(kvh + 1) * 64, 0:NCB], okc[:, 0:NCB], reads=[okc.r0], writes=[KCMPT.r0])
                            else:
                                for ct in range(NCT):
                                    m = min(128, NCB - ct * 128)
                                    items = [(pv[0:m, :], h1g[:, hc, ct * 128:ct * 128 + m], w2b[:, hc, :], hc == 0, hc == 1) for hc in range(2)]
                                    P.op("pe", mmgroup(items), reads=[w2b.r0, h1g.r0], writes=[pv.r0])
                                    P.op("dve", C("tensor_copy", VCMP[0:m, kvh, ct, 0:64], pv[0:m, :]),
                                         reads=[pv.r0], writes=[VCMP.r0])
                        P.barrier()
                P.barrier()

        def phase_A1(l):
            with ExitStack() as cx:
                acc = sbuf(cx, "a_acc", [65, S], F32)
                Qt = [sbuf(cx, f"a_q{i}", [128, S], BF16) for i in range(2)]
                Kt = [sbuf(cx, f"a_k{i}", [128, S], BF16) for i in range(2)]
                for t_ in Qt + Kt:
                    P.op("pool", C("memset", t_[64:128, :], 0.0), writes=[t_.r0])
                Vt = [sbuf(cx, f"a_v{i}", [128, NT, 65], BF16) for i in range(2)]
                ps = [psum(cx, f"a_ps{i}", [128, 2, 128]) for i in range(2)]
                po = [psum(cx, f"a_po{i}", [128, 128]) for i in range(2)]
                pbx = [psum(cx, f"a_pb{i}", [128, 512]) for i in range(2)]
                pt = [sbuf(cx, f"a_pt{i}", [128, 2, 128], BF16) for i in range(4)]
                oa = sbuf(cx, "a_oa", [64, S], BF16)
                rd = sbuf(cx, "a_rd", [128, S], F32)
                P.op("pool", C("memset", rd[:], 0.0), writes=[rd.r0])
                for v in Vt:
                    P.op("pool", C("memset", v[:, :, 64:65], 1.0), writes=[v.r0])
                it = 0
                ld = 0
                for j in range(8):
                    for gi, (window, dil) in enumerate(DIL_PAIRS):
                        head = gi * 8 + j
                        r0 = head * 64
                        L = S // dil
                        nb = L // 128
                        q_, k_, v_ = Qt[ld % 2], Kt[ld % 2], Vt[ld % 2]
                        ld += 1
                        P.dma("sp", q_[0:64, :], QAT[r0:r0 + 64, :], reads=dall("QAT"), writes=[q_.r0])
                        P.dma("sp", k_[0:64, :], KAT[r0:r0 + 64, :], reads=dall("KAT"), writes=[k_.r0])
                        for r in range(dil):
                            srcv = VA[r::dil, r0:r0 + 64].rearrange("(jb kk) d -> kk jb d", kk=128)
                            for j0 in range(0, nb, 8):
                                j1 = min(nb, j0 + 8)
                                P.dma("sp", v_[:, r * nb + j0:r * nb + j1, 0:64], srcv[:, j0:j1, :], reads=dall("VA"), writes=[v_.r0])
                        pend = None

                        def pv_acc(b, p_t, r, Jb, c):
                            items = []
                            if Jb > 0:
                                items.append((po[b][0:65, :], v_[:, r * nb + Jb - 1, :], p_t[:, 0, :], True, False))
                            items.append((po[b][0:65, :], v_[:, r * nb + Jb, :], p_t[:, 1, :], Jb == 0, True))
                            P.op("pe", mmgroup(items), reads=[v_.r0, p_t.r0], writes=[po[b].r0])
                            if gi == 0:
                                P.op("dve", C("tensor_copy", acc[:, c], po[b][0:65, :]),
                                     reads=[po[b].r0], writes=[acc.r0])
                            else:
                                P.op("dve", C("tensor_tensor", out=acc[:, c], in0=acc[:, c], in1=po[b][0:65, :], op=ALU.add),
                                     reads=[po[b].r0, acc.r0], writes=[acc.r0])
                        for r in range(dil):
                            for Jb in range(nb):
                                def cols(J):
                                    s0 = J * 128 * dil + r
                                    return slice(s0, s0 + 127 * dil + 1, dil)
                                b = it % 2
                                p_t = pt[it % 4]
                                it += 1
                                items = []
                                if Jb > 0:
                                    items.append((ps[b][:, 0, :], k_[:, cols(Jb - 1)], q_[:, cols(Jb)], True, True))
                                items.append((ps[b][:, 1, :], k_[:, cols(Jb)], q_[:, cols(Jb)], True, True))
                                P.op("pe", mmgroup(items), reads=[k_.r0, q_.r0], writes=[ps[b].r0])
                                e0 = 0 if Jb > 0 else 1
                                P.op("act", C("activation", out=p_t[:, e0:2, :], in_=ps[b][:, e0:2, :], func=AF.Exp, scale=0.125),
                                     reads=[ps[b].r0], writes=[p_t.r0])
                                P.op("dve", C("tensor_tensor", out=p_t[:, e0:2, :], in0=p_t[:, e0:2, :], in1=MASKS[:, e0:2, :], op=ALU.mult),
                                     reads=[p_t.r0, MASKS.r0], writes=[p_t.r0])
                                if pend is not None:
                                    pv_acc(*pend)
                                pend = (b, p_t, r, Jb, cols(Jb))
                        pv_acc(*pend)
                    P.op("act", C("activation", out=rd[64:65, :], in_=acc[64:65, :], func=AF.Ln), reads=[acc.r0], writes=[rd.r0])
                    P.op("act", C("activation", out=rd[64:65, :], in_=rd[64:65, :], func=AF.Exp, scale=-1.0), reads=[rd.r0], writes=[rd.r0])
                    for ch in range(NST):
                        b = ch % 2
                        csl = slice(ch * 512, (ch + 1) * 512)
                        P.op("pe", mmgroup([(pbx[b][0:64, :], ONESF[:, 0:64], rd[:, csl], True, True)]),
                             reads=[rd.r0, ONESF.r0], writes=[pbx[b].r0])
                        P.op("dve", C("tensor_tensor", out=oa[:, csl], in0=acc[0:64, csl], in1=pbx[b][0:64, :], op=ALU.mult),
                             reads=[pbx[b].r0, acc.r0], writes=[oa.r0])
                    P.dma("sp", OAT[j * 64:(j + 1) * 64, :], oa[:], reads=[oa.r0], writes=dall("OAT"))
                P.barrier()

        def phase_A2(l, src, src_name, KCMPT, VCMP):
            with ExitStack() as cx:
                KS = sbuf(cx, "b_ks", [128, S], BF16)
                KW = sbuf(cx, "b_kw", [128, S], BF16)
                VS = sbuf(cx, "b_vs", [128, NT, 2, 65], BF16)
                VW = sbuf(cx, "b_vw", [128, NT, 2, 65], BF16)
                EALL = sbuf(cx, "b_eall", [128, S], BF16)
                MM = sbuf(cx, "b_mm", [128, NCT, 128], BF16)
                CM = sbuf(cx, "b_cm", [128, 17, 128], BF16)
                TK = sbuf(cx, "b_tk", [128, 256], F32)
                TA = sbuf(cx, "b_ta", [128, 256], F32)
                WPA = sbuf(cx, "b_wpa", [128, 4, D], BF16)
                WPB = sbuf(cx, "b_wpb", [128, 8, D], BF16)
                P.op("pool", C("memset", WPB[64:128, :, :], 0.0), writes=[WPB.r0])
                WO = sbuf(cx, "b_wo", [128, 8, D], BF16)
                with ExitStack() as cx2:
                    load_cast(cx2, lambda i: EALL[:, i * 1024:(i + 1) * 1024], lambda i: c_eall[:, i * 1024:(i + 1) * 1024],
                              S // 1024, [128, 1024], EALL.r0, "ea")
                    load_cast(cx2, lambda i: WPA[:, i, :], lambda i: w_proj_a[l, i * 128:(i + 1) * 128, :], 4, [128, D], WPA.r0, "wpa")
                    load_cast(cx2, lambda i: WPB[0:64, i, :], lambda i: w_proj_b[l, i * 64:(i + 1) * 64, :], 8, [64, D], WPB.r0, "wpb")
                    load_cast(cx2, lambda i: WO[:, i, :], lambda i: w_out[l, i * 128:(i + 1) * 128, :], 8, [128, D], WO.r0, "wo")
                    load_cast(cx2, lambda i: MM[:], lambda i: c_mmat, 1, [128, NCT, 128], MM.r0, "mm")
                    load_cast(cx2, lambda i: CM[:], lambda i: c_cm, 1, [128, 17, 128], CM.r0, "cm")
                    P.barrier()
                P.dma("sp", TK[:], c_tk, writes=[TK.r0])
                P.dma("sp", TA[:], c_ta, writes=[TA.r0])
                P.dma("sp", KS[:], KSELT, reads=dall("KSELT"), writes=[KS.r0])
                P.dma("sp", KW[:], KWINT, reads=dall("KWINT"), writes=[KW.r0])
                P.op("pool", C("memset", VS[:, :, :, 64:65], 1.0), writes=[VS.r0])
                P.op("pool", C("memset", VW[:, :, :, 64:65], 1.0), writes=[VW.r0])
                for h in range(2):
                    for k0 in range(0, NT, 8):
                        P.dma("sp", VS[:, k0:k0 + 8, h, 0:64], VSEL[k0 * 128:(k0 + 8) * 128, h * 64:(h + 1) * 64].rearrange("(kt kk) d -> kk kt d", kk=128),
                              reads=dall("VSEL"), writes=[VS.r0])
                        P.dma("sp", VW[:, k0:k0 + 8, h, 0:64], VWIN[k0 * 128:(k0 + 8) * 128, h * 64:(h + 1) * 64].rearrange("(kt kk) d -> kk kt d", kk=128),
                              reads=dall("VWIN"), writes=[VW.r0])
                ST = [psum(cx, f"b_st{i}", [128, 2, 512]) for i in range(2)]
                OT = [psum(cx, f"b_ot{i}", [128, 512]) for i in range(2)]
                UB = psum(cx, "b_ub", [128, 4, 128])
                MISC = psum(cx, "b_misc", [128, 512])
                QB = [sbuf(cx, f"b_qb{i}", [128, 2, 512], BF16) for i in range(2)]
                for q_ in QB:
                    P.op("pool", C("memset", q_[:], 0.0), writes=[q_.r0])
                GR = [sbuf(cx, f"b_gr{i}", [65, 24, 128], F32) for i in range(1)]
                MG = [sbuf(cx, f"b_mg{i}", [128, 16, 128], BF16) for i in range(2)]
                OA = [sbuf(cx, f"b_oa{i}", [128, 4, 128], BF16) for i in range(2)]
                XT = [sbuf(cx, f"b_xt{i}", [128, D], F32) for i in range(2)]
                PC = sbuf(cx, "b_pc", [128, 4, 512], BF16)
                PS_ = [sbuf(cx, f"b_ps{i}", [128, 2, 512], BF16) for i in range(2)]
                rs4 = sbuf(cx, "b_rs4", [128, 4], F32)
                psl = sbuf(cx, "b_psl", [128, 128], F32)
                sc2 = sbuf(cx, "b_sc2", [128, 128], F32)
                m8a = sbuf(cx, "b_m8a", [128, 8], F32)
                m8b = sbuf(cx, "b_m8b", [128, 8], F32)
                selb = sbuf(cx, "b_selb", [128, 128], BF16)
                selT = sbuf(cx, "b_selT", [128, 4, 128], BF16)
                UBR = [sbuf(cx, f"b_ubr{i}", [65, 512], F32) for i in range(3)]
                wrow = sbuf(cx, "b_wrow", [128, 512], F32)
                P.op("pool", C("memset", wrow[:], 0.0), writes=[wrow.r0])
                obf = sbuf(cx, "b_obf", [64, 512], F32)
                BCB = [sbuf(cx, f"b_bc{i}", [64, 512], F32) for i in range(2)]
                otmp = sbuf(cx, "b_otmp", [64, 512], F32)
                OBT = [sbuf(cx, f"b_obt{i}", [128, 8, 128], BF16) for i in range(2)]
                for o_ in OBT:
                    P.op("pool", C("memset", o_[:], 0.0), writes=[o_.r0])
                m1 = sbuf(cx, "b_m1", [128, 8, 128], F32)
                m2 = sbuf(cx, "b_m2", [128, 8, 128], F32)
                mx = sbuf(cx, "b_mx", [128, 8, 128], BF16)
                sti = 0
                pend_proj = None

                def proj(oa_, mg, xt, OBt, tsl, stq):
                    nonlocal sti
                    ya = ST[sti % 2]
                    sti += 1
                    yb = ST[sti % 2]
                    sti += 1
                    yav = ya[:].rearrange("p a (b q) -> p (a b) q", q=128)
                    ybv = yb[:].rearrange("p a (b q) -> p (a b) q", q=128)
                    items = []
                    for cc in range(8):
                        for kc in range(4):
                            items.append((yav[:, cc, :], WPA[:, kc, cc * 128:(cc + 1) * 128], oa_[:, kc, :], kc == 0, kc == 3))
                    P.op("pe", mmgroup(items), reads=[WPA.r0, oa_.r0], writes=[ya.r0])
                    items = []
                    for cc in range(8):
                        for hd in range(8):
                            items.append((ybv[:, cc, :], WPB[:, hd, cc * 128:(cc + 1) * 128], OBt[:, hd, :], hd == 0, hd == 7))
                    P.op("pe", mmgroup(items), reads=[WPB.r0, OBt.r0], writes=[yb.r0])
                    P.op("dve", C("tensor_tensor", out=m1[:], in0=yav, in1=mg[:, 0:8, :], op=ALU.mult), reads=[ya.r0, mg.r0], writes=[m1.r0])
                    P.op("dve", C("tensor_tensor", out=m2[:], in0=ybv, in1=mg[:, 8:16, :], op=ALU.mult), reads=[yb.r0, mg.r0], writes=[m2.r0])
                    P.op("pool", C("tensor_tensor", out=mx[:], in0=m1[:], in1=m2[:], op=ALU.add), reads=[m1.r0, m2.r0], writes=[mx.r0])

                def proj2(oa_, mg, xt, OBt, tsl, stq):
                    nonlocal sti
                    z = ST[sti % 2]
                    sti += 1
                    items = []
                    for hf in range(2):
                        for kc in range(8):
                            items.append((z[:, hf, :], mx[:, kc, :], WO[:, kc, hf * 512:(hf + 1) * 512], kc == 0, kc == 7))
                    P.op("pe", mmgroup(items), reads=[mx.r0, WO.r0], writes=[z.r0])
                    P.op("dve", C("tensor_tensor", out=xt[:], in0=xt[:], in1=z[:].rearrange("p a b -> p (a b)"), op=ALU.add),
                         reads=[xt.r0, z.r0], writes=[xt.r0])
                    P.dma("pool", XR1[tsl, :], xt[:], reads=[xt.r0], writes=[dres["XR1"][stq]])
                psi = 0
                oti = 0
                ubi = 0
                import os as _os
                for i in range(int(_os.environ.get('A2_I0', 0)), min(NT, int(_os.environ.get('A2_I1', NT)))):
                    t0 = i * 128
                    tsl = slice(t0, t0 + 128)
                    stq = i // 4
                    qb, gr, mg, oa_, xt = QB[i % 2], GR[0], MG[i % 2], OA[i % 2], XT[i % 2]
                    OBt = OBT[i % 2]
                    for h in range(2):
                        P.dma("sp", qb[h * 64:(h + 1) * 64, h, :].rearrange("p (g q) -> p g q", g=4), QBT[h * 256:(h + 1) * 256, tsl].rearrange("(g d) t -> d g t", d=64),
                              reads=[dres["QBT"][stq]], writes=[qb.r0])
                    P.dma("sp", gr[64:65, :, :], GT[:, tsl].rearrange("(o r) t -> o r t", o=1), reads=[dres["GT"][stq]], writes=[gr.r0])
                    P.dma("sp", mg[:], MGT[:, tsl].rearrange("(c p) t -> p c t", p=128), reads=[dres["MGT"][stq]], writes=[mg.r0])
                    P.dma("sp", oa_[:], OAT[:, tsl].rearrange("(c p) t -> p c t", p=128), reads=[dres["OAT"][stq]], writes=[oa_.r0])
                    P.dma("sp", xt[:], src[tsl, :], reads=([dres[src_name][stq]] if src_name else []), writes=[xt.r0])
                    for h in range(2):
                        hs = slice(h * 64, (h + 1) * 64)
                        qh = qb[:, h, :]

                        def score_tiles(kts, ksrc, extra_bias, dst, dsti):
                            nonlocal sti
                            b = sti % 2
                            sti += 1
                            items = []
                            for e_, kt in enumerate(kts):
                                eb = extra_bias and kt != i
                                items.append((ST[b][:, e_, :], ksrc[:, kt * 128:(kt + 1) * 128], qh, True, not eb))
                                if eb:
                                    items.append((ST[b][:, e_, :], EALL[0:NJ, kt * 128:(kt + 1) * 128],
                                                  selT[0:NJ, :, :].rearrange("p g q -> p (g q)"), False, True))
                            rd_ = [qb.r0, ksrc_res[id(ksrc)]] + ([EALL.r0, selT.r0] if extra_bias else [])
                            P.op("pe", mmgroup(items), reads=rd_, writes=[ST[b].r0])
                            n = len(kts)
                            P.op("act", C("activation", out=dst[:, dsti:dsti + n, :], in_=ST[b][:, 0:n, :], func=AF.Exp, scale=0.125),
                                 reads=[ST[b].r0], writes=[dst.r0])

                        ksrc_res = {id(KCMPT): KCMPT.r0, id(KS): KS.r0, id(KW): KW.r0}

                        def maskmul(dst, di, mask_ap, mres, eng="pool"):
                            P.op(eng, C("tensor_tensor",
                                out=dst[:, di, :].rearrange("p (g q) -> p g q", g=4),
                                in0=dst[:, di, :].rearrange("p (g q) -> p g q", g=4),
                                in1=mask_ap.unsqueeze(1).to_broadcast([128, 4, 128]), op=ALU.mult),
                                 reads=[dst.r0, mres], writes=[dst.r0])

                        nct = i // 16 + 1
                        for c0 in range(0, nct, 2):
                            kts = list(range(c0, min(c0 + 2, nct)))
                            score_tiles(kts, KCMPT, False, PC, c0)
                        if h == 1 and pend_proj is not None:
                            proj(*pend_proj)
                        maskmul(PC, nct - 1, CM[:, i % 16, :], CM.r0)
                        if i % 16 == 0 and i > 0:
                            maskmul(PC, nct - 2, CM[:, 16, :], CM.r0)
                        otc = OT[oti % 2]
                        oti += 1
                        items = [(otc[0:65, :], VCMP[:, h, ct, :], PC[:, ct, :], ct == 0, ct == nct - 1) for ct in range(nct)]
                        P.op("pe", mmgroup(items), reads=[VCMP.r0, PC.r0], writes=[otc.r0])
                        items = []
                        for g in range(4):
                            for ct in range(nct):
                                items.append((UB[:, g, 0:NJ], PC[:, ct, g * 128:(g + 1) * 128], MM[:, ct, 0:NJ], ct == 0, ct == nct - 1))
                        P.op("pe", mmgroup(items), reads=[PC.r0, MM.r0], writes=[UB.r0])
                        P.op("dve", C("tensor_reduce", out=rs4[:], in_=UB[:, :, 0:NJ], axis=AX.X, op=ALU.add), reads=[UB.r0], writes=[rs4.r0])
                        P.op("dve", C("tensor_scalar", rs4[:], rs4[:], 0.5, None, op0=ALU.mult), reads=[rs4.r0], writes=[rs4.r0])
                        P.op("dve", C("tensor_scalar", rs4[:], rs4[:], TINY, None, op0=ALU.max), reads=[rs4.r0], writes=[rs4.r0])
                        P.op("dve", C("reciprocal", rs4[:], rs4[:]), reads=[rs4.r0], writes=[rs4.r0])
                        P.op("dve", C("tensor_scalar", psl[:, 0:NJ], UB[:, 0, 0:NJ], rs4[:, 0:1], None, op0=ALU.mult), reads=[UB.r0, rs4.r0], writes=[psl.r0])
                        for g in range(1, 4):
                            P.op("dve", C("scalar_tensor_tensor", out=psl[:, 0:NJ], in0=UB[:, g, 0:NJ], scalar=rs4[:, g:g + 1], in1=psl[:, 0:NJ],
                                                                          op0=ALU.mult, op1=ALU.add), reads=[UB.r0, rs4.r0, psl.r0], writes=[psl.r0])
                        o_tab = 127 - 2 * i
                        P.op("dve", C("tensor_tensor", out=psl[:, 0:NJ], in0=psl[:, 0:NJ], in1=TK[:, o_tab:o_tab + NJ], op=ALU.mult),
                             reads=[psl.r0, TK.r0], writes=[psl.r0])
                        P.op("dve", C("tensor_tensor", out=psl[:, 0:NJ], in0=psl[:, 0:NJ], in1=TA[:, o_tab:o_tab + NJ], op=ALU.add),
                             reads=[psl.r0, TA.r0], writes=[psl.r0])
                        P.op("dve", C("memset", psl[:, 0:1], 3e9), writes=[psl.r0])
                        P.op("dve", C("max", out=m8a[:], in_=psl[:, 0:NJ]), reads=[psl.r0], writes=[m8a.r0])
                        P.op("dve", C("match_replace", out=sc2[:, 0:NJ], in_to_replace=m8a[:], in_values=psl[:, 0:NJ], imm_value=-1e9),
                             reads=[psl.r0, m8a.r0], writes=[sc2.r0])
                        P.op("dve", C("max", out=m8b[:], in_=sc2[:, 0:NJ]), reads=[sc2.r0], writes=[m8b.r0])
                        P.op("dve", C("tensor_reduce", out=m8a[:, 0:1], in_=m8b[:], axis=AX.X, op=ALU.min), reads=[m8b.r0], writes=[m8a.r0])
                        P.op("dve", C("tensor_scalar", sc2[:, 0:NJ], psl[:, 0:NJ], m8a[:, 0:1], None, op0=ALU.is_lt),
                             reads=[psl.r0, m8a.r0], writes=[sc2.r0])
                        P.op("dve", C("tensor_scalar", selb[:, 0:NJ], sc2[:, 0:NJ], NEGB, None, op0=ALU.mult),
                             reads=[sc2.r0], writes=[selb.r0])

                        def dense_branch(kts_all, ksrc, vsrc, bias, masks, meng="dve"):
                            nonlocal psi, oti
                            ot = OT[oti % 2]
                            oti += 1
                            nk = len(kts_all)
                            pend = None

                            def pv(kts, pbuf):
                                items = [(ot[0:65, :], vsrc[:, kt, h, :], pbuf[:, e_, :], kt == kts_all[0], kt == kts_all[-1])
                                         for e_, kt in enumerate(kts)]
                                P.op("pe", mmgroup(items), reads=[vsrc.r0, pbuf.r0], writes=[ot.r0])
                            for c0 in range(0, nk, 2):
                                kts = kts_all[c0:c0 + 2]
                                pbuf = PS_[psi % 2]
                                psi += 1
                                score_tiles(kts, ksrc, bias, pbuf, 0)
                                for e_, kt in enumerate(kts):
                                    if kt in masks:
                                        maskmul(pbuf, e_, MASKS[:, masks[kt], :], MASKS.r0, eng=meng)
                                if pend is not None:
                                    pv(*pend)
                                pend = (kts, pbuf)
                            pv(*pend)
                            return ot

                        def combine(ot, br, first):
                            nonlocal ubi
                            ub = UBR[ubi % 3]
                            ubi += 1
                            P.op("dve", C("tensor_copy", ub[:], ot[0:65, :]), reads=[ot.r0], writes=[ub.r0])
                            P.op("dve", C("tensor_scalar", wrow[64:65, :], ub[64:65, :], TINY, None, op0=ALU.max), reads=[ub.r0], writes=[wrow.r0])
                            P.op("act", C("activation", out=wrow[64:65, :], in_=wrow[64:65, :], func=AF.Ln), reads=[wrow.r0], writes=[wrow.r0])
                            P.op("act", C("activation", out=wrow[64:65, :], in_=wrow[64:65, :], func=AF.Exp, scale=-1.0), reads=[wrow.r0], writes=[wrow.r0])
                            P.op("dve", C("tensor_tensor", out=wrow[64:65, :].rearrange("p (g q) -> p g q", g=4),
                                                                       in0=wrow[64:65, :].rearrange("p (g q) -> p g q", g=4),
                                                                       in1=gr[64:65, br * 8 + h * 4: br * 8 + h * 4 + 4, :], op=ALU.mult),
                                 reads=[wrow.r0, gr.r0], writes=[wrow.r0])
                            bc = BCB[ubi % 2]
                            ws = ubi % 6
                            P.dma("sp", WSCR[ws:ws + 1, :], wrow[64:65, :], reads=[wrow.r0], writes=[wscr_res[ws]])
                            P.dma("sp", bc[:], WSCR[ws:ws + 1, :].to_broadcast([64, 512]), reads=[wscr_res[ws]], writes=[bc.r0])
                            if first:
                                P.op("dve", C("tensor_tensor", out=obf[:], in0=ub[0:64, :], in1=bc[:], op=ALU.mult),
                                     reads=[ub.r0, bc.r0], writes=[obf.r0])
                            else:
                                P.op("dve", C("tensor_tensor", out=otmp[:], in0=ub[0:64, :], in1=bc[:], op=ALU.mult),
                                     reads=[ub.r0, bc.r0], writes=[otmp.r0])
                                P.op("pool", C("tensor_tensor", out=obf[:], in0=obf[:], in1=otmp[:], op=ALU.add),
                                     reads=[obf.r0, otmp.r0], writes=[obf.r0])

                        wm = {i: 1}
                        if i - 4 >= 0:
                            wm[i - 4] = 2
                        otw = dense_branch(list(range(max(0, i - 4), i + 1)), KW, VW, False, wm, meng="pool")
                        P.op("pe", mmgroup([(MISC[0:NJ, 0:128], selb[:, 0:NJ], IDN, True, True)]), reads=[selb.r0, MATS.r0], writes=[MISC.r0])
                        P.op("dve", C("tensor_copy", selT[0:NJ, :, :], MISC[0:NJ, 0:128].unsqueeze(1).to_broadcast([NJ, 4, 128])),
                             reads=[MISC.r0], writes=[selT.r0])
                        combine(otc, 0, True)
                        ots = dense_branch(list(range(0, i + 1)), KS, VS, True, {i: 1})
                        combine(otw, 2, False)
                        combine(ots, 1, False)
                        P.op("pool", C("tensor_copy", OBt[0:64, h * 4:(h + 1) * 4, :], obf[:].rearrange("p (g q) -> p g q", g=4)),
                             reads=[obf.r0], writes=[OBt.r0])
                        if pend_proj is not None and h == 1:
                            proj2(*pend_proj)
                            pend_proj = None
                    pend_proj = (oa_, mg, xt, OBt, tsl, stq)
                if pend_proj is not None:
                    proj(*pend_proj)
                    proj2(*pend_proj)
                P.barrier()

        def phase_F(l, dst, dst_name):
            NF = DFF // 128
            with ExitStack() as cx:
                WU = sbuf(cx, "f_wu", [128, 8, 2 * DFF], BF16)
                WD = sbuf(cx, "f_wd", [128, NF, D], BF16)
                with ExitStack() as cx2:
                    H4 = 2 * DFF // 4
                    load_cast(cx2, lambda i: WU[:, i // 4, (i % 4) * H4:(i % 4 + 1) * H4],
                              lambda i: w_up[l, (i // 4) * 128:(i // 4 + 1) * 128, (i % 4) * H4:(i % 4 + 1) * H4], 32, [128, H4], WU.r0, "wu")
                    load_cast(cx2, lambda i: WD[:, i, :], lambda i: w_down[l, i * 128:(i + 1) * 128, :], NF, [128, D], WD.r0, "wd")
                    P.barrier()
                gT = sbuf(cx, "f_gT", [128, 8], F32)
                P.dma("sp", gT[:], norm_ffn_g[l].rearrange("(kc p) -> p kc", p=128), writes=[gT.r0], slow=True)
                CW = sbuf(cx, "f_cw", [128, 3, NF], F32)
                CB = sbuf(cx, "f_cb", [128, NF], F32)
                for k in range(3):
                    P.dma("sp", CW[:, k, :], conv_w[l, k].rearrange("(fc p) -> p fc", p=128), writes=[CW.r0], slow=True)
                P.dma("sp", CB[:], conv_b[l].rearrange("(fc p) -> p fc", p=128), writes=[CB.r0], slow=True)
                HAL = sbuf(cx, "f_hal", [128, NF, 2], F32)
                P.op("pool", C("memset", HAL[:], 0.0), writes=[HAL.r0])
                hT = sbuf(cx, "f_hT", [128, 8, 512], BF16)
                xts = [sbuf(cx, f"f_x{i}", [128, D], F32) for i in range(4)]
                junk = sbuf(cx, "f_nj", [128, D], BF16)
                ss = sbuf(cx, "f_ss", [128, 1], F32)
                rs = sbuf(cx, "f_rs", [128, 1], F32)
                xs = sbuf(cx, "f_xs", [128, D], BF16)
                tp = psum(cx, "f_tp", [128, 8, 128])
                pg = [psum(cx, f"f_pg{i}", [128, 512]) for i in range(2)]
                pu = [psum(cx, f"f_pu{i}", [128, 512]) for i in range(2)]
                pz = psum(cx, "f_pz", [128, 2, 512])
                Gt = [sbuf(cx, f"f_gt{i}", [128, 514], F32) for i in range(2)]
                cv = [sbuf(cx, f"f_cv{i}", [128, 512], F32) for i in range(2)]
                sl = [sbuf(cx, f"f_sl{i}", [128, 512], F32) for i in range(2)]
                actT = sbuf(cx, "f_act", [128, NF, 512], BF16, nres=NF)
                xo = [sbuf(cx, f"f_xo{i}", [128, D], F32) for i in range(1)]
                for st in range(NST):
                    for tt in range(4):
                        norm_to_hT((xts[tt], junk, ss, rs, xs, tp), gT, XR1, st * 512 + tt * 128, hT, hT.r0, tt)
                    for fc in range(NF):
                        b = fc % 2
                        items = [(pg[b][:], WU[:, kc, fc * 128:(fc + 1) * 128], hT[:, kc, :], kc == 0, kc == 7) for kc in range(8)]
                        P.op("pe", mmgroup(items), reads=[WU.r0, hT.r0], writes=[pg[b].r0])
                        items = [(pu[b][:], WU[:, kc, DFF + fc * 128:DFF + (fc + 1) * 128], hT[:, kc, :], kc == 0, kc == 7) for kc in range(8)]
                        P.op("pe", mmgroup(items), reads=[WU.r0, hT.r0], writes=[pu[b].r0])
                        g_ = Gt[b]
                        P.op("pool", C("tensor_copy", g_[:, 0:2], HAL[:, fc, :]), reads=[HAL.r0], writes=[g_.r0])
                        P.op("act", C("activation", out=g_[:, 2:514], in_=pg[b][:], func=AF.Identity), reads=[pg[b].r0], writes=[g_.r0])
                        P.op("pool", C("tensor_copy", HAL[:, fc, :], g_[:, 512:514]), reads=[g_.r0], writes=[HAL.r0])
                        c_ = cv[b]
                        P.op("dve", C("tensor_scalar", c_[:], g_[:, 2:514], CW[:, 2, fc:fc + 1], CB[:, fc:fc + 1], op0=ALU.mult, op1=ALU.add),
                             reads=[g_.r0, CW.r0, CB.r0], writes=[c_.r0])
                        P.op("dve", C("scalar_tensor_tensor", out=c_[:], in0=g_[:, 1:513], scalar=CW[:, 1, fc:fc + 1], in1=c_[:], op0=ALU.mult, op1=ALU.add),
                             reads=[g_.r0, CW.r0, c_.r0], writes=[c_.r0])
                        P.op("dve", C("scalar_tensor_tensor", out=c_[:], in0=g_[:, 0:512], scalar=CW[:, 0, fc:fc + 1], in1=c_[:], op0=ALU.mult, op1=ALU.add),
                             reads=[g_.r0, CW.r0, c_.r0], writes=[c_.r0])
                        s_ = sl[b]
                        P.op("act", C("activation", out=s_[:], in_=c_[:], func=AF.Silu), reads=[c_.r0], writes=[s_.r0])
                        P.op("dve", C("tensor_tensor", out=actT[:, fc, :], in0=s_[:], in1=pu[b][:], op=ALU.mult),
                             reads=[s_.r0, pu[b].r0], writes=[actT.res[fc]])
                    for tt in range(4):
                        t0 = st * 512 + tt * 128
                        items = []
                        for hf in range(2):
                            for fc in range(NF):
                                items.append((pz[:, hf, :], actT[:, fc, tt * 128:(tt + 1) * 128], WD[:, fc, hf * 512:(hf + 1) * 512], fc == 0, fc == NF - 1))
                        P.op("pe", mmgroup(items), reads=[WD.r0] + actT.res, writes=[pz.r0])
                        o = xo[0]
                        P.op("dve", C("tensor_tensor", out=o[:], in0=xts[tt][:], in1=pz[:].rearrange("p a b -> p (a b)"), op=ALU.add),
                             reads=[xts[tt].r0, pz.r0], writes=[o.r0])
                        P.dma("pool", dst[t0:t0 + 128, :], o[:], reads=[o.r0], writes=[dres[dst_name][st]])
                P.barrier()

        src, src_name = x_in, None
        for l in range(DEPTH):
            phase_P(l, src, src_name)
            if stop_after == "P":
                break
            with ExitStack() as lc:
                KCMPT = sbuf(lc, "KCMPT", [128, NCT * 128], BF16)
                VCMP = sbuf(lc, "VCMP", [128, 2, NCT, 65], BF16)
                phase_C(l, KCMPT, VCMP)
                if stop_after != "C":
                    phase_A1(l)
                    if stop_after != "A1":
                        phase_A2(l, src, src_name, KCMPT, VCMP)
            if stop_after in ("C", "A1", "A2"):
                break
            last = (l == DEPTH - 1)
            phase_F(l, y_out if last else XR2, "Y" if last else "XR2")
            src, src_name = XR2, "XR2"

        final_ev = P.all_events()
        with nc.Block() as block:
            @block.tensor
            def _(e):
                P.replay("pe", e)

            @block.scalar
            def _(e):
                P.replay("act", e)

            @block.vector
            def _(e):
                P.replay("dve", e)

            @block.gpsimd
            def _(e):
                P.replay("pool", e)

            @block.sync
            def _(e):
                P.replay("sp", e)
                for sk, v in final_ev:
                    e.wait_ge(P.sem[sk], v)
        nc._n_rec = P.n_inst
    return nc


_CACHE = {}


def _get_prog(S, depth):
    key = (S, depth)
    if key not in _CACHE:
        _CACHE[key] = build(S, depth)
    return _CACHE[key]


WNAMES = ("norm_mix_g", "w_in", "a_q_g", "a_k_g", "b_q_g", "b_k_g", "cmp_pos", "cmp_w1", "cmp_w2",
          "w_proj_a", "w_proj_b", "w_out", "norm_ffn_g", "w_up", "conv_w", "conv_b", "w_down")


def kernel(**inputs):
    x = np.ascontiguousarray(np.asarray(inputs["x"], dtype=np.float32))
    B, S, _ = x.shape
    depth = inputs["w_in"].shape[0]
    consts = host_consts(S)
    ws = {k: np.ascontiguousarray(np.asarray(inputs[k], dtype=np.float32)) for k in WNAMES}
    n = 8
    if FUSED:
        nc = _get_prog(S, depth)
        in_maps = []
        for c in range(n):
            m = {"x": x[c % B]}
            m.update(ws)
            m.update(consts)
            in_maps.append(m)
        res = run_bass_kernel_spmd(nc, in_maps, core_ids=list(range(n)))
        return np.stack([res.results[b]["y"] for b in range(B)], axis=0).astype(np.float32)
    nc = _get_prog(S, 1)
    cur = [x[c % B] for c in range(n)]
    for l in range(depth):
        in_maps = []
        for c in range(n):
            m = {"x": np.ascontiguousarray(cur[c])}
            m.update({k: np.ascontiguousarray(v[l:l + 1]) for k, v in ws.items()})
            m.update(consts)
            in_maps.append(m)
        res = run_bass_kernel_spmd(nc, in_maps, core_ids=list(range(n)))
        cur = [res.results[c]["y"] for c in range(n)]
    return np.stack([cur[b] for b in range(B)], axis=0).astype(np.float32)
```
